# Optimizing a Trainium2 kernel written in Bass

```python
import jax, jax.numpy as jnp
from jax import lax
import numpy as np

D_MODEL = 2048
BATCH = 2
SEQ = 4096
DEPTH = 1

CHUNK = 64
Q_BLOCK = 128
PLE_DIM = 256
NORM_EPS = 1e-6

RW_HEADS = 16
RW_HEAD_DIM = 64
RW_WIDTH = RW_HEADS * RW_HEAD_DIM
RW_DECAY_LORA = 64
RW_A_LORA = 64
RW_GATE_LORA = 160
RW_GN_EPS = 64e-5
RW_COLS = 3 * RW_WIDTH + RW_DECAY_LORA + RW_A_LORA + RW_GATE_LORA
RW_SPLITS = (RW_WIDTH, 2 * RW_WIDTH, 3 * RW_WIDTH, 3 * RW_WIDTH + RW_DECAY_LORA, 3 * RW_WIDTH + RW_DECAY_LORA + RW_A_LORA)

MLA_HEADS = 8
MLA_Q_RANK = 512
MLA_KV_RANK = 512
MLA_NOPE = 128
MLA_ROPE = 64
MLA_V = 128
MLA_COLS = MLA_Q_RANK + MLA_KV_RANK + MLA_ROPE
ROPE_BASE = 10000.0

GATE_COLS = 2 * D_MODEL
IN_COLS = RW_COLS + MLA_COLS + GATE_COLS

D_FF = 5632
CONV_W = 3

kernel_name = 'hybrid_rwkv7_mla_convffn_block'


def rms_norm(x, g):
    xf = x.astype(jnp.float32)
    y = xf * lax.rsqrt(jnp.mean(xf * xf, axis=-1, keepdims=True) + NORM_EPS)
    return (y * g.astype(jnp.float32)).astype(x.dtype)


def token_shift(z):
    return jnp.pad(z, ((0, 0), (1, 0), (0, 0)))[:, :-1]


def rope(x, cos, sin):
    x1, x2 = jnp.split(x, 2, axis=-1)
    return jnp.concatenate([x1 * cos - x2 * sin, x2 * cos + x1 * sin], axis=-1)


def rwkv7_scan(r, decay, k, v, kk, a):
    B, S, H, N = r.shape

    def step(state, inp):
        r_t, w_t, k_t, v_t, kk_t, a_t = inp
        sa = jnp.einsum('bhvk,bhk->bhv', state, -kk_t)
        state = (state * w_t[:, :, None, :]
                 + sa[..., None] * (kk_t * a_t)[:, :, None, :]
                 + v_t[..., None] * k_t[:, :, None, :])
        y_t = jnp.einsum('bhvk,bhk->bhv', state, r_t)
        return state, y_t

    xs = tuple(jnp.moveaxis(t.astype(jnp.float32), 1, 0) for t in (r, decay, k, v, kk, a))
    s0 = jnp.zeros((B, H, N, N), jnp.float32)
    _, y = lax.scan(step, s0, xs)
    return jnp.moveaxis(y, 0, 1)


def rwkv7_mixer(z, mu, w0, w2, a0, a2, g2, k_k, k_a, r_k, lnx_w, lnx_b):
    B, S, _ = z.shape
    z = z + (token_shift(z) - z) * mu
    r, k, v, wd, ad, gd = jnp.split(z, RW_SPLITS, axis=-1)
    w_log = -jax.nn.softplus(-(w0 + jnp.tanh(wd) @ w2).astype(jnp.float32)) - 0.5
    decay = jnp.exp(-jnp.exp(w_log))
    a = jax.nn.sigmoid(a0 + ad @ a2)
    g = jax.nn.sigmoid(gd) @ g2

    def heads(t):
        return t.reshape(B, S, RW_HEADS, RW_HEAD_DIM)

    kk = heads(k * k_k).astype(jnp.float32)
    kk = kk / jnp.maximum(jnp.sqrt(jnp.sum(kk * kk, axis=-1, keepdims=True)), 1e-12)
    k = k * (1 + (a - 1) * k_a)
    r_h, k_h, v_h, a_h = heads(r), heads(k), heads(v), heads(a)
    y = rwkv7_scan(r_h, heads(decay), k_h, v_h, kk, a_h)
    mean = jnp.mean(y, axis=-1, keepdims=True)
    var = jnp.mean(jnp.square(y - mean), axis=-1, keepdims=True)
    y = ((y - mean) * lax.rsqrt(var + RW_GN_EPS)).reshape(B, S, RW_WIDTH)
    y = (y * lnx_w.astype(jnp.float32) + lnx_b.astype(jnp.float32)).astype(z.dtype)
    bonus = (jnp.sum(r_h * k_h * r_k, axis=-1, keepdims=True) * v_h).reshape(B, S, RW_WIDTH)
    return (y + bonus) * g


def chunk_causal_attention(q, k, v):
    B, S, H, Dqk = q.shape
    Dv = v.shape[-1]
    nb = S // Q_BLOCK
    scale = Dqk ** -0.5
    kf = k.astype(jnp.float32)
    vf = v.astype(jnp.float32)
    key_chunk = jnp.arange(S) // CHUNK
    qb = q.reshape(B, nb, Q_BLOCK, H, Dqk).transpose(1, 0, 3, 2, 4)

    def one_block(args):
        q_blk, i = args
        s = jnp.einsum('bhqd,bkhd->bhqk', q_blk.astype(jnp.float32), kf) * scale
        q_chunk = (i * Q_BLOCK + jnp.arange(Q_BLOCK)) // CHUNK
        mask = key_chunk[None, :] <= q_chunk[:, None]
        s = jnp.where(mask, s, -1e30)
        pr = jax.nn.softmax(s, axis=-1)
        return jnp.einsum('bhqk,bkhd->bqhd', pr, vf)

    o = lax.map(one_block, (qb, jnp.arange(nb)))
    return o.transpose(1, 0, 2, 3, 4).reshape(B, S, H * Dv).astype(v.dtype)


def mla_mixer(z, cos, sin, q_norm, w_q_up, kv_norm, w_kv_up):
    B, S, _ = z.shape
    cq, ckv, k_pe = jnp.split(z, (MLA_Q_RANK, MLA_Q_RANK + MLA_KV_RANK), axis=-1)
    q = (rms_norm(cq, q_norm) @ w_q_up).reshape(B, S, MLA_HEADS, MLA_NOPE + MLA_ROPE)
    kv = (rms_norm(ckv, kv_norm) @ w_kv_up).reshape(B, S, MLA_HEADS, MLA_NOPE + MLA_V)
    q_nope, q_pe = jnp.split(q, (MLA_NOPE,), axis=-1)
    k_nope, v = jnp.split(kv, (MLA_NOPE,), axis=-1)
    q_pe = rope(q_pe, cos[:, :, None, :], sin[:, :, None, :])
    k_pe = rope(k_pe, cos, sin)[:, :, None, :]
    q = jnp.concatenate([q_nope, q_pe], axis=-1)
    k = jnp.concatenate([k_nope, jnp.broadcast_to(k_pe, (B, S, MLA_HEADS, MLA_ROPE))], axis=-1)
    return chunk_causal_attention(q, k, v)


def causal_dwconv(x, w, b):
    C = x.shape[-1]
    y = lax.conv_general_dilated(x, w[:, None, :], window_strides=(1,), padding=[(CONV_W - 1, 0)],
                                 dimension_numbers=('NWC', 'WIO', 'NWC'), feature_group_count=C)
    return y + b


def setup_inputs(seed: int = 0) -> dict:
    key = jax.random.key(seed)
    ks = iter(jax.random.split(key, 48))
    L, D = DEPTH, D_MODEL

    def nrm(shape, fan_in, s=1.0):
        return jax.random.normal(next(ks), shape, jnp.float32) * (s * fan_in ** -0.5)

    def gain(shape):
        return 1.0 + 0.02 * jax.random.normal(next(ks), shape, jnp.float32)

    def small(shape, s):
        return s * jax.random.normal(next(ks), shape, jnp.float32)

    def unif(shape, lo, hi):
        return jax.random.uniform(next(ks), shape, jnp.float32, lo, hi)

    x = jax.random.normal(next(ks), (BATCH, SEQ, D), jnp.float32)
    p = jax.random.normal(next(ks), (L, BATCH, SEQ, PLE_DIM), jnp.float32)
    start = jax.random.randint(next(ks), (BATCH, 1), 0, 4096, jnp.int32)
    positions = start + jnp.arange(SEQ, dtype=jnp.int32)[None, :]
    return {
        'x': x,
        'p': p,
        'positions': positions,
        'pre_mix_norm': gain((L, D)),
        'w_in': nrm((L, D, IN_COLS), D),
        'rw_mu': unif((L, RW_COLS), 0.0, 1.0),
        'rw_w0': unif((L, RW_WIDTH), -6.0, 1.0),
        'rw_w2': nrm((L, RW_DECAY_LORA, RW_WIDTH), RW_DECAY_LORA, 0.1),
        'rw_a0': small((L, RW_WIDTH), 0.5),
        'rw_a2': nrm((L, RW_A_LORA, RW_WIDTH), RW_A_LORA, 0.5),
        'rw_g2': nrm((L, RW_GATE_LORA, RW_WIDTH), RW_GATE_LORA),
        'rw_k_k': 0.85 + small((L, RW_WIDTH), 0.05),
        'rw_k_a': 1.0 + small((L, RW_WIDTH), 0.05),
        'rw_r_k': small((L, RW_HEADS, RW_HEAD_DIM), 0.1),
        'rw_lnx_w': gain((L, RW_WIDTH)),
        'rw_lnx_b': small((L, RW_WIDTH), 0.02),
        'mla_q_norm': gain((L, MLA_Q_RANK)),
        'mla_w_q_up': nrm((L, MLA_Q_RANK, MLA_HEADS * (MLA_NOPE + MLA_ROPE)), MLA_Q_RANK),
        'mla_kv_norm': gain((L, MLA_KV_RANK)),
        'mla_w_kv_up': nrm((L, MLA_KV_RANK, MLA_HEADS * (MLA_NOPE + MLA_V)), MLA_KV_RANK),
        'w_branch_rw': nrm((L, RW_WIDTH, D), RW_WIDTH),
        'w_branch_mla': nrm((L, MLA_HEADS * MLA_V, D), MLA_HEADS * MLA_V),
        'w_out': nrm((L, D, D), D),
        'post_mix_norm': gain((L, D)),
        'pre_ffn_norm': gain((L, D)),
        'w_up': nrm((L, D, 2 * D_FF), D),
        'conv_w': nrm((L, CONV_W, 2 * D_FF), CONV_W),
        'conv_b': small((L, 2 * D_FF), 0.02),
        'w_down': nrm((L, D_FF, D), D_FF),
        'post_ffn_norm': gain((L, D)),
        'w_ple': nrm((L, PLE_DIM, D), PLE_DIM),
        'w_ple_gate': nrm((L, D, D), D),
        'ple_norm': gain((L, D)),
    }


def reference(x, p, positions, pre_mix_norm, w_in, rw_mu, rw_w0, rw_w2, rw_a0, rw_a2, rw_g2,
              rw_k_k, rw_k_a, rw_r_k, rw_lnx_w, rw_lnx_b, mla_q_norm, mla_w_q_up, mla_kv_norm,
              mla_w_kv_up, w_branch_rw, w_branch_mla, w_out, post_mix_norm, pre_ffn_norm, w_up,
              conv_w, conv_b, w_down, post_ffn_norm, w_ple, w_ple_gate, ple_norm):
    inv_freq = ROPE_BASE ** (-jnp.arange(0, MLA_ROPE, 2, dtype=jnp.float32) / MLA_ROPE)
    ang = positions.astype(jnp.float32)[..., None] * inv_freq
    cos = jnp.cos(ang).astype(x.dtype)
    sin = jnp.sin(ang).astype(x.dtype)
    h = x
    for i in range(DEPTH):
        u = rms_norm(h, pre_mix_norm[i])
        z = u @ w_in[i]
        z_rw, z_mla, z_gate = jnp.split(z, (RW_COLS, RW_COLS + MLA_COLS), axis=-1)
        y_rw = rwkv7_mixer(z_rw, rw_mu[i], rw_w0[i], rw_w2[i], rw_a0[i], rw_a2[i], rw_g2[i],
                           rw_k_k[i], rw_k_a[i], rw_r_k[i], rw_lnx_w[i], rw_lnx_b[i])
        y_mla = mla_mixer(z_mla, cos, sin, mla_q_norm[i], mla_w_q_up[i], mla_kv_norm[i], mla_w_kv_up[i])
        g_rw, g_mla = jnp.split(jax.nn.sigmoid(z_gate), 2, axis=-1)
        mix = (g_rw * (y_rw @ w_branch_rw[i]) + g_mla * (y_mla @ w_branch_mla[i])) @ w_out[i]
        h = h + rms_norm(mix, post_mix_norm[i])
        u = rms_norm(h, pre_ffn_norm[i])
        up = causal_dwconv(u @ w_up[i], conv_w[i], conv_b[i])
        gate, val = jnp.split(up, 2, axis=-1)
        f = (jax.nn.gelu(gate, approximate=True) * val) @ w_down[i]
        h = h + rms_norm(f, post_ffn_norm[i])
        e = p[i] @ w_ple[i]
        h = h + rms_norm(jax.nn.sigmoid(h @ w_ple_gate[i]) * e, ple_norm[i])
    return h
```

```python
import contextlib
import numpy as np
import concourse.bass as bass
import concourse.mybir as mybir

F32 = mybir.dt.float32
BF16 = mybir.dt.bfloat16
I32 = mybir.dt.int32
ALU = mybir.AluOpType
AF = mybir.ActivationFunctionType
AX = mybir.AxisListType


class _Rec:
    def __init__(self):
        self.calls = []

    def __getattr__(self, name):
        def f(*a, **k):
            self.calls.append((name, a, k))
            return self

        return f


def _eager(fn):
    rec = _Rec()
    fn(rec)
    assert len(rec.calls) == 1, rec.calls
    name, a, k = rec.calls[0]
    return lambda e: getattr(e, name)(*a, **k)


class Prog:
    COMPUTE = ("pe", "act", "dve", "pool")

    def __init__(self, nc, stack):
        self.nc = nc
        self.stack = stack
        self.stack0 = stack
        self.streams = {e: [] for e in ("pe", "act", "dve", "pool", "sp")}
        self.sems = {}
        self.cnt = {}
        for e in self.COMPUTE:
            self.sems[e] = stack.enter_context(nc.semaphore("s_" + e))
            self.cnt[e] = 0
        self.seen = {e: {} for e in self.streams}
        self.res = {}
        self.dma_sems = {}
        self.n_ops = 0

    def sb(self, name, shape, dt):
        return self.stack.enter_context(self.nc.sbuf_tensor(name, list(shape), dt))

    def ps(self, name, shape, dt=F32):
        return self.stack.enter_context(self.nc.psum_tensor(name, list(shape), dt))

    def _dma_sem(self, key):
        if key not in self.dma_sems:
            self.dma_sems[key] = self.stack0.enter_context(self.nc.semaphore("d_" + key))
            self.sems["D:" + key] = self.dma_sems[key]
            self.cnt["D:" + key] = 0
        return "D:" + key

    def _deps(self, eng, reads, writes, exclude=None):
        need = {}
        for r in reads:
            ent = self.res.get(r)
            if ent:
                for k, v in ent[0].items():
                    need[k] = max(need.get(k, 0), v)
                if "ps" in r or "small" in r or "trp" in r:
                    for k, v in ent[1].items():
                        if k != eng:
                            need[k] = max(need.get(k, 0), v)
        for w in writes:
            ent = self.res.get(w)
            if ent:
                if not w.startswith("dram:"):
                    for k, v in ent[0].items():
                        need[k] = max(need.get(k, 0), v)
                for k, v in ent[1].items():
                    need[k] = max(need.get(k, 0), v)
        waits = []
        for k, v in need.items():
            if k == eng and eng == "pe":
                continue
            if k == exclude:
                continue
            if self.seen[eng].get(k, 0) >= v:
                continue
            self.seen[eng][k] = v
            waits.append((k, v))
        return waits

    def _mark(self, key, val, reads, writes):
        for r in reads:
            ent = self.res.setdefault(r, [{}, {}])
            ent[1][key] = val
        for w in writes:
            ent = self.res.setdefault(w, [{}, {}])
            if w.startswith("dram:"):
                ent[0][key] = val
            else:
                ent[0] = {key: val}
                ent[1] = {}
            self.res[w] = ent

    def op(self, eng, fn, reads=(), writes=()):
        fn = _eager(fn)
        waits = self._deps(eng, reads, writes)
        self.cnt[eng] += 1
        val = self.cnt[eng]
        sem = self.sems[eng]
        sems = self.sems

        def emit(e, waits=waits, fn=fn, sem=sem):
            for k, v in waits:
                e.wait_ge(sems[k], v)
            fn(e).then_inc(sem, 1)

        self.streams[eng].append(emit)
        self._mark(eng, val, reads, writes)
        self.n_ops += 1

    def dma(self, queue, out, in_, reads=(), writes=(), key=None, **kw):
        assert key is not None
        sk = self._dma_sem(key)
        waits = self._deps(queue, reads, writes, exclude=sk)
        self.cnt[sk] += 16
        val = self.cnt[sk]
        sem = self.sems[sk]
        sems = self.sems

        def emit(e, waits=waits, sem=sem):
            for k, v in waits:
                e.wait_ge(sems[k], v)
            e.dma_start(out=out, in_=in_, **kw).then_inc(sem, 16)

        self.streams[queue].append(emit)
        self._mark(sk, val, reads, writes)
        self.n_ops += 1

    def custom(self, queue, fn, reads=(), writes=(), key=None):
        fn = _eager(fn)
        sk = self._dma_sem(key)
        waits = self._deps(queue, reads, writes, exclude=sk)
        self.cnt[sk] += 16
        val = self.cnt[sk]
        sem = self.sems[sk]
        sems = self.sems

        def emit(e, waits=waits, sem=sem):
            for k, v in waits:
                e.wait_ge(sems[k], v)
            fn(e).then_inc(sem, 16)

        self.streams[queue].append(emit)
        self._mark(sk, val, reads, writes)
        self.n_ops += 1

    def barrier(self):
        snap = dict(self.cnt)
        sems = self.sems
        for eng in self.streams:
            waits = []
            for k, v in snap.items():
                if v <= 0 or self.seen[eng].get(k, 0) >= v:
                    continue
                if k == eng and eng == "pe":
                    continue
                self.seen[eng][k] = v
                waits.append((k, v))

            def emit(e, waits=waits):
                for k, v in waits:
                    e.wait_ge(sems[k], v)

            self.streams[eng].append(emit)

    def wait_all(self, eng, resources):
        waits = self._deps(eng, resources, ())
        sems = self.sems

        def emit(e, waits=waits):
            for k, v in waits:
                e.wait_ge(sems[k], v)

        self.streams[eng].append(emit)

    def emit(self):
        nc = self.nc
        with nc.Block() as block:
            @block.tensor
            def _(e):
                for f in self.streams["pe"]:
                    f(e)

            @block.scalar
            def _(e):
                for f in self.streams["act"]:
                    f(e)

            @block.vector
            def _(e):
                for f in self.streams["dve"]:
                    f(e)

            @block.gpsimd
            def _(e):
                for f in self.streams["pool"]:
                    f(e)

            @block.sync
            def _(e):
                for f in self.streams["sp"]:
                    f(e)

import math
from concourse.bass_utils import run_bass_kernel_spmd

T = 4096
D = 2048
TB = 512
NB = T // TB
DFF = 5632
EPS = 1e-6
GN_EPS = 64e-5
NV = 528
DBG = {}


def vec_pc(v):
    v = np.asarray(v, np.float32).reshape(-1)
    return np.ascontiguousarray(v.reshape(-1, 128).T)


def pack_vecs(inp):
    V = np.zeros((128, NV), np.float32)
    V[:, 0:16] = vec_pc(inp["pre_mix_norm"])
    V[:, 16:32] = vec_pc(inp["post_mix_norm"])
    V[:, 32:48] = vec_pc(inp["pre_ffn_norm"])
    V[:, 48:64] = vec_pc(inp["post_ffn_norm"])
    V[:, 64:80] = vec_pc(inp["ple_norm"])
    mu = np.asarray(inp["rw_mu"], np.float32).reshape(-1)
    V[:, 80:104] = vec_pc(mu[0:3072])
    V[:, 104:112] = vec_pc(inp["rw_w0"])
    V[:, 112:120] = vec_pc(inp["rw_a0"])
    V[:, 120:128] = vec_pc(inp["rw_k_k"])
    V[:, 128:136] = vec_pc(inp["rw_k_a"])
    V[:, 136:144] = vec_pc(inp["rw_r_k"])
    V[:, 144:152] = vec_pc(inp["rw_lnx_w"])
    V[:, 152:160] = vec_pc(inp["rw_lnx_b"])
    V[:, 160:164] = vec_pc(inp["mla_q_norm"])
    V[:, 164:168] = vec_pc(inp["mla_kv_norm"])
    V[:, 168:256] = vec_pc(inp["conv_b"])
    cw = np.asarray(inp["conv_w"], np.float32).reshape(3, -1)
    V[:, 256:344] = vec_pc(cw[0])
    V[:, 344:432] = vec_pc(cw[1])
    V[:, 432:520] = vec_pc(cw[2])
    V[0:64, 520] = mu[3072:3136]
    V[0:64, 521] = mu[3136:3200]
    V[0:128, 522] = mu[3200:3328]
    V[0:32, 523] = mu[3328:3360]
    invf = (10000.0 ** (-np.arange(0, 64, 2, dtype=np.float32) / 64)).astype(np.float32)
    V[0:32, 524] = invf
    V[32:64, 524] = invf
    return V


def build_program(debug=()):
    nc = bass.Bass("TRN2", target_bir_lowering=False)
    DBG.clear()
    for k_ in debug:
        if "=" in k_:
            DBG[k_.split("=")[0]] = int(k_.split("=")[1])
        else:
            DBG[k_] = 1

    def din(name, shape, dt=F32):
        return nc.dram_tensor(name, list(shape), dt, kind="ExternalInput").ap()

    def dscr(name, shape, dt):
        if name in debug:
            return nc.dram_tensor(name, list(shape), dt, kind="ExternalOutput").ap()
        return nc.dram_tensor(name, list(shape), dt).ap()

    x = din("x", [T, D])
    p_in = din("p", [T, 256])
    pos = din("pos", [1, T], I32)
    vecs = din("vecs", [128, NV])
    w_in = din("w_in", [D, 8544])
    rw_w2 = din("rw_w2", [64, 1024])
    rw_a2 = din("rw_a2", [64, 1024])
    rw_g2 = din("rw_g2", [160, 1024])
    w_q_up = din("w_q_up", [512, 1536])
    w_kv_up = din("w_kv_up", [512, 2048])
    w_brw = din("w_brw", [1024, D])
    w_bmla = din("w_bmla", [1024, D])
    w_out = din("w_out", [D, D])
    w_up = din("w_up", [D, 2 * DFF])
    w_down = din("w_down", [DFF, D])
    w_ple = din("w_ple", [256, D])
    w_pg = din("w_pg", [D, D])
    out = nc.dram_tensor("out", [T, D], F32, kind="ExternalOutput").ap()

    hT = dscr("hT", [D, T], F32)
    pT = dscr("pT", [256, T], F32)
    uT = dscr("uT", [D, T], BF16)
    zrwT = dscr("zrwT", [3360, T], F32)
    zmlaT = dscr("zmlaT", [1088, T], F32)
    gT = dscr("gT", [4096, T], BF16)
    yrwT = dscr("yrwT", [1024, T], BF16)
    ymlaT = dscr("ymlaT", [1024, T], BF16)
    cqnT = dscr("cqnT", [512, T], BF16)
    ckvnT = dscr("ckvnT", [512, T], BF16)
    mT = dscr("mT", [D, T], BF16)
    fT = dscr("fT", [D, T], F32)
    ffT = dscr("ffT", [DFF, T], BF16)

    with contextlib.ExitStack() as st0:
        P = Prog(nc, st0)
        V = P.sb("V", [128, NV], F32)
        P.dma("sp", V[:], vecs, writes=["V"], key="V")
        ident = P.sb("ident", [128, 128], F32)
        ident_bf = P.sb("ident_bf", [128, 128], BF16)
        ones_bf = P.sb("ones_bf", [128, 128], BF16)
        bones_bf = P.sb("bones_bf", [128, 128], BF16)
        maskP = P.sb("maskP", [128, 2, 128], F32)
        maskL = P.sb("maskL", [128, 64], F32)
        eye = P.sb("eye", [128, 64], F32)
        rmask = P.sb("rmask", [128, TB], F32)
        omka = P.sb("omka", [128, 8], F32)
        su = P.sb("su", [128, 64], F32)
        ui = P.sb("ui", [128, 64], F32)

        def pool(fn, reads=(), writes=()):
            P.op("pool", fn, reads, writes)

        pool(lambda e: e.memset(ident[:], 1.0), writes=["ident"])
        pool(lambda e: e.affine_select(out=ident[:], in_=ident[:], pattern=[[-1, 128]], compare_op=ALU.is_equal,
                                       fill=0.0, base=0, channel_multiplier=1), reads=["ident"], writes=["ident"])
        pool(lambda e: e.tensor_copy(out=ident_bf[:], in_=ident[:]), reads=["ident"], writes=["ident_bf"])
        pool(lambda e: e.memset(ones_bf[:], 1.0), writes=["ones_bf"])
        pool(lambda e: e.memset(bones_bf[:], 0.0), writes=["bones_bf"])
        pool(lambda e: e.memset(bones_bf[0:64, 0:64], 1.0), reads=["bones_bf"], writes=["bones_bf"])
        pool(lambda e: e.memset(bones_bf[64:128, 64:128], 1.0), reads=["bones_bf"], writes=["bones_bf"])
        for (tl, pat, cm, b0, b1, op_) in ((su, 1, -1, -1, -1, ALU.is_ge), (ui, 1, -1, 0, 0, ALU.is_ge),
                                           (maskL, -1, 1, -1, -1, ALU.is_ge), (eye, -1, 1, 0, 0, ALU.is_equal)):
            pool(lambda e, tl=tl: e.memset(tl[:], 1.0), writes=["msk"])
            for h, bb in ((0, b0), (1, b1)):
                sl = slice(64 * h, 64 * h + 64)
                pool(lambda e, tl=tl, sl=sl, pat=pat, cm=cm, bb=bb, op_=op_: e.affine_select(
                    out=tl[sl, :], in_=tl[sl, :], pattern=[[pat, 64]], compare_op=op_, fill=0.0, base=bb,
                    channel_multiplier=cm), reads=["msk"], writes=["msk"])
        for xx in range(2):
            pool(lambda e, xx=xx: e.tensor_copy(out=maskP[:, xx, 0:64], in_=su[:]), reads=["msk"], writes=["msk"])
            pool(lambda e, xx=xx: e.tensor_copy(out=maskP[:, xx, 64:128], in_=ui[:]), reads=["msk"], writes=["msk"])
        pool(lambda e: e.memset(rmask[:], 1.0), reads=["msk"], writes=["msk"])
        for c in range(8):
            pool(lambda e, c=c: e.memset(rmask[:, c * 64:c * 64 + 1], 0.0), reads=["msk"], writes=["msk"])
        P.op("dve", lambda e: e.tensor_scalar(out=omka[:], in0=V[:, 128:136], scalar1=-1.0, scalar2=1.0, op0=ALU.mult,
                                              op1=ALU.add), reads=["V", "msk"], writes=["omka"])
        CONST = ["V", "msk", "ident", "ident_bf", "ones_bf", "bones_bf", "omka"]

        rr = {"evac": 0}

        class _Stop(Exception):
            pass

        def stop_if(name):
            if name in debug:
                raise _Stop()

        def evac_copy(out_ap, in_ap, reads, writes, scale=None):
            rr["evac"] += 1
            if scale is not None or rr["evac"] % 2 == 0:
                P.op("act", lambda e: e.activation(out=out_ap, in_=in_ap, func=AF.Copy,
                                                   scale=(1.0 if scale is None else scale)), reads, writes)
            else:
                P.op("dve", lambda e: e.tensor_copy(out=out_ap, in_=in_ap), reads, writes)

        def transpose_stage(tag, src, dst, R, C, src_res, dst_res):
            with contextlib.ExitStack() as st:
                prev_stack = P.stack
                P.stack = st
                nb = 2
                tin = [P.sb(f"{tag}_in{i}", [128, C], F32) for i in range(nb)]
                tout = [P.sb(f"{tag}_out{i}", [128, C // 128, 128], F32) for i in range(nb)]
                pst = [P.ps(f"{tag}_ps{i}", [128, 4, 128], F32) for i in range(2)]
                k = 0
                for r in range(R // 128):
                    b = r % nb
                    P.dma("sp", tin[b][:], src[r * 128:(r + 1) * 128, :], reads=[src_res], writes=[f"TR_in{b}"],
                          key=f"TR_in{b}")
                    for c4 in range(0, C // 128, 4):
                        pb = k % 2
                        k += 1
                        n4 = min(4, C // 128 - c4)
                        for i in range(n4):
                            c = c4 + i
                            P.op("pe", lambda e, pb=pb, i=i, c=c, b=b: e.transpose(
                                out=pst[pb][:, i, :], in_=tin[b][:, c * 128:(c + 1) * 128], identity=ident[:]),
                                reads=[f"TR_in{b}", "ident"], writes=[f"TR_ps{pb}"])
                        evac_copy(tout[b][:, c4:c4 + n4, :], pst[pb][:, 0:n4, :], [f"TR_ps{pb}"], [f"TR_out{b}"])
                    P.dma("sp", dst[:, r * 128:(r + 1) * 128].rearrange("(c p) t -> p c t", p=128), tout[b][:],
                          reads=[f"TR_out{b}"], writes=[dst_res], key=f"TR_st{b}")
                P.barrier()
                P.stack = prev_stack

        def rmsnorm_stage(tag, src, F, gcol, src_res, dst=None, dst_res=None, resid=None, resid_res=None):
            FC = F // 128
            with contextlib.ExitStack() as st:
                prev_stack = P.stack
                P.stack = st
                s = P.sb(f"{tag}_s", [128, FC, TB], F32)
                sq = P.sb(f"{tag}_sq", [128, FC, TB], BF16)
                rstd = P.sb(f"{tag}_rstd", [128, TB], F32)
                ps = P.ps(f"{tag}_ps", [128, TB], F32)
                if resid is None:
                    o = P.sb(f"{tag}_o", [128, FC, TB], BF16)
                else:
                    o = P.sb(f"{tag}_o", [128, FC, TB], F32)
                    hh = P.sb(f"{tag}_h", [128, FC, TB], F32)
                for j in range(NB):
                    cs = slice(j * TB, (j + 1) * TB)
                    P.dma("sp", s[:], src[:, cs].rearrange("(c p) t -> p c t", p=128), reads=[src_res],
                          writes=[f"RN_s"], key=f"RN_s")
                    if resid is not None:
                        P.dma("sp", hh[:], resid[:, cs].rearrange("(c p) t -> p c t", p=128), reads=[resid_res],
                              writes=[f"RN_h"], key=f"RN_h")
                    P.op("act", lambda e: e.activation(out=sq[:], in_=s[:], func=AF.Square), reads=[f"RN_s"],
                         writes=[f"RN_sq"])
                    for c in range(FC):
                        P.op("pe", lambda e, c=c: e.matmul(ps[:], lhsT=ones_bf[:], rhs=sq[:, c, :], start=(c == 0),
                                                           stop=(c == FC - 1)),
                             reads=[f"RN_sq", "ones_bf"], writes=[f"RN_ps"])
                    P.op("act", lambda e: e.activation(out=rstd[:], in_=ps[:], func=AF.Sqrt, bias=EPS, scale=1.0 / F),
                         reads=[f"RN_ps"], writes=[f"RN_rstd"])
                    P.op("dve", lambda e: e.reciprocal(out=rstd[:], in_=rstd[:]), reads=[f"RN_rstd"],
                         writes=[f"RN_rstd"])
                    for c in range(FC):
                        eng = "pool"
                        P.op("dve", lambda e, c=c: e.scalar_tensor_tensor(
                            out=o[:, c, :], in0=s[:, c, :], scalar=V[:, gcol + c:gcol + c + 1], in1=rstd[:],
                            op0=ALU.mult, op1=ALU.mult), reads=[f"RN_s", f"RN_rstd", "V"], writes=[f"RN_o{c}"])
                        if resid is not None:
                            P.op(eng, lambda e, c=c: e.tensor_tensor(out=o[:, c, :], in0=o[:, c, :], in1=hh[:, c, :],
                                                                     op=ALU.add),
                                 reads=[f"RN_o{c}", f"RN_h"], writes=[f"RN_o{c}"])
                    allo = [f"RN_o{c}" for c in range(FC)]
                    if resid is None:
                        P.dma("sp", dst[:, cs].rearrange("(c p) t -> p c t", p=128), o[:], reads=allo,
                              writes=[dst_res], key=f"RN_st")
                    else:
                        P.dma("sp", resid[:, cs].rearrange("(c p) t -> p c t", p=128), o[:], reads=allo,
                              writes=[resid_res], key=f"RN_st")
                P.barrier()
                P.stack = prev_stack

        def linear_stage(tag, srcs, groups, epi, wbuf_cols, nps=2):
            with contextlib.ExitStack() as st:
                prev_stack = P.stack
                P.stack = st
                wsb = []
                sbs = []
                for si, (src, K, sdt, sres, W) in enumerate(srcs):
                    KC = (K + 127) // 128
                    wsb.append([P.sb(f"{tag}_w{si}_{i}", [128, KC, wbuf_cols], BF16) for i in range(2)])
                    sbs.append([P.sb(f"{tag}_x{si}_{i}", [128, KC, TB], BF16) for i in range(2)])
                pss = [P.ps(f"{tag}_ps{i}", [128, TB], F32) for i in range(nps)]
                state = {"ps": 0, "xb": 0}
                for gi, grp in enumerate(groups):
                    wb = gi % 2
                    offs = []
                    off = 0
                    for (c0, w) in grp:
                        offs.append(off)
                        off += w
                    assert off <= wbuf_cols
                    ranges = []
                    for (c0, w), o_ in zip(grp, offs):
                        if ranges and ranges[-1][0] + ranges[-1][1] == c0 and ranges[-1][2] + ranges[-1][1] == o_:
                            ranges[-1][1] += w
                        else:
                            ranges.append([c0, w, o_])
                    for si, (src, K, sdt, sres, W) in enumerate(srcs):
                        KC = (K + 127) // 128
                        for (c0, w, o_) in ranges:
                            if False:
                                P.dma("pool", wsb[si][wb][:, :, o_:o_ + w],
                                      W[:, c0:c0 + w].rearrange("(k p) m -> p k m", p=128),
                                      writes=[f"LN_w{si}_{wb}"], key=f"LN_w{si}_{wb}")
                            else:
                                for kc in range(KC):
                                    kr = min(128, K - kc * 128)
                                    P.dma("pool", wsb[si][wb][0:kr, kc, o_:o_ + w], W[kc * 128:kc * 128 + kr, c0:c0 + w],
                                          writes=[f"LN_w{si}_{wb}"], key=f"LN_w{si}_{wb}")
                    for j in range(NB):
                        cs = slice(j * TB, (j + 1) * TB)
                        xb = state["xb"] % 2
                        state["xb"] += 1
                        for si, (src, K, sdt, sres, W) in enumerate(srcs):
                            KC = (K + 127) // 128
                            q = "sp" if sdt == BF16 else "pool"
                            if K % 128 == 0 and q == "sp":
                                P.dma(q, sbs[si][xb][:], src[:, cs].rearrange("(k p) t -> p k t", p=128), reads=[sres],
                                      writes=[f"LN_x{si}_{xb}"], key=f"LN_x{si}_{xb}")
                            else:
                                for kc in range(KC):
                                    kr = min(128, K - kc * 128)
                                    P.dma(q, sbs[si][xb][0:kr, kc, :], src[kc * 128:kc * 128 + kr, cs], reads=[sres],
                                          writes=[f"LN_x{si}_{xb}"], key=f"LN_x{si}_{xb}")
                        for ci, ((c0, w), o_) in enumerate(zip(grp, offs)):
                            pi = state["ps"] % nps
                            state["ps"] += 1
                            nmm = sum((K + 127) // 128 for (_, K, _, _, _) in srcs)
                            i = 0
                            for si, (src, K, sdt, sres, W) in enumerate(srcs):
                                KC = (K + 127) // 128
                                for kc in range(KC):
                                    kr = min(128, K - kc * 128)
                                    P.op("pe", lambda e, pi=pi, si=si, wb=wb, kc=kc, kr=kr, o_=o_, w=w, xb=xb, i=i, nmm=nmm:
                                         e.matmul(pss[pi][0:w, :], lhsT=wsb[si][wb][0:kr, kc, o_:o_ + w],
                                                  rhs=sbs[si][xb][0:kr, kc, :], start=(i == 0), stop=(i == nmm - 1)),
                                         reads=[f"LN_w{si}_{wb}", f"LN_x{si}_{xb}"], writes=[f"LN_ps{pi}"])
                                    i += 1
                            epi(gi, ci, (c0, w), j, pss[pi], f"LN_ps{pi}")
                P.barrier()
                P.stack = prev_stack

        def store_epi(tag, dst, dst_res, odt, func=AF.Copy, row_of=None, nbuf=3):
            bufs = [P.sb(f"{tag}_eo{i}", [128, TB], odt) for i in range(nbuf)]
            stt = {"i": 0}

            def epi(gi, ci, cw, j, ps, ps_res):
                c0, w = cw
                b = stt["i"] % nbuf
                stt["i"] += 1
                r0 = c0 if row_of is None else row_of(c0)
                if func == AF.Copy:
                    evac_copy(bufs[b][0:w, :], ps[0:w, :], [ps_res], [f"EO_eo{b}"])
                else:
                    P.op("act", lambda e: e.activation(out=bufs[b][0:w, :], in_=ps[0:w, :], func=func), [ps_res],
                         [f"EO_eo{b}"])
                P.dma("sp", dst[r0:r0 + w, j * TB:(j + 1) * TB], bufs[b][0:w, :], reads=[f"EO_eo{b}"],
                      writes=[dst_res], key=f"EO_eo{b}")

            return epi

        def chunks(c0, n, w=128):
            res = []
            c = c0
            while c < c0 + n:
                ww = min(w, c0 + n - c)
                res.append((c, ww))
                c += ww
            return res

        def grouped(ch, n):
            return [ch[i:i + n] for i in range(0, len(ch), n)]

        def main_seq():
            if "skip_s01m" in debug:
                rmsnorm_stage("nq", zmlaT[0:512, :], 512, 160, "dram:zmlaT", dst=cqnT, dst_res="dram:cqnT")
                rmsnorm_stage("nk", zmlaT[512:1024, :], 512, 164, "dram:zmlaT", dst=ckvnT, dst_res="dram:ckvnT")
                mla_stage(P, nc, st0, V, pos, zmlaT, cqnT, ckvnT, w_q_up, w_kv_up, ymlaT, ones_bf, evac_copy)
                stop_if("skip_s01m")
            if "skip_s01" in debug:
                rwkv_stage(P, nc, st0, V, zrwT, yrwT, rw_w2, rw_a2, rw_g2, ident_bf, bones_bf, maskP, maskL, eye, rmask, omka,
                           evac_copy)
                stop_if("skip_s01")
            transpose_stage("tx", x, hT, T, D, "dram:x", "dram:hT")
            transpose_stage("tp", p_in, pT, T, 256, "dram:p", "dram:pT")
            stop_if("stop0")
            rmsnorm_stage("n1", hT, D, 0, "dram:hT", dst=uT, dst_res="dram:uT")
            stop_if("stop0b")
            with contextlib.ExitStack() as stx:
                P.stack = stx
                e1 = store_epi("l1a", zrwT, "dram:zrwT", F32)
                linear_stage("l1a", [(uT, D, BF16, "dram:uT", w_in)], grouped(chunks(0, 3360), 8), e1, 1024)
                P.barrier()
                P.stack = st0
            stop_if("stop0c")
            with contextlib.ExitStack() as stx:
                P.stack = stx
                e2 = store_epi("l1b", zmlaT, "dram:zmlaT", F32, row_of=lambda c0: c0 - 3360)
                linear_stage("l1b", [(uT, D, BF16, "dram:uT", w_in)], grouped(chunks(3360, 1088), 9), e2, 1152)
                P.barrier()
                P.stack = st0
            with contextlib.ExitStack() as stx:
                P.stack = stx
                e3 = store_epi("l1c", gT, "dram:gT", BF16, func=AF.Sigmoid, row_of=lambda c0: c0 - 4448)
                linear_stage("l1c", [(uT, D, BF16, "dram:uT", w_in)], grouped(chunks(4448, 4096), 8), e3, 1024)
                P.barrier()
                P.stack = st0

            stop_if("stop1")
            if "skip_rwkv" not in debug:
                rwkv_stage(P, nc, st0, V, zrwT, yrwT, rw_w2, rw_a2, rw_g2, ident_bf, bones_bf, maskP, maskL, eye, rmask, omka,
                           evac_copy)
            stop_if("stop2")
            if "skip_mla" not in debug:
                rmsnorm_stage("nq", zmlaT[0:512, :], 512, 160, "dram:zmlaT", dst=cqnT, dst_res="dram:cqnT")
                rmsnorm_stage("nk", zmlaT[512:1024, :], 512, 164, "dram:zmlaT", dst=ckvnT, dst_res="dram:ckvnT")
                mla_stage(P, nc, st0, V, pos, zmlaT, cqnT, ckvnT, w_q_up, w_kv_up, ymlaT, ones_bf, evac_copy)

            stop_if("stop3")
            with contextlib.ExitStack() as stx:
                P.stack = stx
                gA = [P.sb(f"s4_gA{i}", [128, TB], BF16) for i in range(2)]
                gB = [P.sb(f"s4_gB{i}", [128, TB], BF16) for i in range(2)]
                t1 = [P.sb(f"s4_t1{i}", [128, TB], F32) for i in range(2)]
                mo = [P.sb(f"s4_mo{i}", [128, TB], BF16) for i in range(2)]
                t2 = [P.sb(f"s4_t2{i}", [128, TB], F32) for i in range(2)]
                stt = {"i": 0}
                def epiA(gi, ci, cw, j, ps, ps_res):
                    c0, w = cw
                    b = stt["i"] % 2
                    stt["i"] += 1
                    P.dma("sp", gA[b][:], gT[c0:c0 + 128, j * TB:(j + 1) * TB], reads=["dram:gT"], writes=[f"s4_gA{b}"],
                          key=f"s4_gA{b}")
                    P.op("dve", lambda e: e.tensor_tensor(out=t1[b][:], in0=ps[:], in1=gA[b][:], op=ALU.mult),
                         [ps_res, f"s4_gA{b}"], [f"s4_t1{b}"])
                    P.dma("sp", fT[c0:c0 + 128, j * TB:(j + 1) * TB], t1[b][:], reads=[f"s4_t1{b}"], writes=["dram:fT"],
                          key=f"s4_t1{b}")

                linear_stage("l4a", [(yrwT, 1024, BF16, "dram:yrwT", w_brw)], grouped(chunks(0, D), 8), epiA, 1024)

                def epiB(gi, ci, cw, j, ps, ps_res):
                    c0, w = cw
                    b = stt["i"] % 2
                    stt["i"] += 1
                    P.dma("sp", gB[b][:], gT[2048 + c0:2048 + c0 + 128, j * TB:(j + 1) * TB], reads=["dram:gT"],
                          writes=[f"s4_gB{b}"], key=f"s4_gB{b}")
                    P.dma("sp", t1[b][:], fT[c0:c0 + 128, j * TB:(j + 1) * TB], reads=["dram:fT"], writes=[f"s4_t1{b}"],
                          key=f"s4_t1l{b}")
                    P.op("dve", lambda e: e.tensor_tensor(out=t2[b][:], in0=ps[:], in1=gB[b][:], op=ALU.mult),
                         [ps_res, f"s4_gB{b}"], [f"s4_t2{b}"])
                    P.op("dve", lambda e: e.tensor_tensor(out=mo[b][:], in0=t2[b][:], in1=t1[b][:], op=ALU.add),
                         [f"s4_t2{b}", f"s4_t1{b}"], [f"s4_mo{b}"])
                    P.dma("sp", mT[c0:c0 + 128, j * TB:(j + 1) * TB], mo[b][:], reads=[f"s4_mo{b}"], writes=["dram:mT"],
                          key=f"s4_mo{b}")

                linear_stage("l4b", [(ymlaT, 1024, BF16, "dram:ymlaT", w_bmla)], grouped(chunks(0, D), 8), epiB, 1024)
                P.barrier()
                P.stack = st0
            stop_if("stop4")
            with contextlib.ExitStack() as stx:
                P.stack = stx
                e5 = store_epi("l5", fT, "dram:fT", F32)
                linear_stage("l5", [(mT, D, BF16, "dram:mT", w_out)], grouped(chunks(0, D), 8), e5, 1024)
                P.barrier()
                P.stack = st0
            rmsnorm_stage("n5", fT, D, 16, "dram:fT", resid=hT, resid_res="dram:hT")
            stop_if("stop5")
            rmsnorm_stage("n6", hT, D, 32, "dram:hT", dst=uT, dst_res="dram:uT")
            with contextlib.ExitStack() as stx:
                P.stack = stx
                NCH = 4
                ub = [[P.sb(f"s6_u{h}_{i}", [128, TB + 2], F32) for i in range(NCH)] for h in range(2)]
                cv = [P.sb(f"s6_cv{h}", [128, TB], F32) for h in range(2)]
                tq = P.sb("s6_tq", [128, TB], F32)
                fo = [P.sb(f"s6_fo{i}", [128, TB], BF16) for i in range(2)]
                stt = {"i": 0}

                def epi6(gi, ci, cw, j, ps, ps_res):
                    c0, w = cw
                    half = 0 if ci < NCH else 1
                    cc = ci % NCH
                    u = ub[half][cc]
                    ur = f"s6_u{half}_{cc}"
                    if j == 0:
                        P.op("pool", lambda e: e.memset(u[:, 0:2], 0.0), [ur], [ur])
                    else:
                        P.op("pool", lambda e: e.tensor_copy(out=u[:, 0:2], in_=u[:, TB:TB + 2]), [ur], [ur])
                    evac_copy(u[:, 2:TB + 2], ps[:], [ps_res, ur], [ur])
                    if half == 1:
                        chn = c0 // 128
                        chg = chn - 44
                        for hh, ch in ((0, chg), (1, chn)):
                            uu = ub[hh][cc]
                            uur = f"s6_u{hh}_{cc}"
                            eng = "dve"
                            P.op(eng, lambda e, uu=uu, ch=ch, hh=hh: e.tensor_scalar(
                                out=cv[hh][:], in0=uu[:, 0:TB], scalar1=V[:, 256 + ch:257 + ch], scalar2=V[:, 168 + ch:169 + ch],
                                op0=ALU.mult, op1=ALU.add), [uur, "V"], [f"s6_cv{hh}"])
                            P.op(eng, lambda e, uu=uu, ch=ch, hh=hh: e.scalar_tensor_tensor(
                                out=cv[hh][:], in0=uu[:, 1:TB + 1], scalar=V[:, 344 + ch:345 + ch], in1=cv[hh][:],
                                op0=ALU.mult, op1=ALU.add), [uur, "V", f"s6_cv{hh}"], [f"s6_cv{hh}"])
                            P.op(eng, lambda e, uu=uu, ch=ch, hh=hh: e.scalar_tensor_tensor(
                                out=cv[hh][:], in0=uu[:, 2:TB + 2], scalar=V[:, 432 + ch:433 + ch], in1=cv[hh][:],
                                op0=ALU.mult, op1=ALU.add), [uur, "V", f"s6_cv{hh}"], [f"s6_cv{hh}"])
                        P.op("dve", lambda e: e.tensor_tensor(out=tq[:], in0=cv[0][:], in1=cv[0][:], op=ALU.mult),
                             ["s6_cv0"], ["s6_tq"])
                        P.op("dve", lambda e: e.tensor_scalar(out=tq[:], in0=tq[:], scalar1=0.044715, scalar2=1.0,
                                                              op0=ALU.mult, op1=ALU.add), ["s6_tq"], ["s6_tq"])
                        P.op("dve", lambda e: e.tensor_tensor(out=tq[:], in0=tq[:], in1=cv[0][:], op=ALU.mult),
                             ["s6_tq", "s6_cv0"], ["s6_tq"])
                        P.op("act", lambda e: e.activation(out=tq[:], in_=tq[:], func=AF.Sigmoid, scale=1.5957691216),
                             ["s6_tq"], ["s6_tq"])
                        P.op("dve", lambda e: e.tensor_tensor(out=tq[:], in0=tq[:], in1=cv[0][:], op=ALU.mult),
                             ["s6_tq", "s6_cv0"], ["s6_tq"])
                        b = stt["i"] % 2
                        stt["i"] += 1
                        P.op("dve", lambda e: e.tensor_tensor(out=fo[b][:], in0=tq[:], in1=cv[1][:], op=ALU.mult),
                             ["s6_tq", "s6_cv1"], [f"s6_fo{b}"])
                        P.dma("sp", ffT[chg * 128:(chg + 1) * 128, j * TB:(j + 1) * TB], fo[b][:], reads=[f"s6_fo{b}"],
                              writes=["dram:ffT"], key=f"s6_fo{b}")

                grps = []
                for g0 in range(0, 44, NCH):
                    grps.append(chunks(g0 * 128, NCH * 128) + chunks(DFF + g0 * 128, NCH * 128))
                linear_stage("l6", [(uT, D, BF16, "dram:uT", w_up)], grps, epi6, 2 * NCH * 128)
                P.barrier()
                P.stack = st0
            with contextlib.ExitStack() as stx:
                P.stack = stx
                e7 = store_epi("l7", fT, "dram:fT", F32)
                linear_stage("l7", [(ffT, DFF, BF16, "dram:ffT", w_down)], grouped(chunks(0, D), 4), e7, 512)
                P.barrier()
                P.stack = st0
            rmsnorm_stage("n7", fT, D, 48, "dram:fT", resid=hT, resid_res="dram:hT")
            stop_if("stop7")
            with contextlib.ExitStack() as stx:
                P.stack = stx
                e8 = store_epi("l8a", mT, "dram:mT", BF16, func=AF.Sigmoid)
                linear_stage("l8a", [(hT, D, F32, "dram:hT", w_pg)], grouped(chunks(0, D), 8), e8, 1024)
                gl = [P.sb(f"s8_g{i}", [128, TB], BF16) for i in range(2)]
                eo = [P.sb(f"s8_eo{i}", [128, TB], F32) for i in range(2)]
                stt = {"i": 0}

                def epi8(gi, ci, cw, j, ps, ps_res):
                    c0, w = cw
                    b = stt["i"] % 2
                    stt["i"] += 1
                    P.dma("sp", gl[b][:], mT[c0:c0 + 128, j * TB:(j + 1) * TB], reads=["dram:mT"], writes=[f"s8_g{b}"],
                          key=f"s8_g{b}")
                    P.op("dve", lambda e: e.tensor_tensor(out=eo[b][:], in0=ps[:], in1=gl[b][:], op=ALU.mult),
                         [ps_res, f"s8_g{b}"], [f"s8_eo{b}"])
                    P.dma("sp", fT[c0:c0 + 128, j * TB:(j + 1) * TB], eo[b][:], reads=[f"s8_eo{b}"], writes=["dram:fT"],
                          key=f"s8_eo{b}")

                linear_stage("l8b", [(pT, 256, F32, "dram:pT", w_ple)], grouped(chunks(0, D), 16), epi8, 2048)
                P.barrier()
                P.stack = st0
            rmsnorm_stage("n8", fT, D, 64, "dram:fT", resid=hT, resid_res="dram:hT")
            transpose_stage("to", hT, out, D, T, "dram:hT", "dram:out")

        try:
            main_seq()
        except _Stop:
            P.stack = st0
        P.wait_all("sp", [f"dram:{n}" for n in (["out"] + list(debug)) if not n.startswith("skip") and not n.startswith("stop")])
        P.emit()
    return nc


def rwkv_stage(P, nc, st0, V, zrwT, yrwT, rw_w2, rw_a2, rw_g2, ident_bf, bones_bf, maskP, maskL, eye, rmask, omka,
               evac_copy):
    NEG_E = -math.exp(-0.5)
    with contextlib.ExitStack() as st:
        prev_stack = P.stack
        P.stack = st
        sb = P.sb
        w2b = sb("rk_w2b", [64, 1024], BF16)
        a2b = sb("rk_a2b", [64, 1024], BF16)
        g2b = sb("rk_g2b", [128, 2, 1024], BF16)
        P.dma("pool", w2b[:], rw_w2, writes=["rk_w"], key="rk_w")
        P.dma("pool", a2b[:], rw_a2, writes=["rk_w"], key="rk_w")
        P.dma("pool", g2b[:, 0, :], rw_g2[0:128, :], writes=["rk_w"], key="rk_w")
        P.dma("pool", g2b[0:32, 1, :], rw_g2[128:160, :], writes=["rk_w"], key="rk_w")
        twd = sb("rk_twd", [64, T], BF16)
        ads = sb("rk_ads", [64, T], BF16)
        sg1 = sb("rk_sg1", [128, T], BF16)
        sg2 = sb("rk_sg2", [32, T], BF16)
        zin = sb("rk_zin", [128, TB + 1], F32)
        dd = sb("rk_dd", [128, TB], F32)
        for (row0, nr, mcol, dst, func) in ((3072, 64, 520, twd, AF.Tanh), (3136, 64, 521, ads, AF.Copy),
                                            (3200, 128, 522, sg1, AF.Sigmoid), (3328, 32, 523, sg2, AF.Sigmoid)):
            for j in range(NB):
                if j == 0:
                    P.op("dve", lambda e: e.memset(zin[:, 0:1], 0.0), [], ["rk_zin"])
                    P.dma("sp", zin[0:nr, 1:TB + 1], zrwT[row0:row0 + nr, 0:TB], reads=["dram:zrwT"], writes=["rk_zin"],
                          key="rk_zin")
                else:
                    P.dma("sp", zin[0:nr, :], zrwT[row0:row0 + nr, j * TB - 1:(j + 1) * TB], reads=["dram:zrwT"],
                          writes=["rk_zin"], key="rk_zin")
                P.op("dve", lambda e, nr=nr: e.tensor_tensor(out=dd[0:nr, :], in0=zin[0:nr, 0:TB], in1=zin[0:nr, 1:TB + 1],
                                                             op=ALU.subtract), ["rk_zin"], ["rk_dd"])
                P.op("dve", lambda e, nr=nr, mcol=mcol: e.scalar_tensor_tensor(
                    out=dd[0:nr, :], in0=dd[0:nr, :], scalar=V[0:nr, mcol:mcol + 1], in1=zin[0:nr, 1:TB + 1],
                    op0=ALU.mult, op1=ALU.add), ["rk_dd", "rk_zin", "V"], ["rk_dd"])
                P.op("act", lambda e, nr=nr, dst=dst, func=func, j=j: e.activation(
                    out=dst[0:nr, j * TB:(j + 1) * TB], in_=dd[0:nr, :], func=func), ["rk_dd"], ["rk_lin"])
        zt = {n: sb(f"rk_z{n}", [128, TB + 1], F32) for n in "rkv"}
        xs = {n: sb(f"rk_s{n}", [128, TB], F32) for n in "rkv"}
        tA = sb("rk_tA", [128, TB], F32)
        tB = sb("rk_tB", [128, TB], F32)
        tC = sb("rk_tC", [128, TB], F32)
        av = sb("rk_a", [128, TB], F32)
        gv = sb("rk_g", [128, TB], F32)
        kkn = sb("rk_kkn", [128, TB], F32)
        k2 = sb("rk_k2", [128, TB], F32)
        logw = sb("rk_logw", [128, TB], F32)
        cum = sb("rk_cum", [128, TB], F32)
        gam = sb("rk_gam", [128, TB], F32)
        ginv = sb("rk_ginv", [128, TB], F32)
        gprev = sb("rk_gprev", [128, TB], F32)
        bonus = sb("rk_bonus", [128, TB], F32)
        tbf = sb("rk_tbf", [128, TB], BF16)
        vbf = sb("rk_vbf", [128, TB], BF16)
        AR = sb("rk_AR", [128, 8, 2, 64], BF16)
        BK = sb("rk_BK", [128, 8, 2, 64], BF16)
        PM = sb("rk_PM", [128, 8, 2, 128], BF16)
        TM = sb("rk_TM", [128, 8, 3, 64], BF16)
        MTs = sb("rk_MT", [128, 8, 64], BF16)
        NS = sb("rk_NS", [128, 2, 64], BF16)
        Lc = sb("rk_L", [128, 64], BF16)
        Hf = sb("rk_Hf", [128, 64], F32)
        HG = sb("rk_HG", [128, 64], F32)
        Hb = sb("rk_Hb", [128, 64], BF16)
        Zs = sb("rk_Zs", [128, 64], BF16)
        Us = sb("rk_Us", [128, 64], BF16)
        Yt = sb("rk_Y", [128, TB], F32)
        yo = sb("rk_yo", [128, TB], BF16)
        ps_a = P.ps("rk_psa", [128, TB], F32)
        ps_b = P.ps("rk_psb", [128, TB], F32)
        small = P.ps("rk_small", [128, 512], F32)
        small2 = P.ps("rk_small2", [128, 512], F32)
        pp = small[:, 0:256].rearrange("p (a t) -> p a t", a=2)
        lp = small[:, 256:320]
        ps1 = small[:, 320:448]
        small3 = P.ps("rk_small3", [128, 512], F32)
        ps2 = small3[:, 0:64]
        sq = small2[:, 0:192].rearrange("p (a t) -> p a t", a=3)
        trp_full = P.ps("rk_trp", [128, 1024], BF16)
        trp = trp_full[:, 0:192].rearrange("p (a t) -> p a t", a=3)
        yps = P.ps("rk_yps", [128, TB], F32)

        def dve(fn, r, w):
            P.op("dve", fn, r, w)

        def act(fn, r, w):
            P.op("act", fn, r, w)

        def pe(fn, r, w):
            P.op("pe", fn, r, w)

        def c3(ap):
            return ap.rearrange("p (c t) -> p c t", t=64)

        for pr in range(DBG.get("rkp", 8)):
            pc = slice(pr * 128, (pr + 1) * 128)
            vcol = lambda base: V[:, base + pr:base + pr + 1]
            dve(lambda e: e.memset(Hf[:], 0.0), [], ["rk_Hf"])
            dve(lambda e: e.memset(Hb[:], 0.0), [], ["rk_Hb"])
            for j in range(DBG.get("rkb", NB)):
                for gi, n in enumerate("rkv"):
                    row0 = gi * 1024 + pr * 128
                    if j == 0:
                        dve(lambda e, n=n: e.memset(zt[n][:, 0:1], 0.0), [], [f"rk_z{n}"])
                        P.dma("sp", zt[n][:, 1:TB + 1], zrwT[row0:row0 + 128, 0:TB], reads=["dram:zrwT"],
                              writes=[f"rk_z{n}"], key=f"rk_z{n}")
                    else:
                        P.dma("sp", zt[n][:], zrwT[row0:row0 + 128, j * TB - 1:(j + 1) * TB], reads=["dram:zrwT"],
                              writes=[f"rk_z{n}"], key=f"rk_z{n}")
                    mcol = 80 + gi * 8 + pr
                    dve(lambda e, n=n: e.tensor_tensor(out=xs[n][:], in0=zt[n][:, 0:TB], in1=zt[n][:, 1:TB + 1],
                                                       op=ALU.subtract), [f"rk_z{n}"], [f"rk_s{n}"])
                    dve(lambda e, n=n, mcol=mcol: e.scalar_tensor_tensor(
                        out=xs[n][:], in0=xs[n][:], scalar=V[:, mcol:mcol + 1], in1=zt[n][:, 1:TB + 1], op0=ALU.mult,
                        op1=ALU.add), [f"rk_s{n}", f"rk_z{n}", "V"], [f"rk_s{n}"])
                js = slice(j * TB, (j + 1) * TB)
                pe(lambda e: e.matmul(ps_a[:], lhsT=w2b[:, pc], rhs=twd[:, js], start=True, stop=True),
                   ["rk_w", "rk_lin"], ["rk_psa"])
                act(lambda e: e.activation(out=logw[:], in_=ps_a[:], func=AF.Sigmoid, bias=vcol(104)), ["rk_psa", "V"],
                    ["rk_logw"])
                dve(lambda e: e.tensor_scalar(out=logw[:], in0=logw[:], scalar1=NEG_E, scalar2=None, op0=ALU.mult),
                    ["rk_logw"], ["rk_logw"])
                pe(lambda e: e.matmul(ps_b[:], lhsT=a2b[:, pc], rhs=ads[:, js], start=True, stop=True),
                   ["rk_w", "rk_lin"], ["rk_psb"])
                act(lambda e: e.activation(out=av[:], in_=ps_b[:], func=AF.Sigmoid, bias=vcol(112)), ["rk_psb", "V"],
                    ["rk_a"])
                pe(lambda e: e.matmul(ps_a[:], lhsT=g2b[:, 0, pc], rhs=sg1[:, js], start=True, stop=False),
                   ["rk_w", "rk_lin"], ["rk_psa"])
                pe(lambda e: e.matmul(ps_a[:], lhsT=g2b[0:32, 1, pc], rhs=sg2[:, js], start=False, stop=True),
                   ["rk_w", "rk_lin"], ["rk_psa"])
                act(lambda e: e.activation(out=gv[:], in_=ps_a[:], func=AF.Copy), ["rk_psa"], ["rk_g"])
                dve(lambda e: e.tensor_scalar(out=tA[:], in0=xs["k"][:], scalar1=vcol(120), scalar2=None, op0=ALU.mult),
                    ["rk_sk", "V"], ["rk_tA"])
                dve(lambda e: e.tensor_tensor(out=tbf[:], in0=tA[:], in1=tA[:], op=ALU.mult), ["rk_tA"], ["rk_tbf"])
                pe(lambda e: e.matmul(ps_b[:], lhsT=bones_bf[:], rhs=tbf[:], start=True, stop=True),
                   ["bones_bf", "rk_tbf"], ["rk_psb"])
                act(lambda e: e.activation(out=tB[:], in_=ps_b[:], func=AF.Sqrt), ["rk_psb"], ["rk_tB"])
                dve(lambda e: e.tensor_scalar(out=tB[:], in0=tB[:], scalar1=1e-12, scalar2=None, op0=ALU.max),
                    ["rk_tB"], ["rk_tB"])
                dve(lambda e: e.reciprocal(out=tB[:], in_=tB[:]), ["rk_tB"], ["rk_tB"])
                dve(lambda e: e.tensor_tensor(out=kkn[:], in0=tA[:], in1=tB[:], op=ALU.mult), ["rk_tA", "rk_tB"],
                    ["rk_kkn"])
                dve(lambda e: e.tensor_scalar(out=tA[:], in0=av[:], scalar1=vcol(128), scalar2=omka[:, pr:pr + 1],
                                              op0=ALU.mult, op1=ALU.add), ["rk_a", "V", "omka"], ["rk_tA"])
                dve(lambda e: e.tensor_tensor(out=k2[:], in0=xs["k"][:], in1=tA[:], op=ALU.mult), ["rk_sk", "rk_tA"],
                    ["rk_k2"])
                dve(lambda e: e.tensor_tensor_scan(out=cum[:], data0=rmask[:], data1=logw[:], initial=0.0, op0=ALU.mult,
                                                   op1=ALU.add), ["msk", "rk_logw"], ["rk_cum"])
                act(lambda e: e.activation(out=gam[:], in_=cum[:], func=AF.Exp), ["rk_cum"], ["rk_gam"])
                act(lambda e: e.activation(out=ginv[:], in_=cum[:], func=AF.Exp, scale=-1.0), ["rk_cum"], ["rk_ginv"])
                dve(lambda e: e.tensor_tensor(out=tB[:], in0=cum[:], in1=logw[:], op=ALU.subtract),
                    ["rk_cum", "rk_logw"], ["rk_tB"])
                act(lambda e: e.activation(out=gprev[:], in_=tB[:], func=AF.Exp), ["rk_tB"], ["rk_gprev"])
                dve(lambda e: e.scalar_tensor_tensor(out=AR[:, :, 0, :], in0=c3(kkn[:]), scalar=-1.0, in1=c3(gprev[:]),
                                                     op0=ALU.mult, op1=ALU.mult), ["rk_kkn", "rk_gprev"], ["rk_AR"])
                dve(lambda e: e.tensor_tensor(out=AR[:, :, 1, :], in0=c3(xs["r"][:]), in1=c3(gam[:]), op=ALU.mult),
                    ["rk_sr", "rk_gam", "rk_AR"], ["rk_AR"])
                dve(lambda e: e.tensor_tensor(out=tA[:], in0=kkn[:], in1=av[:], op=ALU.mult), ["rk_kkn", "rk_a"],
                    ["rk_tA"])
                dve(lambda e: e.tensor_tensor(out=BK[:, :, 0, :], in0=c3(tA[:]), in1=c3(ginv[:]), op=ALU.mult),
                    ["rk_tA", "rk_ginv"], ["rk_BK"])
                dve(lambda e: e.tensor_tensor(out=BK[:, :, 1, :], in0=c3(k2[:]), in1=c3(ginv[:]), op=ALU.mult),
                    ["rk_k2", "rk_ginv", "rk_BK"], ["rk_BK"])
                act(lambda e: e.activation(out=vbf[:], in_=xs["v"][:], func=AF.Copy), ["rk_sv"], ["rk_vbf"])
                dve(lambda e: e.tensor_tensor(out=tC[:], in0=xs["r"][:], in1=k2[:], op=ALU.mult), ["rk_sr", "rk_k2"],
                    ["rk_tC"])
                dve(lambda e: e.tensor_scalar(out=tbf[:], in0=tC[:], scalar1=vcol(136), scalar2=None, op0=ALU.mult),
                    ["rk_tC", "V", "rk_tbf"], ["rk_tbf"])
                pe(lambda e: e.matmul(ps_b[:], lhsT=bones_bf[:], rhs=tbf[:], start=True, stop=True),
                   ["bones_bf", "rk_tbf"], ["rk_psb"])
                dve(lambda e: e.tensor_tensor(out=bonus[:], in0=ps_b[:], in1=xs["v"][:], op=ALU.mult),
                    ["rk_psb", "rk_sv"], ["rk_bonus"])
                for c in range(DBG.get("rkc1", 8)):
                    cc = slice(c * 64, (c + 1) * 64)
                    for h in range(0 if not (DBG.get("rkx", 0) & 1) else 99, 2):
                        sl = slice(64 * h, 64 * h + 64)
                        for ti, (src, sres) in enumerate(((None, "rk_BK"), (None, "rk_BK"), (vbf, "rk_vbf"))):
                            in_ap = BK[sl, c, ti, :] if ti < 2 else vbf[sl, cc]
                            pe(lambda e, sl=sl, ti=ti, in_ap=in_ap: e.transpose(out=trp[sl, ti, :], in_=in_ap,
                                                                               identity=ident_bf[sl, sl]),
                               [sres, "ident_bf"], ["rk_trp"])
                    if not (DBG.get("rkx", 0) & 1):
                        evac_copy(TM[:, c, :, :], trp, ["rk_trp"], ["rk_TM"])
                    if DBG.get("rkx", 0) & 2:
                        continue
                    for h in range(2):
                        sl = slice(64 * h, 64 * h + 64)
                        arv = AR[sl, c, :, :].rearrange("p a t -> p (a t)")
                        pe(lambda e, sl=sl, c=c, arv=arv: e.matmul(pp[sl, 0, :], lhsT=BK[sl, c, 0, :], rhs=arv, start=True,
                                                                   stop=True), ["rk_BK", "rk_AR"], ["rk_small"])
                        pe(lambda e, sl=sl, c=c, arv=arv: e.matmul(pp[sl, 1, :], lhsT=BK[sl, c, 1, :], rhs=arv, start=True,
                                                                   stop=True), ["rk_BK", "rk_AR"], ["rk_small"])
                        pe(lambda e, sl=sl, c=c: e.matmul(lp[sl, :], lhsT=AR[sl, c, 0, :], rhs=BK[sl, c, 0, :], start=True,
                                                          stop=True), ["rk_BK", "rk_AR"], ["rk_small"])
                    dve(lambda e, c=c: e.tensor_tensor(out=PM[:, c, :, :], in0=pp[:], in1=maskP[:], op=ALU.mult),
                        ["rk_small", "msk"], ["rk_PM"])
                    dve(lambda e: e.tensor_tensor(out=Lc[:], in0=lp[:], in1=maskL[:], op=ALU.mult), ["rk_small", "msk"],
                        ["rk_L"])
                    act(lambda e, c=c: e.activation(out=NS[:, 0, :], in_=PM[:, c, 0, 0:64], func=AF.Copy), ["rk_PM"],
                        ["rk_NS"])
                    dve(lambda e, c=c: e.tensor_copy(out=NS[:, 1, :], in_=eye[:]), ["msk", "rk_NS"], ["rk_NS"])
                    for step in range(6 if not (DBG.get("rkx", 0) & 4) else 0):
                        last = step == 5
                        for h in range(2):
                            sl = slice(64 * h, 64 * h + 64)
                            if not last:
                                nsv = NS[sl, :, :].rearrange("p a t -> p (a t)")
                                pe(lambda e, sl=sl, nsv=nsv: e.matmul(ps1[sl, :], lhsT=Lc[sl, :], rhs=nsv, start=True,
                                                                      stop=True), ["rk_L", "rk_NS"], ["rk_small"])
                                pe(lambda e, sl=sl: e.matmul(ps2[sl, :], lhsT=NS[sl, 0, :], rhs=Lc[sl, :], start=True,
                                                             stop=True), ["rk_L", "rk_NS"], ["rk_small3"])
                            else:
                                pe(lambda e, sl=sl: e.matmul(ps1[sl, 64:128], lhsT=Lc[sl, :], rhs=NS[sl, 1, :], start=True,
                                                             stop=True), ["rk_L", "rk_NS"], ["rk_small"])
                        if not last:
                            dve(lambda e: e.tensor_copy(out=NS[:, 0, :], in_=ps1[:, 0:64]),
                                ["rk_small", "rk_NS"], ["rk_NS"])
                            dve(lambda e: e.tensor_tensor(out=NS[:, 1, :], in0=NS[:, 1, :], in1=ps1[:, 64:128], op=ALU.add),
                                ["rk_small", "rk_NS"], ["rk_NS"])
                            act(lambda e: e.activation(out=Lc[:], in_=ps2[:], func=AF.Copy), ["rk_small3", "rk_L"], ["rk_L"])
                        else:
                            dve(lambda e, c=c: e.tensor_tensor(out=MTs[:, c, :], in0=NS[:, 1, :], in1=ps1[:, 64:128],
                                                               op=ALU.add), ["rk_small", "rk_NS"], ["rk_MT"])
                for c in range(DBG.get("rkc2", 8)):
                    cc = slice(c * 64, (c + 1) * 64)
                    gcol = gam[:, c * 64 + 63:c * 64 + 64]
                    for h in range(2):
                        sl = slice(64 * h, 64 * h + 64)
                        pe(lambda e, sl=sl, c=c: e.matmul(sq[sl, 0, :], lhsT=AR[sl, c, 0, :], rhs=Hb[sl, :], start=True,
                                                          stop=False), ["rk_AR", "rk_Hb"], ["rk_small2"])
                        pe(lambda e, sl=sl, c=c: e.matmul(sq[sl, 0, :], lhsT=PM[sl, c, 1, 0:64], rhs=TM[sl, c, 2, :],
                                                          start=False, stop=True), ["rk_PM", "rk_TM"], ["rk_small2"])
                    act(lambda e: e.activation(out=Zs[:], in_=sq[:, 0, :], func=AF.Copy), ["rk_small2"], ["rk_Zs"])
                    for h in range(2):
                        sl = slice(64 * h, 64 * h + 64)
                        pe(lambda e, sl=sl, c=c: e.matmul(sq[sl, 1, :], lhsT=MTs[sl, c, :], rhs=Zs[sl, :], start=True,
                                                          stop=True), ["rk_MT", "rk_Zs"], ["rk_small2"])
                    dve(lambda e: e.tensor_copy(out=Us[:], in_=sq[:, 1, :]), ["rk_small2"], ["rk_Us"])
                    for h in range(2):
                        sl = slice(64 * h, 64 * h + 64)
                        pe(lambda e, sl=sl, c=c, cc=cc: e.matmul(yps[sl, cc], lhsT=Hb[sl, :], rhs=AR[sl, c, 1, :], start=True,
                                                                 stop=False), ["rk_AR", "rk_Hb"], ["rk_yps"])
                        pe(lambda e, sl=sl, c=c, cc=cc: e.matmul(yps[sl, cc], lhsT=Us[sl, :], rhs=PM[sl, c, 0, 64:128],
                                                                 start=False, stop=False), ["rk_Us", "rk_PM"], ["rk_yps"])
                        pe(lambda e, sl=sl, c=c, cc=cc: e.matmul(yps[sl, cc], lhsT=TM[sl, c, 2, :], rhs=PM[sl, c, 1, 64:128],
                                                                 start=False, stop=True), ["rk_TM", "rk_PM"], ["rk_yps"])
                        pe(lambda e, sl=sl, c=c: e.matmul(sq[sl, 2, :], lhsT=TM[sl, c, 0, :], rhs=Us[sl, :], start=True,
                                                          stop=False), ["rk_TM", "rk_Us"], ["rk_small2"])
                        pe(lambda e, sl=sl, c=c: e.matmul(sq[sl, 2, :], lhsT=TM[sl, c, 1, :], rhs=TM[sl, c, 2, :],
                                                          start=False, stop=True), ["rk_TM"], ["rk_small2"])
                    P.op("dve", lambda e, gcol=gcol: e.tensor_scalar(out=HG[:], in0=Hf[:], scalar1=gcol, scalar2=None,
                                                                      op0=ALU.mult), ["rk_Hf", "rk_gam"], ["rk_HG"])
                    dve(lambda e, gcol=gcol: e.scalar_tensor_tensor(out=Hf[:], in0=sq[:, 2, :], scalar=gcol, in1=HG[:],
                                                                    op0=ALU.mult, op1=ALU.add),
                        ["rk_small2", "rk_HG", "rk_gam", "rk_Hf"], ["rk_Hf"])
                    act(lambda e: e.activation(out=Hb[:], in_=Hf[:], func=AF.Copy), ["rk_Hf", "rk_Hb"], ["rk_Hb"])
                act(lambda e: e.activation(out=Yt[:], in_=yps[:], func=AF.Copy), ["rk_yps"], ["rk_Y"])
                dve(lambda e: e.tensor_copy(out=tbf[:], in_=Yt[:]), ["rk_Y", "rk_tbf"], ["rk_tbf"])
                pe(lambda e: e.matmul(ps_a[:], lhsT=bones_bf[:], rhs=tbf[:], start=True, stop=True), ["bones_bf", "rk_tbf"],
                   ["rk_psa"])
                dve(lambda e: e.scalar_tensor_tensor(out=tA[:], in0=ps_a[:], scalar=-1.0 / 64, in1=Yt[:], op0=ALU.mult,
                                                     op1=ALU.add), ["rk_psa", "rk_Y"], ["rk_tA"])
                dve(lambda e: e.tensor_tensor(out=tbf[:], in0=tA[:], in1=tA[:], op=ALU.mult), ["rk_tA", "rk_tbf"],
                    ["rk_tbf"])
                pe(lambda e: e.matmul(ps_a[:], lhsT=bones_bf[:], rhs=tbf[:], start=True, stop=True), ["bones_bf", "rk_tbf"],
                   ["rk_psa"])
                act(lambda e: e.activation(out=tB[:], in_=ps_a[:], func=AF.Sqrt, bias=GN_EPS, scale=1.0 / 64),
                    ["rk_psa"], ["rk_tB"])
                dve(lambda e: e.reciprocal(out=tB[:], in_=tB[:]), ["rk_tB"], ["rk_tB"])
                dve(lambda e: e.tensor_tensor(out=tA[:], in0=tA[:], in1=tB[:], op=ALU.mult), ["rk_tA", "rk_tB"], ["rk_tA"])
                dve(lambda e: e.tensor_scalar(out=tA[:], in0=tA[:], scalar1=vcol(144), scalar2=vcol(152), op0=ALU.mult,
                                              op1=ALU.add), ["rk_tA", "V"], ["rk_tA"])
                dve(lambda e: e.tensor_tensor(out=tA[:], in0=tA[:], in1=bonus[:], op=ALU.add), ["rk_tA", "rk_bonus"],
                    ["rk_tA"])
                dve(lambda e: e.tensor_tensor(out=yo[:], in0=tA[:], in1=gv[:], op=ALU.mult), ["rk_tA", "rk_g"], ["rk_yo"])
                P.dma("sp", yrwT[pc, js], yo[:], reads=["rk_yo"], writes=["dram:yrwT"], key="rk_yo")
        P.barrier()
        P.stack = prev_stack


def mla_stage(P, nc, st0, V, pos, zmlaT, cqnT, ckvnT, w_q_up, w_kv_up, ymlaT, ones_bf, evac_copy):
    SCALE = 192 ** -0.5
    TWO_PI = 2.0 * math.pi
    with contextlib.ExitStack() as st:
        prev_stack = P.stack
        P.stack = st
        sb = P.sb
        wq = sb("ml_wq", [128, 4, 1536], BF16)
        wkv = sb("ml_wkv", [128, 4, 2048], BF16)
        P.dma("pool", wq[:], w_q_up.rearrange("(k p) m -> p k m", p=128), writes=["ml_w"], key="ml_w")
        P.dma("pool", wkv[:], w_kv_up.rearrange("(k p) m -> p k m", p=128), writes=["ml_w"], key="ml_w")
        cs = sb("ml_cs", [64, T], F32)
        sn = sb("ml_sn", [64, T], F32)
        with contextlib.ExitStack() as st2:
            P.stack = st2
            posi = P.sb("ml_posi", [64, T], I32)
            ang = P.sb("ml_ang", [64, T], F32)
            tt = P.sb("ml_tt", [64, T], F32)
            kf = P.sb("ml_kf", [64, T], F32)
            P.dma("sp", posi[:], pos[0:1, :].to_broadcast([64, T]),
                  writes=["ml_posi"], key="ml_posi")
            P.op("dve", lambda e: e.tensor_copy(out=ang[:], in_=posi[:]), ["ml_posi"], ["ml_ang"])
            P.op("dve", lambda e: e.tensor_scalar(out=ang[:], in0=ang[:], scalar1=V[0:64, 524:525], scalar2=None,
                                                  op0=ALU.mult), ["ml_ang", "V"], ["ml_ang"])
            for (dst, shift) in ((sn, 0.5), (cs, 0.75)):
                P.op("dve", lambda e, shift=shift: e.tensor_scalar(out=tt[:], in0=ang[:], scalar1=1.0 / TWO_PI,
                                                                   scalar2=shift, op0=ALU.mult, op1=ALU.add),
                     ["ml_ang", "ml_tt"], ["ml_tt"])
                P.op("dve", lambda e: e.tensor_copy(out=posi[:], in_=tt[:]), ["ml_tt", "ml_posi", "ml_ang"], ["ml_posi"])
                P.op("dve", lambda e: e.tensor_copy(out=kf[:], in_=posi[:]), ["ml_posi"], ["ml_kf"])
                P.op("dve", lambda e: e.tensor_tensor(out=tt[:], in0=tt[:], in1=kf[:], op=ALU.subtract), ["ml_tt", "ml_kf"],
                     ["ml_tt"])
                P.op("dve", lambda e: e.tensor_scalar(out=kf[:], in0=tt[:], scalar1=0.0, scalar2=None, op0=ALU.is_lt),
                     ["ml_tt", "ml_kf"], ["ml_kf"])
                P.op("dve", lambda e: e.scalar_tensor_tensor(out=tt[:], in0=kf[:], scalar=-0.5, in1=tt[:], op0=ALU.add,
                                                             op1=ALU.add), ["ml_tt", "ml_kf"], ["ml_tt"])
                P.op("act", lambda e, dst=dst: e.activation(out=dst[:], in_=tt[:], func=AF.Sin, scale=TWO_PI),
                     ["ml_tt"], ["ml_rope"])
            P.barrier()
            P.stack = st
        kr = sb("ml_kr", [64, T], BF16)
        kn = sb("ml_kn", [128, T], BF16)
        vtm = sb("ml_vtm", [128, 32, 128], BF16)
        xq = [sb(f"ml_xq{i}", [128, 4, TB], BF16) for i in range(2)]
        raw = sb("ml_raw", [64, TB], F32)
        t1 = sb("ml_t1", [64, TB], F32)
        t2 = sb("ml_t2", [64, TB], F32)
        qn = sb("ml_qn", [128, TB], BF16)
        qr = sb("ml_qr", [64, TB], BF16)
        pT = [sb(f"ml_pT{i}", [128, TB], BF16) for i in range(3)]
        rinv = sb("ml_rinv", [128, TB], F32)
        yo = [sb(f"ml_yo{i}", [128, TB], BF16) for i in range(2)]
        psq = P.ps("ml_psq", [128, TB], F32)
        psr = P.ps("ml_psr", [64, TB], F32)
        psv = P.ps("ml_psv", [128, 4, 128], F32)
        sps = [P.ps(f"ml_sps{i}", [128, TB], F32) for i in range(2)]
        ops_ = P.ps("ml_ops", [128, TB], F32)
        lps = P.ps("ml_lps", [128, TB], F32)

        def rope(src_res, js, out_ap, out_res, scale):
            lo, hi = slice(0, 32), slice(32, 64)
            d = lambda fn, r, w: P.op("dve", fn, r, w)
            d(lambda e: e.tensor_tensor(out=t1[lo, :], in0=raw[lo, :], in1=cs[lo, js], op=ALU.mult), [src_res, "ml_rope"],
              ["ml_t1"])
            d(lambda e: e.tensor_tensor(out=t2[lo, :], in0=raw[hi, :], in1=sn[hi, js], op=ALU.mult), [src_res, "ml_rope"],
              ["ml_t2"])
            d(lambda e: e.tensor_tensor(out=t1[hi, :], in0=raw[hi, :], in1=cs[hi, js], op=ALU.mult),
              [src_res, "ml_rope", "ml_t1"], ["ml_t1"])
            d(lambda e: e.tensor_tensor(out=t2[hi, :], in0=raw[lo, :], in1=sn[lo, js], op=ALU.mult),
              [src_res, "ml_rope", "ml_t2"], ["ml_t2"])
            d(lambda e: e.tensor_tensor(out=t1[lo, :], in0=t1[lo, :], in1=t2[lo, :], op=ALU.subtract), ["ml_t1", "ml_t2"],
              ["ml_t1"])
            d(lambda e: e.tensor_tensor(out=t1[hi, :], in0=t1[hi, :], in1=t2[hi, :], op=ALU.add), ["ml_t1", "ml_t2"],
              ["ml_t1"])
            P.op("act", lambda e: e.activation(out=out_ap, in_=t1[:], func=AF.Copy, scale=scale), ["ml_t1"], [out_res])

        for j in range(NB):
            js = slice(j * TB, (j + 1) * TB)
            P.dma("sp", raw[:], zmlaT[1024:1088, js], reads=["dram:zmlaT"], writes=["ml_raw"], key="ml_raw")
            rope("ml_raw", js, kr[:, js], "ml_kr", 1.0)
        xi = 0
        for hd in range(DBG.get("mlh", 8)):
            for j in range(NB):
                js = slice(j * TB, (j + 1) * TB)
                b = xi % 2
                xi += 1
                P.dma("sp", xq[b][:], ckvnT[:, js].rearrange("(k p) t -> p k t", p=128), reads=["dram:ckvnT"],
                      writes=[f"ml_xq{b}"], key=f"ml_xq{b}")
                for kc in range(4):
                    P.op("pe", lambda e, kc=kc, b=b: e.matmul(psq[:], lhsT=wkv[:, kc, hd * 256:hd * 256 + 128],
                                                              rhs=xq[b][:, kc, :], start=(kc == 0), stop=(kc == 3)),
                         ["ml_w", f"ml_xq{b}"], ["ml_psq"])
                evac_copy(kn[:, js], psq[:], ["ml_psq"], ["ml_kn"])
                for tt_ in range(4):
                    for kc in range(4):
                        P.op("pe", lambda e, kc=kc, b=b, tt_=tt_: e.matmul(
                            psv[:, tt_, :], lhsT=xq[b][:, kc, tt_ * 128:(tt_ + 1) * 128],
                            rhs=wkv[:, kc, hd * 256 + 128:hd * 256 + 256], start=(kc == 0), stop=(kc == 3)),
                            ["ml_w", f"ml_xq{b}"], ["ml_psv"])
                evac_copy(vtm[:, j * 4:(j + 1) * 4, :], psv[:], ["ml_psv"], ["ml_vtm"])
            for j in range(DBG.get("mlb", NB)):
                js = slice(j * TB, (j + 1) * TB)
                b = xi % 2
                xi += 1
                P.dma("sp", xq[b][:], cqnT[:, js].rearrange("(k p) t -> p k t", p=128), reads=["dram:cqnT"],
                      writes=[f"ml_xq{b}"], key=f"ml_xq{b}")
                for kc in range(4):
                    P.op("pe", lambda e, kc=kc, b=b: e.matmul(psq[:], lhsT=wq[:, kc, hd * 192:hd * 192 + 128],
                                                              rhs=xq[b][:, kc, :], start=(kc == 0), stop=(kc == 3)),
                         ["ml_w", f"ml_xq{b}"], ["ml_psq"])
                P.op("act", lambda e: e.activation(out=qn[:], in_=psq[:], func=AF.Copy, scale=SCALE), ["ml_psq"], ["ml_qn"])
                for kc in range(4):
                    P.op("pe", lambda e, kc=kc, b=b: e.matmul(psr[:], lhsT=wq[:, kc, hd * 192 + 128:hd * 192 + 192],
                                                              rhs=xq[b][:, kc, :], start=(kc == 0), stop=(kc == 3)),
                         ["ml_w", f"ml_xq{b}"], ["ml_psr"])
                P.op("act", lambda e: e.activation(out=raw[:], in_=psr[:], func=AF.Copy), ["ml_psr", "ml_raw"], ["ml_raw"])
                rope("ml_raw", js, qr[:], "ml_qr", SCALE)
                nkt = 4 * j + 4
                for kt in range(nkt):
                    r = kt - 4 * j
                    q0 = max(0, r) * 128
                    si = kt % 2
                    pi = kt % 3
                    ks = slice(kt * 128, (kt + 1) * 128)
                    P.op("pe", lambda e, si=si, ks=ks, q0=q0: e.matmul(sps[si][:, q0:TB], lhsT=kn[:, ks], rhs=qn[:, q0:TB],
                                                                       start=True, stop=False),
                         ["ml_kn", "ml_qn"], [f"ml_sps{si}"])
                    P.op("pe", lambda e, si=si, ks=ks, q0=q0: e.matmul(sps[si][:, q0:TB], lhsT=kr[:, ks], rhs=qr[:, q0:TB],
                                                                       start=False, stop=True),
                         ["ml_kr", "ml_qr"], [f"ml_sps{si}"])
                    P.op("act", lambda e, si=si, pi=pi, q0=q0: e.activation(out=pT[pi][:, q0:TB], in_=sps[si][:, q0:TB],
                                                                           func=AF.Exp), [f"ml_sps{si}"], [f"ml_pT{pi}"])
                    if r >= 0:
                        P.op("dve", lambda e, pi=pi, q0=q0: e.memset(pT[pi][64:128, q0:q0 + 64], 0.0), [f"ml_pT{pi}"],
                             [f"ml_pT{pi}"])
                    P.op("pe", lambda e, pi=pi, kt=kt, q0=q0, nkt=nkt: e.matmul(
                        ops_[:, q0:TB], lhsT=vtm[:, kt, :], rhs=pT[pi][:, q0:TB], start=(kt == 0), stop=(kt == nkt - 1),
                        skip_group_check=True), ["ml_vtm", f"ml_pT{pi}"], ["ml_ops"])
                    P.op("pe", lambda e, pi=pi, kt=kt, q0=q0, nkt=nkt: e.matmul(
                        lps[:, q0:TB], lhsT=ones_bf[:], rhs=pT[pi][:, q0:TB], start=(kt == 0), stop=(kt == nkt - 1),
                        skip_group_check=True), ["ones_bf", f"ml_pT{pi}"], ["ml_lps"])
                P.op("dve", lambda e: e.reciprocal(out=rinv[:], in_=lps[:]), ["ml_lps"], ["ml_rinv"])
                ob = j % 2
                P.op("dve", lambda e, ob=ob: e.tensor_tensor(out=yo[ob][:], in0=ops_[:], in1=rinv[:], op=ALU.mult),
                     ["ml_ops", "ml_rinv"], [f"ml_yo{ob}"])
                P.dma("sp", ymlaT[hd * 128:(hd + 1) * 128, js], yo[ob][:], reads=[f"ml_yo{ob}"], writes=["dram:ymlaT"],
                      key=f"ml_yo{ob}")
        P.barrier()
        P.stack = prev_stack


_CACHE = {}


def make_in_maps(inputs):
    sq = lambda a: np.ascontiguousarray(np.asarray(a)[0])
    V = pack_vecs(inputs)
    common = {
        "vecs": V,
        "w_in": sq(inputs["w_in"]), "rw_w2": sq(inputs["rw_w2"]), "rw_a2": sq(inputs["rw_a2"]), "rw_g2": sq(inputs["rw_g2"]),
        "w_q_up": sq(inputs["mla_w_q_up"]), "w_kv_up": sq(inputs["mla_w_kv_up"]),
        "w_brw": sq(inputs["w_branch_rw"]), "w_bmla": sq(inputs["w_branch_mla"]), "w_out": sq(inputs["w_out"]),
        "w_up": sq(inputs["w_up"]), "w_down": sq(inputs["w_down"]), "w_ple": sq(inputs["w_ple"]),
        "w_pg": sq(inputs["w_ple_gate"]),
    }
    maps = []
    for b in range(2):
        m = dict(common)
        m["x"] = np.ascontiguousarray(np.asarray(inputs["x"], np.float32)[b])
        m["p"] = np.ascontiguousarray(np.asarray(inputs["p"], np.float32)[0, b])
        m["pos"] = np.ascontiguousarray(np.asarray(inputs["positions"], np.int32)[b:b + 1])
        maps.append(m)
    return maps


def kernel(**inputs):
    if "nc" not in _CACHE:
        _CACHE["nc"] = build_program()
    nc = _CACHE["nc"]
    maps = make_in_maps(inputs)
    res = run_bass_kernel_spmd(nc, maps, core_ids=[0, 1])
    return np.stack([np.asarray(res.results[b]["out"], np.float32) for b in range(2)], axis=0)
```

```python
import contextlib
import numpy as np
import concourse.bass as bass
import concourse.mybir as mybir

F32 = mybir.dt.float32
BF16 = mybir.dt.bfloat16
I32 = mybir.dt.int32
ALU = mybir.AluOpType
AF = mybir.ActivationFunctionType
AX = mybir.AxisListType


class _Rec:
    def __init__(self):
        self.calls = []

    def __getattr__(self, name):
        def f(*a, **k):
            self.calls.append((name, a, k))
            return self

        return f


def _eager(fn):
    rec = _Rec()
    fn(rec)
    assert len(rec.calls) == 1, rec.calls
    name, a, k = rec.calls[0]
    return lambda e: getattr(e, name)(*a, **k)


class Prog:
    COMPUTE = ("pe", "act", "dve", "pool")

    def __init__(self, nc, stack):
        self.nc = nc
        self.stack = stack
        self.stack0 = stack
        self.streams = {e: [] for e in ("pe", "act", "dve", "pool", "sp")}
        self.sems = {}
        self.cnt = {}
        for e in self.COMPUTE:
            self.sems[e] = stack.enter_context(nc.semaphore("s_" + e))
            self.cnt[e] = 0
        self.seen = {e: {} for e in self.streams}
        self.res = {}
        self.dma_sems = {}
        self.n_ops = 0

    def sb(self, name, shape, dt):
        return self.stack.enter_context(self.nc.sbuf_tensor(name, list(shape), dt))

    def ps(self, name, shape, dt=F32):
        return self.stack.enter_context(self.nc.psum_tensor(name, list(shape), dt))

    def _dma_sem(self, key):
        if key not in self.dma_sems:
            self.dma_sems[key] = self.stack0.enter_context(self.nc.semaphore("d_" + key))
            self.sems["D:" + key] = self.dma_sems[key]
            self.cnt["D:" + key] = 0
        return "D:" + key

    def _deps(self, eng, reads, writes, exclude=None):
        need = {}
        for r in reads:
            ent = self.res.get(r)
            if ent:
                for k, v in ent[0].items():
                    need[k] = max(need.get(k, 0), v)
                if "ps" in r or "small" in r or "trp" in r:
                    for k, v in ent[1].items():
                        if k != eng:
                            need[k] = max(need.get(k, 0), v)
        for w in writes:
            ent = self.res.get(w)
            if ent:
                if not w.startswith("dram:"):
                    for k, v in ent[0].items():
                        need[k] = max(need.get(k, 0), v)
                for k, v in ent[1].items():
                    need[k] = max(need.get(k, 0), v)
        waits = []
        for k, v in need.items():
            if k == eng and eng == "pe":
                continue
            if k == exclude:
                continue
            if self.seen[eng].get(k, 0) >= v:
                continue
            self.seen[eng][k] = v
            waits.append((k, v))
        return waits

    def _mark(self, key, val, reads, writes):
        for r in reads:
            ent = self.res.setdefault(r, [{}, {}])
            ent[1][key] = val
        for w in writes:
            ent = self.res.setdefault(w, [{}, {}])
            if w.startswith("dram:"):
                ent[0][key] = val
            else:
                ent[0] = {key: val}
                ent[1] = {}
            self.res[w] = ent

    def op(self, eng, fn, reads=(), writes=()):
        fn = _eager(fn)
        waits = self._deps(eng, reads, writes)
        self.cnt[eng] += 1
        val = self.cnt[eng]
        sem = self.sems[eng]
        sems = self.sems

        def emit(e, waits=waits, fn=fn, sem=sem):
            for k, v in waits:
                e.wait_ge(sems[k], v)
            fn(e).then_inc(sem, 1)

        self.streams[eng].append(emit)
        self._mark(eng, val, reads, writes)
        self.n_ops += 1

    def dma(self, queue, out, in_, reads=(), writes=(), key=None, **kw):
        assert key is not None
        sk = self._dma_sem(key)
        waits = self._deps(queue, reads, writes, exclude=sk)
        self.cnt[sk] += 16
        val = self.cnt[sk]
        sem = self.sems[sk]
        sems = self.sems

        def emit(e, waits=waits, sem=sem):
            for k, v in waits:
                e.wait_ge(sems[k], v)
            e.dma_start(out=out, in_=in_, **kw).then_inc(sem, 16)

        self.streams[queue].append(emit)
        self._mark(sk, val, reads, writes)
        self.n_ops += 1

    def custom(self, queue, fn, reads=(), writes=(), key=None, raw=False):
        if not raw:
            fn = _eager(fn)
        sk = self._dma_sem(key)
        waits = self._deps(queue, reads, writes, exclude=sk)
        self.cnt[sk] += 16
        val = self.cnt[sk]
        sem = self.sems[sk]
        sems = self.sems

        def emit(e, waits=waits, sem=sem):
            for k, v in waits:
                e.wait_ge(sems[k], v)
            fn(e).then_inc(sem, 16)

        self.streams[queue].append(emit)
        self._mark(sk, val, reads, writes)
        self.n_ops += 1

    def barrier(self):
        snap = dict(self.cnt)
        sems = self.sems
        for eng in self.streams:
            waits = []
            for k, v in snap.items():
                if v <= 0 or self.seen[eng].get(k, 0) >= v:
                    continue
                if k == eng and eng == "pe":
                    continue
                self.seen[eng][k] = v
                waits.append((k, v))

            def emit(e, waits=waits):
                for k, v in waits:
                    e.wait_ge(sems[k], v)

            self.streams[eng].append(emit)

    def wait_all(self, eng, resources):
        waits = self._deps(eng, resources, ())
        sems = self.sems

        def emit(e, waits=waits):
            for k, v in waits:
                e.wait_ge(sems[k], v)

        self.streams[eng].append(emit)

    def emit(self):
        nc = self.nc
        with nc.Block() as block:
            @block.tensor
            def _(e):
                for f in self.streams["pe"]:
                    f(e)

            @block.scalar
            def _(e):
                for f in self.streams["act"]:
                    f(e)

            @block.vector
            def _(e):
                for f in self.streams["dve"]:
                    f(e)

            @block.gpsimd
            def _(e):
                for f in self.streams["pool"]:
                    f(e)

            @block.sync
            def _(e):
                for f in self.streams["sp"]:
                    f(e)

import math
from concourse.bass_utils import run_bass_kernel_spmd

T = 4096
D = 2048
TB = 512
NB = T // TB
DFF = 5632
EPS = 1e-6
GN_EPS = 64e-5
NV = 528
DBG = {}
TW = 1026
WB = [(0, 342), (342, 342), (684, 342)]
FB = [(j * 512, 512) for j in range(8)]


_RANK = {}


def get_rank(e):
    if "r" not in _RANK:
        _RANK["r"] = (e.partition_id() % 4) * 1024
    return _RANK["r"]


def vec_pc(v):
    v = np.asarray(v, np.float32).reshape(-1)
    return np.ascontiguousarray(v.reshape(-1, 128).T)


def pack_vecs(inp):
    V = np.zeros((128, NV), np.float32)
    V[:, 0:16] = vec_pc(inp["pre_mix_norm"])
    V[:, 16:32] = vec_pc(inp["post_mix_norm"])
    V[:, 32:48] = vec_pc(inp["pre_ffn_norm"])
    V[:, 48:64] = vec_pc(inp["post_ffn_norm"])
    V[:, 64:80] = vec_pc(inp["ple_norm"])
    mu = np.asarray(inp["rw_mu"], np.float32).reshape(-1)
    V[:, 80:104] = vec_pc(mu[0:3072])
    V[:, 104:112] = vec_pc(inp["rw_w0"])
    V[:, 112:120] = vec_pc(inp["rw_a0"])
    V[:, 120:128] = vec_pc(inp["rw_k_k"])
    V[:, 128:136] = vec_pc(inp["rw_k_a"])
    V[:, 136:144] = vec_pc(inp["rw_r_k"])
    V[:, 144:152] = vec_pc(inp["rw_lnx_w"])
    V[:, 152:160] = vec_pc(inp["rw_lnx_b"])
    V[:, 160:164] = vec_pc(inp["mla_q_norm"])
    V[:, 164:168] = vec_pc(inp["mla_kv_norm"])
    V[:, 168:256] = vec_pc(inp["conv_b"])
    cw = np.asarray(inp["conv_w"], np.float32).reshape(3, -1)
    V[:, 256:344] = vec_pc(cw[0])
    V[:, 344:432] = vec_pc(cw[1])
    V[:, 432:520] = vec_pc(cw[2])
    V[0:64, 520] = mu[3072:3136]
    V[0:64, 521] = mu[3136:3200]
    V[0:128, 522] = mu[3200:3328]
    V[0:32, 523] = mu[3328:3360]
    invf = (10000.0 ** (-np.arange(0, 64, 2, dtype=np.float32) / 64)).astype(np.float32)
    V[0:32, 524] = invf
    V[32:64, 524] = invf
    return V


def build_program(debug=()):
    nc = bass.Bass("TRN2", target_bir_lowering=False)
    DBG.clear()
    _RANK.clear()
    for k_ in debug:
        if "=" in k_:
            DBG[k_.split("=")[0]] = int(k_.split("=")[1])
        else:
            DBG[k_] = 1

    def din(name, shape, dt=F32):
        return nc.dram_tensor(name, list(shape), dt, kind="ExternalInput").ap()

    def dscr(name, shape, dt):
        if name in debug:
            return nc.dram_tensor(name, list(shape), dt, kind="ExternalOutput").ap()
        return nc.dram_tensor(name, list(shape), dt).ap()

    x = din("x", [T, D])
    p_in = din("p", [1024, 256])
    pos = din("pos", [1, T], I32)
    pos_w = din("pos_w", [1, TW], I32)
    qk_tab_d = din("qk_tab", [128, 3 * 342])
    d_tab_d = din("d_tab", [128, 32])
    vecs = din("vecs", [128, NV])
    w_in = din("w_in", [D, 8544])
    rw_w2 = din("rw_w2", [64, 1024])
    rw_a2 = din("rw_a2", [64, 1024])
    rw_g2 = din("rw_g2", [160, 1024])
    w_q_up = din("w_q_up", [512, 1536])
    w_kv_up = din("w_kv_up", [512, 2048])
    w_brw = din("w_brw", [1024, D])
    w_bmla = din("w_bmla", [1024, D])
    w_out = din("w_out", [D, D])
    w_up = din("w_up", [D, 2 * DFF])
    w_down = din("w_down", [DFF, D])
    w_ple = din("w_ple", [256, D])
    w_pg = din("w_pg", [D, D])
    out = nc.dram_tensor("out", [1024, D], F32, kind="ExternalOutput").ap()

    hT_full = dscr("hT", [D, T + 2], F32)
    uT_full = dscr("uT", [D, T + 2], BF16)
    yrwT_full = dscr("yrwT", [1024, T + 2], BF16)
    cqnT_full = dscr("cqnT", [512, T + 2], BF16)
    hT = hT_full[:, 2:]
    uT = uT_full[:, 2:]
    yrwT = yrwT_full[:, 2:]
    cqnT = cqnT_full[:, 2:]
    zrwT = dscr("zrwT", [3360, T], F32)
    zmlaT = dscr("zmlaT", [1088, T], F32)
    ckvnT = dscr("ckvnT", [512, T], BF16)
    hwT = dscr("hwT", [D, TW], F32)
    uwT = dscr("uwT", [D, TW], BF16)
    ywT = dscr("ywT", [1024, TW], BF16)
    cqwT = dscr("cqwT", [512, TW], BF16)
    ymlaT = dscr("ymlaT", [1024, TW], BF16)
    pT = dscr("pT", [256, 1024], F32)
    gT = dscr("gT", [4096, TW], BF16)
    mT = dscr("mT", [D, TW], BF16)
    fT = dscr("fT", [D, TW], F32)
    ffT = dscr("ffT", [DFF, TW], BF16)
    qk_tab = qk_tab_d
    d_tab = d_tab_d

    with contextlib.ExitStack() as st0:
        P = Prog(nc, st0)
        V = P.sb("V", [128, NV], F32)
        P.dma("sp", V[:], vecs, writes=["V"], key="V")
        ident = P.sb("ident", [128, 128], F32)
        ident_bf = P.sb("ident_bf", [128, 128], BF16)
        ones_bf = P.sb("ones_bf", [128, 128], BF16)
        bones_bf = P.sb("bones_bf", [128, 128], BF16)
        maskP = P.sb("maskP", [128, 2, 128], F32)
        maskL = P.sb("maskL", [128, 64], F32)
        eye = P.sb("eye", [128, 64], F32)
        rmask = P.sb("rmask", [128, TB], F32)
        omka = P.sb("omka", [128, 8], F32)
        su = P.sb("su", [128, 64], F32)
        ui = P.sb("ui", [128, 64], F32)

        def pool(fn, reads=(), writes=()):
            P.op("pool", fn, reads, writes)

        pool(lambda e: e.memset(ident[:], 1.0), writes=["ident"])
        pool(lambda e: e.affine_select(out=ident[:], in_=ident[:], pattern=[[-1, 128]], compare_op=ALU.is_equal,
                                       fill=0.0, base=0, channel_multiplier=1), reads=["ident"], writes=["ident"])
        pool(lambda e: e.tensor_copy(out=ident_bf[:], in_=ident[:]), reads=["ident"], writes=["ident_bf"])
        pool(lambda e: e.memset(ones_bf[:], 1.0), writes=["ones_bf"])
        pool(lambda e: e.memset(bones_bf[:], 0.0), writes=["bones_bf"])
        pool(lambda e: e.memset(bones_bf[0:64, 0:64], 1.0), reads=["bones_bf"], writes=["bones_bf"])
        pool(lambda e: e.memset(bones_bf[64:128, 64:128], 1.0), reads=["bones_bf"], writes=["bones_bf"])
        for (tl, pat, cm, b0, b1, op_) in ((su, 1, -1, -1, -1, ALU.is_ge), (ui, 1, -1, 0, 0, ALU.is_ge),
                                           (maskL, -1, 1, -1, -1, ALU.is_ge), (eye, -1, 1, 0, 0, ALU.is_equal)):
            pool(lambda e, tl=tl: e.memset(tl[:], 1.0), writes=["msk"])
            for h, bb in ((0, b0), (1, b1)):
                sl = slice(64 * h, 64 * h + 64)
                pool(lambda e, tl=tl, sl=sl, pat=pat, cm=cm, bb=bb, op_=op_: e.affine_select(
                    out=tl[sl, :], in_=tl[sl, :], pattern=[[pat, 64]], compare_op=op_, fill=0.0, base=bb,
                    channel_multiplier=cm), reads=["msk"], writes=["msk"])
        for xx in range(2):
            pool(lambda e, xx=xx: e.tensor_copy(out=maskP[:, xx, 0:64], in_=su[:]), reads=["msk"], writes=["msk"])
            pool(lambda e, xx=xx: e.tensor_copy(out=maskP[:, xx, 64:128], in_=ui[:]), reads=["msk"], writes=["msk"])
        pool(lambda e: e.memset(rmask[:], 1.0), reads=["msk"], writes=["msk"])
        for c in range(8):
            pool(lambda e, c=c: e.memset(rmask[:, c * 64:c * 64 + 1], 0.0), reads=["msk"], writes=["msk"])
        P.op("dve", lambda e: e.tensor_scalar(out=omka[:], in0=V[:, 128:136], scalar1=-1.0, scalar2=1.0, op0=ALU.mult,
                                              op1=ALU.add), reads=["V", "msk"], writes=["omka"])
        CONST = ["V", "msk", "ident", "ident_bf", "ones_bf", "bones_bf", "omka"]

        rr = {"evac": 0}

        class _Stop(Exception):
            pass

        def stop_if(name):
            if name in debug:
                raise _Stop()

        def evac_copy(out_ap, in_ap, reads, writes, scale=None):
            rr["evac"] += 1
            if scale is not None or rr["evac"] % 2 == 0:
                P.op("act", lambda e: e.activation(out=out_ap, in_=in_ap, func=AF.Copy,
                                                   scale=(1.0 if scale is None else scale)), reads, writes)
            else:
                P.op("dve", lambda e: e.tensor_copy(out=out_ap, in_=in_ap), reads, writes)

        def transpose_stage(tag, src, dst, R, C, src_res, dst_res):
            with contextlib.ExitStack() as st:
                prev_stack = P.stack
                P.stack = st
                nb = 2
                tin = [P.sb(f"{tag}_in{i}", [128, C], F32) for i in range(nb)]
                tout = [P.sb(f"{tag}_out{i}", [128, C // 128, 128], F32) for i in range(nb)]
                pst = [P.ps(f"{tag}_ps{i}", [128, 4, 128], F32) for i in range(2)]
                k = 0
                for r in range(R // 128):
                    b = r % nb
                    P.dma("sp", tin[b][:], src[r * 128:(r + 1) * 128, :], reads=[src_res], writes=[f"TR_in{b}"],
                          key=f"TR_in{b}")
                    for c4 in range(0, C // 128, 4):
                        pb = k % 2
                        k += 1
                        n4 = min(4, C // 128 - c4)
                        for i in range(n4):
                            c = c4 + i
                            P.op("pe", lambda e, pb=pb, i=i, c=c, b=b: e.transpose(
                                out=pst[pb][:, i, :], in_=tin[b][:, c * 128:(c + 1) * 128], identity=ident[:]),
                                reads=[f"TR_in{b}", "ident"], writes=[f"TR_ps{pb}"])
                        evac_copy(tout[b][:, c4:c4 + n4, :], pst[pb][:, 0:n4, :], [f"TR_ps{pb}"], [f"TR_out{b}"])
                    P.dma("sp", dst[:, r * 128:(r + 1) * 128].rearrange("(c p) t -> p c t", p=128), tout[b][:],
                          reads=[f"TR_out{b}"], writes=[dst_res], key=f"TR_st{b}")
                P.barrier()
                P.stack = prev_stack

        def rmsnorm_stage(tag, src, F, gcol, src_res, dst=None, dst_res=None, resid=None, resid_res=None, blocks=FB):
            FC = F // 128
            with contextlib.ExitStack() as st:
                prev_stack = P.stack
                P.stack = st
                s = P.sb(f"{tag}_s", [128, FC, TB], F32)
                sq = P.sb(f"{tag}_sq", [128, FC, TB], BF16)
                rstd = P.sb(f"{tag}_rstd", [128, TB], F32)
                ps = P.ps(f"{tag}_ps", [128, TB], F32)
                if resid is None:
                    o = P.sb(f"{tag}_o", [128, FC, TB], BF16)
                else:
                    o = P.sb(f"{tag}_o", [128, FC, TB], F32)
                    hh = P.sb(f"{tag}_h", [128, FC, TB], F32)
                for j, (b0, bs) in enumerate(blocks):
                    cs = slice(b0, b0 + bs)
                    P.dma("sp", s[:, :, 0:bs], src[:, cs].rearrange("(c p) t -> p c t", p=128), reads=[src_res],
                          writes=[f"RN_s"], key=f"RN_s")
                    if resid is not None:
                        P.dma("sp", hh[:, :, 0:bs], resid[:, cs].rearrange("(c p) t -> p c t", p=128), reads=[resid_res],
                              writes=[f"RN_h"], key=f"RN_h")
                    P.op("act", lambda e: e.activation(out=sq[:, :, 0:bs], in_=s[:, :, 0:bs], func=AF.Square), reads=[f"RN_s"],
                         writes=[f"RN_sq"])
                    for c in range(FC):
                        P.op("pe", lambda e, c=c: e.matmul(ps[:, 0:bs], lhsT=ones_bf[:], rhs=sq[:, c, 0:bs], start=(c == 0),
                                                           stop=(c == FC - 1)),
                             reads=[f"RN_sq", "ones_bf"], writes=[f"RN_ps"])
                    P.op("act", lambda e: e.activation(out=rstd[:, 0:bs], in_=ps[:, 0:bs], func=AF.Sqrt, bias=EPS, scale=1.0 / F),
                         reads=[f"RN_ps"], writes=[f"RN_rstd"])
                    P.op("dve", lambda e: e.reciprocal(out=rstd[:, 0:bs], in_=rstd[:, 0:bs]), reads=[f"RN_rstd"],
                         writes=[f"RN_rstd"])
                    for c in range(FC):
                        eng = "pool"
                        P.op("dve", lambda e, c=c: e.scalar_tensor_tensor(
                            out=o[:, c, 0:bs], in0=s[:, c, 0:bs], scalar=V[:, gcol + c:gcol + c + 1], in1=rstd[:, 0:bs],
                            op0=ALU.mult, op1=ALU.mult), reads=[f"RN_s", f"RN_rstd", "V"], writes=[f"RN_o{c}"])
                        if resid is not None:
                            P.op(eng, lambda e, c=c: e.tensor_tensor(out=o[:, c, 0:bs], in0=o[:, c, 0:bs], in1=hh[:, c, 0:bs],
                                                                     op=ALU.add),
                                 reads=[f"RN_o{c}", f"RN_h"], writes=[f"RN_o{c}"])
                    allo = [f"RN_o{c}" for c in range(FC)]
                    if resid is None:
                        P.dma("sp", dst[:, cs].rearrange("(c p) t -> p c t", p=128), o[:, :, 0:bs], reads=allo,
                              writes=[dst_res], key=f"RN_st")
                    else:
                        P.dma("sp", resid[:, cs].rearrange("(c p) t -> p c t", p=128), o[:, :, 0:bs], reads=allo,
                              writes=[resid_res], key=f"RN_st")
                P.barrier()
                P.stack = prev_stack

        def linear_stage(tag, srcs, groups, epi, wbuf_cols, nps=2, blocks=FB):
            with contextlib.ExitStack() as st:
                prev_stack = P.stack
                P.stack = st
                wsb = []
                sbs = []
                for si, (src, K, sdt, sres, W) in enumerate(srcs):
                    KC = (K + 127) // 128
                    wsb.append([P.sb(f"{tag}_w{si}_{i}", [128, KC, wbuf_cols], BF16) for i in range(2)])
                    sbs.append([P.sb(f"{tag}_x{si}_{i}", [128, KC, TB], BF16) for i in range(2)])
                pss = [P.ps(f"{tag}_ps{i}", [128, TB], F32) for i in range(nps)]
                state = {"ps": 0, "xb": 0}
                for gi, grp in enumerate(groups):
                    wb = gi % 2
                    offs = []
                    off = 0
                    for (c0, w) in grp:
                        offs.append(off)
                        off += w
                    assert off <= wbuf_cols
                    ranges = []
                    for (c0, w), o_ in zip(grp, offs):
                        if ranges and ranges[-1][0] + ranges[-1][1] == c0 and ranges[-1][2] + ranges[-1][1] == o_:
                            ranges[-1][1] += w
                        else:
                            ranges.append([c0, w, o_])
                    for si, (src, K, sdt, sres, W) in enumerate(srcs):
                        KC = (K + 127) // 128
                        for (c0, w, o_) in ranges:
                            if False:
                                P.dma("pool", wsb[si][wb][:, :, o_:o_ + w],
                                      W[:, c0:c0 + w].rearrange("(k p) m -> p k m", p=128),
                                      writes=[f"LN_w{si}_{wb}"], key=f"LN_w{si}_{wb}")
                            else:
                                for kc in range(KC):
                                    kr = min(128, K - kc * 128)
                                    P.dma("pool", wsb[si][wb][0:kr, kc, o_:o_ + w], W[kc * 128:kc * 128 + kr, c0:c0 + w],
                                          writes=[f"LN_w{si}_{wb}"], key=f"LN_w{si}_{wb}")
                    for j, (b0, bs) in enumerate(blocks):
                        cs = slice(b0, b0 + bs)
                        xb = state["xb"] % 2
                        state["xb"] += 1
                        for si, (src, K, sdt, sres, W) in enumerate(srcs):
                            KC = (K + 127) // 128
                            q = "sp" if sdt == BF16 else "pool"
                            if K % 128 == 0 and q == "sp":
                                P.dma(q, sbs[si][xb][:, :, 0:bs], src[:, cs].rearrange("(k p) t -> p k t", p=128), reads=[sres],
                                      writes=[f"LN_x{si}_{xb}"], key=f"LN_x{si}_{xb}")
                            else:
                                for kc in range(KC):
                                    kr = min(128, K - kc * 128)
                                    P.dma(q, sbs[si][xb][0:kr, kc, 0:bs], src[kc * 128:kc * 128 + kr, cs], reads=[sres],
                                          writes=[f"LN_x{si}_{xb}"], key=f"LN_x{si}_{xb}")
                        for ci, ((c0, w), o_) in enumerate(zip(grp, offs)):
                            pi = state["ps"] % nps
                            state["ps"] += 1
                            nmm = sum((K + 127) // 128 for (_, K, _, _, _) in srcs)
                            i = 0
                            for si, (src, K, sdt, sres, W) in enumerate(srcs):
                                KC = (K + 127) // 128
                                for kc in range(KC):
                                    kr = min(128, K - kc * 128)
                                    P.op("pe", lambda e, pi=pi, si=si, wb=wb, kc=kc, kr=kr, o_=o_, w=w, xb=xb, i=i, nmm=nmm:
                                         e.matmul(pss[pi][0:w, 0:bs], lhsT=wsb[si][wb][0:kr, kc, o_:o_ + w],
                                                  rhs=sbs[si][xb][0:kr, kc, 0:bs], start=(i == 0), stop=(i == nmm - 1)),
                                         reads=[f"LN_w{si}_{wb}", f"LN_x{si}_{xb}"], writes=[f"LN_ps{pi}"])
                                    i += 1
                            epi(gi, ci, (c0, w), j, pss[pi], f"LN_ps{pi}", (b0, bs))
                P.barrier()
                P.stack = prev_stack

        def store_epi(tag, dst, dst_res, odt, func=AF.Copy, row_of=None, nbuf=3):
            bufs = [P.sb(f"{tag}_eo{i}", [128, TB], odt) for i in range(nbuf)]
            stt = {"i": 0}

            def epi(gi, ci, cw, j, ps, ps_res, blk):
                c0, w = cw
                b0_, bs = blk
                b = stt["i"] % nbuf
                stt["i"] += 1
                r0 = c0 if row_of is None else row_of(c0)
                if func == AF.Copy:
                    evac_copy(bufs[b][0:w, 0:bs], ps[0:w, 0:bs], [ps_res], [f"EO_eo{b}"])
                else:
                    P.op("act", lambda e: e.activation(out=bufs[b][0:w, 0:bs], in_=ps[0:w, 0:bs], func=func), [ps_res],
                         [f"EO_eo{b}"])
                P.dma("sp", dst[r0:r0 + w, b0_:b0_ + bs], bufs[b][0:w, 0:bs], reads=[f"EO_eo{b}"],
                      writes=[dst_res], key=f"EO_eo{b}")

            return epi

        def chunks(c0, n, w=128):
            res = []
            c = c0
            while c < c0 + n:
                ww = min(w, c0 + n - c)
                res.append((c, ww))
                c += ww
            return res

        def grouped(ch, n):
            return [ch[i:i + n] for i in range(0, len(ch), n)]

        def main_seq():
            if "skip_s01m" in debug:
                rmsnorm_stage("nq", zmlaT[0:512, :], 512, 160, "dram:zmlaT", dst=cqnT, dst_res="dram:cqnT")
                rmsnorm_stage("nk", zmlaT[512:1024, :], 512, 164, "dram:zmlaT", dst=ckvnT, dst_res="dram:ckvnT")
                mla_stage(P, nc, st0, V, pos, pos_w, qk_tab, d_tab, zmlaT, cqnT_full, ckvnT, w_q_up, w_kv_up, ymlaT, ones_bf,
                          evac_copy)
                stop_if("skip_s01m")
            def win_copy(dst, src_full, rows, res_src, res_dst, key):
                step = 512
                for r0 in range(0, rows, step):
                    def fn(e, r0=r0):
                        off = get_rank(e)
                        return e.dma_start(out=dst[r0:r0 + step, :], in_=src_full[r0:r0 + step, bass.ds(off, TW)])
                    P.custom("pool", fn, reads=[res_src], writes=[res_dst], key=key, raw=True)

            zt_f = P.sb("zpad_f", [128, 16, 2], F32)
            zt_b = P.sb("zpad_b", [128, 16, 2], BF16)
            P.op("pool", lambda e: e.memset(zt_f[:], 0.0), [], ["zpad"])
            P.op("pool", lambda e: e.memset(zt_b[:], 0.0), [], ["zpad"])
            P.dma("sp", hT_full[:, 0:2].rearrange("(c p) t -> p c t", p=128), zt_f[:], reads=["zpad"], writes=["dram:hT"], key="zp")
            P.dma("sp", uT_full[:, 0:2].rearrange("(c p) t -> p c t", p=128), zt_b[:], reads=["zpad"], writes=["dram:uT"], key="zp")
            P.dma("sp", yrwT_full[:, 0:2].rearrange("(c p) t -> p c t", p=128), zt_b[:, 0:8, :], reads=["zpad"],
                  writes=["dram:yrwT"], key="zp")
            P.dma("sp", cqnT_full[:, 0:2].rearrange("(c p) t -> p c t", p=128), zt_b[:, 0:4, :], reads=["zpad"],
                  writes=["dram:cqnT"], key="zp")
            transpose_stage("tx", x, hT, T, D, "dram:x", "dram:hT")
            stop_if("stop0")
            rmsnorm_stage("n1", hT, D, 0, "dram:hT", dst=uT, dst_res="dram:uT")
            stop_if("stop0b")
            with contextlib.ExitStack() as stx:
                P.stack = stx
                e1 = store_epi("l1a", zrwT, "dram:zrwT", F32)
                linear_stage("l1a", [(uT, D, BF16, "dram:uT", w_in)], grouped(chunks(0, 3360), 8), e1, 1024)
                P.barrier()
                P.stack = st0
            with contextlib.ExitStack() as stx:
                P.stack = stx
                e2 = store_epi("l1b", zmlaT, "dram:zmlaT", F32, row_of=lambda c0: c0 - 3360)
                linear_stage("l1b", [(uT, D, BF16, "dram:uT", w_in)], grouped(chunks(3360, 1088), 9), e2, 1152)
                P.barrier()
                P.stack = st0
            stop_if("stop1")
            rwkv_stage(P, nc, st0, V, zrwT, yrwT, rw_w2, rw_a2, rw_g2, ident_bf, bones_bf, maskP, maskL, eye, rmask, omka,
                       evac_copy)
            stop_if("stop2")
            rmsnorm_stage("nq", zmlaT[0:512, :], 512, 160, "dram:zmlaT", dst=cqnT, dst_res="dram:cqnT")
            rmsnorm_stage("nk", zmlaT[512:1024, :], 512, 164, "dram:zmlaT", dst=ckvnT, dst_res="dram:ckvnT")
            win_copy(cqwT, cqnT_full, 512, "dram:cqnT", "dram:cqwT", "wc3")
            P.barrier()
            mla_stage(P, nc, st0, V, pos, pos_w, qk_tab, d_tab, zmlaT, cqwT, ckvnT, w_q_up, w_kv_up, ymlaT, ones_bf,
                      evac_copy)
            stop_if("stop3")
            win_copy(hwT, hT_full, D, "dram:hT", "dram:hwT", "wc0")
            win_copy(uwT, uT_full, D, "dram:uT", "dram:uwT", "wc1")
            win_copy(ywT, yrwT_full, 1024, "dram:yrwT", "dram:ywT", "wc2")
            P.barrier()
            transpose_stage("tp", p_in, pT, 1024, 256, "dram:p", "dram:pT")
            with contextlib.ExitStack() as stx:
                P.stack = stx
                e3 = store_epi("l1c", gT, "dram:gT", BF16, func=AF.Sigmoid, row_of=lambda c0: c0 - 4448)
                linear_stage("l1c", [(uwT, D, BF16, "dram:uwT", w_in)], grouped(chunks(4448, 4096), 8), e3, 1024, blocks=WB)
                P.barrier()
                P.stack = st0
            with contextlib.ExitStack() as stx:
                P.stack = stx
                gA = [P.sb(f"s4_gA{i}", [128, TB], BF16) for i in range(2)]
                gB = [P.sb(f"s4_gB{i}", [128, TB], BF16) for i in range(2)]
                t1 = [P.sb(f"s4_t1{i}", [128, TB], F32) for i in range(2)]
                mo = [P.sb(f"s4_mo{i}", [128, TB], BF16) for i in range(2)]
                t2 = [P.sb(f"s4_t2{i}", [128, TB], F32) for i in range(2)]
                stt = {"i": 0}

                def epiA(gi, ci, cw, j, ps, ps_res, blk):
                    c0, w = cw
                    b0_, bs = blk
                    b = stt["i"] % 2
                    stt["i"] += 1
                    P.dma("sp", gA[b][:, 0:bs], gT[c0:c0 + 128, b0_:b0_ + bs], reads=["dram:gT"], writes=[f"s4_gA{b}"],
                          key=f"s4_gA{b}")
                    P.op("dve", lambda e: e.tensor_tensor(out=t1[b][:, 0:bs], in0=ps[:, 0:bs], in1=gA[b][:, 0:bs], op=ALU.mult),
                         [ps_res, f"s4_gA{b}"], [f"s4_t1{b}"])
                    P.dma("sp", fT[c0:c0 + 128, b0_:b0_ + bs], t1[b][:, 0:bs], reads=[f"s4_t1{b}"], writes=["dram:fT"],
                          key=f"s4_t1{b}")

                linear_stage("l4a", [(ywT, 1024, BF16, "dram:ywT", w_brw)], grouped(chunks(0, D), 8), epiA, 1024, blocks=WB)

                def epiB(gi, ci, cw, j, ps, ps_res, blk):
                    c0, w = cw
                    b0_, bs = blk
                    b = stt["i"] % 2
                    stt["i"] += 1
                    P.dma("sp", gB[b][:, 0:bs], gT[2048 + c0:2048 + c0 + 128, b0_:b0_ + bs], reads=["dram:gT"],
                          writes=[f"s4_gB{b}"], key=f"s4_gB{b}")
                    P.dma("sp", t1[b][:, 0:bs], fT[c0:c0 + 128, b0_:b0_ + bs], reads=["dram:fT"], writes=[f"s4_t1{b}"],
                          key=f"s4_t1l{b}")
                    P.op("dve", lambda e: e.tensor_tensor(out=t2[b][:, 0:bs], in0=ps[:, 0:bs], in1=gB[b][:, 0:bs], op=ALU.mult),
                         [ps_res, f"s4_gB{b}"], [f"s4_t2{b}"])
                    P.op("dve", lambda e: e.tensor_tensor(out=mo[b][:, 0:bs], in0=t2[b][:, 0:bs], in1=t1[b][:, 0:bs], op=ALU.add),
                         [f"s4_t2{b}", f"s4_t1{b}"], [f"s4_mo{b}"])
                    P.dma("sp", mT[c0:c0 + 128, b0_:b0_ + bs], mo[b][:, 0:bs], reads=[f"s4_mo{b}"], writes=["dram:mT"],
                          key=f"s4_mo{b}")

                linear_stage("l4b", [(ymlaT, 1024, BF16, "dram:ymlaT", w_bmla)], grouped(chunks(0, D), 8), epiB, 1024, blocks=WB)
                P.barrier()
                P.stack = st0
            stop_if("stop4")
            with contextlib.ExitStack() as stx:
                P.stack = stx
                e5 = store_epi("l5", fT, "dram:fT", F32)
                linear_stage("l5", [(mT, D, BF16, "dram:mT", w_out)], grouped(chunks(0, D), 8), e5, 1024, blocks=WB)
                P.barrier()
                P.stack = st0
            rmsnorm_stage("n5", fT, D, 16, "dram:fT", resid=hwT, resid_res="dram:hwT", blocks=WB)
            stop_if("stop5")
            rmsnorm_stage("n6", hwT, D, 32, "dram:hwT", dst=uwT, dst_res="dram:uwT", blocks=WB)
            with contextlib.ExitStack() as stx:
                P.stack = stx
                NCH = 4
                ub = [[P.sb(f"s6_u{h}_{i}", [128, TB + 2], F32) for i in range(NCH)] for h in range(2)]
                cv = [P.sb(f"s6_cv{h}", [128, TB], F32) for h in range(2)]
                tq = P.sb("s6_tq", [128, TB], F32)
                fo = [P.sb(f"s6_fo{i}", [128, TB], BF16) for i in range(2)]
                stt = {"i": 0}

                def epi6(gi, ci, cw, j, ps, ps_res, blk):
                    c0, w = cw
                    b0_, bs = blk
                    half = 0 if ci < NCH else 1
                    cc = ci % NCH
                    u = ub[half][cc]
                    ur = f"s6_u{half}_{cc}"
                    if j == 0:
                        P.op("pool", lambda e: e.memset(u[:, 0:2], 0.0), [ur], [ur])
                    else:
                        P.op("pool", lambda e: e.tensor_copy(out=u[:, 0:2], in_=u[:, bs:bs + 2]), [ur], [ur])
                    evac_copy(u[:, 2:bs + 2], ps[:, 0:bs], [ps_res, ur], [ur])
                    if half == 1:
                        chn = c0 // 128
                        chg = chn - 44
                        for hh, ch in ((0, chg), (1, chn)):
                            uu = ub[hh][cc]
                            uur = f"s6_u{hh}_{cc}"
                            P.op("dve", lambda e, uu=uu, ch=ch, hh=hh: e.tensor_scalar(
                                out=cv[hh][:, 0:bs], in0=uu[:, 0:bs], scalar1=V[:, 256 + ch:257 + ch], scalar2=V[:, 168 + ch:169 + ch],
                                op0=ALU.mult, op1=ALU.add), [uur, "V"], [f"s6_cv{hh}"])
                            P.op("dve", lambda e, uu=uu, ch=ch, hh=hh: e.scalar_tensor_tensor(
                                out=cv[hh][:, 0:bs], in0=uu[:, 1:bs + 1], scalar=V[:, 344 + ch:345 + ch], in1=cv[hh][:, 0:bs],
                                op0=ALU.mult, op1=ALU.add), [uur, "V", f"s6_cv{hh}"], [f"s6_cv{hh}"])
                            P.op("dve", lambda e, uu=uu, ch=ch, hh=hh: e.scalar_tensor_tensor(
                                out=cv[hh][:, 0:bs], in0=uu[:, 2:bs + 2], scalar=V[:, 432 + ch:433 + ch], in1=cv[hh][:, 0:bs],
                                op0=ALU.mult, op1=ALU.add), [uur, "V", f"s6_cv{hh}"], [f"s6_cv{hh}"])
                        P.op("pool", lambda e: e.tensor_tensor(out=tq[:, 0:bs], in0=cv[0][:, 0:bs], in1=cv[0][:, 0:bs], op=ALU.mult),
                             ["s6_cv0"], ["s6_tq"])
                        P.op("pool", lambda e: e.tensor_scalar(out=tq[:, 0:bs], in0=tq[:, 0:bs], scalar1=0.044715, scalar2=1.0,
                                                               op0=ALU.mult, op1=ALU.add), ["s6_tq"], ["s6_tq"])
                        P.op("pool", lambda e: e.tensor_tensor(out=tq[:, 0:bs], in0=tq[:, 0:bs], in1=cv[0][:, 0:bs], op=ALU.mult),
                             ["s6_tq", "s6_cv0"], ["s6_tq"])
                        P.op("act", lambda e: e.activation(out=tq[:, 0:bs], in_=tq[:, 0:bs], func=AF.Sigmoid, scale=1.5957691216),
                             ["s6_tq"], ["s6_tq"])
                        P.op("pool", lambda e: e.tensor_tensor(out=tq[:, 0:bs], in0=tq[:, 0:bs], in1=cv[0][:, 0:bs], op=ALU.mult),
                             ["s6_tq", "s6_cv0"], ["s6_tq"])
                        b = stt["i"] % 2
                        stt["i"] += 1
                        P.op("pool", lambda e: e.tensor_tensor(out=fo[b][:, 0:bs], in0=tq[:, 0:bs], in1=cv[1][:, 0:bs], op=ALU.mult),
                             ["s6_tq", "s6_cv1"], [f"s6_fo{b}"])
                        P.dma("sp", ffT[chg * 128:(chg + 1) * 128, b0_:b0_ + bs], fo[b][:, 0:bs], reads=[f"s6_fo{b}"],
                              writes=["dram:ffT"], key=f"s6_fo{b}")

                grps = []
                for g0 in range(0, 44, NCH):
                    grps.append(chunks(g0 * 128, NCH * 128) + chunks(DFF + g0 * 128, NCH * 128))
                linear_stage("l6", [(uwT, D, BF16, "dram:uwT", w_up)], grps, epi6, 2 * NCH * 128, blocks=WB)
                P.barrier()
                P.stack = st0
            with contextlib.ExitStack() as stx:
                P.stack = stx
                e7 = store_epi("l7", fT, "dram:fT", F32)
                linear_stage("l7", [(ffT, DFF, BF16, "dram:ffT", w_down)], grouped(chunks(0, D), 4), e7, 512, blocks=WB)
                P.barrier()
                P.stack = st0
            rmsnorm_stage("n7", fT, D, 48, "dram:fT", resid=hwT, resid_res="dram:hwT", blocks=WB)
            stop_if("stop7")
            PB = [(0, 512), (512, 512)]
            with contextlib.ExitStack() as stx:
                P.stack = stx
                e8 = store_epi("l8a", mT, "dram:mT", BF16, func=AF.Sigmoid)
                linear_stage("l8a", [(hwT[:, 2:TW], D, F32, "dram:hwT", w_pg)], grouped(chunks(0, D), 8), e8, 1024, blocks=PB)
                gl = [P.sb(f"s8_g{i}", [128, TB], BF16) for i in range(2)]
                eo = [P.sb(f"s8_eo{i}", [128, TB], F32) for i in range(2)]
                stt = {"i": 0}

                def epi8(gi, ci, cw, j, ps, ps_res, blk):
                    c0, w = cw
                    b0_, bs = blk
                    b = stt["i"] % 2
                    stt["i"] += 1
                    P.dma("sp", gl[b][:, 0:bs], mT[c0:c0 + 128, b0_:b0_ + bs], reads=["dram:mT"], writes=[f"s8_g{b}"],
                          key=f"s8_g{b}")
                    P.op("dve", lambda e: e.tensor_tensor(out=eo[b][:, 0:bs], in0=ps[:, 0:bs], in1=gl[b][:, 0:bs], op=ALU.mult),
                         [ps_res, f"s8_g{b}"], [f"s8_eo{b}"])
                    P.dma("sp", fT[c0:c0 + 128, b0_:b0_ + bs], eo[b][:, 0:bs], reads=[f"s8_eo{b}"], writes=["dram:fT"],
                          key=f"s8_eo{b}")

                linear_stage("l8b", [(pT, 256, F32, "dram:pT", w_ple)], grouped(chunks(0, D), 16), epi8, 2048, blocks=PB)
                P.barrier()
                P.stack = st0
            rmsnorm_stage("n8", fT[:, 0:1024], D, 64, "dram:fT", resid=hwT[:, 2:TW], resid_res="dram:hwT", blocks=PB)
            transpose_stage("to", hwT[:, 2:TW], out, D, 1024, "dram:hwT", "dram:out")

        try:
            main_seq()
        except _Stop:
            P.stack = st0
        P.wait_all("sp", [f"dram:{n}" for n in (["out"] + list(debug)) if not n.startswith("skip") and not n.startswith("stop")])
        P.emit()
    return nc


def rwkv_stage(P, nc, st0, V, zrwT, yrwT, rw_w2, rw_a2, rw_g2, ident_bf, bones_bf, maskP, maskL, eye, rmask, omka,
               evac_copy):
    NEG_E = -math.exp(-0.5)
    with contextlib.ExitStack() as st:
        prev_stack = P.stack
        P.stack = st
        sb = P.sb
        w2b = sb("rk_w2b", [64, 1024], BF16)
        a2b = sb("rk_a2b", [64, 1024], BF16)
        g2b = sb("rk_g2b", [128, 2, 1024], BF16)
        P.dma("pool", w2b[:], rw_w2, writes=["rk_w"], key="rk_w")
        P.dma("pool", a2b[:], rw_a2, writes=["rk_w"], key="rk_w")
        P.dma("pool", g2b[:, 0, :], rw_g2[0:128, :], writes=["rk_w"], key="rk_w")
        P.dma("pool", g2b[0:32, 1, :], rw_g2[128:160, :], writes=["rk_w"], key="rk_w")
        twd = sb("rk_twd", [64, T], BF16)
        ads = sb("rk_ads", [64, T], BF16)
        sg1 = sb("rk_sg1", [128, T], BF16)
        sg2 = sb("rk_sg2", [32, T], BF16)
        zin = sb("rk_zin", [128, TB + 1], F32)
        dd = sb("rk_dd", [128, TB], F32)
        for (row0, nr, mcol, dst, func) in ((3072, 64, 520, twd, AF.Tanh), (3136, 64, 521, ads, AF.Copy),
                                            (3200, 128, 522, sg1, AF.Sigmoid), (3328, 32, 523, sg2, AF.Sigmoid)):
            for j in range(NB):
                if j == 0:
                    P.op("dve", lambda e: e.memset(zin[:, 0:1], 0.0), [], ["rk_zin"])
                    P.dma("sp", zin[0:nr, 1:TB + 1], zrwT[row0:row0 + nr, 0:TB], reads=["dram:zrwT"], writes=["rk_zin"],
                          key="rk_zin")
                else:
                    P.dma("sp", zin[0:nr, :], zrwT[row0:row0 + nr, j * TB - 1:(j + 1) * TB], reads=["dram:zrwT"],
                          writes=["rk_zin"], key="rk_zin")
                P.op("dve", lambda e, nr=nr: e.tensor_tensor(out=dd[0:nr, :], in0=zin[0:nr, 0:TB], in1=zin[0:nr, 1:TB + 1],
                                                             op=ALU.subtract), ["rk_zin"], ["rk_dd"])
                P.op("dve", lambda e, nr=nr, mcol=mcol: e.scalar_tensor_tensor(
                    out=dd[0:nr, :], in0=dd[0:nr, :], scalar=V[0:nr, mcol:mcol + 1], in1=zin[0:nr, 1:TB + 1],
                    op0=ALU.mult, op1=ALU.add), ["rk_dd", "rk_zin", "V"], ["rk_dd"])
                P.op("act", lambda e, nr=nr, dst=dst, func=func, j=j: e.activation(
                    out=dst[0:nr, j * TB:(j + 1) * TB], in_=dd[0:nr, :], func=func), ["rk_dd"], ["rk_lin"])
        zt = {n: sb(f"rk_z{n}", [128, TB + 1], F32) for n in "rkv"}
        xs = {n: sb(f"rk_s{n}", [128, TB], F32) for n in "rkv"}
        tA = sb("rk_tA", [128, TB], F32)
        tB = sb("rk_tB", [128, TB], F32)
        tC = sb("rk_tC", [128, TB], F32)
        av = sb("rk_a", [128, TB], F32)
        gv = sb("rk_g", [128, TB], F32)
        kkn = sb("rk_kkn", [128, TB], F32)
        k2 = sb("rk_k2", [128, TB], F32)
        logw = sb("rk_logw", [128, TB], F32)
        cum = sb("rk_cum", [128, TB], F32)
        gam = sb("rk_gam", [128, TB], F32)
        ginv = sb("rk_ginv", [128, TB], F32)
        gprev = sb("rk_gprev", [128, TB], F32)
        bonus = sb("rk_bonus", [128, TB], F32)
        tbf = sb("rk_tbf", [128, TB], BF16)
        vbf = sb("rk_vbf", [128, TB], BF16)
        AR = sb("rk_AR", [128, 8, 2, 64], BF16)
        BK = sb("rk_BK", [128, 8, 2, 64], BF16)
        PM = sb("rk_PM", [128, 8, 2, 128], BF16)
        TM = sb("rk_TM", [128, 8, 3, 64], BF16)
        MTs = sb("rk_MT", [128, 8, 64], BF16)
        NS = sb("rk_NS", [128, 2, 64], BF16)
        Lc = sb("rk_L", [128, 64], BF16)
        Hf = sb("rk_Hf", [128, 64], F32)
        HG = sb("rk_HG", [128, 64], F32)
        Hb = sb("rk_Hb", [128, 64], BF16)
        Zs = sb("rk_Zs", [128, 64], BF16)
        Us = sb("rk_Us", [128, 64], BF16)
        Yt = sb("rk_Y", [128, TB], F32)
        yo = sb("rk_yo", [128, TB], BF16)
        ps_a = P.ps("rk_psa", [128, TB], F32)
        ps_b = P.ps("rk_psb", [128, TB], F32)
        small = P.ps("rk_small", [128, 512], F32)
        small2 = P.ps("rk_small2", [128, 512], F32)
        pp = small[:, 0:256].rearrange("p (a t) -> p a t", a=2)
        lp = small[:, 256:320]
        ps1 = small[:, 320:448]
        small3 = P.ps("rk_small3", [128, 512], F32)
        ps2 = small3[:, 0:64]
        sq = small2[:, 0:192].rearrange("p (a t) -> p a t", a=3)
        trp_full = P.ps("rk_trp", [128, 1024], BF16)
        trp = trp_full[:, 0:192].rearrange("p (a t) -> p a t", a=3)
        yps = P.ps("rk_yps", [128, TB], F32)

        def dve(fn, r, w):
            P.op("dve", fn, r, w)

        def act(fn, r, w):
            P.op("act", fn, r, w)

        def pe(fn, r, w):
            P.op("pe", fn, r, w)

        def c3(ap):
            return ap.rearrange("p (c t) -> p c t", t=64)

        for pr in range(DBG.get("rkp", 8)):
            pc = slice(pr * 128, (pr + 1) * 128)
            vcol = lambda base: V[:, base + pr:base + pr + 1]
            dve(lambda e: e.memset(Hf[:], 0.0), [], ["rk_Hf"])
            dve(lambda e: e.memset(Hb[:], 0.0), [], ["rk_Hb"])
            for j in range(DBG.get("rkb", NB)):
                for gi, n in enumerate("rkv"):
                    row0 = gi * 1024 + pr * 128
                    if j == 0:
                        dve(lambda e, n=n: e.memset(zt[n][:, 0:1], 0.0), [], [f"rk_z{n}"])
                        P.dma("sp", zt[n][:, 1:TB + 1], zrwT[row0:row0 + 128, 0:TB], reads=["dram:zrwT"],
                              writes=[f"rk_z{n}"], key=f"rk_z{n}")
                    else:
                        P.dma("sp", zt[n][:], zrwT[row0:row0 + 128, j * TB - 1:(j + 1) * TB], reads=["dram:zrwT"],
                              writes=[f"rk_z{n}"], key=f"rk_z{n}")
                    mcol = 80 + gi * 8 + pr
                    dve(lambda e, n=n: e.tensor_tensor(out=xs[n][:], in0=zt[n][:, 0:TB], in1=zt[n][:, 1:TB + 1],
                                                       op=ALU.subtract), [f"rk_z{n}"], [f"rk_s{n}"])
                    dve(lambda e, n=n, mcol=mcol: e.scalar_tensor_tensor(
                        out=xs[n][:], in0=xs[n][:], scalar=V[:, mcol:mcol + 1], in1=zt[n][:, 1:TB + 1], op0=ALU.mult,
                        op1=ALU.add), [f"rk_s{n}", f"rk_z{n}", "V"], [f"rk_s{n}"])
                js = slice(j * TB, (j + 1) * TB)
                pe(lambda e: e.matmul(ps_a[:], lhsT=w2b[:, pc], rhs=twd[:, js], start=True, stop=True),
                   ["rk_w", "rk_lin"], ["rk_psa"])
                act(lambda e: e.activation(out=logw[:], in_=ps_a[:], func=AF.Sigmoid, bias=vcol(104)), ["rk_psa", "V"],
                    ["rk_logw"])
                dve(lambda e: e.tensor_scalar(out=logw[:], in0=logw[:], scalar1=NEG_E, scalar2=None, op0=ALU.mult),
                    ["rk_logw"], ["rk_logw"])
                pe(lambda e: e.matmul(ps_b[:], lhsT=a2b[:, pc], rhs=ads[:, js], start=True, stop=True),
                   ["rk_w", "rk_lin"], ["rk_psb"])
                act(lambda e: e.activation(out=av[:], in_=ps_b[:], func=AF.Sigmoid, bias=vcol(112)), ["rk_psb", "V"],
                    ["rk_a"])
                pe(lambda e: e.matmul(ps_a[:], lhsT=g2b[:, 0, pc], rhs=sg1[:, js], start=True, stop=False),
                   ["rk_w", "rk_lin"], ["rk_psa"])
                pe(lambda e: e.matmul(ps_a[:], lhsT=g2b[0:32, 1, pc], rhs=sg2[:, js], start=False, stop=True),
                   ["rk_w", "rk_lin"], ["rk_psa"])
                act(lambda e: e.activation(out=gv[:], in_=ps_a[:], func=AF.Copy), ["rk_psa"], ["rk_g"])
                dve(lambda e: e.tensor_scalar(out=tA[:], in0=xs["k"][:], scalar1=vcol(120), scalar2=None, op0=ALU.mult),
                    ["rk_sk", "V"], ["rk_tA"])
                dve(lambda e: e.tensor_tensor(out=tbf[:], in0=tA[:], in1=tA[:], op=ALU.mult), ["rk_tA"], ["rk_tbf"])
                pe(lambda e: e.matmul(ps_b[:], lhsT=bones_bf[:], rhs=tbf[:], start=True, stop=True),
                   ["bones_bf", "rk_tbf"], ["rk_psb"])
                act(lambda e: e.activation(out=tB[:], in_=ps_b[:], func=AF.Sqrt), ["rk_psb"], ["rk_tB"])
                dve(lambda e: e.tensor_scalar(out=tB[:], in0=tB[:], scalar1=1e-12, scalar2=None, op0=ALU.max),
                    ["rk_tB"], ["rk_tB"])
                dve(lambda e: e.reciprocal(out=tB[:], in_=tB[:]), ["rk_tB"], ["rk_tB"])
                dve(lambda e: e.tensor_tensor(out=kkn[:], in0=tA[:], in1=tB[:], op=ALU.mult), ["rk_tA", "rk_tB"],
                    ["rk_kkn"])
                dve(lambda e: e.tensor_scalar(out=tA[:], in0=av[:], scalar1=vcol(128), scalar2=omka[:, pr:pr + 1],
                                              op0=ALU.mult, op1=ALU.add), ["rk_a", "V", "omka"], ["rk_tA"])
                dve(lambda e: e.tensor_tensor(out=k2[:], in0=xs["k"][:], in1=tA[:], op=ALU.mult), ["rk_sk", "rk_tA"],
                    ["rk_k2"])
                dve(lambda e: e.tensor_tensor_scan(out=cum[:], data0=rmask[:], data1=logw[:], initial=0.0, op0=ALU.mult,
                                                   op1=ALU.add), ["msk", "rk_logw"], ["rk_cum"])
                act(lambda e: e.activation(out=gam[:], in_=cum[:], func=AF.Exp), ["rk_cum"], ["rk_gam"])
                act(lambda e: e.activation(out=ginv[:], in_=cum[:], func=AF.Exp, scale=-1.0), ["rk_cum"], ["rk_ginv"])
                dve(lambda e: e.tensor_tensor(out=tB[:], in0=cum[:], in1=logw[:], op=ALU.subtract),
                    ["rk_cum", "rk_logw"], ["rk_tB"])
                act(lambda e: e.activation(out=gprev[:], in_=tB[:], func=AF.Exp), ["rk_tB"], ["rk_gprev"])
                dve(lambda e: e.scalar_tensor_tensor(out=AR[:, :, 0, :], in0=c3(kkn[:]), scalar=-1.0, in1=c3(gprev[:]),
                                                     op0=ALU.mult, op1=ALU.mult), ["rk_kkn", "rk_gprev"], ["rk_AR"])
                dve(lambda e: e.tensor_tensor(out=AR[:, :, 1, :], in0=c3(xs["r"][:]), in1=c3(gam[:]), op=ALU.mult),
                    ["rk_sr", "rk_gam", "rk_AR"], ["rk_AR"])
                dve(lambda e: e.tensor_tensor(out=tA[:], in0=kkn[:], in1=av[:], op=ALU.mult), ["rk_kkn", "rk_a"],
                    ["rk_tA"])
                dve(lambda e: e.tensor_tensor(out=BK[:, :, 0, :], in0=c3(tA[:]), in1=c3(ginv[:]), op=ALU.mult),
                    ["rk_tA", "rk_ginv"], ["rk_BK"])
                dve(lambda e: e.tensor_tensor(out=BK[:, :, 1, :], in0=c3(k2[:]), in1=c3(ginv[:]), op=ALU.mult),
                    ["rk_k2", "rk_ginv", "rk_BK"], ["rk_BK"])
                act(lambda e: e.activation(out=vbf[:], in_=xs["v"][:], func=AF.Copy), ["rk_sv"], ["rk_vbf"])
                dve(lambda e: e.tensor_tensor(out=tC[:], in0=xs["r"][:], in1=k2[:], op=ALU.mult), ["rk_sr", "rk_k2"],
                    ["rk_tC"])
                dve(lambda e: e.tensor_scalar(out=tbf[:], in0=tC[:], scalar1=vcol(136), scalar2=None, op0=ALU.mult),
                    ["rk_tC", "V", "rk_tbf"], ["rk_tbf"])
                pe(lambda e: e.matmul(ps_b[:], lhsT=bones_bf[:], rhs=tbf[:], start=True, stop=True),
                   ["bones_bf", "rk_tbf"], ["rk_psb"])
                dve(lambda e: e.tensor_tensor(out=bonus[:], in0=ps_b[:], in1=xs["v"][:], op=ALU.mult),
                    ["rk_psb", "rk_sv"], ["rk_bonus"])
                for c in range(DBG.get("rkc1", 8)):
                    cc = slice(c * 64, (c + 1) * 64)
                    for h in range(0 if not (DBG.get("rkx", 0) & 1) else 99, 2):
                        sl = slice(64 * h, 64 * h + 64)
                        for ti, (src, sres) in enumerate(((None, "rk_BK"), (None, "rk_BK"), (vbf, "rk_vbf"))):
                            in_ap = BK[sl, c, ti, :] if ti < 2 else vbf[sl, cc]
                            pe(lambda e, sl=sl, ti=ti, in_ap=in_ap: e.transpose(out=trp[sl, ti, :], in_=in_ap,
                                                                               identity=ident_bf[sl, sl]),
                               [sres, "ident_bf"], ["rk_trp"])
                    if not (DBG.get("rkx", 0) & 1):
                        evac_copy(TM[:, c, :, :], trp, ["rk_trp"], ["rk_TM"])
                    if DBG.get("rkx", 0) & 2:
                        continue
                    for h in range(2):
                        sl = slice(64 * h, 64 * h + 64)
                        arv = AR[sl, c, :, :].rearrange("p a t -> p (a t)")
                        pe(lambda e, sl=sl, c=c, arv=arv: e.matmul(pp[sl, 0, :], lhsT=BK[sl, c, 0, :], rhs=arv, start=True,
                                                                   stop=True), ["rk_BK", "rk_AR"], ["rk_small"])
                        pe(lambda e, sl=sl, c=c, arv=arv: e.matmul(pp[sl, 1, :], lhsT=BK[sl, c, 1, :], rhs=arv, start=True,
                                                                   stop=True), ["rk_BK", "rk_AR"], ["rk_small"])
                        pe(lambda e, sl=sl, c=c: e.matmul(lp[sl, :], lhsT=AR[sl, c, 0, :], rhs=BK[sl, c, 0, :], start=True,
                                                          stop=True), ["rk_BK", "rk_AR"], ["rk_small"])
                    dve(lambda e, c=c: e.tensor_tensor(out=PM[:, c, :, :], in0=pp[:], in1=maskP[:], op=ALU.mult),
                        ["rk_small", "msk"], ["rk_PM"])
                    dve(lambda e: e.tensor_tensor(out=Lc[:], in0=lp[:], in1=maskL[:], op=ALU.mult), ["rk_small", "msk"],
                        ["rk_L"])
                    act(lambda e, c=c: e.activation(out=NS[:, 0, :], in_=PM[:, c, 0, 0:64], func=AF.Copy), ["rk_PM"],
                        ["rk_NS"])
                    dve(lambda e, c=c: e.tensor_copy(out=NS[:, 1, :], in_=eye[:]), ["msk", "rk_NS"], ["rk_NS"])
                    for step in range(6 if not (DBG.get("rkx", 0) & 4) else 0):
                        last = step == 5
                        for h in range(2):
                            sl = slice(64 * h, 64 * h + 64)
                            if not last:
                                nsv = NS[sl, :, :].rearrange("p a t -> p (a t)")
                                pe(lambda e, sl=sl, nsv=nsv: e.matmul(ps1[sl, :], lhsT=Lc[sl, :], rhs=nsv, start=True,
                                                                      stop=True), ["rk_L", "rk_NS"], ["rk_small"])
                                pe(lambda e, sl=sl: e.matmul(ps2[sl, :], lhsT=NS[sl, 0, :], rhs=Lc[sl, :], start=True,
                                                             stop=True), ["rk_L", "rk_NS"], ["rk_small3"])
                            else:
                                pe(lambda e, sl=sl: e.matmul(ps1[sl, 64:128], lhsT=Lc[sl, :], rhs=NS[sl, 1, :], start=True,
                                                             stop=True), ["rk_L", "rk_NS"], ["rk_small"])
                        if not last:
                            dve(lambda e: e.tensor_copy(out=NS[:, 0, :], in_=ps1[:, 0:64]),
                                ["rk_small", "rk_NS"], ["rk_NS"])
                            dve(lambda e: e.tensor_tensor(out=NS[:, 1, :], in0=NS[:, 1, :], in1=ps1[:, 64:128], op=ALU.add),
                                ["rk_small", "rk_NS"], ["rk_NS"])
                            act(lambda e: e.activation(out=Lc[:], in_=ps2[:], func=AF.Copy), ["rk_small3", "rk_L"], ["rk_L"])
                        else:
                            dve(lambda e, c=c: e.tensor_tensor(out=MTs[:, c, :], in0=NS[:, 1, :], in1=ps1[:, 64:128],
                                                               op=ALU.add), ["rk_small", "rk_NS"], ["rk_MT"])
                for c in range(DBG.get("rkc2", 8)):
                    cc = slice(c * 64, (c + 1) * 64)
                    gcol = gam[:, c * 64 + 63:c * 64 + 64]
                    for h in range(2):
                        sl = slice(64 * h, 64 * h + 64)
                        pe(lambda e, sl=sl, c=c: e.matmul(sq[sl, 0, :], lhsT=AR[sl, c, 0, :], rhs=Hb[sl, :], start=True,
                                                          stop=False), ["rk_AR", "rk_Hb"], ["rk_small2"])
                        pe(lambda e, sl=sl, c=c: e.matmul(sq[sl, 0, :], lhsT=PM[sl, c, 1, 0:64], rhs=TM[sl, c, 2, :],
                                                          start=False, stop=True), ["rk_PM", "rk_TM"], ["rk_small2"])
                    act(lambda e: e.activation(out=Zs[:], in_=sq[:, 0, :], func=AF.Copy), ["rk_small2"], ["rk_Zs"])
                    for h in range(2):
                        sl = slice(64 * h, 64 * h + 64)
                        pe(lambda e, sl=sl, c=c: e.matmul(sq[sl, 1, :], lhsT=MTs[sl, c, :], rhs=Zs[sl, :], start=True,
                                                          stop=True), ["rk_MT", "rk_Zs"], ["rk_small2"])
                    dve(lambda e: e.tensor_copy(out=Us[:], in_=sq[:, 1, :]), ["rk_small2"], ["rk_Us"])
                    for h in range(2):
                        sl = slice(64 * h, 64 * h + 64)
                        pe(lambda e, sl=sl, c=c, cc=cc: e.matmul(yps[sl, cc], lhsT=Hb[sl, :], rhs=AR[sl, c, 1, :], start=True,
                                                                 stop=False), ["rk_AR", "rk_Hb"], ["rk_yps"])
                        pe(lambda e, sl=sl, c=c, cc=cc: e.matmul(yps[sl, cc], lhsT=Us[sl, :], rhs=PM[sl, c, 0, 64:128],
                                                                 start=False, stop=False), ["rk_Us", "rk_PM"], ["rk_yps"])
                        pe(lambda e, sl=sl, c=c, cc=cc: e.matmul(yps[sl, cc], lhsT=TM[sl, c, 2, :], rhs=PM[sl, c, 1, 64:128],
                                                                 start=False, stop=True), ["rk_TM", "rk_PM"], ["rk_yps"])
                        pe(lambda e, sl=sl, c=c: e.matmul(sq[sl, 2, :], lhsT=TM[sl, c, 0, :], rhs=Us[sl, :], start=True,
                                                          stop=False), ["rk_TM", "rk_Us"], ["rk_small2"])
                        pe(lambda e, sl=sl, c=c: e.matmul(sq[sl, 2, :], lhsT=TM[sl, c, 1, :], rhs=TM[sl, c, 2, :],
                                                          start=False, stop=True), ["rk_TM"], ["rk_small2"])
                    P.op("dve", lambda e, gcol=gcol: e.tensor_scalar(out=HG[:], in0=Hf[:], scalar1=gcol, scalar2=None,
                                                                      op0=ALU.mult), ["rk_Hf", "rk_gam"], ["rk_HG"])
                    dve(lambda e, gcol=gcol: e.scalar_tensor_tensor(out=Hf[:], in0=sq[:, 2, :], scalar=gcol, in1=HG[:],
                                                                    op0=ALU.mult, op1=ALU.add),
                        ["rk_small2", "rk_HG", "rk_gam", "rk_Hf"], ["rk_Hf"])
                    act(lambda e: e.activation(out=Hb[:], in_=Hf[:], func=AF.Copy), ["rk_Hf", "rk_Hb"], ["rk_Hb"])
                act(lambda e: e.activation(out=Yt[:], in_=yps[:], func=AF.Copy), ["rk_yps"], ["rk_Y"])
                dve(lambda e: e.tensor_copy(out=tbf[:], in_=Yt[:]), ["rk_Y", "rk_tbf"], ["rk_tbf"])
                pe(lambda e: e.matmul(ps_a[:], lhsT=bones_bf[:], rhs=tbf[:], start=True, stop=True), ["bones_bf", "rk_tbf"],
                   ["rk_psa"])
                dve(lambda e: e.scalar_tensor_tensor(out=tA[:], in0=ps_a[:], scalar=-1.0 / 64, in1=Yt[:], op0=ALU.mult,
                                                     op1=ALU.add), ["rk_psa", "rk_Y"], ["rk_tA"])
                dve(lambda e: e.tensor_tensor(out=tbf[:], in0=tA[:], in1=tA[:], op=ALU.mult), ["rk_tA", "rk_tbf"],
                    ["rk_tbf"])
                pe(lambda e: e.matmul(ps_a[:], lhsT=bones_bf[:], rhs=tbf[:], start=True, stop=True), ["bones_bf", "rk_tbf"],
                   ["rk_psa"])
                act(lambda e: e.activation(out=tB[:], in_=ps_a[:], func=AF.Sqrt, bias=GN_EPS, scale=1.0 / 64),
                    ["rk_psa"], ["rk_tB"])
                dve(lambda e: e.reciprocal(out=tB[:], in_=tB[:]), ["rk_tB"], ["rk_tB"])
                dve(lambda e: e.tensor_tensor(out=tA[:], in0=tA[:], in1=tB[:], op=ALU.mult), ["rk_tA", "rk_tB"], ["rk_tA"])
                dve(lambda e: e.tensor_scalar(out=tA[:], in0=tA[:], scalar1=vcol(144), scalar2=vcol(152), op0=ALU.mult,
                                              op1=ALU.add), ["rk_tA", "V"], ["rk_tA"])
                dve(lambda e: e.tensor_tensor(out=tA[:], in0=tA[:], in1=bonus[:], op=ALU.add), ["rk_tA", "rk_bonus"],
                    ["rk_tA"])
                dve(lambda e: e.tensor_tensor(out=yo[:], in0=tA[:], in1=gv[:], op=ALU.mult), ["rk_tA", "rk_g"], ["rk_yo"])
                P.dma("sp", yrwT[pc, js], yo[:], reads=["rk_yo"], writes=["dram:yrwT"], key="rk_yo")
        P.barrier()
        P.stack = prev_stack


def mla_stage(P, nc, st0, V, pos, pos_w, qk_tab, d_tab, zmlaT, cqnT_full, ckvnT, w_q_up, w_kv_up, ymlaT, ones_bf, evac_copy):
    SCALE = 192 ** -0.5
    TWO_PI = 2.0 * math.pi
    QB = 342
    with contextlib.ExitStack() as st:
        prev_stack = P.stack
        P.stack = st
        sb = P.sb
        wq = sb("ml_wq", [128, 4, 1536], BF16)
        wkv = sb("ml_wkv", [128, 4, 2048], BF16)
        for kc in range(4):
            P.dma("pool", wq[:, kc, :], w_q_up[kc * 128:(kc + 1) * 128, :], writes=["ml_w"], key="ml_w")
            P.dma("pool", wkv[:, kc, :], w_kv_up[kc * 128:(kc + 1) * 128, :], writes=["ml_w"], key="ml_w")
        qk = sb("ml_qk", [128, 3, QB], F32)
        dt = sb("ml_dt", [128, 32], F32)
        P.dma("sp", qk[:], qk_tab.rearrange("p (a t) -> p a t", a=3), writes=["ml_tab"], key="ml_tab")
        P.dma("sp", dt[:], d_tab, writes=["ml_tab"], key="ml_tab")
        cs = sb("ml_cs", [64, T], F32)
        sn = sb("ml_sn", [64, T], F32)
        csw = sb("ml_csw", [64, TW], F32)
        snw = sb("ml_snw", [64, TW], F32)
        with contextlib.ExitStack() as st2:
            P.stack = st2
            posi = P.sb("ml_posi", [64, T], I32)
            ang = P.sb("ml_ang", [64, T], F32)
            tt = P.sb("ml_tt", [64, T], F32)
            kf = P.sb("ml_kf", [64, T], F32)
            for (psrc, n, cdst, sdst) in ((pos, T, cs, sn), (pos_w, TW, csw, snw)):
                P.dma("sp", posi[:, 0:n], psrc[0:1, :].to_broadcast([64, n]), writes=["ml_posi"], key="ml_posi")
                P.op("dve", lambda e: e.tensor_copy(out=ang[:, 0:n], in_=posi[:, 0:n]), ["ml_posi"], ["ml_ang"])
                P.op("dve", lambda e: e.tensor_scalar(out=ang[:, 0:n], in0=ang[:, 0:n], scalar1=V[0:64, 524:525], scalar2=None,
                                                      op0=ALU.mult), ["ml_ang", "V"], ["ml_ang"])
                for (dst, shift) in ((sdst, 0.5), (cdst, 0.75)):
                    P.op("dve", lambda e, shift=shift: e.tensor_scalar(out=tt[:, 0:n], in0=ang[:, 0:n], scalar1=1.0 / TWO_PI,
                                                                       scalar2=shift, op0=ALU.mult, op1=ALU.add),
                         ["ml_ang", "ml_tt"], ["ml_tt"])
                    P.op("dve", lambda e: e.tensor_copy(out=posi[:, 0:n], in_=tt[:, 0:n]), ["ml_tt", "ml_posi", "ml_ang"],
                         ["ml_posi"])
                    P.op("dve", lambda e: e.tensor_copy(out=kf[:, 0:n], in_=posi[:, 0:n]), ["ml_posi"], ["ml_kf"])
                    P.op("dve", lambda e: e.tensor_tensor(out=tt[:, 0:n], in0=tt[:, 0:n], in1=kf[:, 0:n], op=ALU.subtract),
                         ["ml_tt", "ml_kf"], ["ml_tt"])
                    P.op("dve", lambda e: e.tensor_scalar(out=kf[:, 0:n], in0=tt[:, 0:n], scalar1=0.0, scalar2=None,
                                                          op0=ALU.is_lt), ["ml_tt", "ml_kf"], ["ml_kf"])
                    P.op("dve", lambda e: e.scalar_tensor_tensor(out=tt[:, 0:n], in0=kf[:, 0:n], scalar=-0.5, in1=tt[:, 0:n],
                                                                 op0=ALU.add, op1=ALU.add), ["ml_tt", "ml_kf"], ["ml_tt"])
                    P.op("act", lambda e, dst=dst: e.activation(out=dst[:, 0:n], in_=tt[:, 0:n], func=AF.Sin, scale=TWO_PI),
                         ["ml_tt"], ["ml_rope"])
            P.barrier()
            P.stack = st
        kr = sb("ml_kr", [64, T], BF16)
        kn = sb("ml_kn", [128, T], BF16)
        vtm = sb("ml_vtm", [128, 32, 128], BF16)
        xq = [sb(f"ml_xq{i}", [128, 4, TB], BF16) for i in range(2)]
        raw = sb("ml_raw", [64, TB], F32)
        t1 = sb("ml_t1", [64, TB], F32)
        t2 = sb("ml_t2", [64, TB], F32)
        qn = sb("ml_qn", [128, TB], BF16)
        qr = sb("ml_qr", [64, TB], BF16)
        pT = [sb(f"ml_pT{i}", [128, TB], BF16) for i in range(3)]
        rinv = sb("ml_rinv", [128, TB], F32)
        yo = [sb(f"ml_yo{i}", [128, TB], BF16) for i in range(2)]
        psq = P.ps("ml_psq", [128, TB], F32)
        psr = P.ps("ml_psr", [64, TB], F32)
        psv = P.ps("ml_psv", [128, 4, 128], F32)
        sps = [P.ps(f"ml_sps{i}", [128, TB], F32) for i in range(2)]
        ops_ = P.ps("ml_ops", [128, TB], F32)
        lps = P.ps("ml_lps", [128, TB], F32)

        def rope(src_res, cst, snt, c0, n, out_ap, out_res, scale):
            lo, hi = slice(0, 32), slice(32, 64)
            js = slice(c0, c0 + n)
            d = lambda fn, r, w: P.op("dve", fn, r, w)
            d(lambda e: e.tensor_tensor(out=t1[lo, 0:n], in0=raw[lo, 0:n], in1=cst[lo, js], op=ALU.mult), [src_res, "ml_rope"],
              ["ml_t1"])
            d(lambda e: e.tensor_tensor(out=t2[lo, 0:n], in0=raw[hi, 0:n], in1=snt[hi, js], op=ALU.mult), [src_res, "ml_rope"],
              ["ml_t2"])
            d(lambda e: e.tensor_tensor(out=t1[hi, 0:n], in0=raw[hi, 0:n], in1=cst[hi, js], op=ALU.mult),
              [src_res, "ml_rope", "ml_t1"], ["ml_t1"])
            d(lambda e: e.tensor_tensor(out=t2[hi, 0:n], in0=raw[lo, 0:n], in1=snt[lo, js], op=ALU.mult),
              [src_res, "ml_rope", "ml_t2"], ["ml_t2"])
            d(lambda e: e.tensor_tensor(out=t1[lo, 0:n], in0=t1[lo, 0:n], in1=t2[lo, 0:n], op=ALU.subtract), ["ml_t1", "ml_t2"],
              ["ml_t1"])
            d(lambda e: e.tensor_tensor(out=t1[hi, 0:n], in0=t1[hi, 0:n], in1=t2[hi, 0:n], op=ALU.add), ["ml_t1", "ml_t2"],
              ["ml_t1"])
            P.op("act", lambda e: e.activation(out=out_ap, in_=t1[:, 0:n], func=AF.Copy, scale=scale), ["ml_t1"], [out_res])

        for j in range(NB):
            js = slice(j * TB, (j + 1) * TB)
            P.dma("sp", raw[:], zmlaT[1024:1088, js], reads=["dram:zmlaT"], writes=["ml_raw"], key="ml_raw")
            rope("ml_raw", cs, sn, j * TB, TB, kr[:, js], "ml_kr", 1.0)
        xi = 0
        for hd in range(DBG.get("mlh", 8)):
            for j in range(NB):
                js = slice(j * TB, (j + 1) * TB)
                b = xi % 2
                xi += 1
                P.dma("sp", xq[b][:], ckvnT[:, js].rearrange("(k p) t -> p k t", p=128), reads=["dram:ckvnT"],
                      writes=[f"ml_xq{b}"], key=f"ml_xq{b}")
                for kc in range(4):
                    P.op("pe", lambda e, kc=kc, b=b: e.matmul(psq[:], lhsT=wkv[:, kc, hd * 256:hd * 256 + 128],
                                                              rhs=xq[b][:, kc, :], start=(kc == 0), stop=(kc == 3)),
                         ["ml_w", f"ml_xq{b}"], ["ml_psq"])
                evac_copy(kn[:, js], psq[:], ["ml_psq"], ["ml_kn"])
                for tt_ in range(4):
                    for kc in range(4):
                        P.op("pe", lambda e, kc=kc, b=b, tt_=tt_: e.matmul(
                            psv[:, tt_, :], lhsT=xq[b][:, kc, tt_ * 128:(tt_ + 1) * 128],
                            rhs=wkv[:, kc, hd * 256 + 128:hd * 256 + 256], start=(kc == 0), stop=(kc == 3)),
                            ["ml_w", f"ml_xq{b}"], ["ml_psv"])
                evac_copy(vtm[:, j * 4:(j + 1) * 4, :], psv[:], ["ml_psv"], ["ml_vtm"])
            for jj in range(DBG.get("mlb", 3)):
                n = QB
                b = xi % 2
                xi += 1

                P.dma("sp", xq[b][:, :, 0:QB], cqnT_full[:, jj * QB:(jj + 1) * QB].rearrange("(k p) t -> p k t", p=128),
                      reads=["dram:cqwT"], writes=[f"ml_xq{b}"], key=f"ml_xq{b}")
                for kc in range(4):
                    P.op("pe", lambda e, kc=kc, b=b: e.matmul(psq[:, 0:n], lhsT=wq[:, kc, hd * 192:hd * 192 + 128],
                                                              rhs=xq[b][:, kc, 0:n], start=(kc == 0), stop=(kc == 3)),
                         ["ml_w", f"ml_xq{b}"], ["ml_psq"])
                P.op("act", lambda e: e.activation(out=qn[:, 0:n], in_=psq[:, 0:n], func=AF.Copy, scale=SCALE), ["ml_psq"],
                     ["ml_qn"])
                for kc in range(4):
                    P.op("pe", lambda e, kc=kc, b=b: e.matmul(psr[:, 0:n], lhsT=wq[:, kc, hd * 192 + 128:hd * 192 + 192],
                                                              rhs=xq[b][:, kc, 0:n], start=(kc == 0), stop=(kc == 3)),
                         ["ml_w", f"ml_xq{b}"], ["ml_psr"])
                P.op("act", lambda e: e.activation(out=raw[:, 0:n], in_=psr[:, 0:n], func=AF.Copy), ["ml_psr", "ml_raw"],
                     ["ml_raw"])
                rope("ml_raw", csw, snw, jj * QB, n, qr[:, 0:n], "ml_qr", SCALE)
                nkt = 32
                for kt in range(nkt):
                    si = kt % 2
                    pi = kt % 3
                    ks = slice(kt * 128, (kt + 1) * 128)
                    P.op("pe", lambda e, si=si, ks=ks: e.matmul(sps[si][:, 0:n], lhsT=kn[:, ks], rhs=qn[:, 0:n],
                                                                start=True, stop=False),
                         ["ml_kn", "ml_qn"], [f"ml_sps{si}"])
                    P.op("pe", lambda e, si=si, ks=ks: e.matmul(sps[si][:, 0:n], lhsT=kr[:, ks], rhs=qr[:, 0:n],
                                                                start=False, stop=True),
                         ["ml_kr", "ml_qr"], [f"ml_sps{si}"])
                    P.op("act", lambda e, si=si, pi=pi: e.activation(out=pT[pi][:, 0:n], in_=sps[si][:, 0:n], func=AF.Exp),
                         [f"ml_sps{si}"], [f"ml_pT{pi}"])
                    P.op("dve", lambda e, pi=pi, kt=kt, jj=jj: e.scalar_tensor_tensor(
                        out=pT[pi][:, 0:n], in0=qk[:, jj, :], scalar=dt[:, kt:kt + 1], in1=pT[pi][:, 0:n], op0=ALU.is_ge,
                        op1=ALU.mult), [f"ml_pT{pi}", "ml_tab"], [f"ml_pT{pi}"])
                    P.op("pe", lambda e, pi=pi, kt=kt: e.matmul(
                        ops_[:, 0:n], lhsT=vtm[:, kt, :], rhs=pT[pi][:, 0:n], start=(kt == 0), stop=(kt == nkt - 1)),
                        ["ml_vtm", f"ml_pT{pi}"], ["ml_ops"])
                    P.op("pe", lambda e, pi=pi, kt=kt: e.matmul(
                        lps[:, 0:n], lhsT=ones_bf[:], rhs=pT[pi][:, 0:n], start=(kt == 0), stop=(kt == nkt - 1)),
                        ["ones_bf", f"ml_pT{pi}"], ["ml_lps"])
                P.op("dve", lambda e: e.tensor_scalar(out=rinv[:, 0:n], in0=lps[:, 0:n], scalar1=1e-30, scalar2=None,
                                                      op0=ALU.max), ["ml_lps"], ["ml_rinv"])
                P.op("dve", lambda e: e.reciprocal(out=rinv[:, 0:n], in_=rinv[:, 0:n]), ["ml_rinv"], ["ml_rinv"])
                ob = jj % 2
                P.op("dve", lambda e, ob=ob: e.tensor_tensor(out=yo[ob][:, 0:n], in0=ops_[:, 0:n], in1=rinv[:, 0:n], op=ALU.mult),
                     ["ml_ops", "ml_rinv"], [f"ml_yo{ob}"])
                P.dma("sp", ymlaT[hd * 128:(hd + 1) * 128, jj * QB:(jj + 1) * QB], yo[ob][:, 0:n], reads=[f"ml_yo{ob}"],
                      writes=["dram:ymlaT"], key=f"ml_yo{ob}")
        P.barrier()
        P.stack = prev_stack


_CACHE = {}


def make_in_maps(inputs):
    sq = lambda a: np.ascontiguousarray(np.asarray(a)[0])
    V = pack_vecs(inputs)
    common = {
        "vecs": V,
        "w_in": sq(inputs["w_in"]), "rw_w2": sq(inputs["rw_w2"]), "rw_a2": sq(inputs["rw_a2"]), "rw_g2": sq(inputs["rw_g2"]),
        "w_q_up": sq(inputs["mla_w_q_up"]), "w_kv_up": sq(inputs["mla_w_kv_up"]),
        "w_brw": sq(inputs["w_branch_rw"]), "w_bmla": sq(inputs["w_branch_mla"]), "w_out": sq(inputs["w_out"]),
        "w_up": sq(inputs["w_up"]), "w_down": sq(inputs["w_down"]), "w_ple": sq(inputs["w_ple"]),
        "w_pg": sq(inputs["w_ple_gate"]),
    }
    i_ = np.arange(342)
    qk = np.zeros((128, 3, 342), np.float32)
    for jj in range(3):
        qc = np.floor_divide(342 * jj - 2 + i_, 64).astype(np.float32)
        qk[:, jj, :] = qc[None, :] - (np.arange(128)[:, None] >= 64).astype(np.float32)
    common["qk_tab"] = np.ascontiguousarray(qk.reshape(128, 3 * 342))
    xs = np.asarray(inputs["x"], np.float32)
    ps = np.asarray(inputs["p"], np.float32)[0]
    posn = np.asarray(inputs["positions"], np.int32)
    maps = []
    for c in range(8):
        b, q = c // 4, c % 4
        m = dict(common)
        m["x"] = np.ascontiguousarray(xs[b])
        m["p"] = np.ascontiguousarray(ps[b, 1024 * q:1024 * (q + 1)])
        m["pos"] = np.ascontiguousarray(posn[b:b + 1])
        pw = np.zeros((1, TW), np.int32)
        lo = 1024 * q - 2
        if lo < 0:
            pw[0, 2:] = posn[b, 0:1024]
        else:
            pw[0, :] = posn[b, lo:lo + TW]
        m["pos_w"] = pw
        m["d_tab"] = np.ascontiguousarray(np.broadcast_to((2.0 * np.arange(32) - 16.0 * q).astype(np.float32)[None, :], (128, 32)))
        maps.append(m)
    return maps


def kernel(**inputs):
    if "nc" not in _CACHE:
        _CACHE["nc"] = build_program()
    nc = _CACHE["nc"]
    maps = make_in_maps(inputs)
    res = run_bass_kernel_spmd(nc, maps, core_ids=list(range(8)))
    outs = [np.asarray(res.results[c]["out"], np.float32) for c in range(8)]
    return np.stack([np.concatenate(outs[0:4], axis=0), np.concatenate(outs[4:8], axis=0)], axis=0)
```

```python
import contextlib
import numpy as np
import concourse.bass as bass
import concourse.mybir as mybir

F32 = mybir.dt.float32
BF16 = mybir.dt.bfloat16
I32 = mybir.dt.int32
ALU = mybir.AluOpType
AF = mybir.ActivationFunctionType
AX = mybir.AxisListType


class _Rec:
    def __init__(self):
        self.calls = []

    def __getattr__(self, name):
        def f(*a, **k):
            self.calls.append((name, a, k))
            return self

        return f


def _eager(fn):
    rec = _Rec()
    fn(rec)
    assert len(rec.calls) == 1, rec.calls
    name, a, k = rec.calls[0]
    return lambda e: getattr(e, name)(*a, **k)


class Prog:
    COMPUTE = ("pe", "act", "dve", "pool")

    def __init__(self, nc, stack):
        self.nc = nc
        self.stack = stack
        self.stack0 = stack
        self.streams = {e: [] for e in ("pe", "act", "dve", "pool", "sp")}
        self.sems = {}
        self.cnt = {}
        for e in self.COMPUTE:
            self.sems[e] = stack.enter_context(nc.semaphore("s_" + e))
            self.cnt[e] = 0
        self.seen = {e: {} for e in self.streams}
        self.res = {}
        self.dma_sems = {}
        self.n_ops = 0

    def sb(self, name, shape, dt):
        return self.stack.enter_context(self.nc.sbuf_tensor(name, list(shape), dt))

    def ps(self, name, shape, dt=F32):
        return self.stack.enter_context(self.nc.psum_tensor(name, list(shape), dt))

    def _dma_sem(self, key):
        if key not in self.dma_sems:
            self.dma_sems[key] = self.stack0.enter_context(self.nc.semaphore("d_" + key))
            self.sems["D:" + key] = self.dma_sems[key]
            self.cnt["D:" + key] = 0
        return "D:" + key

    def _deps(self, eng, reads, writes, exclude=None):
        need = {}
        for r in reads:
            ent = self.res.get(r)
            if ent:
                for k, v in ent[0].items():
                    need[k] = max(need.get(k, 0), v)
                if "ps" in r or "small" in r or "trp" in r:
                    for k, v in ent[1].items():
                        if k != eng:
                            need[k] = max(need.get(k, 0), v)
        for w in writes:
            ent = self.res.get(w)
            if ent:
                if not w.startswith("dram:"):
                    for k, v in ent[0].items():
                        need[k] = max(need.get(k, 0), v)
                for k, v in ent[1].items():
                    need[k] = max(need.get(k, 0), v)
        waits = []
        for k, v in need.items():
            if k == eng and eng == "pe":
                continue
            if k == exclude:
                continue
            if self.seen[eng].get(k, 0) >= v:
                continue
            self.seen[eng][k] = v
            waits.append((k, v))
        return waits

    def _mark(self, key, val, reads, writes):
        for r in reads:
            ent = self.res.setdefault(r, [{}, {}])
            ent[1][key] = val
        for w in writes:
            ent = self.res.setdefault(w, [{}, {}])
            if w.startswith("dram:"):
                ent[0][key] = val
            else:
                ent[0] = {key: val}
                ent[1] = {}
            self.res[w] = ent

    def op(self, eng, fn, reads=(), writes=()):
        fn = _eager(fn)
        waits = self._deps(eng, reads, writes)
        self.cnt[eng] += 1
        val = self.cnt[eng]
        sem = self.sems[eng]
        sems = self.sems

        def emit(e, waits=waits, fn=fn, sem=sem):
            for k, v in waits:
                e.wait_ge(sems[k], v)
            fn(e).then_inc(sem, 1)

        self.streams[eng].append(emit)
        self._mark(eng, val, reads, writes)
        self.n_ops += 1

    def dma(self, queue, out, in_, reads=(), writes=(), key=None, **kw):
        assert key is not None
        sk = self._dma_sem(key)
        waits = self._deps(queue, reads, writes, exclude=sk)
        self.cnt[sk] += 16
        val = self.cnt[sk]
        sem = self.sems[sk]
        sems = self.sems

        def emit(e, waits=waits, sem=sem):
            for k, v in waits:
                e.wait_ge(sems[k], v)
            e.dma_start(out=out, in_=in_, **kw).then_inc(sem, 16)

        self.streams[queue].append(emit)
        self._mark(sk, val, reads, writes)
        self.n_ops += 1

    def custom(self, queue, fn, reads=(), writes=(), key=None, raw=False):
        if not raw:
            fn = _eager(fn)
        sk = self._dma_sem(key)
        waits = self._deps(queue, reads, writes, exclude=sk)
        self.cnt[sk] += 16
        val = self.cnt[sk]
        sem = self.sems[sk]
        sems = self.sems

        def emit(e, waits=waits, sem=sem):
            for k, v in waits:
                e.wait_ge(sems[k], v)
            fn(e).then_inc(sem, 16)

        self.streams[queue].append(emit)
        self._mark(sk, val, reads, writes)
        self.n_ops += 1

    def barrier(self):
        snap = dict(self.cnt)
        sems = self.sems
        for eng in self.streams:
            waits = []
            for k, v in snap.items():
                if v <= 0 or self.seen[eng].get(k, 0) >= v:
                    continue
                if k == eng and eng == "pe":
                    continue
                self.seen[eng][k] = v
                waits.append((k, v))

            def emit(e, waits=waits):
                for k, v in waits:
                    e.wait_ge(sems[k], v)

            self.streams[eng].append(emit)

    def wait_all(self, eng, resources):
        waits = self._deps(eng, resources, ())
        sems = self.sems

        def emit(e, waits=waits):
            for k, v in waits:
                e.wait_ge(sems[k], v)

        self.streams[eng].append(emit)

    def emit(self):
        nc = self.nc
        with nc.Block() as block:
            @block.tensor
            def _(e):
                for f in self.streams["pe"]:
                    f(e)

            @block.scalar
            def _(e):
                for f in self.streams["act"]:
                    f(e)

            @block.vector
            def _(e):
                for f in self.streams["dve"]:
                    f(e)

            @block.gpsimd
            def _(e):
                for f in self.streams["pool"]:
                    f(e)

            @block.sync
            def _(e):
                for f in self.streams["sp"]:
                    f(e)

import math
from concourse.bass_utils import run_bass_kernel_spmd

T = 4096
D = 2048
TB = 512
NB = T // TB
DFF = 5632
EPS = 1e-6
GN_EPS = 64e-5
NV = 528
DBG = {}
TW = 1026
WB = [(0, 342), (342, 342), (684, 342)]
FB = [(j * 512, 512) for j in range(8)]


_RANK = {}


def get_rank(e):
    if "r" not in _RANK:
        _RANK["r"] = (e.partition_id() % 4) * 1024
    return _RANK["r"]


def vec_pc(v):
    v = np.asarray(v, np.float32).reshape(-1)
    return np.ascontiguousarray(v.reshape(-1, 128).T)


def pack_vecs(inp):
    V = np.zeros((128, NV), np.float32)
    V[:, 0:16] = vec_pc(inp["pre_mix_norm"])
    V[:, 16:32] = vec_pc(inp["post_mix_norm"])
    V[:, 32:48] = vec_pc(inp["pre_ffn_norm"])
    V[:, 48:64] = vec_pc(inp["post_ffn_norm"])
    V[:, 64:80] = vec_pc(inp["ple_norm"])
    mu = np.asarray(inp["rw_mu"], np.float32).reshape(-1)
    V[:, 80:104] = vec_pc(mu[0:3072])
    V[:, 104:112] = vec_pc(inp["rw_w0"])
    V[:, 112:120] = vec_pc(inp["rw_a0"])
    V[:, 120:128] = vec_pc(inp["rw_k_k"])
    V[:, 128:136] = vec_pc(inp["rw_k_a"])
    V[:, 136:144] = vec_pc(inp["rw_r_k"])
    V[:, 144:152] = vec_pc(inp["rw_lnx_w"])
    V[:, 152:160] = vec_pc(inp["rw_lnx_b"])
    V[:, 160:164] = vec_pc(inp["mla_q_norm"])
    V[:, 164:168] = vec_pc(inp["mla_kv_norm"])
    V[:, 168:256] = vec_pc(inp["conv_b"])
    cw = np.asarray(inp["conv_w"], np.float32).reshape(3, -1)
    V[:, 256:344] = vec_pc(cw[0])
    V[:, 344:432] = vec_pc(cw[1])
    V[:, 432:520] = vec_pc(cw[2])
    V[0:64, 520] = mu[3072:3136]
    V[0:64, 521] = mu[3136:3200]
    V[0:128, 522] = mu[3200:3328]
    V[0:32, 523] = mu[3328:3360]
    invf = (10000.0 ** (-np.arange(0, 64, 2, dtype=np.float32) / 64)).astype(np.float32)
    V[0:32, 524] = invf
    V[32:64, 524] = invf
    return V


def build_program(debug=()):
    nc = bass.Bass("TRN2", target_bir_lowering=False)
    DBG.clear()
    _RANK.clear()
    for k_ in debug:
        if "=" in k_:
            DBG[k_.split("=")[0]] = int(k_.split("=")[1])
        else:
            DBG[k_] = 1

    def din(name, shape, dt=F32):
        return nc.dram_tensor(name, list(shape), dt, kind="ExternalInput").ap()

    def dscr(name, shape, dt):
        if name in debug:
            return nc.dram_tensor(name, list(shape), dt, kind="ExternalOutput").ap()
        return nc.dram_tensor(name, list(shape), dt).ap()

    x = din("x", [T, D])
    p_in = din("p", [1024, 256])
    pos = din("pos", [1, T], I32)
    pos_w = din("pos_w", [1, TW], I32)
    qk_tab_d = din("qk_tab", [128, 3 * 342])
    d_tab_d = din("d_tab", [128, 32])
    vecs = din("vecs", [128, NV])
    w_in = din("w_in", [D, 8544])
    rw_w2 = din("rw_w2", [64, 1024])
    rw_a2 = din("rw_a2", [64, 1024])
    rw_g2 = din("rw_g2", [160, 1024])
    w_q_up = din("w_q_up", [512, 1536])
    w_kv_up = din("w_kv_up", [512, 2048])
    w_brw = din("w_brw", [1024, D])
    w_bmla = din("w_bmla", [1024, D])
    w_out = din("w_out", [D, D])
    w_up = din("w_up", [D, 2 * DFF])
    w_down = din("w_down", [DFF, D])
    w_ple = din("w_ple", [256, D])
    w_pg = din("w_pg", [D, D])
    out = nc.dram_tensor("out", [1024, D], F32, kind="ExternalOutput").ap()

    hT_full = dscr("hT", [D, T + 2], F32)
    uT_full = dscr("uT", [D, T + 2], BF16)
    yrwT_full = dscr("yrwT", [1024, T + 2], BF16)
    cqnT_full = dscr("cqnT", [512, T + 2], BF16)
    hT = hT_full[:, 2:]
    uT = uT_full[:, 2:]
    yrwT = yrwT_full[:, 2:]
    cqnT = cqnT_full[:, 2:]
    zrwT = dscr("zrwT", [3360, T], F32)
    zmlaT = dscr("zmlaT", [1088, T], F32)
    ckvnT = dscr("ckvnT", [512, T], BF16)
    hwT = dscr("hwT", [D, TW], F32)
    uwT = dscr("uwT", [D, TW], BF16)
    ywT = dscr("ywT", [1024, TW], BF16)
    cqwT = dscr("cqwT", [512, TW], BF16)
    ymlaT = dscr("ymlaT", [1024, TW], BF16)
    pT = dscr("pT", [256, 1024], F32)
    gT = dscr("gT", [4096, TW], BF16)
    mT = dscr("mT", [D, TW], BF16)
    fT = dscr("fT", [D, TW], F32)
    ffT = dscr("ffT", [DFF, TW], BF16)
    qk_tab = qk_tab_d
    d_tab = d_tab_d

    with contextlib.ExitStack() as st0:
        P = Prog(nc, st0)
        V = P.sb("V", [128, NV], F32)
        P.dma("sp", V[:], vecs, writes=["V"], key="V")
        ident = P.sb("ident", [128, 128], F32)
        ident_bf = P.sb("ident_bf", [128, 128], BF16)
        ones_bf = P.sb("ones_bf", [128, 128], BF16)
        bones_bf = P.sb("bones_bf", [128, 128], BF16)
        maskP = P.sb("maskP", [128, 2, 128], F32)
        maskL = P.sb("maskL", [128, 64], F32)
        eye = P.sb("eye", [128, 64], F32)
        rmask = P.sb("rmask", [128, TB], F32)
        omka = P.sb("omka", [128, 8], F32)
        su = P.sb("su", [128, 64], F32)
        ui = P.sb("ui", [128, 64], F32)

        def pool(fn, reads=(), writes=()):
            P.op("pool", fn, reads, writes)

        pool(lambda e: e.memset(ident[:], 1.0), writes=["ident"])
        pool(lambda e: e.affine_select(out=ident[:], in_=ident[:], pattern=[[-1, 128]], compare_op=ALU.is_equal,
                                       fill=0.0, base=0, channel_multiplier=1), reads=["ident"], writes=["ident"])
        pool(lambda e: e.tensor_copy(out=ident_bf[:], in_=ident[:]), reads=["ident"], writes=["ident_bf"])
        pool(lambda e: e.memset(ones_bf[:], 1.0), writes=["ones_bf"])
        pool(lambda e: e.memset(bones_bf[:], 0.0), writes=["bones_bf"])
        pool(lambda e: e.memset(bones_bf[0:64, 0:64], 1.0), reads=["bones_bf"], writes=["bones_bf"])
        pool(lambda e: e.memset(bones_bf[64:128, 64:128], 1.0), reads=["bones_bf"], writes=["bones_bf"])
        for (tl, pat, cm, b0, b1, op_) in ((su, 1, -1, -1, -1, ALU.is_ge), (ui, 1, -1, 0, 0, ALU.is_ge),
                                           (maskL, -1, 1, -1, -1, ALU.is_ge), (eye, -1, 1, 0, 0, ALU.is_equal)):
            pool(lambda e, tl=tl: e.memset(tl[:], 1.0), writes=["msk"])
            for h, bb in ((0, b0), (1, b1)):
                sl = slice(64 * h, 64 * h + 64)
                pool(lambda e, tl=tl, sl=sl, pat=pat, cm=cm, bb=bb, op_=op_: e.affine_select(
                    out=tl[sl, :], in_=tl[sl, :], pattern=[[pat, 64]], compare_op=op_, fill=0.0, base=bb,
                    channel_multiplier=cm), reads=["msk"], writes=["msk"])
        for xx in range(2):
            pool(lambda e, xx=xx: e.tensor_copy(out=maskP[:, xx, 0:64], in_=su[:]), reads=["msk"], writes=["msk"])
            pool(lambda e, xx=xx: e.tensor_copy(out=maskP[:, xx, 64:128], in_=ui[:]), reads=["msk"], writes=["msk"])
        pool(lambda e: e.memset(rmask[:], 1.0), reads=["msk"], writes=["msk"])
        for c in range(8):
            pool(lambda e, c=c: e.memset(rmask[:, c * 64:c * 64 + 1], 0.0), reads=["msk"], writes=["msk"])
        P.op("dve", lambda e: e.tensor_scalar(out=omka[:], in0=V[:, 128:136], scalar1=-1.0, scalar2=1.0, op0=ALU.mult,
                                              op1=ALU.add), reads=["V", "msk"], writes=["omka"])
        CONST = ["V", "msk", "ident", "ident_bf", "ones_bf", "bones_bf", "omka"]

        rr = {"evac": 0}

        class _Stop(Exception):
            pass

        def stop_if(name):
            if name in debug:
                raise _Stop()

        def evac_copy(out_ap, in_ap, reads, writes, scale=None):
            rr["evac"] += 1
            if scale is not None or rr["evac"] % 2 == 0:
                P.op("act", lambda e: e.activation(out=out_ap, in_=in_ap, func=AF.Copy,
                                                   scale=(1.0 if scale is None else scale)), reads, writes)
            else:
                P.op("dve", lambda e: e.tensor_copy(out=out_ap, in_=in_ap), reads, writes)

        def transpose_stage(tag, src, dst, R, C, src_res, dst_res):
            with contextlib.ExitStack() as st:
                prev_stack = P.stack
                P.stack = st
                nb = 2
                tin = [P.sb(f"{tag}_in{i}", [128, C], F32) for i in range(nb)]
                tout = [P.sb(f"{tag}_out{i}", [128, C // 128, 128], F32) for i in range(nb)]
                pst = [P.ps(f"{tag}_ps{i}", [128, 4, 128], F32) for i in range(2)]
                k = 0
                for r in range(R // 128):
                    b = r % nb
                    P.dma("sp", tin[b][:], src[r * 128:(r + 1) * 128, :], reads=[src_res], writes=[f"TR_in{b}"],
                          key=f"TR_in{b}")
                    for c4 in range(0, C // 128, 4):
                        pb = k % 2
                        k += 1
                        n4 = min(4, C // 128 - c4)
                        for i in range(n4):
                            c = c4 + i
                            P.op("pe", lambda e, pb=pb, i=i, c=c, b=b: e.transpose(
                                out=pst[pb][:, i, :], in_=tin[b][:, c * 128:(c + 1) * 128], identity=ident[:]),
                                reads=[f"TR_in{b}", "ident"], writes=[f"TR_ps{pb}"])
                        evac_copy(tout[b][:, c4:c4 + n4, :], pst[pb][:, 0:n4, :], [f"TR_ps{pb}"], [f"TR_out{b}"])
                    P.dma("sp", dst[:, r * 128:(r + 1) * 128].rearrange("(c p) t -> p c t", p=128), tout[b][:],
                          reads=[f"TR_out{b}"], writes=[dst_res], key=f"TR_st{b}")
                P.barrier()
                P.stack = prev_stack

        def rmsnorm_stage(tag, src, F, gcol, src_res, dst=None, dst_res=None, resid=None, resid_res=None, blocks=FB):
            FC = F // 128
            with contextlib.ExitStack() as st:
                prev_stack = P.stack
                P.stack = st
                s = P.sb(f"{tag}_s", [128, FC, TB], F32)
                sq = P.sb(f"{tag}_sq", [128, FC, TB], BF16)
                rstd = P.sb(f"{tag}_rstd", [128, TB], F32)
                ps = P.ps(f"{tag}_ps", [128, TB], F32)
                if resid is None:
                    o = P.sb(f"{tag}_o", [128, FC, TB], BF16)
                else:
                    o = P.sb(f"{tag}_o", [128, FC, TB], F32)
                    hh = P.sb(f"{tag}_h", [128, FC, TB], F32)
                for j, (b0, bs) in enumerate(blocks):
                    cs = slice(b0, b0 + bs)
                    P.dma("sp", s[:, :, 0:bs], src[:, cs].rearrange("(c p) t -> p c t", p=128), reads=[src_res],
                          writes=[f"RN_s"], key=f"RN_s")
                    if resid is not None:
                        P.dma("sp", hh[:, :, 0:bs], resid[:, cs].rearrange("(c p) t -> p c t", p=128), reads=[resid_res],
                              writes=[f"RN_h"], key=f"RN_h")
                    P.op("act", lambda e: e.activation(out=sq[:, :, 0:bs], in_=s[:, :, 0:bs], func=AF.Square), reads=[f"RN_s"],
                         writes=[f"RN_sq"])
                    for c in range(FC):
                        P.op("pe", lambda e, c=c: e.matmul(ps[:, 0:bs], lhsT=ones_bf[:], rhs=sq[:, c, 0:bs], start=(c == 0),
                                                           stop=(c == FC - 1)),
                             reads=[f"RN_sq", "ones_bf"], writes=[f"RN_ps"])
                    P.op("act", lambda e: e.activation(out=rstd[:, 0:bs], in_=ps[:, 0:bs], func=AF.Sqrt, bias=EPS, scale=1.0 / F),
                         reads=[f"RN_ps"], writes=[f"RN_rstd"])
                    P.op("dve", lambda e: e.reciprocal(out=rstd[:, 0:bs], in_=rstd[:, 0:bs]), reads=[f"RN_rstd"],
                         writes=[f"RN_rstd"])
                    for c in range(FC):
                        eng = "pool"
                        P.op("dve", lambda e, c=c: e.scalar_tensor_tensor(
                            out=o[:, c, 0:bs], in0=s[:, c, 0:bs], scalar=V[:, gcol + c:gcol + c + 1], in1=rstd[:, 0:bs],
                            op0=ALU.mult, op1=ALU.mult), reads=[f"RN_s", f"RN_rstd", "V"], writes=[f"RN_o{c}"])
                        if resid is not None:
                            P.op(eng, lambda e, c=c: e.tensor_tensor(out=o[:, c, 0:bs], in0=o[:, c, 0:bs], in1=hh[:, c, 0:bs],
                                                                     op=ALU.add),
                                 reads=[f"RN_o{c}", f"RN_h"], writes=[f"RN_o{c}"])
                    allo = [f"RN_o{c}" for c in range(FC)]
                    if resid is None:
                        P.dma("sp", dst[:, cs].rearrange("(c p) t -> p c t", p=128), o[:, :, 0:bs], reads=allo,
                              writes=[dst_res], key=f"RN_st")
                    else:
                        P.dma("sp", resid[:, cs].rearrange("(c p) t -> p c t", p=128), o[:, :, 0:bs], reads=allo,
                              writes=[resid_res], key=f"RN_st")
                P.barrier()
                P.stack = prev_stack

        def linear_stage(tag, srcs, groups, epi, wbuf_cols, nps=2, blocks=FB):
            with contextlib.ExitStack() as st:
                prev_stack = P.stack
                P.stack = st
                wsb = []
                sbs = []
                for si, (src, K, sdt, sres, W) in enumerate(srcs):
                    KC = (K + 127) // 128
                    wsb.append([P.sb(f"{tag}_w{si}_{i}", [128, KC, wbuf_cols], BF16) for i in range(2)])
                    sbs.append([P.sb(f"{tag}_x{si}_{i}", [128, KC, TB], BF16) for i in range(2)])
                pss = [P.ps(f"{tag}_ps{i}", [128, TB], F32) for i in range(nps)]
                state = {"ps": 0, "xb": 0}
                for gi, grp in enumerate(groups):
                    wb = gi % 2
                    offs = []
                    off = 0
                    for (c0, w) in grp:
                        offs.append(off)
                        off += w
                    assert off <= wbuf_cols
                    ranges = []
                    for (c0, w), o_ in zip(grp, offs):
                        if ranges and ranges[-1][0] + ranges[-1][1] == c0 and ranges[-1][2] + ranges[-1][1] == o_:
                            ranges[-1][1] += w
                        else:
                            ranges.append([c0, w, o_])
                    for si, (src, K, sdt, sres, W) in enumerate(srcs):
                        KC = (K + 127) // 128
                        for (c0, w, o_) in ranges:
                            if False:
                                P.dma("pool", wsb[si][wb][:, :, o_:o_ + w],
                                      W[:, c0:c0 + w].rearrange("(k p) m -> p k m", p=128),
                                      writes=[f"LN_w{si}_{wb}"], key=f"LN_w{si}_{wb}")
                            else:
                                for kc in range(KC):
                                    kr = min(128, K - kc * 128)
                                    P.dma("pool", wsb[si][wb][0:kr, kc, o_:o_ + w], W[kc * 128:kc * 128 + kr, c0:c0 + w],
                                          writes=[f"LN_w{si}_{wb}"], key=f"LN_w{si}_{wb}")
                    for j, (b0, bs) in enumerate(blocks):
                        cs = slice(b0, b0 + bs)
                        xb = state["xb"] % 2
                        state["xb"] += 1
                        for si, (src, K, sdt, sres, W) in enumerate(srcs):
                            KC = (K + 127) // 128
                            q = "sp" if sdt == BF16 else "pool"
                            if K % 128 == 0 and q == "sp":
                                P.dma(q, sbs[si][xb][:, :, 0:bs], src[:, cs].rearrange("(k p) t -> p k t", p=128), reads=[sres],
                                      writes=[f"LN_x{si}_{xb}"], key=f"LN_x{si}_{xb}")
                            else:
                                for kc in range(KC):
                                    kr = min(128, K - kc * 128)
                                    P.dma(q, sbs[si][xb][0:kr, kc, 0:bs], src[kc * 128:kc * 128 + kr, cs], reads=[sres],
                                          writes=[f"LN_x{si}_{xb}"], key=f"LN_x{si}_{xb}")
                        for ci, ((c0, w), o_) in enumerate(zip(grp, offs)):
                            pi = state["ps"] % nps
                            state["ps"] += 1
                            nmm = sum((K + 127) // 128 for (_, K, _, _, _) in srcs)
                            i = 0
                            for si, (src, K, sdt, sres, W) in enumerate(srcs):
                                KC = (K + 127) // 128
                                for kc in range(KC):
                                    kr = min(128, K - kc * 128)
                                    P.op("pe", lambda e, pi=pi, si=si, wb=wb, kc=kc, kr=kr, o_=o_, w=w, xb=xb, i=i, nmm=nmm:
                                         e.matmul(pss[pi][0:w, 0:bs], lhsT=wsb[si][wb][0:kr, kc, o_:o_ + w],
                                                  rhs=sbs[si][xb][0:kr, kc, 0:bs], start=(i == 0), stop=(i == nmm - 1)),
                                         reads=[f"LN_w{si}_{wb}", f"LN_x{si}_{xb}"], writes=[f"LN_ps{pi}"])
                                    i += 1
                            epi(gi, ci, (c0, w), j, pss[pi], f"LN_ps{pi}", (b0, bs))
                P.barrier()
                P.stack = prev_stack

        def store_epi(tag, dst, dst_res, odt, func=AF.Copy, row_of=None, nbuf=3):
            bufs = [P.sb(f"{tag}_eo{i}", [128, TB], odt) for i in range(nbuf)]
            stt = {"i": 0}

            def epi(gi, ci, cw, j, ps, ps_res, blk):
                c0, w = cw
                b0_, bs = blk
                b = stt["i"] % nbuf
                stt["i"] += 1
                r0 = c0 if row_of is None else row_of(c0)
                if func == AF.Copy:
                    evac_copy(bufs[b][0:w, 0:bs], ps[0:w, 0:bs], [ps_res], [f"EO_eo{b}"])
                else:
                    P.op("act", lambda e: e.activation(out=bufs[b][0:w, 0:bs], in_=ps[0:w, 0:bs], func=func), [ps_res],
                         [f"EO_eo{b}"])
                P.dma("sp", dst[r0:r0 + w, b0_:b0_ + bs], bufs[b][0:w, 0:bs], reads=[f"EO_eo{b}"],
                      writes=[dst_res], key=f"EO_eo{b}")

            return epi

        def chunks(c0, n, w=128):
            res = []
            c = c0
            while c < c0 + n:
                ww = min(w, c0 + n - c)
                res.append((c, ww))
                c += ww
            return res

        def grouped(ch, n):
            return [ch[i:i + n] for i in range(0, len(ch), n)]

        def main_seq():
            if "skip_s01" in debug:
                rwkv_stage(P, nc, st0, V, zrwT, yrwT, rw_w2, rw_a2, rw_g2, ident_bf, bones_bf, maskP, maskL, eye, rmask, omka,
                           evac_copy)
                stop_if("skip_s01")
            if "skip_s01m" in debug:
                rmsnorm_stage("nq", zmlaT[0:512, :], 512, 160, "dram:zmlaT", dst=cqnT, dst_res="dram:cqnT")
                rmsnorm_stage("nk", zmlaT[512:1024, :], 512, 164, "dram:zmlaT", dst=ckvnT, dst_res="dram:ckvnT")
                mla_stage(P, nc, st0, V, pos, pos_w, qk_tab, d_tab, zmlaT, cqnT_full, ckvnT, w_q_up, w_kv_up, ymlaT, ones_bf,
                          evac_copy)
                stop_if("skip_s01m")
            def win_copy(dst, src_full, rows, res_src, res_dst, key):
                step = 512
                for r0 in range(0, rows, step):
                    def fn(e, r0=r0):
                        off = get_rank(e)
                        return e.dma_start(out=dst[r0:r0 + step, :], in_=src_full[r0:r0 + step, bass.ds(off, TW)])
                    P.custom("pool", fn, reads=[res_src], writes=[res_dst], key=key, raw=True)

            zt_f = P.sb("zpad_f", [128, 16, 2], F32)
            zt_b = P.sb("zpad_b", [128, 16, 2], BF16)
            P.op("pool", lambda e: e.memset(zt_f[:], 0.0), [], ["zpad"])
            P.op("pool", lambda e: e.memset(zt_b[:], 0.0), [], ["zpad"])
            P.dma("sp", hT_full[:, 0:2].rearrange("(c p) t -> p c t", p=128), zt_f[:], reads=["zpad"], writes=["dram:hT"], key="zp")
            P.dma("sp", uT_full[:, 0:2].rearrange("(c p) t -> p c t", p=128), zt_b[:], reads=["zpad"], writes=["dram:uT"], key="zp")
            P.dma("sp", yrwT_full[:, 0:2].rearrange("(c p) t -> p c t", p=128), zt_b[:, 0:8, :], reads=["zpad"],
                  writes=["dram:yrwT"], key="zp")
            P.dma("sp", cqnT_full[:, 0:2].rearrange("(c p) t -> p c t", p=128), zt_b[:, 0:4, :], reads=["zpad"],
                  writes=["dram:cqnT"], key="zp")
            transpose_stage("tx", x, hT, T, D, "dram:x", "dram:hT")
            stop_if("stop0")
            rmsnorm_stage("n1", hT, D, 0, "dram:hT", dst=uT, dst_res="dram:uT")
            stop_if("stop0b")
            with contextlib.ExitStack() as stx:
                P.stack = stx
                e1 = store_epi("l1a", zrwT, "dram:zrwT", F32)
                linear_stage("l1a", [(uT, D, BF16, "dram:uT", w_in)], grouped(chunks(0, 3360), 8), e1, 1024)
                P.barrier()
                P.stack = st0
            with contextlib.ExitStack() as stx:
                P.stack = stx
                e2 = store_epi("l1b", zmlaT, "dram:zmlaT", F32, row_of=lambda c0: c0 - 3360)
                linear_stage("l1b", [(uT, D, BF16, "dram:uT", w_in)], grouped(chunks(3360, 1088), 9), e2, 1152)
                P.barrier()
                P.stack = st0
            stop_if("stop1")
            rwkv_stage(P, nc, st0, V, zrwT, yrwT, rw_w2, rw_a2, rw_g2, ident_bf, bones_bf, maskP, maskL, eye, rmask, omka,
                       evac_copy)
            stop_if("stop2")
            rmsnorm_stage("nq", zmlaT[0:512, :], 512, 160, "dram:zmlaT", dst=cqnT, dst_res="dram:cqnT")
            rmsnorm_stage("nk", zmlaT[512:1024, :], 512, 164, "dram:zmlaT", dst=ckvnT, dst_res="dram:ckvnT")
            win_copy(cqwT, cqnT_full, 512, "dram:cqnT", "dram:cqwT", "wc3")
            P.barrier()
            mla_stage(P, nc, st0, V, pos, pos_w, qk_tab, d_tab, zmlaT, cqwT, ckvnT, w_q_up, w_kv_up, ymlaT, ones_bf,
                      evac_copy)
            stop_if("stop3")
            win_copy(hwT, hT_full, D, "dram:hT", "dram:hwT", "wc0")
            win_copy(uwT, uT_full, D, "dram:uT", "dram:uwT", "wc1")
            win_copy(ywT, yrwT_full, 1024, "dram:yrwT", "dram:ywT", "wc2")
            P.barrier()
            transpose_stage("tp", p_in, pT, 1024, 256, "dram:p", "dram:pT")
            with contextlib.ExitStack() as stx:
                P.stack = stx
                e3 = store_epi("l1c", gT, "dram:gT", BF16, func=AF.Sigmoid, row_of=lambda c0: c0 - 4448)
                linear_stage("l1c", [(uwT, D, BF16, "dram:uwT", w_in)], grouped(chunks(4448, 4096), 8), e3, 1024, blocks=WB)
                P.barrier()
                P.stack = st0
            with contextlib.ExitStack() as stx:
                P.stack = stx
                gA = [P.sb(f"s4_gA{i}", [128, TB], BF16) for i in range(2)]
                gB = [P.sb(f"s4_gB{i}", [128, TB], BF16) for i in range(2)]
                t1 = [P.sb(f"s4_t1{i}", [128, TB], F32) for i in range(2)]
                mo = [P.sb(f"s4_mo{i}", [128, TB], BF16) for i in range(2)]
                t2 = [P.sb(f"s4_t2{i}", [128, TB], F32) for i in range(2)]
                stt = {"i": 0}

                def epiA(gi, ci, cw, j, ps, ps_res, blk):
                    c0, w = cw
                    b0_, bs = blk
                    b = stt["i"] % 2
                    stt["i"] += 1
                    P.dma("sp", gA[b][:, 0:bs], gT[c0:c0 + 128, b0_:b0_ + bs], reads=["dram:gT"], writes=[f"s4_gA{b}"],
                          key=f"s4_gA{b}")
                    P.op("dve", lambda e: e.tensor_tensor(out=t1[b][:, 0:bs], in0=ps[:, 0:bs], in1=gA[b][:, 0:bs], op=ALU.mult),
                         [ps_res, f"s4_gA{b}"], [f"s4_t1{b}"])
                    P.dma("sp", fT[c0:c0 + 128, b0_:b0_ + bs], t1[b][:, 0:bs], reads=[f"s4_t1{b}"], writes=["dram:fT"],
                          key=f"s4_t1{b}")

                linear_stage("l4a", [(ywT, 1024, BF16, "dram:ywT", w_brw)], grouped(chunks(0, D), 8), epiA, 1024, blocks=WB)

                def epiB(gi, ci, cw, j, ps, ps_res, blk):
                    c0, w = cw
                    b0_, bs = blk
                    b = stt["i"] % 2
                    stt["i"] += 1
                    P.dma("sp", gB[b][:, 0:bs], gT[2048 + c0:2048 + c0 + 128, b0_:b0_ + bs], reads=["dram:gT"],
                          writes=[f"s4_gB{b}"], key=f"s4_gB{b}")
                    P.dma("sp", t1[b][:, 0:bs], fT[c0:c0 + 128, b0_:b0_ + bs], reads=["dram:fT"], writes=[f"s4_t1{b}"],
                          key=f"s4_t1l{b}")
                    P.op("dve", lambda e: e.tensor_tensor(out=t2[b][:, 0:bs], in0=ps[:, 0:bs], in1=gB[b][:, 0:bs], op=ALU.mult),
                         [ps_res, f"s4_gB{b}"], [f"s4_t2{b}"])
                    P.op("dve", lambda e: e.tensor_tensor(out=mo[b][:, 0:bs], in0=t2[b][:, 0:bs], in1=t1[b][:, 0:bs], op=ALU.add),
                         [f"s4_t2{b}", f"s4_t1{b}"], [f"s4_mo{b}"])
                    P.dma("sp", mT[c0:c0 + 128, b0_:b0_ + bs], mo[b][:, 0:bs], reads=[f"s4_mo{b}"], writes=["dram:mT"],
                          key=f"s4_mo{b}")

                linear_stage("l4b", [(ymlaT, 1024, BF16, "dram:ymlaT", w_bmla)], grouped(chunks(0, D), 8), epiB, 1024, blocks=WB)
                P.barrier()
                P.stack = st0
            stop_if("stop4")
            with contextlib.ExitStack() as stx:
                P.stack = stx
                e5 = store_epi("l5", fT, "dram:fT", F32)
                linear_stage("l5", [(mT, D, BF16, "dram:mT", w_out)], grouped(chunks(0, D), 8), e5, 1024, blocks=WB)
                P.barrier()
                P.stack = st0
            rmsnorm_stage("n5", fT, D, 16, "dram:fT", resid=hwT, resid_res="dram:hwT", blocks=WB)
            stop_if("stop5")
            rmsnorm_stage("n6", hwT, D, 32, "dram:hwT", dst=uwT, dst_res="dram:uwT", blocks=WB)
            with contextlib.ExitStack() as stx:
                P.stack = stx
                NCH = 4
                ub = [[P.sb(f"s6_u{h}_{i}", [128, TB + 2], F32) for i in range(NCH)] for h in range(2)]
                cv = [P.sb(f"s6_cv{h}", [128, TB], F32) for h in range(2)]
                tq = P.sb("s6_tq", [128, TB], F32)
                fo = [P.sb(f"s6_fo{i}", [128, TB], BF16) for i in range(2)]
                stt = {"i": 0}

                def epi6(gi, ci, cw, j, ps, ps_res, blk):
                    c0, w = cw
                    b0_, bs = blk
                    half = 0 if ci < NCH else 1
                    cc = ci % NCH
                    u = ub[half][cc]
                    ur = f"s6_u{half}_{cc}"
                    if j == 0:
                        P.op("pool", lambda e: e.memset(u[:, 0:2], 0.0), [ur], [ur])
                    else:
                        P.op("pool", lambda e: e.tensor_copy(out=u[:, 0:2], in_=u[:, bs:bs + 2]), [ur], [ur])
                    evac_copy(u[:, 2:bs + 2], ps[:, 0:bs], [ps_res, ur], [ur])
                    if half == 1:
                        chn = c0 // 128
                        chg = chn - 44
                        for hh, ch in ((0, chg), (1, chn)):
                            uu = ub[hh][cc]
                            uur = f"s6_u{hh}_{cc}"
                            P.op("dve", lambda e, uu=uu, ch=ch, hh=hh: e.tensor_scalar(
                                out=cv[hh][:, 0:bs], in0=uu[:, 0:bs], scalar1=V[:, 256 + ch:257 + ch], scalar2=V[:, 168 + ch:169 + ch],
                                op0=ALU.mult, op1=ALU.add), [uur, "V"], [f"s6_cv{hh}"])
                            P.op("dve", lambda e, uu=uu, ch=ch, hh=hh: e.scalar_tensor_tensor(
                                out=cv[hh][:, 0:bs], in0=uu[:, 1:bs + 1], scalar=V[:, 344 + ch:345 + ch], in1=cv[hh][:, 0:bs],
                                op0=ALU.mult, op1=ALU.add), [uur, "V", f"s6_cv{hh}"], [f"s6_cv{hh}"])
                            P.op("dve", lambda e, uu=uu, ch=ch, hh=hh: e.scalar_tensor_tensor(
                                out=cv[hh][:, 0:bs], in0=uu[:, 2:bs + 2], scalar=V[:, 432 + ch:433 + ch], in1=cv[hh][:, 0:bs],
                                op0=ALU.mult, op1=ALU.add), [uur, "V", f"s6_cv{hh}"], [f"s6_cv{hh}"])
                        P.op("pool", lambda e: e.tensor_tensor(out=tq[:, 0:bs], in0=cv[0][:, 0:bs], in1=cv[0][:, 0:bs], op=ALU.mult),
                             ["s6_cv0"], ["s6_tq"])
                        P.op("pool", lambda e: e.tensor_scalar(out=tq[:, 0:bs], in0=tq[:, 0:bs], scalar1=0.044715, scalar2=1.0,
                                                               op0=ALU.mult, op1=ALU.add), ["s6_tq"], ["s6_tq"])
                        P.op("pool", lambda e: e.tensor_tensor(out=tq[:, 0:bs], in0=tq[:, 0:bs], in1=cv[0][:, 0:bs], op=ALU.mult),
                             ["s6_tq", "s6_cv0"], ["s6_tq"])
                        P.op("act", lambda e: e.activation(out=tq[:, 0:bs], in_=tq[:, 0:bs], func=AF.Sigmoid, scale=1.5957691216),
                             ["s6_tq"], ["s6_tq"])
                        P.op("pool", lambda e: e.tensor_tensor(out=tq[:, 0:bs], in0=tq[:, 0:bs], in1=cv[0][:, 0:bs], op=ALU.mult),
                             ["s6_tq", "s6_cv0"], ["s6_tq"])
                        b = stt["i"] % 2
                        stt["i"] += 1
                        P.op("pool", lambda e: e.tensor_tensor(out=fo[b][:, 0:bs], in0=tq[:, 0:bs], in1=cv[1][:, 0:bs], op=ALU.mult),
                             ["s6_tq", "s6_cv1"], [f"s6_fo{b}"])
                        P.dma("sp", ffT[chg * 128:(chg + 1) * 128, b0_:b0_ + bs], fo[b][:, 0:bs], reads=[f"s6_fo{b}"],
                              writes=["dram:ffT"], key=f"s6_fo{b}")

                grps = []
                for g0 in range(0, 44, NCH):
                    grps.append(chunks(g0 * 128, NCH * 128) + chunks(DFF + g0 * 128, NCH * 128))
                linear_stage("l6", [(uwT, D, BF16, "dram:uwT", w_up)], grps, epi6, 2 * NCH * 128, blocks=WB)
                P.barrier()
                P.stack = st0
            with contextlib.ExitStack() as stx:
                P.stack = stx
                e7 = store_epi("l7", fT, "dram:fT", F32)
                linear_stage("l7", [(ffT, DFF, BF16, "dram:ffT", w_down)], grouped(chunks(0, D), 4), e7, 512, blocks=WB)
                P.barrier()
                P.stack = st0
            rmsnorm_stage("n7", fT, D, 48, "dram:fT", resid=hwT, resid_res="dram:hwT", blocks=WB)
            stop_if("stop7")
            PB = [(0, 512), (512, 512)]
            with contextlib.ExitStack() as stx:
                P.stack = stx
                e8 = store_epi("l8a", mT, "dram:mT", BF16, func=AF.Sigmoid)
                linear_stage("l8a", [(hwT[:, 2:TW], D, F32, "dram:hwT", w_pg)], grouped(chunks(0, D), 8), e8, 1024, blocks=PB)
                gl = [P.sb(f"s8_g{i}", [128, TB], BF16) for i in range(2)]
                eo = [P.sb(f"s8_eo{i}", [128, TB], F32) for i in range(2)]
                stt = {"i": 0}

                def epi8(gi, ci, cw, j, ps, ps_res, blk):
                    c0, w = cw
                    b0_, bs = blk
                    b = stt["i"] % 2
                    stt["i"] += 1
                    P.dma("sp", gl[b][:, 0:bs], mT[c0:c0 + 128, b0_:b0_ + bs], reads=["dram:mT"], writes=[f"s8_g{b}"],
                          key=f"s8_g{b}")
                    P.op("dve", lambda e: e.tensor_tensor(out=eo[b][:, 0:bs], in0=ps[:, 0:bs], in1=gl[b][:, 0:bs], op=ALU.mult),
                         [ps_res, f"s8_g{b}"], [f"s8_eo{b}"])
                    P.dma("sp", fT[c0:c0 + 128, b0_:b0_ + bs], eo[b][:, 0:bs], reads=[f"s8_eo{b}"], writes=["dram:fT"],
                          key=f"s8_eo{b}")

                linear_stage("l8b", [(pT, 256, F32, "dram:pT", w_ple)], grouped(chunks(0, D), 16), epi8, 2048, blocks=PB)
                P.barrier()
                P.stack = st0
            rmsnorm_stage("n8", fT[:, 0:1024], D, 64, "dram:fT", resid=hwT[:, 2:TW], resid_res="dram:hwT", blocks=PB)
            transpose_stage("to", hwT[:, 2:TW], out, D, 1024, "dram:hwT", "dram:out")

        try:
            main_seq()
        except _Stop:
            P.stack = st0
        P.wait_all("sp", [f"dram:{n}" for n in (["out"] + list(debug)) if not n.startswith("skip") and not n.startswith("stop")])
        P.emit()
    return nc


def rwkv_stage(P, nc, st0, V, zrwT, yrwT, rw_w2, rw_a2, rw_g2, ident_bf, bones_bf, maskP, maskL, eye, rmask, omka,
               evac_copy):
    NEG_E = -math.exp(-0.5)
    with contextlib.ExitStack() as st:
        prev_stack = P.stack
        P.stack = st
        sb = P.sb
        w2b = sb("rk_w2b", [64, 1024], BF16)
        a2b = sb("rk_a2b", [64, 1024], BF16)
        g2b = sb("rk_g2b", [128, 2, 1024], BF16)
        P.dma("pool", w2b[:], rw_w2, writes=["rk_w"], key="rk_w")
        P.dma("pool", a2b[:], rw_a2, writes=["rk_w"], key="rk_w")
        P.dma("pool", g2b[:, 0, :], rw_g2[0:128, :], writes=["rk_w"], key="rk_w")
        P.dma("pool", g2b[0:32, 1, :], rw_g2[128:160, :], writes=["rk_w"], key="rk_w")
        twd = sb("rk_twd", [64, T], BF16)
        ads = sb("rk_ads", [64, T], BF16)
        sg1 = sb("rk_sg1", [128, T], BF16)
        sg2 = sb("rk_sg2", [32, T], BF16)
        zin = sb("rk_zin", [128, TB + 1], F32)
        dd = sb("rk_dd", [128, TB], F32)
        for (row0, nr, mcol, dst, func) in ((3072, 64, 520, twd, AF.Tanh), (3136, 64, 521, ads, AF.Copy),
                                            (3200, 128, 522, sg1, AF.Sigmoid), (3328, 32, 523, sg2, AF.Sigmoid)):
            for j in range(NB):
                if j == 0:
                    P.op("dve", lambda e: e.memset(zin[:, 0:1], 0.0), [], ["rk_zin"])
                    P.dma("sp", zin[0:nr, 1:TB + 1], zrwT[row0:row0 + nr, 0:TB], reads=["dram:zrwT"], writes=["rk_zin"],
                          key="rk_zin")
                else:
                    P.dma("sp", zin[0:nr, :], zrwT[row0:row0 + nr, j * TB - 1:(j + 1) * TB], reads=["dram:zrwT"],
                          writes=["rk_zin"], key="rk_zin")
                P.op("dve", lambda e, nr=nr: e.tensor_tensor(out=dd[0:nr, :], in0=zin[0:nr, 0:TB], in1=zin[0:nr, 1:TB + 1],
                                                             op=ALU.subtract), ["rk_zin"], ["rk_dd"])
                P.op("dve", lambda e, nr=nr, mcol=mcol: e.scalar_tensor_tensor(
                    out=dd[0:nr, :], in0=dd[0:nr, :], scalar=V[0:nr, mcol:mcol + 1], in1=zin[0:nr, 1:TB + 1],
                    op0=ALU.mult, op1=ALU.add), ["rk_dd", "rk_zin", "V"], ["rk_dd"])
                P.op("act", lambda e, nr=nr, dst=dst, func=func, j=j: e.activation(
                    out=dst[0:nr, j * TB:(j + 1) * TB], in_=dd[0:nr, :], func=func), ["rk_dd"], ["rk_lin"])
        zt = {n: sb(f"rk_z{n}", [128, TB + 1], F32) for n in "rkv"}
        xs = {n: sb(f"rk_s{n}", [128, TB], F32) for n in "rkv"}
        tA = sb("rk_tA", [128, TB], F32)
        tB = sb("rk_tB", [128, TB], F32)
        tC = sb("rk_tC", [128, TB], F32)
        av = sb("rk_a", [128, TB], F32)
        kkn = sb("rk_kkn", [128, TB], F32)
        k2 = sb("rk_k2", [128, TB], F32)
        logw = sb("rk_logw", [128, TB], F32)
        cum = sb("rk_cum", [128, TB], F32)
        ginv = sb("rk_ginv", [128, TB], F32)
        gprev = sb("rk_gprev", [128, TB], F32)
        tbf = sb("rk_tbf", [128, TB], BF16)
        vbf = sb("rk_vbf", [128, TB], BF16)
        BK = sb("rk_BK", [128, 8, 2, 64], BF16)
        gam2 = [sb(f"rk_gam{i}", [128, TB], F32) for i in range(2)]
        gv2 = [sb(f"rk_g{i}", [128, TB], F32) for i in range(2)]
        bonus2 = [sb(f"rk_bonus{i}", [128, TB], F32) for i in range(2)]
        AR2 = [sb(f"rk_AR{i}", [128, 8, 2, 64], BF16) for i in range(2)]
        PM2 = [sb(f"rk_PM{i}", [128, 8, 2, 128], BF16) for i in range(2)]
        TM2 = [sb(f"rk_TM{i}", [128, 8, 3, 64], BF16) for i in range(2)]
        MT2 = [sb(f"rk_MT{i}", [128, 8, 64], BF16) for i in range(2)]
        NS8 = sb("rk_NS", [128, 8, 2, 64], BF16)
        L8 = sb("rk_L", [128, 8, 64], BF16)
        maskP8 = sb("rk_maskP8", [128, 8, 2, 128], F32)
        maskL8 = sb("rk_maskL8", [128, 8, 64], F32)
        eye8 = sb("rk_eye8", [128, 8, 64], F32)
        for c in range(8):
            P.op("pool", lambda e, c=c: e.tensor_copy(out=maskP8[:, c, :, :], in_=maskP[:]), ["msk"], ["rk_m8"])
            P.op("pool", lambda e, c=c: e.tensor_copy(out=maskL8[:, c, :], in_=maskL[:]), ["msk"], ["rk_m8"])
            P.op("pool", lambda e, c=c: e.tensor_copy(out=eye8[:, c, :], in_=eye[:]), ["msk"], ["rk_m8"])
        Hf = sb("rk_Hf", [128, 64], F32)
        HG = sb("rk_HG", [128, 64], F32)
        Hb = sb("rk_Hb", [128, 64], BF16)
        Zs = sb("rk_Zs", [128, 64], BF16)
        Us = sb("rk_Us", [128, 64], BF16)
        Yt = sb("rk_Y", [128, TB], F32)
        yo = sb("rk_yo", [128, TB], BF16)
        ps_a = P.ps("rk_psa", [128, TB], F32)
        ps_b = ps_a
        W = P.ps("rk_Wps", [128, 2048], F32)
        W5 = P.ps("rk_W5ps", [128, 512], F32)
        small2 = P.ps("rk_small2", [128, 512], F32)
        yps = P.ps("rk_yps", [128, TB], F32)
        trpv = W[:, 0:1536].rearrange("p (c a t) -> p c a t", c=8, a=3)
        ppv = W[:, 0:2048].rearrange("p (c a t) -> p c a t", c=8, a=2)
        ps1v = W[:, 0:1024].rearrange("p (c t) -> p c t", c=8)
        ps2v = W[:, 1024:1536].rearrange("p (c t) -> p c t", c=8)
        lpv = W5[:, 0:512].rearrange("p (c t) -> p c t", c=8)
        sq = small2[:, 0:192].rearrange("p (a t) -> p a t", a=3)

        def dve(fn, r, w):
            P.op("dve", fn, r, w)

        def act(fn, r, w):
            P.op("act", fn, r, w)

        def pe(fn, r, w):
            P.op("pe", fn, r, w)

        def c3(ap):
            return ap.rearrange("p (c t) -> p c t", t=64)

        nblk = 0
        for pr in range(DBG.get("rkp", 8)):
            pc = slice(pr * 128, (pr + 1) * 128)
            vcol = lambda base: V[:, base + pr:base + pr + 1]
            dve(lambda e: e.memset(Hf[:], 0.0), [], ["rk_Hf"])
            dve(lambda e: e.memset(Hb[:], 0.0), [], ["rk_Hb"])
            for j in range(DBG.get("rkb", NB)):
                par = nblk % 2
                nblk += 1
                gam, gv, bonus, AR, PM, TM, MTs = gam2[par], gv2[par], bonus2[par], AR2[par], PM2[par], TM2[par], MT2[par]
                rGAM, rG, rBON, rAR, rPM, rTM, rMT = (f"rk_gam{par}", f"rk_g{par}", f"rk_bonus{par}", f"rk_AR{par}",
                                                      f"rk_PM{par}", f"rk_TM{par}", f"rk_MT{par}")
                for gi, n in enumerate("rkv"):
                    row0 = gi * 1024 + pr * 128
                    if j == 0:
                        dve(lambda e, n=n: e.memset(zt[n][:, 0:1], 0.0), [], [f"rk_z{n}"])
                        P.dma("sp", zt[n][:, 1:TB + 1], zrwT[row0:row0 + 128, 0:TB], reads=["dram:zrwT"],
                              writes=[f"rk_z{n}"], key=f"rk_z{n}")
                    else:
                        P.dma("sp", zt[n][:], zrwT[row0:row0 + 128, j * TB - 1:(j + 1) * TB], reads=["dram:zrwT"],
                              writes=[f"rk_z{n}"], key=f"rk_z{n}")
                    mcol = 80 + gi * 8 + pr
                    dve(lambda e, n=n: e.tensor_tensor(out=xs[n][:], in0=zt[n][:, 0:TB], in1=zt[n][:, 1:TB + 1],
                                                       op=ALU.subtract), [f"rk_z{n}"], [f"rk_s{n}"])
                    dve(lambda e, n=n, mcol=mcol: e.scalar_tensor_tensor(
                        out=xs[n][:], in0=xs[n][:], scalar=V[:, mcol:mcol + 1], in1=zt[n][:, 1:TB + 1], op0=ALU.mult,
                        op1=ALU.add), [f"rk_s{n}", f"rk_z{n}", "V"], [f"rk_s{n}"])
                js = slice(j * TB, (j + 1) * TB)
                pe(lambda e: e.matmul(ps_a[:], lhsT=w2b[:, pc], rhs=twd[:, js], start=True, stop=True),
                   ["rk_w", "rk_lin"], ["rk_psa"])
                act(lambda e: e.activation(out=logw[:], in_=ps_a[:], func=AF.Sigmoid, bias=vcol(104)), ["rk_psa", "V"],
                    ["rk_logw"])
                dve(lambda e: e.tensor_scalar(out=logw[:], in0=logw[:], scalar1=NEG_E, scalar2=None, op0=ALU.mult),
                    ["rk_logw"], ["rk_logw"])
                pe(lambda e: e.matmul(ps_b[:], lhsT=a2b[:, pc], rhs=ads[:, js], start=True, stop=True),
                   ["rk_w", "rk_lin"], ["rk_psa"])
                act(lambda e: e.activation(out=av[:], in_=ps_b[:], func=AF.Sigmoid, bias=vcol(112)), ["rk_psa", "V"],
                    ["rk_a"])
                pe(lambda e: e.matmul(ps_a[:], lhsT=g2b[:, 0, pc], rhs=sg1[:, js], start=True, stop=False),
                   ["rk_w", "rk_lin"], ["rk_psa"])
                pe(lambda e: e.matmul(ps_a[:], lhsT=g2b[0:32, 1, pc], rhs=sg2[:, js], start=False, stop=True),
                   ["rk_w", "rk_lin"], ["rk_psa"])
                act(lambda e: e.activation(out=gv[:], in_=ps_a[:], func=AF.Copy), ["rk_psa"], [rG])
                dve(lambda e: e.tensor_scalar(out=tA[:], in0=xs["k"][:], scalar1=vcol(120), scalar2=None, op0=ALU.mult),
                    ["rk_sk", "V"], ["rk_tA"])
                dve(lambda e: e.tensor_tensor(out=tbf[:], in0=tA[:], in1=tA[:], op=ALU.mult), ["rk_tA"], ["rk_tbf"])
                pe(lambda e: e.matmul(ps_b[:], lhsT=bones_bf[:], rhs=tbf[:], start=True, stop=True),
                   ["bones_bf", "rk_tbf"], ["rk_psa"])
                act(lambda e: e.activation(out=tB[:], in_=ps_b[:], func=AF.Sqrt), ["rk_psa"], ["rk_tB"])
                dve(lambda e: e.tensor_scalar(out=tB[:], in0=tB[:], scalar1=1e-12, scalar2=None, op0=ALU.max),
                    ["rk_tB"], ["rk_tB"])
                dve(lambda e: e.reciprocal(out=tB[:], in_=tB[:]), ["rk_tB"], ["rk_tB"])
                dve(lambda e: e.tensor_tensor(out=kkn[:], in0=tA[:], in1=tB[:], op=ALU.mult), ["rk_tA", "rk_tB"],
                    ["rk_kkn"])
                dve(lambda e: e.tensor_scalar(out=tA[:], in0=av[:], scalar1=vcol(128), scalar2=omka[:, pr:pr + 1],
                                              op0=ALU.mult, op1=ALU.add), ["rk_a", "V", "omka"], ["rk_tA"])
                dve(lambda e: e.tensor_tensor(out=k2[:], in0=xs["k"][:], in1=tA[:], op=ALU.mult), ["rk_sk", "rk_tA"],
                    ["rk_k2"])
                dve(lambda e: e.tensor_tensor_scan(out=cum[:], data0=rmask[:], data1=logw[:], initial=0.0, op0=ALU.mult,
                                                   op1=ALU.add), ["msk", "rk_logw"], ["rk_cum"])
                act(lambda e: e.activation(out=gam[:], in_=cum[:], func=AF.Exp), ["rk_cum"], [rGAM])
                act(lambda e: e.activation(out=ginv[:], in_=cum[:], func=AF.Exp, scale=-1.0), ["rk_cum"], ["rk_ginv"])
                dve(lambda e: e.tensor_tensor(out=tB[:], in0=cum[:], in1=logw[:], op=ALU.subtract),
                    ["rk_cum", "rk_logw"], ["rk_tB"])
                act(lambda e: e.activation(out=gprev[:], in_=tB[:], func=AF.Exp), ["rk_tB"], ["rk_gprev"])
                dve(lambda e: e.scalar_tensor_tensor(out=AR[:, :, 0, :], in0=c3(kkn[:]), scalar=-1.0, in1=c3(gprev[:]),
                                                     op0=ALU.mult, op1=ALU.mult), ["rk_kkn", "rk_gprev"], [rAR])
                dve(lambda e: e.tensor_tensor(out=AR[:, :, 1, :], in0=c3(xs["r"][:]), in1=c3(gam[:]), op=ALU.mult),
                    ["rk_sr", rGAM, rAR], [rAR])
                dve(lambda e: e.tensor_tensor(out=tA[:], in0=kkn[:], in1=av[:], op=ALU.mult), ["rk_kkn", "rk_a"],
                    ["rk_tA"])
                dve(lambda e: e.tensor_tensor(out=BK[:, :, 0, :], in0=c3(tA[:]), in1=c3(ginv[:]), op=ALU.mult),
                    ["rk_tA", "rk_ginv"], ["rk_BK"])
                dve(lambda e: e.tensor_tensor(out=BK[:, :, 1, :], in0=c3(k2[:]), in1=c3(ginv[:]), op=ALU.mult),
                    ["rk_k2", "rk_ginv", "rk_BK"], ["rk_BK"])
                act(lambda e: e.activation(out=vbf[:], in_=xs["v"][:], func=AF.Copy), ["rk_sv"], ["rk_vbf"])
                dve(lambda e: e.tensor_tensor(out=tC[:], in0=xs["r"][:], in1=k2[:], op=ALU.mult), ["rk_sr", "rk_k2"],
                    ["rk_tC"])
                dve(lambda e: e.tensor_scalar(out=tbf[:], in0=tC[:], scalar1=vcol(136), scalar2=None, op0=ALU.mult),
                    ["rk_tC", "V", "rk_tbf"], ["rk_tbf"])
                pe(lambda e: e.matmul(ps_b[:], lhsT=bones_bf[:], rhs=tbf[:], start=True, stop=True),
                   ["bones_bf", "rk_tbf"], ["rk_psa"])
                dve(lambda e: e.tensor_tensor(out=bonus[:], in0=ps_b[:], in1=xs["v"][:], op=ALU.mult),
                    ["rk_psa", "rk_sv"], [rBON])
                for c in range(8):
                    cc = slice(c * 64, (c + 1) * 64)
                    for h in range(2):
                        sl = slice(64 * h, 64 * h + 64)
                        for ti in range(3):
                            in_ap = BK[sl, c, ti, :] if ti < 2 else vbf[sl, cc]
                            pe(lambda e, sl=sl, ti=ti, in_ap=in_ap, c=c: e.matmul(trpv[sl, c, ti, :], lhsT=in_ap,
                                                                                 rhs=ident_bf[sl, sl], start=True, stop=True),
                               ["rk_BK", "rk_vbf", "ident_bf"], ["rk_Wps"])
                act(lambda e: e.activation(out=TM[:], in_=trpv, func=AF.Copy), ["rk_Wps"], [rTM])
                for c in range(8):
                    for h in range(2):
                        sl = slice(64 * h, 64 * h + 64)
                        arv = AR[sl, c, :, :].rearrange("p a t -> p (a t)")
                        pe(lambda e, sl=sl, c=c, arv=arv: e.matmul(ppv[sl, c, 0, :], lhsT=BK[sl, c, 0, :], rhs=arv, start=True,
                                                                   stop=True), ["rk_BK", rAR], ["rk_Wps"])
                        pe(lambda e, sl=sl, c=c, arv=arv: e.matmul(ppv[sl, c, 1, :], lhsT=BK[sl, c, 1, :], rhs=arv, start=True,
                                                                   stop=True), ["rk_BK", rAR], ["rk_Wps"])
                        pe(lambda e, sl=sl, c=c: e.matmul(lpv[sl, c, :], lhsT=AR[sl, c, 0, :], rhs=BK[sl, c, 0, :], start=True,
                                                          stop=True), ["rk_BK", rAR], ["rk_W5ps"])
                dve(lambda e: e.tensor_tensor(out=PM[:], in0=ppv, in1=maskP8[:], op=ALU.mult), ["rk_Wps", "rk_m8"], [rPM])
                dve(lambda e: e.tensor_tensor(out=L8[:], in0=lpv, in1=maskL8[:], op=ALU.mult), ["rk_W5ps", "rk_m8"], ["rk_L"])
                act(lambda e: e.activation(out=NS8[:, :, 0, :], in_=PM[:, :, 0, 0:64], func=AF.Copy), [rPM], ["rk_NS"])
                P.op("pool", lambda e: e.tensor_copy(out=NS8[:, :, 1, :], in_=eye8[:]), ["rk_m8", "rk_NS"], ["rk_NS"])
                for step in range(6):
                    last = step == 5
                    for c in range(8):
                        for h in range(2):
                            sl = slice(64 * h, 64 * h + 64)
                            if not last:
                                nsv = NS8[sl, c, :, :].rearrange("p a t -> p (a t)")
                                pe(lambda e, sl=sl, nsv=nsv, c=c: e.matmul(ps1v[sl, c, :], lhsT=L8[sl, c, :], rhs=nsv, start=True,
                                                                           stop=True), ["rk_L", "rk_NS"], ["rk_Wps"])
                                pe(lambda e, sl=sl, c=c: e.matmul(ps2v[sl, c, :], lhsT=NS8[sl, c, 0, :], rhs=L8[sl, c, :],
                                                                  start=True, stop=True), ["rk_L", "rk_NS"], ["rk_Wps"])
                            else:
                                pe(lambda e, sl=sl, c=c: e.matmul(ps1v[sl, c, 64:128], lhsT=L8[sl, c, :], rhs=NS8[sl, c, 1, :],
                                                                  start=True, stop=True), ["rk_L", "rk_NS"], ["rk_Wps"])
                    if not last:
                        dve(lambda e: e.tensor_copy(out=NS8[:, :, 0, :], in_=ps1v[:, :, 0:64]), ["rk_Wps", "rk_NS"], ["rk_NS"])
                        dve(lambda e: e.tensor_tensor(out=NS8[:, :, 1, :], in0=NS8[:, :, 1, :], in1=ps1v[:, :, 64:128],
                                                      op=ALU.add), ["rk_Wps", "rk_NS"], ["rk_NS"])
                        dve(lambda e: e.tensor_copy(out=L8[:], in_=ps2v), ["rk_Wps", "rk_L"], ["rk_L"])
                    else:
                        dve(lambda e: e.tensor_tensor(out=MTs[:], in0=NS8[:, :, 1, :], in1=ps1v[:, :, 64:128], op=ALU.add),
                            ["rk_Wps", "rk_NS"], [rMT])
                for c in range(8):
                    cc = slice(c * 64, (c + 1) * 64)
                    gcol = gam[:, c * 64 + 63:c * 64 + 64]
                    for h in range(2):
                        sl = slice(64 * h, 64 * h + 64)
                        pe(lambda e, sl=sl, c=c: e.matmul(sq[sl, 0, :], lhsT=AR[sl, c, 0, :], rhs=Hb[sl, :], start=True,
                                                          stop=False), [rAR, "rk_Hb"], ["rk_small2"])
                        pe(lambda e, sl=sl, c=c: e.matmul(sq[sl, 0, :], lhsT=PM[sl, c, 1, 0:64], rhs=TM[sl, c, 2, :],
                                                          start=False, stop=True), [rPM, rTM], ["rk_small2"])
                    act(lambda e: e.activation(out=Zs[:], in_=sq[:, 0, :], func=AF.Copy), ["rk_small2"], ["rk_Zs"])
                    for h in range(2):
                        sl = slice(64 * h, 64 * h + 64)
                        pe(lambda e, sl=sl, c=c: e.matmul(sq[sl, 1, :], lhsT=MTs[sl, c, :], rhs=Zs[sl, :], start=True,
                                                          stop=True), [rMT, "rk_Zs"], ["rk_small2"])
                    act(lambda e: e.activation(out=Us[:], in_=sq[:, 1, :], func=AF.Copy), ["rk_small2"], ["rk_Us"])
                    for h in range(2):
                        sl = slice(64 * h, 64 * h + 64)
                        pe(lambda e, sl=sl, c=c: e.matmul(sq[sl, 2, :], lhsT=TM[sl, c, 0, :], rhs=Us[sl, :], start=True,
                                                          stop=False), [rTM, "rk_Us"], ["rk_small2"])
                        pe(lambda e, sl=sl, c=c: e.matmul(sq[sl, 2, :], lhsT=TM[sl, c, 1, :], rhs=TM[sl, c, 2, :],
                                                          start=False, stop=True), [rTM], ["rk_small2"])
                        pe(lambda e, sl=sl, c=c, cc=cc: e.matmul(yps[sl, cc], lhsT=Hb[sl, :], rhs=AR[sl, c, 1, :], start=True,
                                                                 stop=False), [rAR, "rk_Hb"], ["rk_yps"])
                        pe(lambda e, sl=sl, c=c, cc=cc: e.matmul(yps[sl, cc], lhsT=Us[sl, :], rhs=PM[sl, c, 0, 64:128],
                                                                 start=False, stop=False), ["rk_Us", rPM], ["rk_yps"])
                        pe(lambda e, sl=sl, c=c, cc=cc: e.matmul(yps[sl, cc], lhsT=TM[sl, c, 2, :], rhs=PM[sl, c, 1, 64:128],
                                                                 start=False, stop=True), [rTM, rPM], ["rk_yps"])
                    P.op("pool", lambda e, gcol=gcol: e.tensor_scalar(out=HG[:], in0=Hf[:], scalar1=gcol, scalar2=None,
                                                                       op0=ALU.mult), ["rk_Hf", rGAM], ["rk_HG"])
                    dve(lambda e, gcol=gcol: e.scalar_tensor_tensor(out=Hf[:], in0=sq[:, 2, :], scalar=gcol, in1=HG[:],
                                                                    op0=ALU.mult, op1=ALU.add),
                        ["rk_small2", "rk_HG", rGAM, "rk_Hf"], ["rk_Hf"])
                    act(lambda e: e.activation(out=Hb[:], in_=Hf[:], func=AF.Copy), ["rk_Hf", "rk_Hb"], ["rk_Hb"])
                act(lambda e: e.activation(out=Yt[:], in_=yps[:], func=AF.Copy), ["rk_yps"], ["rk_Y"])
                dve(lambda e: e.tensor_copy(out=tbf[:], in_=Yt[:]), ["rk_Y", "rk_tbf"], ["rk_tbf"])
                pe(lambda e: e.matmul(ps_a[:], lhsT=bones_bf[:], rhs=tbf[:], start=True, stop=True), ["bones_bf", "rk_tbf"],
                   ["rk_psa"])
                dve(lambda e: e.scalar_tensor_tensor(out=tA[:], in0=ps_a[:], scalar=-1.0 / 64, in1=Yt[:], op0=ALU.mult,
                                                     op1=ALU.add), ["rk_psa", "rk_Y"], ["rk_tA"])
                dve(lambda e: e.tensor_tensor(out=tbf[:], in0=tA[:], in1=tA[:], op=ALU.mult), ["rk_tA", "rk_tbf"],
                    ["rk_tbf"])
                pe(lambda e: e.matmul(ps_a[:], lhsT=bones_bf[:], rhs=tbf[:], start=True, stop=True), ["bones_bf", "rk_tbf"],
                   ["rk_psa"])
                act(lambda e: e.activation(out=tB[:], in_=ps_a[:], func=AF.Sqrt, bias=GN_EPS, scale=1.0 / 64),
                    ["rk_psa"], ["rk_tB"])
                dve(lambda e: e.reciprocal(out=tB[:], in_=tB[:]), ["rk_tB"], ["rk_tB"])
                dve(lambda e: e.tensor_tensor(out=tA[:], in0=tA[:], in1=tB[:], op=ALU.mult), ["rk_tA", "rk_tB"], ["rk_tA"])
                dve(lambda e: e.tensor_scalar(out=tA[:], in0=tA[:], scalar1=vcol(144), scalar2=vcol(152), op0=ALU.mult,
                                              op1=ALU.add), ["rk_tA", "V"], ["rk_tA"])
                dve(lambda e: e.tensor_tensor(out=tA[:], in0=tA[:], in1=bonus[:], op=ALU.add), ["rk_tA", rBON],
                    ["rk_tA"])
                dve(lambda e: e.tensor_tensor(out=yo[:], in0=tA[:], in1=gv[:], op=ALU.mult), ["rk_tA", rG], ["rk_yo"])
                P.dma("sp", yrwT[pc, js], yo[:], reads=["rk_yo"], writes=["dram:yrwT"], key="rk_yo")
        P.barrier()
        P.stack = prev_stack


def mla_stage(P, nc, st0, V, pos, pos_w, qk_tab, d_tab, zmlaT, cqnT_full, ckvnT, w_q_up, w_kv_up, ymlaT, ones_bf, evac_copy):
    SCALE = 192 ** -0.5
    TWO_PI = 2.0 * math.pi
    QB = 342
    with contextlib.ExitStack() as st:
        prev_stack = P.stack
        P.stack = st
        sb = P.sb
        wq = sb("ml_wq", [128, 4, 1536], BF16)
        wkv = sb("ml_wkv", [128, 4, 2048], BF16)
        for kc in range(4):
            P.dma("pool", wq[:, kc, :], w_q_up[kc * 128:(kc + 1) * 128, :], writes=["ml_w"], key="ml_w")
            P.dma("pool", wkv[:, kc, :], w_kv_up[kc * 128:(kc + 1) * 128, :], writes=["ml_w"], key="ml_w")
        qk = sb("ml_qk", [128, 3, QB], F32)
        dt = sb("ml_dt", [128, 32], F32)
        P.dma("sp", qk[:], qk_tab.rearrange("p (a t) -> p a t", a=3), writes=["ml_tab"], key="ml_tab")
        P.dma("sp", dt[:], d_tab, writes=["ml_tab"], key="ml_tab")
        cs = sb("ml_cs", [64, T], F32)
        sn = sb("ml_sn", [64, T], F32)
        csw = sb("ml_csw", [64, TW], F32)
        snw = sb("ml_snw", [64, TW], F32)
        with contextlib.ExitStack() as st2:
            P.stack = st2
            posi = P.sb("ml_posi", [64, T], I32)
            ang = P.sb("ml_ang", [64, T], F32)
            tt = P.sb("ml_tt", [64, T], F32)
            kf = P.sb("ml_kf", [64, T], F32)
            for (psrc, n, cdst, sdst) in ((pos, T, cs, sn), (pos_w, TW, csw, snw)):
                P.dma("sp", posi[:, 0:n], psrc[0:1, :].to_broadcast([64, n]), writes=["ml_posi"], key="ml_posi")
                P.op("dve", lambda e: e.tensor_copy(out=ang[:, 0:n], in_=posi[:, 0:n]), ["ml_posi"], ["ml_ang"])
                P.op("dve", lambda e: e.tensor_scalar(out=ang[:, 0:n], in0=ang[:, 0:n], scalar1=V[0:64, 524:525], scalar2=None,
                                                      op0=ALU.mult), ["ml_ang", "V"], ["ml_ang"])
                for (dst, shift) in ((sdst, 0.5), (cdst, 0.75)):
                    P.op("dve", lambda e, shift=shift: e.tensor_scalar(out=tt[:, 0:n], in0=ang[:, 0:n], scalar1=1.0 / TWO_PI,
                                                                       scalar2=shift, op0=ALU.mult, op1=ALU.add),
                         ["ml_ang", "ml_tt"], ["ml_tt"])
                    P.op("dve", lambda e: e.tensor_copy(out=posi[:, 0:n], in_=tt[:, 0:n]), ["ml_tt", "ml_posi", "ml_ang"],
                         ["ml_posi"])
                    P.op("dve", lambda e: e.tensor_copy(out=kf[:, 0:n], in_=posi[:, 0:n]), ["ml_posi"], ["ml_kf"])
                    P.op("dve", lambda e: e.tensor_tensor(out=tt[:, 0:n], in0=tt[:, 0:n], in1=kf[:, 0:n], op=ALU.subtract),
                         ["ml_tt", "ml_kf"], ["ml_tt"])
                    P.op("dve", lambda e: e.tensor_scalar(out=kf[:, 0:n], in0=tt[:, 0:n], scalar1=0.0, scalar2=None,
                                                          op0=ALU.is_lt), ["ml_tt", "ml_kf"], ["ml_kf"])
                    P.op("dve", lambda e: e.scalar_tensor_tensor(out=tt[:, 0:n], in0=kf[:, 0:n], scalar=-0.5, in1=tt[:, 0:n],
                                                                 op0=ALU.add, op1=ALU.add), ["ml_tt", "ml_kf"], ["ml_tt"])
                    P.op("act", lambda e, dst=dst: e.activation(out=dst[:, 0:n], in_=tt[:, 0:n], func=AF.Sin, scale=TWO_PI),
                         ["ml_tt"], ["ml_rope"])
            P.barrier()
            P.stack = st
        kr = sb("ml_kr", [64, T], BF16)
        kn = sb("ml_kn", [128, T], BF16)
        vtm = sb("ml_vtm", [128, 32, 128], BF16)
        xq = [sb(f"ml_xq{i}", [128, 4, TB], BF16) for i in range(2)]
        raw = sb("ml_raw", [64, TB], F32)
        t1 = sb("ml_t1", [64, TB], F32)
        t2 = sb("ml_t2", [64, TB], F32)
        qn = sb("ml_qn", [128, TB], BF16)
        qr = sb("ml_qr", [64, TB], BF16)
        pT = [sb(f"ml_pT{i}", [128, TB], BF16) for i in range(3)]
        rinv = sb("ml_rinv", [128, TB], F32)
        yo = [sb(f"ml_yo{i}", [128, TB], BF16) for i in range(2)]
        psq = P.ps("ml_psq", [128, TB], F32)
        psr = P.ps("ml_psr", [64, TB], F32)
        psv = P.ps("ml_psv", [128, 4, 128], F32)
        sps = [P.ps(f"ml_sps{i}", [128, TB], F32) for i in range(2)]
        ops_ = P.ps("ml_ops", [128, TB], F32)
        lps = P.ps("ml_lps", [128, TB], F32)

        def rope(src_res, cst, snt, c0, n, out_ap, out_res, scale):
            lo, hi = slice(0, 32), slice(32, 64)
            js = slice(c0, c0 + n)
            d = lambda fn, r, w: P.op("dve", fn, r, w)
            d(lambda e: e.tensor_tensor(out=t1[lo, 0:n], in0=raw[lo, 0:n], in1=cst[lo, js], op=ALU.mult), [src_res, "ml_rope"],
              ["ml_t1"])
            d(lambda e: e.tensor_tensor(out=t2[lo, 0:n], in0=raw[hi, 0:n], in1=snt[hi, js], op=ALU.mult), [src_res, "ml_rope"],
              ["ml_t2"])
            d(lambda e: e.tensor_tensor(out=t1[hi, 0:n], in0=raw[hi, 0:n], in1=cst[hi, js], op=ALU.mult),
              [src_res, "ml_rope", "ml_t1"], ["ml_t1"])
            d(lambda e: e.tensor_tensor(out=t2[hi, 0:n], in0=raw[lo, 0:n], in1=snt[lo, js], op=ALU.mult),
              [src_res, "ml_rope", "ml_t2"], ["ml_t2"])
            d(lambda e: e.tensor_tensor(out=t1[lo, 0:n], in0=t1[lo, 0:n], in1=t2[lo, 0:n], op=ALU.subtract), ["ml_t1", "ml_t2"],
              ["ml_t1"])
            d(lambda e: e.tensor_tensor(out=t1[hi, 0:n], in0=t1[hi, 0:n], in1=t2[hi, 0:n], op=ALU.add), ["ml_t1", "ml_t2"],
              ["ml_t1"])
            P.op("act", lambda e: e.activation(out=out_ap, in_=t1[:, 0:n], func=AF.Copy, scale=scale), ["ml_t1"], [out_res])

        for j in range(NB):
            js = slice(j * TB, (j + 1) * TB)
            P.dma("sp", raw[:], zmlaT[1024:1088, js], reads=["dram:zmlaT"], writes=["ml_raw"], key="ml_raw")
            rope("ml_raw", cs, sn, j * TB, TB, kr[:, js], "ml_kr", 1.0)
        xi = 0
        for hd in range(DBG.get("mlh", 8)):
            for j in range(NB):
                js = slice(j * TB, (j + 1) * TB)
                b = xi % 2
                xi += 1
                P.dma("sp", xq[b][:], ckvnT[:, js].rearrange("(k p) t -> p k t", p=128), reads=["dram:ckvnT"],
                      writes=[f"ml_xq{b}"], key=f"ml_xq{b}")
                for kc in range(4):
                    P.op("pe", lambda e, kc=kc, b=b: e.matmul(psq[:], lhsT=wkv[:, kc, hd * 256:hd * 256 + 128],
                                                              rhs=xq[b][:, kc, :], start=(kc == 0), stop=(kc == 3)),
                         ["ml_w", f"ml_xq{b}"], ["ml_psq"])
                evac_copy(kn[:, js], psq[:], ["ml_psq"], ["ml_kn"])
                for tt_ in range(4):
                    for kc in range(4):
                        P.op("pe", lambda e, kc=kc, b=b, tt_=tt_: e.matmul(
                            psv[:, tt_, :], lhsT=xq[b][:, kc, tt_ * 128:(tt_ + 1) * 128],
                            rhs=wkv[:, kc, hd * 256 + 128:hd * 256 + 256], start=(kc == 0), stop=(kc == 3)),
                            ["ml_w", f"ml_xq{b}"], ["ml_psv"])
                evac_copy(vtm[:, j * 4:(j + 1) * 4, :], psv[:], ["ml_psv"], ["ml_vtm"])
            for jj in range(DBG.get("mlb", 3)):
                n = QB
                b = xi % 2
                xi += 1

                P.dma("sp", xq[b][:, :, 0:QB], cqnT_full[:, jj * QB:(jj + 1) * QB].rearrange("(k p) t -> p k t", p=128),
                      reads=["dram:cqwT"], writes=[f"ml_xq{b}"], key=f"ml_xq{b}")
                for kc in range(4):
                    P.op("pe", lambda e, kc=kc, b=b: e.matmul(psq[:, 0:n], lhsT=wq[:, kc, hd * 192:hd * 192 + 128],
                                                              rhs=xq[b][:, kc, 0:n], start=(kc == 0), stop=(kc == 3)),
                         ["ml_w", f"ml_xq{b}"], ["ml_psq"])
                P.op("act", lambda e: e.activation(out=qn[:, 0:n], in_=psq[:, 0:n], func=AF.Copy, scale=SCALE), ["ml_psq"],
                     ["ml_qn"])
                for kc in range(4):
                    P.op("pe", lambda e, kc=kc, b=b: e.matmul(psr[:, 0:n], lhsT=wq[:, kc, hd * 192 + 128:hd * 192 + 192],
                                                              rhs=xq[b][:, kc, 0:n], start=(kc == 0), stop=(kc == 3)),
                         ["ml_w", f"ml_xq{b}"], ["ml_psr"])
                P.op("act", lambda e: e.activation(out=raw[:, 0:n], in_=psr[:, 0:n], func=AF.Copy), ["ml_psr", "ml_raw"],
                     ["ml_raw"])
                rope("ml_raw", csw, snw, jj * QB, n, qr[:, 0:n], "ml_qr", SCALE)
                nkt = 32
                for kt in range(nkt):
                    si = kt % 2
                    pi = kt % 3
                    ks = slice(kt * 128, (kt + 1) * 128)
                    P.op("pe", lambda e, si=si, ks=ks: e.matmul(sps[si][:, 0:n], lhsT=kn[:, ks], rhs=qn[:, 0:n],
                                                                start=True, stop=False),
                         ["ml_kn", "ml_qn"], [f"ml_sps{si}"])
                    P.op("pe", lambda e, si=si, ks=ks: e.matmul(sps[si][:, 0:n], lhsT=kr[:, ks], rhs=qr[:, 0:n],
                                                                start=False, stop=True),
                         ["ml_kr", "ml_qr"], [f"ml_sps{si}"])
                    P.op("act", lambda e, si=si, pi=pi: e.activation(out=pT[pi][:, 0:n], in_=sps[si][:, 0:n], func=AF.Exp),
                         [f"ml_sps{si}"], [f"ml_pT{pi}"])
                    P.op("dve", lambda e, pi=pi, kt=kt, jj=jj: e.scalar_tensor_tensor(
                        out=pT[pi][:, 0:n], in0=qk[:, jj, :], scalar=dt[:, kt:kt + 1], in1=pT[pi][:, 0:n], op0=ALU.is_ge,
                        op1=ALU.mult), [f"ml_pT{pi}", "ml_tab"], [f"ml_pT{pi}"])
                    P.op("pe", lambda e, pi=pi, kt=kt: e.matmul(
                        ops_[:, 0:n], lhsT=vtm[:, kt, :], rhs=pT[pi][:, 0:n], start=(kt == 0), stop=(kt == nkt - 1)),
                        ["ml_vtm", f"ml_pT{pi}"], ["ml_ops"])
                    P.op("pe", lambda e, pi=pi, kt=kt: e.matmul(
                        lps[:, 0:n], lhsT=ones_bf[:], rhs=pT[pi][:, 0:n], start=(kt == 0), stop=(kt == nkt - 1)),
                        ["ones_bf", f"ml_pT{pi}"], ["ml_lps"])
                P.op("dve", lambda e: e.tensor_scalar(out=rinv[:, 0:n], in0=lps[:, 0:n], scalar1=1e-30, scalar2=None,
                                                      op0=ALU.max), ["ml_lps"], ["ml_rinv"])
                P.op("dve", lambda e: e.reciprocal(out=rinv[:, 0:n], in_=rinv[:, 0:n]), ["ml_rinv"], ["ml_rinv"])
                ob = jj % 2
                P.op("dve", lambda e, ob=ob: e.tensor_tensor(out=yo[ob][:, 0:n], in0=ops_[:, 0:n], in1=rinv[:, 0:n], op=ALU.mult),
                     ["ml_ops", "ml_rinv"], [f"ml_yo{ob}"])
                P.dma("sp", ymlaT[hd * 128:(hd + 1) * 128, jj * QB:(jj + 1) * QB], yo[ob][:, 0:n], reads=[f"ml_yo{ob}"],
                      writes=["dram:ymlaT"], key=f"ml_yo{ob}")
        P.barrier()
        P.stack = prev_stack


_CACHE = {}


def make_in_maps(inputs):
    sq = lambda a: np.ascontiguousarray(np.asarray(a)[0])
    V = pack_vecs(inputs)
    common = {
        "vecs": V,
        "w_in": sq(inputs["w_in"]), "rw_w2": sq(inputs["rw_w2"]), "rw_a2": sq(inputs["rw_a2"]), "rw_g2": sq(inputs["rw_g2"]),
        "w_q_up": sq(inputs["mla_w_q_up"]), "w_kv_up": sq(inputs["mla_w_kv_up"]),
        "w_brw": sq(inputs["w_branch_rw"]), "w_bmla": sq(inputs["w_branch_mla"]), "w_out": sq(inputs["w_out"]),
        "w_up": sq(inputs["w_up"]), "w_down": sq(inputs["w_down"]), "w_ple": sq(inputs["w_ple"]),
        "w_pg": sq(inputs["w_ple_gate"]),
    }
    i_ = np.arange(342)
    qk = np.zeros((128, 3, 342), np.float32)
    for jj in range(3):
        qc = np.floor_divide(342 * jj - 2 + i_, 64).astype(np.float32)
        qk[:, jj, :] = qc[None, :] - (np.arange(128)[:, None] >= 64).astype(np.float32)
    common["qk_tab"] = np.ascontiguousarray(qk.reshape(128, 3 * 342))
    xs = np.asarray(inputs["x"], np.float32)
    ps = np.asarray(inputs["p"], np.float32)[0]
    posn = np.asarray(inputs["positions"], np.int32)
    maps = []
    for c in range(8):
        b, q = c // 4, c % 4
        m = dict(common)
        m["x"] = np.ascontiguousarray(xs[b])
        m["p"] = np.ascontiguousarray(ps[b, 1024 * q:1024 * (q + 1)])
        m["pos"] = np.ascontiguousarray(posn[b:b + 1])
        pw = np.zeros((1, TW), np.int32)
        lo = 1024 * q - 2
        if lo < 0:
            pw[0, 2:] = posn[b, 0:1024]
        else:
            pw[0, :] = posn[b, lo:lo + TW]
        m["pos_w"] = pw
        m["d_tab"] = np.ascontiguousarray(np.broadcast_to((2.0 * np.arange(32) - 16.0 * q).astype(np.float32)[None, :], (128, 32)))
        maps.append(m)
    return maps


def kernel(**inputs):
    if "nc" not in _CACHE:
        _CACHE["nc"] = build_program()
    nc = _CACHE["nc"]
    maps = make_in_maps(inputs)
    res = run_bass_kernel_spmd(nc, maps, core_ids=list(range(8)))
    outs = [np.asarray(res.results[c]["out"], np.float32) for c in range(8)]
    return np.stack([np.concatenate(outs[0:4], axis=0), np.concatenate(outs[4:8], axis=0)], axis=0)
```

```python
import contextlib
import numpy as np
import concourse.bass as bass
import concourse.mybir as mybir

F32 = mybir.dt.float32
BF16 = mybir.dt.bfloat16
I32 = mybir.dt.int32
ALU = mybir.AluOpType
AF = mybir.ActivationFunctionType
AX = mybir.AxisListType


class _Rec:
    def __init__(self):
        self.calls = []

    def __getattr__(self, name):
        def f(*a, **k):
            self.calls.append((name, a, k))
            return self

        return f


def _eager(fn):
    rec = _Rec()
    fn(rec)
    assert len(rec.calls) == 1, rec.calls
    name, a, k = rec.calls[0]
    return lambda e: getattr(e, name)(*a, **k)


class Prog:
    COMPUTE = ("pe", "act", "dve", "pool")

    def __init__(self, nc, stack):
        self.nc = nc
        self.stack = stack
        self.stack0 = stack
        self.streams = {e: [] for e in ("pe", "act", "dve", "pool", "sp")}
        self.sems = {}
        self.cnt = {}
        for e in self.COMPUTE:
            self.sems[e] = stack.enter_context(nc.semaphore("s_" + e))
            self.cnt[e] = 0
        self.seen = {e: {} for e in self.streams}
        self.res = {}
        self.dma_sems = {}
        self.n_ops = 0

    def sb(self, name, shape, dt):
        return self.stack.enter_context(self.nc.sbuf_tensor(name, list(shape), dt))

    def ps(self, name, shape, dt=F32):
        return self.stack.enter_context(self.nc.psum_tensor(name, list(shape), dt))

    def _dma_sem(self, key):
        if key not in self.dma_sems:
            self.dma_sems[key] = self.stack0.enter_context(self.nc.semaphore("d_" + key))
            self.sems["D:" + key] = self.dma_sems[key]
            self.cnt["D:" + key] = 0
        return "D:" + key

    def _deps(self, eng, reads, writes, exclude=None):
        need = {}
        for r in reads:
            ent = self.res.get(r)
            if ent:
                for k, v in ent[0].items():
                    need[k] = max(need.get(k, 0), v)
                if "ps" in r or "small" in r or "trp" in r:
                    for k, v in ent[1].items():
                        if k != eng:
                            need[k] = max(need.get(k, 0), v)
        for w in writes:
            ent = self.res.get(w)
            if ent:
                if not w.startswith("dram:"):
                    for k, v in ent[0].items():
                        need[k] = max(need.get(k, 0), v)
                for k, v in ent[1].items():
                    need[k] = max(need.get(k, 0), v)
        waits = []
        for k, v in need.items():
            if k == eng and eng == "pe":
                continue
            if k == exclude:
                continue
            if self.seen[eng].get(k, 0) >= v:
                continue
            self.seen[eng][k] = v
            waits.append((k, v))
        return waits

    def _mark(self, key, val, reads, writes):
        for r in reads:
            ent = self.res.setdefault(r, [{}, {}])
            ent[1][key] = val
        for w in writes:
            ent = self.res.setdefault(w, [{}, {}])
            if w.startswith("dram:"):
                ent[0][key] = val
            else:
                ent[0] = {key: val}
                ent[1] = {}
            self.res[w] = ent

    def op(self, eng, fn, reads=(), writes=()):
        fn = _eager(fn)
        waits = self._deps(eng, reads, writes)
        self.cnt[eng] += 1
        val = self.cnt[eng]
        sem = self.sems[eng]
        sems = self.sems

        def emit(e, waits=waits, fn=fn, sem=sem):
            for k, v in waits:
                e.wait_ge(sems[k], v)
            fn(e).then_inc(sem, 1)

        self.streams[eng].append(emit)
        self._mark(eng, val, reads, writes)
        self.n_ops += 1

    def dma(self, queue, out, in_, reads=(), writes=(), key=None, **kw):
        assert key is not None
        sk = self._dma_sem(key)
        waits = self._deps(queue, reads, writes, exclude=sk)
        self.cnt[sk] += 16
        val = self.cnt[sk]
        sem = self.sems[sk]
        sems = self.sems

        def emit(e, waits=waits, sem=sem):
            for k, v in waits:
                e.wait_ge(sems[k], v)
            e.dma_start(out=out, in_=in_, **kw).then_inc(sem, 16)

        self.streams[queue].append(emit)
        self._mark(sk, val, reads, writes)
        self.n_ops += 1

    def custom(self, queue, fn, reads=(), writes=(), key=None, raw=False):
        if not raw:
            fn = _eager(fn)
        sk = self._dma_sem(key)
        waits = self._deps(queue, reads, writes, exclude=sk)
        self.cnt[sk] += 16
        val = self.cnt[sk]
        sem = self.sems[sk]
        sems = self.sems

        def emit(e, waits=waits, sem=sem):
            for k, v in waits:
                e.wait_ge(sems[k], v)
            fn(e).then_inc(sem, 16)

        self.streams[queue].append(emit)
        self._mark(sk, val, reads, writes)
        self.n_ops += 1

    def barrier(self):
        snap = dict(self.cnt)
        sems = self.sems
        for eng in self.streams:
            waits = []
            for k, v in snap.items():
                if v <= 0 or self.seen[eng].get(k, 0) >= v:
                    continue
                if k == eng and eng == "pe":
                    continue
                self.seen[eng][k] = v
                waits.append((k, v))

            def emit(e, waits=waits):
                for k, v in waits:
                    e.wait_ge(sems[k], v)

            self.streams[eng].append(emit)

    def wait_all(self, eng, resources):
        waits = self._deps(eng, resources, ())
        sems = self.sems

        def emit(e, waits=waits):
            for k, v in waits:
                e.wait_ge(sems[k], v)

        self.streams[eng].append(emit)

    def emit(self):
        nc = self.nc
        with nc.Block() as block:
            @block.tensor
            def _(e):
                for f in self.streams["pe"]:
                    f(e)

            @block.scalar
            def _(e):
                for f in self.streams["act"]:
                    f(e)

            @block.vector
            def _(e):
                for f in self.streams["dve"]:
                    f(e)

            @block.gpsimd
            def _(e):
                for f in self.streams["pool"]:
                    f(e)

            @block.sync
            def _(e):
                for f in self.streams["sp"]:
                    f(e)

import math
from concourse.bass_utils import run_bass_kernel_spmd

T = 4096
D = 2048
TB = 512
NB = T // TB
DFF = 5632
EPS = 1e-6
GN_EPS = 64e-5
NV = 528
DBG = {}
TW = 1026
WB = [(0, 342), (342, 342), (684, 342)]
FB = [(j * 512, 512) for j in range(8)]


_RANK = {}


def get_rank(e):
    if "r" not in _RANK:
        _RANK["r"] = (e.partition_id() % 4) * 1024
    return _RANK["r"]


def vec_pc(v):
    v = np.asarray(v, np.float32).reshape(-1)
    return np.ascontiguousarray(v.reshape(-1, 128).T)


def pack_vecs(inp):
    V = np.zeros((128, NV), np.float32)
    V[:, 0:16] = vec_pc(inp["pre_mix_norm"])
    V[:, 16:32] = vec_pc(inp["post_mix_norm"])
    V[:, 32:48] = vec_pc(inp["pre_ffn_norm"])
    V[:, 48:64] = vec_pc(inp["post_ffn_norm"])
    V[:, 64:80] = vec_pc(inp["ple_norm"])
    mu = np.asarray(inp["rw_mu"], np.float32).reshape(-1)
    V[:, 80:104] = vec_pc(mu[0:3072])
    V[:, 104:112] = vec_pc(inp["rw_w0"])
    V[:, 112:120] = vec_pc(inp["rw_a0"])
    V[:, 120:128] = vec_pc(inp["rw_k_k"])
    V[:, 128:136] = vec_pc(inp["rw_k_a"])
    V[:, 136:144] = vec_pc(inp["rw_r_k"])
    V[:, 144:152] = vec_pc(inp["rw_lnx_w"])
    V[:, 152:160] = vec_pc(inp["rw_lnx_b"])
    V[:, 160:164] = vec_pc(inp["mla_q_norm"])
    V[:, 164:168] = vec_pc(inp["mla_kv_norm"])
    V[:, 168:256] = vec_pc(inp["conv_b"])
    cw = np.asarray(inp["conv_w"], np.float32).reshape(3, -1)
    V[:, 256:344] = vec_pc(cw[0])
    V[:, 344:432] = vec_pc(cw[1])
    V[:, 432:520] = vec_pc(cw[2])
    V[0:64, 520] = mu[3072:3136]
    V[0:64, 521] = mu[3136:3200]
    V[0:128, 522] = mu[3200:3328]
    V[0:32, 523] = mu[3328:3360]
    invf = (10000.0 ** (-np.arange(0, 64, 2, dtype=np.float32) / 64)).astype(np.float32)
    V[0:32, 524] = invf
    V[32:64, 524] = invf
    return V


def build_program(debug=()):
    nc = bass.Bass("TRN2", target_bir_lowering=False)
    DBG.clear()
    _RANK.clear()
    for k_ in debug:
        if "=" in k_:
            DBG[k_.split("=")[0]] = int(k_.split("=")[1])
        else:
            DBG[k_] = 1

    def din(name, shape, dt=F32):
        return nc.dram_tensor(name, list(shape), dt, kind="ExternalInput").ap()

    def dscr(name, shape, dt):
        if name in debug:
            return nc.dram_tensor(name, list(shape), dt, kind="ExternalOutput").ap()
        return nc.dram_tensor(name, list(shape), dt).ap()

    x = din("x", [T, D])
    p_in = din("p", [1024, 256])
    pos = din("pos", [1, T], I32)
    pos_w = din("pos_w", [1, TW], I32)
    qk_tab_d = din("qk_tab", [128, 3 * 342])
    d_tab_d = din("d_tab", [128, 32])
    vecs = din("vecs", [128, NV])
    w_in = din("w_in", [D, 8544])
    rw_w2 = din("rw_w2", [64, 1024])
    rw_a2 = din("rw_a2", [64, 1024])
    rw_g2 = din("rw_g2", [160, 1024])
    w_q_up = din("w_q_up", [512, 1536])
    w_kv_up = din("w_kv_up", [512, 2048])
    w_brw = din("w_brw", [1024, D])
    w_bmla = din("w_bmla", [1024, D])
    w_out = din("w_out", [D, D])
    w_up = din("w_up", [D, 2 * DFF])
    w_down = din("w_down", [DFF, D])
    w_ple = din("w_ple", [256, D])
    w_pg = din("w_pg", [D, D])
    out = nc.dram_tensor("out", [1024, D], F32, kind="ExternalOutput").ap()

    hT_full = dscr("hT", [D, T + 2], F32)
    uT_full = dscr("uT", [D, T + 2], BF16)
    yrwT_full = dscr("yrwT", [1024, T + 2], BF16)
    cqnT_full = dscr("cqnT", [512, T + 2], BF16)
    hT = hT_full[:, 2:]
    uT = uT_full[:, 2:]
    yrwT = yrwT_full[:, 2:]
    cqnT = cqnT_full[:, 2:]
    zrwT = dscr("zrwT", [3360, T], F32)
    zmlaT = dscr("zmlaT", [1088, T], F32)
    ckvnT = dscr("ckvnT", [512, T], BF16)
    hwT = dscr("hwT", [D, TW], F32)
    uwT = dscr("uwT", [D, TW], BF16)
    ywT = dscr("ywT", [1024, TW], BF16)
    cqwT = dscr("cqwT", [512, TW], BF16)
    ymlaT = dscr("ymlaT", [1024, TW], BF16)
    pT = dscr("pT", [256, 1024], F32)
    gT = dscr("gT", [4096, TW], BF16)
    mT = dscr("mT", [D, TW], BF16)
    fT = dscr("fT", [D, TW], F32)
    ffT = dscr("ffT", [DFF, TW], BF16)
    qk_tab = qk_tab_d
    d_tab = d_tab_d

    with contextlib.ExitStack() as st0:
        P = Prog(nc, st0)
        V = P.sb("V", [128, NV], F32)
        P.dma("sp", V[:], vecs, writes=["V"], key="V")
        ident = P.sb("ident", [128, 128], F32)
        ident_bf = P.sb("ident_bf", [128, 128], BF16)
        ones_bf = P.sb("ones_bf", [128, 128], BF16)
        bones_bf = P.sb("bones_bf", [128, 128], BF16)
        maskP = P.sb("maskP", [128, 2, 128], F32)
        maskL = P.sb("maskL", [128, 64], F32)
        eye = P.sb("eye", [128, 64], F32)
        rmask = P.sb("rmask", [128, TB], F32)
        omka = P.sb("omka", [128, 8], F32)
        su = P.sb("su", [128, 64], F32)
        ui = P.sb("ui", [128, 64], F32)

        def pool(fn, reads=(), writes=()):
            P.op("pool", fn, reads, writes)

        pool(lambda e: e.memset(ident[:], 1.0), writes=["ident"])
        pool(lambda e: e.affine_select(out=ident[:], in_=ident[:], pattern=[[-1, 128]], compare_op=ALU.is_equal,
                                       fill=0.0, base=0, channel_multiplier=1), reads=["ident"], writes=["ident"])
        pool(lambda e: e.tensor_copy(out=ident_bf[:], in_=ident[:]), reads=["ident"], writes=["ident_bf"])
        pool(lambda e: e.memset(ones_bf[:], 1.0), writes=["ones_bf"])
        pool(lambda e: e.memset(bones_bf[:], 0.0), writes=["bones_bf"])
        pool(lambda e: e.memset(bones_bf[0:64, 0:64], 1.0), reads=["bones_bf"], writes=["bones_bf"])
        pool(lambda e: e.memset(bones_bf[64:128, 64:128], 1.0), reads=["bones_bf"], writes=["bones_bf"])
        for (tl, pat, cm, b0, b1, op_) in ((su, 1, -1, -1, -1, ALU.is_ge), (ui, 1, -1, 0, 0, ALU.is_ge),
                                           (maskL, -1, 1, -1, -1, ALU.is_ge), (eye, -1, 1, 0, 0, ALU.is_equal)):
            pool(lambda e, tl=tl: e.memset(tl[:], 1.0), writes=["msk"])
            for h, bb in ((0, b0), (1, b1)):
                sl = slice(64 * h, 64 * h + 64)
                pool(lambda e, tl=tl, sl=sl, pat=pat, cm=cm, bb=bb, op_=op_: e.affine_select(
                    out=tl[sl, :], in_=tl[sl, :], pattern=[[pat, 64]], compare_op=op_, fill=0.0, base=bb,
                    channel_multiplier=cm), reads=["msk"], writes=["msk"])
        for xx in range(2):
            pool(lambda e, xx=xx: e.tensor_copy(out=maskP[:, xx, 0:64], in_=su[:]), reads=["msk"], writes=["msk"])
            pool(lambda e, xx=xx: e.tensor_copy(out=maskP[:, xx, 64:128], in_=ui[:]), reads=["msk"], writes=["msk"])
        pool(lambda e: e.memset(rmask[:], 1.0), reads=["msk"], writes=["msk"])
        for c in range(8):
            pool(lambda e, c=c: e.memset(rmask[:, c * 64:c * 64 + 1], 0.0), reads=["msk"], writes=["msk"])
        P.op("dve", lambda e: e.tensor_scalar(out=omka[:], in0=V[:, 128:136], scalar1=-1.0, scalar2=1.0, op0=ALU.mult,
                                              op1=ALU.add), reads=["V", "msk"], writes=["omka"])
        CONST = ["V", "msk", "ident", "ident_bf", "ones_bf", "bones_bf", "omka"]

        rr = {"evac": 0}

        class _Stop(Exception):
            pass

        def stop_if(name):
            if name in debug:
                raise _Stop()

        def evac_copy(out_ap, in_ap, reads, writes, scale=None):
            rr["evac"] += 1
            if scale is not None or rr["evac"] % 2 == 0:
                P.op("act", lambda e: e.activation(out=out_ap, in_=in_ap, func=AF.Copy,
                                                   scale=(1.0 if scale is None else scale)), reads, writes)
            else:
                P.op("dve", lambda e: e.tensor_copy(out=out_ap, in_=in_ap), reads, writes)

        def transpose_stage(tag, src, dst, R, C, src_res, dst_res):
            with contextlib.ExitStack() as st:
                prev_stack = P.stack
                P.stack = st
                nb = 2
                tin = [P.sb(f"{tag}_in{i}", [128, C], F32) for i in range(nb)]
                tout = [P.sb(f"{tag}_out{i}", [128, C // 128, 128], F32) for i in range(nb)]
                pst = [P.ps(f"{tag}_ps{i}", [128, 4, 128], F32) for i in range(2)]
                k = 0
                for r in range(R // 128):
                    b = r % nb
                    P.dma("sp", tin[b][:], src[r * 128:(r + 1) * 128, :], reads=[src_res], writes=[f"TR_in{b}"],
                          key=f"TR_in{b}")
                    for c4 in range(0, C // 128, 4):
                        pb = k % 2
                        k += 1
                        n4 = min(4, C // 128 - c4)
                        for i in range(n4):
                            c = c4 + i
                            P.op("pe", lambda e, pb=pb, i=i, c=c, b=b: e.transpose(
                                out=pst[pb][:, i, :], in_=tin[b][:, c * 128:(c + 1) * 128], identity=ident[:]),
                                reads=[f"TR_in{b}", "ident"], writes=[f"TR_ps{pb}"])
                        evac_copy(tout[b][:, c4:c4 + n4, :], pst[pb][:, 0:n4, :], [f"TR_ps{pb}"], [f"TR_out{b}"])
                    P.dma("sp", dst[:, r * 128:(r + 1) * 128].rearrange("(c p) t -> p c t", p=128), tout[b][:],
                          reads=[f"TR_out{b}"], writes=[dst_res], key=f"TR_st{b}")
                P.barrier()
                P.stack = prev_stack

        def rmsnorm_stage(tag, src, F, gcol, src_res, dst=None, dst_res=None, resid=None, resid_res=None, blocks=FB):
            FC = F // 128
            with contextlib.ExitStack() as st:
                prev_stack = P.stack
                P.stack = st
                s = P.sb(f"{tag}_s", [128, FC, TB], F32)
                sq = P.sb(f"{tag}_sq", [128, FC, TB], BF16)
                rstd = P.sb(f"{tag}_rstd", [128, TB], F32)
                ps = P.ps(f"{tag}_ps", [128, TB], F32)
                if resid is None:
                    o = P.sb(f"{tag}_o", [128, FC, TB], BF16)
                else:
                    o = P.sb(f"{tag}_o", [128, FC, TB], F32)
                    hh = P.sb(f"{tag}_h", [128, FC, TB], F32)
                for j, (b0, bs) in enumerate(blocks):
                    cs = slice(b0, b0 + bs)
                    P.dma("sp", s[:, :, 0:bs], src[:, cs].rearrange("(c p) t -> p c t", p=128), reads=[src_res],
                          writes=[f"RN_s"], key=f"RN_s")
                    if resid is not None:
                        P.dma("sp", hh[:, :, 0:bs], resid[:, cs].rearrange("(c p) t -> p c t", p=128), reads=[resid_res],
                              writes=[f"RN_h"], key=f"RN_h")
                    P.op("act", lambda e: e.activation(out=sq[:, :, 0:bs], in_=s[:, :, 0:bs], func=AF.Square), reads=[f"RN_s"],
                         writes=[f"RN_sq"])
                    for c in range(FC):
                        P.op("pe", lambda e, c=c: e.matmul(ps[:, 0:bs], lhsT=ones_bf[:], rhs=sq[:, c, 0:bs], start=(c == 0),
                                                           stop=(c == FC - 1)),
                             reads=[f"RN_sq", "ones_bf"], writes=[f"RN_ps"])
                    P.op("act", lambda e: e.activation(out=rstd[:, 0:bs], in_=ps[:, 0:bs], func=AF.Sqrt, bias=EPS, scale=1.0 / F),
                         reads=[f"RN_ps"], writes=[f"RN_rstd"])
                    P.op("dve", lambda e: e.reciprocal(out=rstd[:, 0:bs], in_=rstd[:, 0:bs]), reads=[f"RN_rstd"],
                         writes=[f"RN_rstd"])
                    for c in range(FC):
                        eng = "pool"
                        P.op("dve", lambda e, c=c: e.scalar_tensor_tensor(
                            out=o[:, c, 0:bs], in0=s[:, c, 0:bs], scalar=V[:, gcol + c:gcol + c + 1], in1=rstd[:, 0:bs],
                            op0=ALU.mult, op1=ALU.mult), reads=[f"RN_s", f"RN_rstd", "V"], writes=[f"RN_o{c}"])
                        if resid is not None:
                            P.op(eng, lambda e, c=c: e.tensor_tensor(out=o[:, c, 0:bs], in0=o[:, c, 0:bs], in1=hh[:, c, 0:bs],
                                                                     op=ALU.add),
                                 reads=[f"RN_o{c}", f"RN_h"], writes=[f"RN_o{c}"])
                    allo = [f"RN_o{c}" for c in range(FC)]
                    if resid is None:
                        P.dma("sp", dst[:, cs].rearrange("(c p) t -> p c t", p=128), o[:, :, 0:bs], reads=allo,
                              writes=[dst_res], key=f"RN_st")
                    else:
                        P.dma("sp", resid[:, cs].rearrange("(c p) t -> p c t", p=128), o[:, :, 0:bs], reads=allo,
                              writes=[resid_res], key=f"RN_st")
                P.barrier()
                P.stack = prev_stack

        def linear_stage(tag, srcs, groups, epi, wbuf_cols, nps=2, blocks=FB):
            with contextlib.ExitStack() as st:
                prev_stack = P.stack
                P.stack = st
                wsb = []
                sbs = []
                for si, (src, K, sdt, sres, W) in enumerate(srcs):
                    KC = (K + 127) // 128
                    wsb.append([P.sb(f"{tag}_w{si}_{i}", [128, KC, wbuf_cols], BF16) for i in range(2)])
                    sbs.append([P.sb(f"{tag}_x{si}_{i}", [128, KC, TB], BF16) for i in range(2)])
                pss = [P.ps(f"{tag}_ps{i}", [128, TB], F32) for i in range(nps)]
                state = {"ps": 0, "xb": 0}
                def grp_layout(grp):
                    offs = []
                    off = 0
                    for (c0, w) in grp:
                        offs.append(off)
                        off += w
                    assert off <= wbuf_cols
                    ranges = []
                    for (c0, w), o_ in zip(grp, offs):
                        if ranges and ranges[-1][0] + ranges[-1][1] == c0 and ranges[-1][2] + ranges[-1][1] == o_:
                            ranges[-1][1] += w
                        else:
                            ranges.append([c0, w, o_])
                    return offs, ranges

                def load_w(gi):
                    wb = gi % 2
                    offs, ranges = grp_layout(groups[gi])
                    for si, (src, K, sdt, sres, W) in enumerate(srcs):
                        KC = (K + 127) // 128
                        for (c0, w, o_) in ranges:
                            for kc in range(KC):
                                kr = min(128, K - kc * 128)
                                P.dma("pool", wsb[si][wb][0:kr, kc, o_:o_ + w], W[kc * 128:kc * 128 + kr, c0:c0 + w],
                                      writes=[f"LN_w{si}_{wb}"], key=f"LN_w{si}_{wb}")

                load_w(0)
                for gi, grp in enumerate(groups):
                    wb = gi % 2
                    offs, ranges = grp_layout(grp)
                    if gi + 1 < len(groups):
                        load_w(gi + 1)
                    for j, (b0, bs) in enumerate(blocks):
                        cs = slice(b0, b0 + bs)
                        xb = state["xb"] % 2
                        state["xb"] += 1
                        for si, (src, K, sdt, sres, W) in enumerate(srcs):
                            KC = (K + 127) // 128
                            q = "sp" if sdt == BF16 else "pool"
                            if K % 128 == 0 and q == "sp":
                                P.dma(q, sbs[si][xb][:, :, 0:bs], src[:, cs].rearrange("(k p) t -> p k t", p=128), reads=[sres],
                                      writes=[f"LN_x{si}_{xb}"], key=f"LN_x{si}_{xb}")
                            else:
                                for kc in range(KC):
                                    kr = min(128, K - kc * 128)
                                    P.dma(q, sbs[si][xb][0:kr, kc, 0:bs], src[kc * 128:kc * 128 + kr, cs], reads=[sres],
                                          writes=[f"LN_x{si}_{xb}"], key=f"LN_x{si}_{xb}")
                        for ci, ((c0, w), o_) in enumerate(zip(grp, offs)):
                            pi = state["ps"] % nps
                            state["ps"] += 1
                            nmm = sum((K + 127) // 128 for (_, K, _, _, _) in srcs)
                            i = 0
                            for si, (src, K, sdt, sres, W) in enumerate(srcs):
                                KC = (K + 127) // 128
                                for kc in range(KC):
                                    kr = min(128, K - kc * 128)
                                    P.op("pe", lambda e, pi=pi, si=si, wb=wb, kc=kc, kr=kr, o_=o_, w=w, xb=xb, i=i, nmm=nmm:
                                         e.matmul(pss[pi][0:w, 0:bs], lhsT=wsb[si][wb][0:kr, kc, o_:o_ + w],
                                                  rhs=sbs[si][xb][0:kr, kc, 0:bs], start=(i == 0), stop=(i == nmm - 1)),
                                         reads=[f"LN_w{si}_{wb}", f"LN_x{si}_{xb}"], writes=[f"LN_ps{pi}"])
                                    i += 1
                            epi(gi, ci, (c0, w), j, pss[pi], f"LN_ps{pi}", (b0, bs))
                P.barrier()
                P.stack = prev_stack

        def store_epi(tag, dst, dst_res, odt, func=AF.Copy, row_of=None, nbuf=3):
            bufs = [P.sb(f"{tag}_eo{i}", [128, TB], odt) for i in range(nbuf)]
            stt = {"i": 0}

            def epi(gi, ci, cw, j, ps, ps_res, blk):
                c0, w = cw
                b0_, bs = blk
                b = stt["i"] % nbuf
                stt["i"] += 1
                r0 = c0 if row_of is None else row_of(c0)
                if func == AF.Copy:
                    evac_copy(bufs[b][0:w, 0:bs], ps[0:w, 0:bs], [ps_res], [f"EO_eo{b}"])
                else:
                    P.op("act", lambda e: e.activation(out=bufs[b][0:w, 0:bs], in_=ps[0:w, 0:bs], func=func), [ps_res],
                         [f"EO_eo{b}"])
                P.dma("act", dst[r0:r0 + w, b0_:b0_ + bs], bufs[b][0:w, 0:bs], reads=[f"EO_eo{b}"],
                      writes=[dst_res], key=f"EO_eo{b}")

            return epi

        def chunks(c0, n, w=128):
            res = []
            c = c0
            while c < c0 + n:
                ww = min(w, c0 + n - c)
                res.append((c, ww))
                c += ww
            return res

        def grouped(ch, n):
            return [ch[i:i + n] for i in range(0, len(ch), n)]

        def main_seq():
            if "skip_s01" in debug:
                rwkv_stage(P, nc, st0, V, zrwT, yrwT, rw_w2, rw_a2, rw_g2, ident_bf, bones_bf, maskP, maskL, eye, rmask, omka,
                           evac_copy)
                stop_if("skip_s01")
            if "skip_s01m" in debug:
                rmsnorm_stage("nq", zmlaT[0:512, :], 512, 160, "dram:zmlaT", dst=cqnT, dst_res="dram:cqnT")
                rmsnorm_stage("nk", zmlaT[512:1024, :], 512, 164, "dram:zmlaT", dst=ckvnT, dst_res="dram:ckvnT")
                mla_stage(P, nc, st0, V, pos, pos_w, qk_tab, d_tab, zmlaT, cqnT_full, ckvnT, w_q_up, w_kv_up, ymlaT, ones_bf,
                          evac_copy)
                stop_if("skip_s01m")
            def win_copy(dst, src_full, rows, res_src, res_dst, key):
                step = 512
                for r0 in range(0, rows, step):
                    def fn(e, r0=r0):
                        off = get_rank(e)
                        return e.dma_start(out=dst[r0:r0 + step, :], in_=src_full[r0:r0 + step, bass.ds(off, TW)])
                    P.custom("pool", fn, reads=[res_src], writes=[res_dst], key=key, raw=True)

            zt_f = P.sb("zpad_f", [128, 16, 2], F32)
            zt_b = P.sb("zpad_b", [128, 16, 2], BF16)
            P.op("pool", lambda e: e.memset(zt_f[:], 0.0), [], ["zpad"])
            P.op("pool", lambda e: e.memset(zt_b[:], 0.0), [], ["zpad"])
            P.dma("sp", hT_full[:, 0:2].rearrange("(c p) t -> p c t", p=128), zt_f[:], reads=["zpad"], writes=["dram:hT"], key="zp")
            P.dma("sp", uT_full[:, 0:2].rearrange("(c p) t -> p c t", p=128), zt_b[:], reads=["zpad"], writes=["dram:uT"], key="zp")
            P.dma("sp", yrwT_full[:, 0:2].rearrange("(c p) t -> p c t", p=128), zt_b[:, 0:8, :], reads=["zpad"],
                  writes=["dram:yrwT"], key="zp")
            P.dma("sp", cqnT_full[:, 0:2].rearrange("(c p) t -> p c t", p=128), zt_b[:, 0:4, :], reads=["zpad"],
                  writes=["dram:cqnT"], key="zp")
            transpose_stage("tx", x, hT, T, D, "dram:x", "dram:hT")
            stop_if("stop0")
            rmsnorm_stage("n1", hT, D, 0, "dram:hT", dst=uT, dst_res="dram:uT")
            stop_if("stop0b")
            with contextlib.ExitStack() as stx:
                P.stack = stx
                e1 = store_epi("l1a", zrwT, "dram:zrwT", F32)
                linear_stage("l1a", [(uT, D, BF16, "dram:uT", w_in)], grouped(chunks(0, 3360), 8), e1, 1024)
                P.barrier()
                P.stack = st0
            with contextlib.ExitStack() as stx:
                P.stack = stx
                e2 = store_epi("l1b", zmlaT, "dram:zmlaT", F32, row_of=lambda c0: c0 - 3360)
                linear_stage("l1b", [(uT, D, BF16, "dram:uT", w_in)], grouped(chunks(3360, 1088), 9), e2, 1152)
                P.barrier()
                P.stack = st0
            stop_if("stop1")
            rwkv_stage(P, nc, st0, V, zrwT, yrwT, rw_w2, rw_a2, rw_g2, ident_bf, bones_bf, maskP, maskL, eye, rmask, omka,
                       evac_copy)
            stop_if("stop2")
            rmsnorm_stage("nq", zmlaT[0:512, :], 512, 160, "dram:zmlaT", dst=cqnT, dst_res="dram:cqnT")
            rmsnorm_stage("nk", zmlaT[512:1024, :], 512, 164, "dram:zmlaT", dst=ckvnT, dst_res="dram:ckvnT")
            win_copy(cqwT, cqnT_full, 512, "dram:cqnT", "dram:cqwT", "wc3")
            P.barrier()
            mla_stage(P, nc, st0, V, pos, pos_w, qk_tab, d_tab, zmlaT, cqwT, ckvnT, w_q_up, w_kv_up, ymlaT, ones_bf,
                      evac_copy)
            stop_if("stop3")
            win_copy(hwT, hT_full, D, "dram:hT", "dram:hwT", "wc0")
            win_copy(uwT, uT_full, D, "dram:uT", "dram:uwT", "wc1")
            win_copy(ywT, yrwT_full, 1024, "dram:yrwT", "dram:ywT", "wc2")
            P.barrier()
            transpose_stage("tp", p_in, pT, 1024, 256, "dram:p", "dram:pT")
            with contextlib.ExitStack() as stx:
                P.stack = stx
                e3 = store_epi("l1c", gT, "dram:gT", BF16, func=AF.Sigmoid, row_of=lambda c0: c0 - 4448)
                linear_stage("l1c", [(uwT, D, BF16, "dram:uwT", w_in)], grouped(chunks(4448, 4096), 8), e3, 1024, blocks=WB)
                P.barrier()
                P.stack = st0
            with contextlib.ExitStack() as stx:
                P.stack = stx
                gA = [P.sb(f"s4_gA{i}", [128, TB], BF16) for i in range(2)]
                gB = [P.sb(f"s4_gB{i}", [128, TB], BF16) for i in range(2)]
                t1 = [P.sb(f"s4_t1{i}", [128, TB], F32) for i in range(2)]
                mo = [P.sb(f"s4_mo{i}", [128, TB], BF16) for i in range(2)]
                t2 = [P.sb(f"s4_t2{i}", [128, TB], F32) for i in range(2)]
                stt = {"i": 0}

                def epiA(gi, ci, cw, j, ps, ps_res, blk):
                    c0, w = cw
                    b0_, bs = blk
                    b = stt["i"] % 2
                    stt["i"] += 1
                    P.dma("sp", gA[b][:, 0:bs], gT[c0:c0 + 128, b0_:b0_ + bs], reads=["dram:gT"], writes=[f"s4_gA{b}"],
                          key=f"s4_gA{b}")
                    P.op("dve", lambda e: e.tensor_tensor(out=t1[b][:, 0:bs], in0=ps[:, 0:bs], in1=gA[b][:, 0:bs], op=ALU.mult),
                         [ps_res, f"s4_gA{b}"], [f"s4_t1{b}"])
                    P.dma("act", fT[c0:c0 + 128, b0_:b0_ + bs], t1[b][:, 0:bs], reads=[f"s4_t1{b}"], writes=["dram:fT"],
                          key=f"s4_t1{b}")

                linear_stage("l4a", [(ywT, 1024, BF16, "dram:ywT", w_brw)], grouped(chunks(0, D), 8), epiA, 1024, blocks=WB)

                def epiB(gi, ci, cw, j, ps, ps_res, blk):
                    c0, w = cw
                    b0_, bs = blk
                    b = stt["i"] % 2
                    stt["i"] += 1
                    P.dma("sp", gB[b][:, 0:bs], gT[2048 + c0:2048 + c0 + 128, b0_:b0_ + bs], reads=["dram:gT"],
                          writes=[f"s4_gB{b}"], key=f"s4_gB{b}")
                    P.dma("sp", t1[b][:, 0:bs], fT[c0:c0 + 128, b0_:b0_ + bs], reads=["dram:fT"], writes=[f"s4_t1{b}"],
                          key=f"s4_t1l{b}")
                    P.op("dve", lambda e: e.tensor_tensor(out=t2[b][:, 0:bs], in0=ps[:, 0:bs], in1=gB[b][:, 0:bs], op=ALU.mult),
                         [ps_res, f"s4_gB{b}"], [f"s4_t2{b}"])
                    P.op("dve", lambda e: e.tensor_tensor(out=mo[b][:, 0:bs], in0=t2[b][:, 0:bs], in1=t1[b][:, 0:bs], op=ALU.add),
                         [f"s4_t2{b}", f"s4_t1{b}"], [f"s4_mo{b}"])
                    P.dma("act", mT[c0:c0 + 128, b0_:b0_ + bs], mo[b][:, 0:bs], reads=[f"s4_mo{b}"], writes=["dram:mT"],
                          key=f"s4_mo{b}")

                linear_stage("l4b", [(ymlaT, 1024, BF16, "dram:ymlaT", w_bmla)], grouped(chunks(0, D), 8), epiB, 1024, blocks=WB)
                P.barrier()
                P.stack = st0
            stop_if("stop4")
            with contextlib.ExitStack() as stx:
                P.stack = stx
                e5 = store_epi("l5", fT, "dram:fT", F32)
                linear_stage("l5", [(mT, D, BF16, "dram:mT", w_out)], grouped(chunks(0, D), 8), e5, 1024, blocks=WB)
                P.barrier()
                P.stack = st0
            rmsnorm_stage("n5", fT, D, 16, "dram:fT", resid=hwT, resid_res="dram:hwT", blocks=WB)
            stop_if("stop5")
            rmsnorm_stage("n6", hwT, D, 32, "dram:hwT", dst=uwT, dst_res="dram:uwT", blocks=WB)
            with contextlib.ExitStack() as stx:
                P.stack = stx
                NCH = 4
                ub = [[P.sb(f"s6_u{h}_{i}", [128, TB + 2], F32) for i in range(NCH)] for h in range(2)]
                cv = [P.sb(f"s6_cv{h}", [128, TB], F32) for h in range(2)]
                tq = P.sb("s6_tq", [128, TB], F32)
                fo = [P.sb(f"s6_fo{i}", [128, TB], BF16) for i in range(2)]
                stt = {"i": 0}

                def epi6(gi, ci, cw, j, ps, ps_res, blk):
                    c0, w = cw
                    b0_, bs = blk
                    half = 0 if ci < NCH else 1
                    cc = ci % NCH
                    u = ub[half][cc]
                    ur = f"s6_u{half}_{cc}"
                    if j == 0:
                        P.op("dve", lambda e: e.memset(u[:, 0:2], 0.0), [ur], [ur])
                    else:
                        P.op("dve", lambda e: e.tensor_copy(out=u[:, 0:2], in_=u[:, bs:bs + 2]), [ur], [ur])
                    evac_copy(u[:, 2:bs + 2], ps[:, 0:bs], [ps_res, ur], [ur])
                    if half == 1:
                        chn = c0 // 128
                        chg = chn - 44
                        for hh, ch in ((0, chg), (1, chn)):
                            uu = ub[hh][cc]
                            uur = f"s6_u{hh}_{cc}"
                            P.op("dve", lambda e, uu=uu, ch=ch, hh=hh: e.tensor_scalar(
                                out=cv[hh][:, 0:bs], in0=uu[:, 0:bs], scalar1=V[:, 256 + ch:257 + ch], scalar2=V[:, 168 + ch:169 + ch],
                                op0=ALU.mult, op1=ALU.add), [uur, "V"], [f"s6_cv{hh}"])
                            P.op("dve", lambda e, uu=uu, ch=ch, hh=hh: e.scalar_tensor_tensor(
                                out=cv[hh][:, 0:bs], in0=uu[:, 1:bs + 1], scalar=V[:, 344 + ch:345 + ch], in1=cv[hh][:, 0:bs],
                                op0=ALU.mult, op1=ALU.add), [uur, "V", f"s6_cv{hh}"], [f"s6_cv{hh}"])
                            P.op("dve", lambda e, uu=uu, ch=ch, hh=hh: e.scalar_tensor_tensor(
                                out=cv[hh][:, 0:bs], in0=uu[:, 2:bs + 2], scalar=V[:, 432 + ch:433 + ch], in1=cv[hh][:, 0:bs],
                                op0=ALU.mult, op1=ALU.add), [uur, "V", f"s6_cv{hh}"], [f"s6_cv{hh}"])
                        P.op("dve", lambda e: e.tensor_tensor(out=tq[:, 0:bs], in0=cv[0][:, 0:bs], in1=cv[0][:, 0:bs], op=ALU.mult),
                             ["s6_cv0"], ["s6_tq"])
                        P.op("dve", lambda e: e.tensor_scalar(out=tq[:, 0:bs], in0=tq[:, 0:bs], scalar1=0.044715, scalar2=1.0,
                                                               op0=ALU.mult, op1=ALU.add), ["s6_tq"], ["s6_tq"])
                        P.op("dve", lambda e: e.tensor_tensor(out=tq[:, 0:bs], in0=tq[:, 0:bs], in1=cv[0][:, 0:bs], op=ALU.mult),
                             ["s6_tq", "s6_cv0"], ["s6_tq"])
                        P.op("act", lambda e: e.activation(out=tq[:, 0:bs], in_=tq[:, 0:bs], func=AF.Sigmoid, scale=1.5957691216),
                             ["s6_tq"], ["s6_tq"])
                        P.op("dve", lambda e: e.tensor_tensor(out=tq[:, 0:bs], in0=tq[:, 0:bs], in1=cv[0][:, 0:bs], op=ALU.mult),
                             ["s6_tq", "s6_cv0"], ["s6_tq"])
                        b = stt["i"] % 2
                        stt["i"] += 1
                        P.op("dve", lambda e: e.tensor_tensor(out=fo[b][:, 0:bs], in0=tq[:, 0:bs], in1=cv[1][:, 0:bs], op=ALU.mult),
                             ["s6_tq", "s6_cv1"], [f"s6_fo{b}"])
                        P.dma("act", ffT[chg * 128:(chg + 1) * 128, b0_:b0_ + bs], fo[b][:, 0:bs], reads=[f"s6_fo{b}"],
                              writes=["dram:ffT"], key=f"s6_fo{b}")

                grps = []
                for g0 in range(0, 44, NCH):
                    grps.append(chunks(g0 * 128, NCH * 128) + chunks(DFF + g0 * 128, NCH * 128))
                linear_stage("l6", [(uwT, D, BF16, "dram:uwT", w_up)], grps, epi6, 2 * NCH * 128, blocks=WB)
                P.barrier()
                P.stack = st0
            with contextlib.ExitStack() as stx:
                P.stack = stx
                e7 = store_epi("l7", fT, "dram:fT", F32)
                linear_stage("l7", [(ffT, DFF, BF16, "dram:ffT", w_down)], grouped(chunks(0, D), 4), e7, 512, blocks=WB)
                P.barrier()
                P.stack = st0
            rmsnorm_stage("n7", fT, D, 48, "dram:fT", resid=hwT, resid_res="dram:hwT", blocks=WB)
            stop_if("stop7")
            PB = [(0, 512), (512, 512)]
            with contextlib.ExitStack() as stx:
                P.stack = stx
                e8 = store_epi("l8a", mT, "dram:mT", BF16, func=AF.Sigmoid)
                linear_stage("l8a", [(hwT[:, 2:TW], D, F32, "dram:hwT", w_pg)], grouped(chunks(0, D), 8), e8, 1024, blocks=PB)
                gl = [P.sb(f"s8_g{i}", [128, TB], BF16) for i in range(2)]
                eo = [P.sb(f"s8_eo{i}", [128, TB], F32) for i in range(2)]
                stt = {"i": 0}

                def epi8(gi, ci, cw, j, ps, ps_res, blk):
                    c0, w = cw
                    b0_, bs = blk
                    b = stt["i"] % 2
                    stt["i"] += 1
                    P.dma("sp", gl[b][:, 0:bs], mT[c0:c0 + 128, b0_:b0_ + bs], reads=["dram:mT"], writes=[f"s8_g{b}"],
                          key=f"s8_g{b}")
                    P.op("dve", lambda e: e.tensor_tensor(out=eo[b][:, 0:bs], in0=ps[:, 0:bs], in1=gl[b][:, 0:bs], op=ALU.mult),
                         [ps_res, f"s8_g{b}"], [f"s8_eo{b}"])
                    P.dma("act", fT[c0:c0 + 128, b0_:b0_ + bs], eo[b][:, 0:bs], reads=[f"s8_eo{b}"], writes=["dram:fT"],
                          key=f"s8_eo{b}")

                linear_stage("l8b", [(pT, 256, F32, "dram:pT", w_ple)], grouped(chunks(0, D), 16), epi8, 2048, blocks=PB)
                P.barrier()
                P.stack = st0
            rmsnorm_stage("n8", fT[:, 0:1024], D, 64, "dram:fT", resid=hwT[:, 2:TW], resid_res="dram:hwT", blocks=PB)
            transpose_stage("to", hwT[:, 2:TW], out, D, 1024, "dram:hwT", "dram:out")

        try:
            main_seq()
        except _Stop:
            P.stack = st0
        P.wait_all("sp", [f"dram:{n}" for n in (["out"] + list(debug)) if not n.startswith("skip") and not n.startswith("stop")])
        P.emit()
    return nc


def rwkv_stage(P, nc, st0, V, zrwT, yrwT, rw_w2, rw_a2, rw_g2, ident_bf, bones_bf, maskP, maskL, eye, rmask, omka,
               evac_copy):
    NEG_E = -math.exp(-0.5)
    with contextlib.ExitStack() as st:
        prev_stack = P.stack
        P.stack = st
        sb = P.sb
        w2b = sb("rk_w2b", [64, 1024], BF16)
        a2b = sb("rk_a2b", [64, 1024], BF16)
        g2b = sb("rk_g2b", [128, 2, 1024], BF16)
        P.dma("pool", w2b[:], rw_w2, writes=["rk_w"], key="rk_w")
        P.dma("pool", a2b[:], rw_a2, writes=["rk_w"], key="rk_w")
        P.dma("pool", g2b[:, 0, :], rw_g2[0:128, :], writes=["rk_w"], key="rk_w")
        P.dma("pool", g2b[0:32, 1, :], rw_g2[128:160, :], writes=["rk_w"], key="rk_w")
        twd = sb("rk_twd", [64, T], BF16)
        ads = sb("rk_ads", [64, T], BF16)
        sg1 = sb("rk_sg1", [128, T], BF16)
        sg2 = sb("rk_sg2", [32, T], BF16)
        zin = sb("rk_zin", [128, TB + 1], F32)
        dd = sb("rk_dd", [128, TB], F32)
        for (row0, nr, mcol, dst, func) in ((3072, 64, 520, twd, AF.Tanh), (3136, 64, 521, ads, AF.Copy),
                                            (3200, 128, 522, sg1, AF.Sigmoid), (3328, 32, 523, sg2, AF.Sigmoid)):
            for j in range(NB):
                if j == 0:
                    P.op("dve", lambda e: e.memset(zin[:, 0:1], 0.0), [], ["rk_zin"])
                    P.dma("sp", zin[0:nr, 1:TB + 1], zrwT[row0:row0 + nr, 0:TB], reads=["dram:zrwT"], writes=["rk_zin"],
                          key="rk_zin")
                else:
                    P.dma("sp", zin[0:nr, :], zrwT[row0:row0 + nr, j * TB - 1:(j + 1) * TB], reads=["dram:zrwT"],
                          writes=["rk_zin"], key="rk_zin")
                P.op("dve", lambda e, nr=nr: e.tensor_tensor(out=dd[0:nr, :], in0=zin[0:nr, 0:TB], in1=zin[0:nr, 1:TB + 1],
                                                             op=ALU.subtract), ["rk_zin"], ["rk_dd"])
                P.op("dve", lambda e, nr=nr, mcol=mcol: e.scalar_tensor_tensor(
                    out=dd[0:nr, :], in0=dd[0:nr, :], scalar=V[0:nr, mcol:mcol + 1], in1=zin[0:nr, 1:TB + 1],
                    op0=ALU.mult, op1=ALU.add), ["rk_dd", "rk_zin", "V"], ["rk_dd"])
                P.op("act", lambda e, nr=nr, dst=dst, func=func, j=j: e.activation(
                    out=dst[0:nr, j * TB:(j + 1) * TB], in_=dd[0:nr, :], func=func), ["rk_dd"], ["rk_lin"])
        zt = {n: sb(f"rk_z{n}", [128, TB + 1], F32) for n in "rkv"}
        xs = {n: sb(f"rk_s{n}", [128, TB], F32) for n in "rkv"}
        tA = sb("rk_tA", [128, TB], F32)
        tB = sb("rk_tB", [128, TB], F32)
        tC = sb("rk_tC", [128, TB], F32)
        av = sb("rk_a", [128, TB], F32)
        kkn = sb("rk_kkn", [128, TB], F32)
        k2 = sb("rk_k2", [128, TB], F32)
        logw = sb("rk_logw", [128, TB], F32)
        cum = sb("rk_cum", [128, TB], F32)
        ginv = sb("rk_ginv", [128, TB], F32)
        gprev = sb("rk_gprev", [128, TB], F32)
        tbf = sb("rk_tbf", [128, TB], BF16)
        vbf = sb("rk_vbf", [128, TB], BF16)
        BK = sb("rk_BK", [128, 8, 2, 64], BF16)
        gam2 = [sb(f"rk_gam{i}", [128, TB], F32) for i in range(2)]
        gv2 = [sb(f"rk_g{i}", [128, TB], F32) for i in range(2)]
        bonus2 = [sb(f"rk_bonus{i}", [128, TB], F32) for i in range(2)]
        AR2 = [sb(f"rk_AR{i}", [128, 8, 2, 64], BF16) for i in range(2)]
        PM2 = [sb(f"rk_PM{i}", [128, 8, 2, 128], BF16) for i in range(2)]
        TM2 = [sb(f"rk_TM{i}", [128, 8, 3, 64], BF16) for i in range(2)]
        MT2 = [sb(f"rk_MT{i}", [128, 8, 64], BF16) for i in range(2)]
        NS8 = sb("rk_NS", [128, 8, 2, 64], BF16)
        L8 = sb("rk_L", [128, 8, 64], BF16)
        maskP8 = sb("rk_maskP8", [128, 8, 2, 128], F32)
        maskL8 = sb("rk_maskL8", [128, 8, 64], F32)
        eye8 = sb("rk_eye8", [128, 8, 64], F32)
        for c in range(8):
            P.op("pool", lambda e, c=c: e.tensor_copy(out=maskP8[:, c, :, :], in_=maskP[:]), ["msk"], ["rk_m8"])
            P.op("pool", lambda e, c=c: e.tensor_copy(out=maskL8[:, c, :], in_=maskL[:]), ["msk"], ["rk_m8"])
            P.op("pool", lambda e, c=c: e.tensor_copy(out=eye8[:, c, :], in_=eye[:]), ["msk"], ["rk_m8"])
        Hf = sb("rk_Hf", [128, 64], F32)
        HG = sb("rk_HG", [128, 64], F32)
        Hb = sb("rk_Hb", [128, 64], BF16)
        Zs = sb("rk_Zs", [128, 64], BF16)
        Us = sb("rk_Us", [128, 64], BF16)
        Yt = sb("rk_Y", [128, TB], F32)
        yo = sb("rk_yo", [128, TB], BF16)
        ps_a = P.ps("rk_psa", [128, TB], F32)
        ps_b = ps_a
        W = P.ps("rk_Wps", [128, 2048], F32)
        W5 = P.ps("rk_W5ps", [128, 512], F32)
        small2 = P.ps("rk_small2", [128, 512], F32)
        yps = P.ps("rk_yps", [128, TB], F32)
        trpv = W[:, 0:1536].rearrange("p (c a t) -> p c a t", c=8, a=3)
        ppv = W[:, 0:2048].rearrange("p (c a t) -> p c a t", c=8, a=2)
        ps1v = W[:, 0:1024].rearrange("p (c t) -> p c t", c=8)
        ps2v = W[:, 1024:1536].rearrange("p (c t) -> p c t", c=8)
        lpv = W5[:, 0:512].rearrange("p (c t) -> p c t", c=8)
        sq = small2[:, 0:192].rearrange("p (a t) -> p a t", a=3)

        def dve(fn, r, w):
            P.op("dve", fn, r, w)

        def act(fn, r, w):
            P.op("act", fn, r, w)

        def pe(fn, r, w):
            P.op("pe", fn, r, w)

        def c3(ap):
            return ap.rearrange("p (c t) -> p c t", t=64)

        nblk = 0
        for pr in range(DBG.get("rkp", 8)):
            pc = slice(pr * 128, (pr + 1) * 128)
            vcol = lambda base: V[:, base + pr:base + pr + 1]
            dve(lambda e: e.memset(Hf[:], 0.0), [], ["rk_Hf"])
            dve(lambda e: e.memset(Hb[:], 0.0), [], ["rk_Hb"])
            for j in range(DBG.get("rkb", NB)):
                par = nblk % 2
                nblk += 1
                gam, gv, bonus, AR, PM, TM, MTs = gam2[par], gv2[par], bonus2[par], AR2[par], PM2[par], TM2[par], MT2[par]
                rGAM, rG, rBON, rAR, rPM, rTM, rMT = (f"rk_gam{par}", f"rk_g{par}", f"rk_bonus{par}", f"rk_AR{par}",
                                                      f"rk_PM{par}", f"rk_TM{par}", f"rk_MT{par}")
                for gi, n in enumerate("rkv"):
                    row0 = gi * 1024 + pr * 128
                    if j == 0:
                        dve(lambda e, n=n: e.memset(zt[n][:, 0:1], 0.0), [], [f"rk_z{n}"])
                        P.dma("sp", zt[n][:, 1:TB + 1], zrwT[row0:row0 + 128, 0:TB], reads=["dram:zrwT"],
                              writes=[f"rk_z{n}"], key=f"rk_z{n}")
                    else:
                        P.dma("sp", zt[n][:], zrwT[row0:row0 + 128, j * TB - 1:(j + 1) * TB], reads=["dram:zrwT"],
                              writes=[f"rk_z{n}"], key=f"rk_z{n}")
                    mcol = 80 + gi * 8 + pr
                    dve(lambda e, n=n: e.tensor_tensor(out=xs[n][:], in0=zt[n][:, 0:TB], in1=zt[n][:, 1:TB + 1],
                                                       op=ALU.subtract), [f"rk_z{n}"], [f"rk_s{n}"])
                    dve(lambda e, n=n, mcol=mcol: e.scalar_tensor_tensor(
                        out=xs[n][:], in0=xs[n][:], scalar=V[:, mcol:mcol + 1], in1=zt[n][:, 1:TB + 1], op0=ALU.mult,
                        op1=ALU.add), [f"rk_s{n}", f"rk_z{n}", "V"], [f"rk_s{n}"])
                js = slice(j * TB, (j + 1) * TB)
                pe(lambda e: e.matmul(ps_a[:], lhsT=w2b[:, pc], rhs=twd[:, js], start=True, stop=True),
                   ["rk_w", "rk_lin"], ["rk_psa"])
                act(lambda e: e.activation(out=logw[:], in_=ps_a[:], func=AF.Sigmoid, bias=vcol(104)), ["rk_psa", "V"],
                    ["rk_logw"])
                dve(lambda e: e.tensor_scalar(out=logw[:], in0=logw[:], scalar1=NEG_E, scalar2=None, op0=ALU.mult),
                    ["rk_logw"], ["rk_logw"])
                pe(lambda e: e.matmul(ps_b[:], lhsT=a2b[:, pc], rhs=ads[:, js], start=True, stop=True),
                   ["rk_w", "rk_lin"], ["rk_psa"])
                act(lambda e: e.activation(out=av[:], in_=ps_b[:], func=AF.Sigmoid, bias=vcol(112)), ["rk_psa", "V"],
                    ["rk_a"])
                pe(lambda e: e.matmul(ps_a[:], lhsT=g2b[:, 0, pc], rhs=sg1[:, js], start=True, stop=False),
                   ["rk_w", "rk_lin"], ["rk_psa"])
                pe(lambda e: e.matmul(ps_a[:], lhsT=g2b[0:32, 1, pc], rhs=sg2[:, js], start=False, stop=True),
                   ["rk_w", "rk_lin"], ["rk_psa"])
                act(lambda e: e.activation(out=gv[:], in_=ps_a[:], func=AF.Copy), ["rk_psa"], [rG])
                dve(lambda e: e.tensor_scalar(out=tA[:], in0=xs["k"][:], scalar1=vcol(120), scalar2=None, op0=ALU.mult),
                    ["rk_sk", "V"], ["rk_tA"])
                dve(lambda e: e.tensor_tensor(out=tbf[:], in0=tA[:], in1=tA[:], op=ALU.mult), ["rk_tA"], ["rk_tbf"])
                pe(lambda e: e.matmul(ps_b[:], lhsT=bones_bf[:], rhs=tbf[:], start=True, stop=True),
                   ["bones_bf", "rk_tbf"], ["rk_psa"])
                act(lambda e: e.activation(out=tB[:], in_=ps_b[:], func=AF.Sqrt), ["rk_psa"], ["rk_tB"])
                dve(lambda e: e.tensor_scalar(out=tB[:], in0=tB[:], scalar1=1e-12, scalar2=None, op0=ALU.max),
                    ["rk_tB"], ["rk_tB"])
                dve(lambda e: e.reciprocal(out=tB[:], in_=tB[:]), ["rk_tB"], ["rk_tB"])
                dve(lambda e: e.tensor_tensor(out=kkn[:], in0=tA[:], in1=tB[:], op=ALU.mult), ["rk_tA", "rk_tB"],
                    ["rk_kkn"])
                dve(lambda e: e.tensor_scalar(out=tA[:], in0=av[:], scalar1=vcol(128), scalar2=omka[:, pr:pr + 1],
                                              op0=ALU.mult, op1=ALU.add), ["rk_a", "V", "omka"], ["rk_tA"])
                dve(lambda e: e.tensor_tensor(out=k2[:], in0=xs["k"][:], in1=tA[:], op=ALU.mult), ["rk_sk", "rk_tA"],
                    ["rk_k2"])
                dve(lambda e: e.tensor_tensor_scan(out=cum[:], data0=rmask[:], data1=logw[:], initial=0.0, op0=ALU.mult,
                                                   op1=ALU.add), ["msk", "rk_logw"], ["rk_cum"])
                act(lambda e: e.activation(out=gam[:], in_=cum[:], func=AF.Exp), ["rk_cum"], [rGAM])
                act(lambda e: e.activation(out=ginv[:], in_=cum[:], func=AF.Exp, scale=-1.0), ["rk_cum"], ["rk_ginv"])
                dve(lambda e: e.tensor_tensor(out=tB[:], in0=cum[:], in1=logw[:], op=ALU.subtract),
                    ["rk_cum", "rk_logw"], ["rk_tB"])
                act(lambda e: e.activation(out=gprev[:], in_=tB[:], func=AF.Exp), ["rk_tB"], ["rk_gprev"])
                dve(lambda e: e.scalar_tensor_tensor(out=AR[:, :, 0, :], in0=c3(kkn[:]), scalar=-1.0, in1=c3(gprev[:]),
                                                     op0=ALU.mult, op1=ALU.mult), ["rk_kkn", "rk_gprev"], [rAR])
                dve(lambda e: e.tensor_tensor(out=AR[:, :, 1, :], in0=c3(xs["r"][:]), in1=c3(gam[:]), op=ALU.mult),
                    ["rk_sr", rGAM, rAR], [rAR])
                dve(lambda e: e.tensor_tensor(out=tA[:], in0=kkn[:], in1=av[:], op=ALU.mult), ["rk_kkn", "rk_a"],
                    ["rk_tA"])
                dve(lambda e: e.tensor_tensor(out=BK[:, :, 0, :], in0=c3(tA[:]), in1=c3(ginv[:]), op=ALU.mult),
                    ["rk_tA", "rk_ginv"], ["rk_BK"])
                dve(lambda e: e.tensor_tensor(out=BK[:, :, 1, :], in0=c3(k2[:]), in1=c3(ginv[:]), op=ALU.mult),
                    ["rk_k2", "rk_ginv", "rk_BK"], ["rk_BK"])
                act(lambda e: e.activation(out=vbf[:], in_=xs["v"][:], func=AF.Copy), ["rk_sv"], ["rk_vbf"])
                dve(lambda e: e.tensor_tensor(out=tC[:], in0=xs["r"][:], in1=k2[:], op=ALU.mult), ["rk_sr", "rk_k2"],
                    ["rk_tC"])
                dve(lambda e: e.tensor_scalar(out=tbf[:], in0=tC[:], scalar1=vcol(136), scalar2=None, op0=ALU.mult),
                    ["rk_tC", "V", "rk_tbf"], ["rk_tbf"])
                pe(lambda e: e.matmul(ps_b[:], lhsT=bones_bf[:], rhs=tbf[:], start=True, stop=True),
                   ["bones_bf", "rk_tbf"], ["rk_psa"])
                dve(lambda e: e.tensor_tensor(out=bonus[:], in0=ps_b[:], in1=xs["v"][:], op=ALU.mult),
                    ["rk_psa", "rk_sv"], [rBON])
                for c in range(8):
                    cc = slice(c * 64, (c + 1) * 64)
                    for h in range(2):
                        sl = slice(64 * h, 64 * h + 64)
                        for ti in range(3):
                            in_ap = BK[sl, c, ti, :] if ti < 2 else vbf[sl, cc]
                            pe(lambda e, sl=sl, ti=ti, in_ap=in_ap, c=c: e.matmul(trpv[sl, c, ti, :], lhsT=in_ap,
                                                                                 rhs=ident_bf[sl, sl], start=True, stop=True),
                               ["rk_BK", "rk_vbf", "ident_bf"], ["rk_Wps"])
                act(lambda e: e.activation(out=TM[:], in_=trpv, func=AF.Copy), ["rk_Wps"], [rTM])
                for c in range(8):
                    for h in range(2):
                        sl = slice(64 * h, 64 * h + 64)
                        arv = AR[sl, c, :, :].rearrange("p a t -> p (a t)")
                        pe(lambda e, sl=sl, c=c, arv=arv: e.matmul(ppv[sl, c, 0, :], lhsT=BK[sl, c, 0, :], rhs=arv, start=True,
                                                                   stop=True), ["rk_BK", rAR], ["rk_Wps"])
                        pe(lambda e, sl=sl, c=c, arv=arv: e.matmul(ppv[sl, c, 1, :], lhsT=BK[sl, c, 1, :], rhs=arv, start=True,
                                                                   stop=True), ["rk_BK", rAR], ["rk_Wps"])
                        pe(lambda e, sl=sl, c=c: e.matmul(lpv[sl, c, :], lhsT=AR[sl, c, 0, :], rhs=BK[sl, c, 0, :], start=True,
                                                          stop=True), ["rk_BK", rAR], ["rk_W5ps"])
                dve(lambda e: e.tensor_tensor(out=PM[:], in0=ppv, in1=maskP8[:], op=ALU.mult), ["rk_Wps", "rk_m8"], [rPM])
                dve(lambda e: e.tensor_tensor(out=L8[:], in0=lpv, in1=maskL8[:], op=ALU.mult), ["rk_W5ps", "rk_m8"], ["rk_L"])
                act(lambda e: e.activation(out=NS8[:, :, 0, :], in_=PM[:, :, 0, 0:64], func=AF.Copy), [rPM], ["rk_NS"])
                P.op("pool", lambda e: e.tensor_copy(out=NS8[:, :, 1, :], in_=eye8[:]), ["rk_m8", "rk_NS"], ["rk_NS"])
                for step in range(6):
                    last = step == 5
                    for c in range(8):
                        for h in range(2):
                            sl = slice(64 * h, 64 * h + 64)
                            if not last:
                                nsv = NS8[sl, c, :, :].rearrange("p a t -> p (a t)")
                                pe(lambda e, sl=sl, nsv=nsv, c=c: e.matmul(ps1v[sl, c, :], lhsT=L8[sl, c, :], rhs=nsv, start=True,
                                                                           stop=True), ["rk_L", "rk_NS"], ["rk_Wps"])
                                pe(lambda e, sl=sl, c=c: e.matmul(ps2v[sl, c, :], lhsT=NS8[sl, c, 0, :], rhs=L8[sl, c, :],
                                                                  start=True, stop=True), ["rk_L", "rk_NS"], ["rk_Wps"])
                            else:
                                pe(lambda e, sl=sl, c=c: e.matmul(ps1v[sl, c, 64:128], lhsT=L8[sl, c, :], rhs=NS8[sl, c, 1, :],
                                                                  start=True, stop=True), ["rk_L", "rk_NS"], ["rk_Wps"])
                    if not last:
                        dve(lambda e: e.tensor_copy(out=NS8[:, :, 0, :], in_=ps1v[:, :, 0:64]), ["rk_Wps", "rk_NS"], ["rk_NS"])
                        dve(lambda e: e.tensor_tensor(out=NS8[:, :, 1, :], in0=NS8[:, :, 1, :], in1=ps1v[:, :, 64:128],
                                                      op=ALU.add), ["rk_Wps", "rk_NS"], ["rk_NS"])
                        dve(lambda e: e.tensor_copy(out=L8[:], in_=ps2v), ["rk_Wps", "rk_L"], ["rk_L"])
                    else:
                        dve(lambda e: e.tensor_tensor(out=MTs[:], in0=NS8[:, :, 1, :], in1=ps1v[:, :, 64:128], op=ALU.add),
                            ["rk_Wps", "rk_NS"], [rMT])
                for c in range(8):
                    cc = slice(c * 64, (c + 1) * 64)
                    gcol = gam[:, c * 64 + 63:c * 64 + 64]
                    for h in range(2):
                        sl = slice(64 * h, 64 * h + 64)
                        pe(lambda e, sl=sl, c=c: e.matmul(sq[sl, 0, :], lhsT=AR[sl, c, 0, :], rhs=Hb[sl, :], start=True,
                                                          stop=False), [rAR, "rk_Hb"], ["rk_small2"])
                        pe(lambda e, sl=sl, c=c: e.matmul(sq[sl, 0, :], lhsT=PM[sl, c, 1, 0:64], rhs=TM[sl, c, 2, :],
                                                          start=False, stop=True), [rPM, rTM], ["rk_small2"])
                    act(lambda e: e.activation(out=Zs[:], in_=sq[:, 0, :], func=AF.Copy), ["rk_small2"], ["rk_Zs"])
                    for h in range(2):
                        sl = slice(64 * h, 64 * h + 64)
                        pe(lambda e, sl=sl, c=c: e.matmul(sq[sl, 1, :], lhsT=MTs[sl, c, :], rhs=Zs[sl, :], start=True,
                                                          stop=True), [rMT, "rk_Zs"], ["rk_small2"])
                    act(lambda e: e.activation(out=Us[:], in_=sq[:, 1, :], func=AF.Copy), ["rk_small2"], ["rk_Us"])
                    for h in range(2):
                        sl = slice(64 * h, 64 * h + 64)
                        pe(lambda e, sl=sl, c=c: e.matmul(sq[sl, 2, :], lhsT=TM[sl, c, 0, :], rhs=Us[sl, :], start=True,
                                                          stop=False), [rTM, "rk_Us"], ["rk_small2"])
                        pe(lambda e, sl=sl, c=c: e.matmul(sq[sl, 2, :], lhsT=TM[sl, c, 1, :], rhs=TM[sl, c, 2, :],
                                                          start=False, stop=True), [rTM], ["rk_small2"])
                        pe(lambda e, sl=sl, c=c, cc=cc: e.matmul(yps[sl, cc], lhsT=Hb[sl, :], rhs=AR[sl, c, 1, :], start=True,
                                                                 stop=False), [rAR, "rk_Hb"], ["rk_yps"])
                        pe(lambda e, sl=sl, c=c, cc=cc: e.matmul(yps[sl, cc], lhsT=Us[sl, :], rhs=PM[sl, c, 0, 64:128],
                                                                 start=False, stop=False), ["rk_Us", rPM], ["rk_yps"])
                        pe(lambda e, sl=sl, c=c, cc=cc: e.matmul(yps[sl, cc], lhsT=TM[sl, c, 2, :], rhs=PM[sl, c, 1, 64:128],
                                                                 start=False, stop=True), [rTM, rPM], ["rk_yps"])
                    P.op("pool", lambda e, gcol=gcol: e.tensor_scalar(out=HG[:], in0=Hf[:], scalar1=gcol, scalar2=None,
                                                                       op0=ALU.mult), ["rk_Hf", rGAM], ["rk_HG"])
                    dve(lambda e, gcol=gcol: e.scalar_tensor_tensor(out=Hf[:], in0=sq[:, 2, :], scalar=gcol, in1=HG[:],
                                                                    op0=ALU.mult, op1=ALU.add),
                        ["rk_small2", "rk_HG", rGAM, "rk_Hf"], ["rk_Hf"])
                    act(lambda e: e.activation(out=Hb[:], in_=Hf[:], func=AF.Copy), ["rk_Hf", "rk_Hb"], ["rk_Hb"])
                act(lambda e: e.activation(out=Yt[:], in_=yps[:], func=AF.Copy), ["rk_yps"], ["rk_Y"])
                dve(lambda e: e.tensor_copy(out=tbf[:], in_=Yt[:]), ["rk_Y", "rk_tbf"], ["rk_tbf"])
                pe(lambda e: e.matmul(ps_a[:], lhsT=bones_bf[:], rhs=tbf[:], start=True, stop=True), ["bones_bf", "rk_tbf"],
                   ["rk_psa"])
                dve(lambda e: e.scalar_tensor_tensor(out=tA[:], in0=ps_a[:], scalar=-1.0 / 64, in1=Yt[:], op0=ALU.mult,
                                                     op1=ALU.add), ["rk_psa", "rk_Y"], ["rk_tA"])
                dve(lambda e: e.tensor_tensor(out=tbf[:], in0=tA[:], in1=tA[:], op=ALU.mult), ["rk_tA", "rk_tbf"],
                    ["rk_tbf"])
                pe(lambda e: e.matmul(ps_a[:], lhsT=bones_bf[:], rhs=tbf[:], start=True, stop=True), ["bones_bf", "rk_tbf"],
                   ["rk_psa"])
                act(lambda e: e.activation(out=tB[:], in_=ps_a[:], func=AF.Sqrt, bias=GN_EPS, scale=1.0 / 64),
                    ["rk_psa"], ["rk_tB"])
                dve(lambda e: e.reciprocal(out=tB[:], in_=tB[:]), ["rk_tB"], ["rk_tB"])
                dve(lambda e: e.tensor_tensor(out=tA[:], in0=tA[:], in1=tB[:], op=ALU.mult), ["rk_tA", "rk_tB"], ["rk_tA"])
                dve(lambda e: e.tensor_scalar(out=tA[:], in0=tA[:], scalar1=vcol(144), scalar2=vcol(152), op0=ALU.mult,
                                              op1=ALU.add), ["rk_tA", "V"], ["rk_tA"])
                dve(lambda e: e.tensor_tensor(out=tA[:], in0=tA[:], in1=bonus[:], op=ALU.add), ["rk_tA", rBON],
                    ["rk_tA"])
                dve(lambda e: e.tensor_tensor(out=yo[:], in0=tA[:], in1=gv[:], op=ALU.mult), ["rk_tA", rG], ["rk_yo"])
                P.dma("act", yrwT[pc, js], yo[:], reads=["rk_yo"], writes=["dram:yrwT"], key="rk_yo")
        P.barrier()
        P.stack = prev_stack


def mla_stage(P, nc, st0, V, pos, pos_w, qk_tab, d_tab, zmlaT, cqnT_full, ckvnT, w_q_up, w_kv_up, ymlaT, ones_bf, evac_copy):
    SCALE = 192 ** -0.5
    TWO_PI = 2.0 * math.pi
    QB = 342
    with contextlib.ExitStack() as st:
        prev_stack = P.stack
        P.stack = st
        sb = P.sb
        wq = sb("ml_wq", [128, 4, 1536], BF16)
        wkv = sb("ml_wkv", [128, 4, 2048], BF16)
        for kc in range(4):
            P.dma("pool", wq[:, kc, :], w_q_up[kc * 128:(kc + 1) * 128, :], writes=["ml_w"], key="ml_w")
            P.dma("pool", wkv[:, kc, :], w_kv_up[kc * 128:(kc + 1) * 128, :], writes=["ml_w"], key="ml_w")
        qk = sb("ml_qk", [128, 3, QB], F32)
        dt = sb("ml_dt", [128, 32], F32)
        P.dma("sp", qk[:], qk_tab.rearrange("p (a t) -> p a t", a=3), writes=["ml_tab"], key="ml_tab")
        P.dma("sp", dt[:], d_tab, writes=["ml_tab"], key="ml_tab")
        cs = sb("ml_cs", [64, T], F32)
        sn = sb("ml_sn", [64, T], F32)
        csw = sb("ml_csw", [64, TW], F32)
        snw = sb("ml_snw", [64, TW], F32)
        with contextlib.ExitStack() as st2:
            P.stack = st2
            posi = P.sb("ml_posi", [64, T], I32)
            ang = P.sb("ml_ang", [64, T], F32)
            tt = P.sb("ml_tt", [64, T], F32)
            kf = P.sb("ml_kf", [64, T], F32)
            for (psrc, n, cdst, sdst) in ((pos, T, cs, sn), (pos_w, TW, csw, snw)):
                P.dma("sp", posi[:, 0:n], psrc[0:1, :].to_broadcast([64, n]), writes=["ml_posi"], key="ml_posi")
                P.op("dve", lambda e: e.tensor_copy(out=ang[:, 0:n], in_=posi[:, 0:n]), ["ml_posi"], ["ml_ang"])
                P.op("dve", lambda e: e.tensor_scalar(out=ang[:, 0:n], in0=ang[:, 0:n], scalar1=V[0:64, 524:525], scalar2=None,
                                                      op0=ALU.mult), ["ml_ang", "V"], ["ml_ang"])
                for (dst, shift) in ((sdst, 0.5), (cdst, 0.75)):
                    P.op("dve", lambda e, shift=shift: e.tensor_scalar(out=tt[:, 0:n], in0=ang[:, 0:n], scalar1=1.0 / TWO_PI,
                                                                       scalar2=shift, op0=ALU.mult, op1=ALU.add),
                         ["ml_ang", "ml_tt"], ["ml_tt"])
                    P.op("dve", lambda e: e.tensor_copy(out=posi[:, 0:n], in_=tt[:, 0:n]), ["ml_tt", "ml_posi", "ml_ang"],
                         ["ml_posi"])
                    P.op("dve", lambda e: e.tensor_copy(out=kf[:, 0:n], in_=posi[:, 0:n]), ["ml_posi"], ["ml_kf"])
                    P.op("dve", lambda e: e.tensor_tensor(out=tt[:, 0:n], in0=tt[:, 0:n], in1=kf[:, 0:n], op=ALU.subtract),
                         ["ml_tt", "ml_kf"], ["ml_tt"])
                    P.op("dve", lambda e: e.tensor_scalar(out=kf[:, 0:n], in0=tt[:, 0:n], scalar1=0.0, scalar2=None,
                                                          op0=ALU.is_lt), ["ml_tt", "ml_kf"], ["ml_kf"])
                    P.op("dve", lambda e: e.scalar_tensor_tensor(out=tt[:, 0:n], in0=kf[:, 0:n], scalar=-0.5, in1=tt[:, 0:n],
                                                                 op0=ALU.add, op1=ALU.add), ["ml_tt", "ml_kf"], ["ml_tt"])
                    P.op("act", lambda e, dst=dst: e.activation(out=dst[:, 0:n], in_=tt[:, 0:n], func=AF.Sin, scale=TWO_PI),
                         ["ml_tt"], ["ml_rope"])
            P.barrier()
            P.stack = st
        kr = sb("ml_kr", [64, T], BF16)
        kn = sb("ml_kn", [128, T], BF16)
        vtm = sb("ml_vtm", [128, 32, 128], BF16)
        xq = [sb(f"ml_xq{i}", [128, 4, TB], BF16) for i in range(2)]
        raw = sb("ml_raw", [64, TB], F32)
        t1 = sb("ml_t1", [64, TB], F32)
        t2 = sb("ml_t2", [64, TB], F32)
        qn = sb("ml_qn", [128, TB], BF16)
        qr = sb("ml_qr", [64, TB], BF16)
        pT = [sb(f"ml_pT{i}", [128, TB], BF16) for i in range(3)]
        rinv = sb("ml_rinv", [128, TB], F32)
        yo = [sb(f"ml_yo{i}", [128, TB], BF16) for i in range(2)]
        psq = P.ps("ml_psq", [128, TB], F32)
        psr = P.ps("ml_psr", [64, TB], F32)
        psv = P.ps("ml_psv", [128, 4, 128], F32)
        sps = [P.ps(f"ml_sps{i}", [128, TB], F32) for i in range(2)]
        ops_ = P.ps("ml_ops", [128, TB], F32)
        lps = P.ps("ml_lps", [128, TB], F32)

        def rope(src_res, cst, snt, c0, n, out_ap, out_res, scale):
            lo, hi = slice(0, 32), slice(32, 64)
            js = slice(c0, c0 + n)
            d = lambda fn, r, w: P.op("dve", fn, r, w)
            d(lambda e: e.tensor_tensor(out=t1[lo, 0:n], in0=raw[lo, 0:n], in1=cst[lo, js], op=ALU.mult), [src_res, "ml_rope"],
              ["ml_t1"])
            d(lambda e: e.tensor_tensor(out=t2[lo, 0:n], in0=raw[hi, 0:n], in1=snt[hi, js], op=ALU.mult), [src_res, "ml_rope"],
              ["ml_t2"])
            d(lambda e: e.tensor_tensor(out=t1[hi, 0:n], in0=raw[hi, 0:n], in1=cst[hi, js], op=ALU.mult),
              [src_res, "ml_rope", "ml_t1"], ["ml_t1"])
            d(lambda e: e.tensor_tensor(out=t2[hi, 0:n], in0=raw[lo, 0:n], in1=snt[lo, js], op=ALU.mult),
              [src_res, "ml_rope", "ml_t2"], ["ml_t2"])
            d(lambda e: e.tensor_tensor(out=t1[lo, 0:n], in0=t1[lo, 0:n], in1=t2[lo, 0:n], op=ALU.subtract), ["ml_t1", "ml_t2"],
              ["ml_t1"])
            d(lambda e: e.tensor_tensor(out=t1[hi, 0:n], in0=t1[hi, 0:n], in1=t2[hi, 0:n], op=ALU.add), ["ml_t1", "ml_t2"],
              ["ml_t1"])
            P.op("act", lambda e: e.activation(out=out_ap, in_=t1[:, 0:n], func=AF.Copy, scale=scale), ["ml_t1"], [out_res])

        for j in range(NB):
            js = slice(j * TB, (j + 1) * TB)
            P.dma("sp", raw[:], zmlaT[1024:1088, js], reads=["dram:zmlaT"], writes=["ml_raw"], key="ml_raw")
            rope("ml_raw", cs, sn, j * TB, TB, kr[:, js], "ml_kr", 1.0)
        xi = 0
        for hd in range(DBG.get("mlh", 8)):
            for j in range(NB):
                js = slice(j * TB, (j + 1) * TB)
                b = xi % 2
                xi += 1
                P.dma("sp", xq[b][:], ckvnT[:, js].rearrange("(k p) t -> p k t", p=128), reads=["dram:ckvnT"],
                      writes=[f"ml_xq{b}"], key=f"ml_xq{b}")
                for kc in range(4):
                    P.op("pe", lambda e, kc=kc, b=b: e.matmul(psq[:], lhsT=wkv[:, kc, hd * 256:hd * 256 + 128],
                                                              rhs=xq[b][:, kc, :], start=(kc == 0), stop=(kc == 3)),
                         ["ml_w", f"ml_xq{b}"], ["ml_psq"])
                evac_copy(kn[:, js], psq[:], ["ml_psq"], ["ml_kn"])
                for tt_ in range(4):
                    for kc in range(4):
                        P.op("pe", lambda e, kc=kc, b=b, tt_=tt_: e.matmul(
                            psv[:, tt_, :], lhsT=xq[b][:, kc, tt_ * 128:(tt_ + 1) * 128],
                            rhs=wkv[:, kc, hd * 256 + 128:hd * 256 + 256], start=(kc == 0), stop=(kc == 3)),
                            ["ml_w", f"ml_xq{b}"], ["ml_psv"])
                evac_copy(vtm[:, j * 4:(j + 1) * 4, :], psv[:], ["ml_psv"], ["ml_vtm"])
            for jj in range(DBG.get("mlb", 3)):
                n = QB
                b = xi % 2
                xi += 1

                P.dma("sp", xq[b][:, :, 0:QB], cqnT_full[:, jj * QB:(jj + 1) * QB].rearrange("(k p) t -> p k t", p=128),
                      reads=["dram:cqwT"], writes=[f"ml_xq{b}"], key=f"ml_xq{b}")
                for kc in range(4):
                    P.op("pe", lambda e, kc=kc, b=b: e.matmul(psq[:, 0:n], lhsT=wq[:, kc, hd * 192:hd * 192 + 128],
                                                              rhs=xq[b][:, kc, 0:n], start=(kc == 0), stop=(kc == 3)),
                         ["ml_w", f"ml_xq{b}"], ["ml_psq"])
                P.op("act", lambda e: e.activation(out=qn[:, 0:n], in_=psq[:, 0:n], func=AF.Copy, scale=SCALE), ["ml_psq"],
                     ["ml_qn"])
                for kc in range(4):
                    P.op("pe", lambda e, kc=kc, b=b: e.matmul(psr[:, 0:n], lhsT=wq[:, kc, hd * 192 + 128:hd * 192 + 192],
                                                              rhs=xq[b][:, kc, 0:n], start=(kc == 0), stop=(kc == 3)),
                         ["ml_w", f"ml_xq{b}"], ["ml_psr"])
                P.op("act", lambda e: e.activation(out=raw[:, 0:n], in_=psr[:, 0:n], func=AF.Copy), ["ml_psr", "ml_raw"],
                     ["ml_raw"])
                rope("ml_raw", csw, snw, jj * QB, n, qr[:, 0:n], "ml_qr", SCALE)
                nkt = 32
                for kt in range(nkt):
                    si = kt % 2
                    pi = kt % 3
                    ks = slice(kt * 128, (kt + 1) * 128)
                    P.op("pe", lambda e, si=si, ks=ks: e.matmul(sps[si][:, 0:n], lhsT=kn[:, ks], rhs=qn[:, 0:n],
                                                                start=True, stop=False),
                         ["ml_kn", "ml_qn"], [f"ml_sps{si}"])
                    P.op("pe", lambda e, si=si, ks=ks: e.matmul(sps[si][:, 0:n], lhsT=kr[:, ks], rhs=qr[:, 0:n],
                                                                start=False, stop=True),
                         ["ml_kr", "ml_qr"], [f"ml_sps{si}"])
                    P.op("act", lambda e, si=si, pi=pi: e.activation(out=pT[pi][:, 0:n], in_=sps[si][:, 0:n], func=AF.Exp),
                         [f"ml_sps{si}"], [f"ml_pT{pi}"])
                    P.op("dve", lambda e, pi=pi, kt=kt, jj=jj: e.scalar_tensor_tensor(
                        out=pT[pi][:, 0:n], in0=qk[:, jj, :], scalar=dt[:, kt:kt + 1], in1=pT[pi][:, 0:n], op0=ALU.is_ge,
                        op1=ALU.mult), [f"ml_pT{pi}", "ml_tab"], [f"ml_pT{pi}"])
                    P.op("pe", lambda e, pi=pi, kt=kt: e.matmul(
                        ops_[:, 0:n], lhsT=vtm[:, kt, :], rhs=pT[pi][:, 0:n], start=(kt == 0), stop=(kt == nkt - 1)),
                        ["ml_vtm", f"ml_pT{pi}"], ["ml_ops"])
                    P.op("pe", lambda e, pi=pi, kt=kt: e.matmul(
                        lps[:, 0:n], lhsT=ones_bf[:], rhs=pT[pi][:, 0:n], start=(kt == 0), stop=(kt == nkt - 1)),
                        ["ones_bf", f"ml_pT{pi}"], ["ml_lps"])
                P.op("dve", lambda e: e.tensor_scalar(out=rinv[:, 0:n], in0=lps[:, 0:n], scalar1=1e-30, scalar2=None,
                                                      op0=ALU.max), ["ml_lps"], ["ml_rinv"])
                P.op("dve", lambda e: e.reciprocal(out=rinv[:, 0:n], in_=rinv[:, 0:n]), ["ml_rinv"], ["ml_rinv"])
                ob = jj % 2
                P.op("dve", lambda e, ob=ob: e.tensor_tensor(out=yo[ob][:, 0:n], in0=ops_[:, 0:n], in1=rinv[:, 0:n], op=ALU.mult),
                     ["ml_ops", "ml_rinv"], [f"ml_yo{ob}"])
                P.dma("act", ymlaT[hd * 128:(hd + 1) * 128, jj * QB:(jj + 1) * QB], yo[ob][:, 0:n], reads=[f"ml_yo{ob}"],
                      writes=["dram:ymlaT"], key=f"ml_yo{ob}")
        P.barrier()
        P.stack = prev_stack


_CACHE = {}


def make_in_maps(inputs):
    sq = lambda a: np.ascontiguousarray(np.asarray(a)[0])
    V = pack_vecs(inputs)
    common = {
        "vecs": V,
        "w_in": sq(inputs["w_in"]), "rw_w2": sq(inputs["rw_w2"]), "rw_a2": sq(inputs["rw_a2"]), "rw_g2": sq(inputs["rw_g2"]),
        "w_q_up": sq(inputs["mla_w_q_up"]), "w_kv_up": sq(inputs["mla_w_kv_up"]),
        "w_brw": sq(inputs["w_branch_rw"]), "w_bmla": sq(inputs["w_branch_mla"]), "w_out": sq(inputs["w_out"]),
        "w_up": sq(inputs["w_up"]), "w_down": sq(inputs["w_down"]), "w_ple": sq(inputs["w_ple"]),
        "w_pg": sq(inputs["w_ple_gate"]),
    }
    i_ = np.arange(342)
    qk = np.zeros((128, 3, 342), np.float32)
    for jj in range(3):
        qc = np.floor_divide(342 * jj - 2 + i_, 64).astype(np.float32)
        qk[:, jj, :] = qc[None, :] - (np.arange(128)[:, None] >= 64).astype(np.float32)
    common["qk_tab"] = np.ascontiguousarray(qk.reshape(128, 3 * 342))
    xs = np.asarray(inputs["x"], np.float32)
    ps = np.asarray(inputs["p"], np.float32)[0]
    posn = np.asarray(inputs["positions"], np.int32)
    maps = []
    for c in range(8):
        b, q = c // 4, c % 4
        m = dict(common)
        m["x"] = np.ascontiguousarray(xs[b])
        m["p"] = np.ascontiguousarray(ps[b, 1024 * q:1024 * (q + 1)])
        m["pos"] = np.ascontiguousarray(posn[b:b + 1])
        pw = np.zeros((1, TW), np.int32)
        lo = 1024 * q - 2
        if lo < 0:
            pw[0, 2:] = posn[b, 0:1024]
        else:
            pw[0, :] = posn[b, lo:lo + TW]
        m["pos_w"] = pw
        m["d_tab"] = np.ascontiguousarray(np.broadcast_to((2.0 * np.arange(32) - 16.0 * q).astype(np.float32)[None, :], (128, 32)))
        maps.append(m)
    return maps


def kernel(**inputs):
    if "nc" not in _CACHE:
        _CACHE["nc"] = build_program()
    nc = _CACHE["nc"]
    maps = make_in_maps(inputs)
    res = run_bass_kernel_spmd(nc, maps, core_ids=list(range(8)))
    outs = [np.asarray(res.results[c]["out"], np.float32) for c in range(8)]
    return np.stack([np.concatenate(outs[0:4], axis=0), np.concatenate(outs[4:8], axis=0)], axis=0)
```

```python
import contextlib
import numpy as np
import concourse.bass as bass
import concourse.mybir as mybir

F32 = mybir.dt.float32
BF16 = mybir.dt.bfloat16
I32 = mybir.dt.int32
ALU = mybir.AluOpType
AF = mybir.ActivationFunctionType
AX = mybir.AxisListType


SAME_ENGINE_SYNC = True


class _Rec:
    def __init__(self):
        self.calls = []

    def __getattr__(self, name):
        def f(*a, **k):
            self.calls.append((name, a, k))
            return self

        return f


def _eager(fn):
    rec = _Rec()
    fn(rec)
    assert len(rec.calls) == 1, rec.calls
    name, a, k = rec.calls[0]
    return lambda e: getattr(e, name)(*a, **k)


class Prog:
    COMPUTE = ("pe", "act", "dve", "pool")

    def __init__(self, nc, stack):
        self.nc = nc
        self.stack = stack
        self.stack0 = stack
        self.streams = {e: [] for e in ("pe", "act", "dve", "pool", "sp")}
        self.sems = {}
        self.cnt = {}
        for e in self.COMPUTE:
            self.sems[e] = stack.enter_context(nc.semaphore("s_" + e))
            self.cnt[e] = 0
        self.seen = {e: {} for e in self.streams}
        self.res = {}
        self.dma_sems = {}
        self.n_ops = 0

    def sb(self, name, shape, dt):
        return self.stack.enter_context(self.nc.sbuf_tensor(name, list(shape), dt))

    def ps(self, name, shape, dt=F32):
        return self.stack.enter_context(self.nc.psum_tensor(name, list(shape), dt))

    def _dma_sem(self, key):
        if key not in self.dma_sems:
            self.dma_sems[key] = self.stack0.enter_context(self.nc.semaphore("d_" + key))
            self.sems["D:" + key] = self.dma_sems[key]
            self.cnt["D:" + key] = 0
        return "D:" + key

    def _deps(self, eng, reads, writes, exclude=None):
        need = {}
        for r in reads:
            ent = self.res.get(r)
            if ent:
                for k, v in ent[0].items():
                    need[k] = max(need.get(k, 0), v)
                if "ps" in r or "small" in r or "trp" in r:
                    for k, v in ent[1].items():
                        if k != eng:
                            need[k] = max(need.get(k, 0), v)
        for w in writes:
            ent = self.res.get(w)
            if ent:
                if not w.startswith("dram:"):
                    for k, v in ent[0].items():
                        need[k] = max(need.get(k, 0), v)
                for k, v in ent[1].items():
                    need[k] = max(need.get(k, 0), v)
        waits = []
        for k, v in need.items():
            if k == eng and (eng == "pe" or not SAME_ENGINE_SYNC):
                continue
            if k == exclude:
                continue
            if self.seen[eng].get(k, 0) >= v:
                continue
            self.seen[eng][k] = v
            waits.append((k, v))
        return waits

    def _mark(self, key, val, reads, writes):
        for r in reads:
            ent = self.res.setdefault(r, [{}, {}])
            ent[1][key] = val
        for w in writes:
            ent = self.res.setdefault(w, [{}, {}])
            if w.startswith("dram:"):
                ent[0][key] = val
            else:
                ent[0] = {key: val}
                ent[1] = {}
            self.res[w] = ent

    def op(self, eng, fn, reads=(), writes=()):
        fn = _eager(fn)
        waits = self._deps(eng, reads, writes)
        self.cnt[eng] += 1
        val = self.cnt[eng]
        sem = self.sems[eng]
        sems = self.sems

        def emit(e, waits=waits, fn=fn, sem=sem):
            for k, v in waits:
                e.wait_ge(sems[k], v)
            fn(e).then_inc(sem, 1)

        self.streams[eng].append(emit)
        self._mark(eng, val, reads, writes)
        self.n_ops += 1

    def dma(self, queue, out, in_, reads=(), writes=(), key=None, **kw):
        assert key is not None
        sk = self._dma_sem(key)
        waits = self._deps(queue, reads, writes, exclude=sk)
        self.cnt[sk] += 16
        val = self.cnt[sk]
        sem = self.sems[sk]
        sems = self.sems

        def emit(e, waits=waits, sem=sem):
            for k, v in waits:
                e.wait_ge(sems[k], v)
            e.dma_start(out=out, in_=in_, **kw).then_inc(sem, 16)

        self.streams[queue].append(emit)
        self._mark(sk, val, reads, writes)
        self.n_ops += 1

    def custom(self, queue, fn, reads=(), writes=(), key=None, raw=False):
        if not raw:
            fn = _eager(fn)
        sk = self._dma_sem(key)
        waits = self._deps(queue, reads, writes, exclude=sk)
        self.cnt[sk] += 16
        val = self.cnt[sk]
        sem = self.sems[sk]
        sems = self.sems

        def emit(e, waits=waits, sem=sem):
            for k, v in waits:
                e.wait_ge(sems[k], v)
            fn(e).then_inc(sem, 16)

        self.streams[queue].append(emit)
        self._mark(sk, val, reads, writes)
        self.n_ops += 1

    def barrier(self):
        snap = dict(self.cnt)
        sems = self.sems
        for eng in self.streams:
            waits = []
            for k, v in snap.items():
                if v <= 0 or self.seen[eng].get(k, 0) >= v:
                    continue
                if k == eng and eng == "pe":
                    continue
                self.seen[eng][k] = v
                waits.append((k, v))

            def emit(e, waits=waits):
                for k, v in waits:
                    e.wait_ge(sems[k], v)

            self.streams[eng].append(emit)

    def wait_all(self, eng, resources):
        waits = self._deps(eng, resources, ())
        sems = self.sems

        def emit(e, waits=waits):
            for k, v in waits:
                e.wait_ge(sems[k], v)

        self.streams[eng].append(emit)

    def emit(self):
        nc = self.nc
        with nc.Block() as block:
            @block.tensor
            def _(e):
                for f in self.streams["pe"]:
                    f(e)

            @block.scalar
            def _(e):
                for f in self.streams["act"]:
                    f(e)

            @block.vector
            def _(e):
                for f in self.streams["dve"]:
                    f(e)

            @block.gpsimd
            def _(e):
                for f in self.streams["pool"]:
                    f(e)

            @block.sync
            def _(e):
                for f in self.streams["sp"]:
                    f(e)

import math
from concourse.bass_utils import run_bass_kernel_spmd

T = 4096
D = 2048
TB = 512
NB = T // TB
DFF = 5632
EPS = 1e-6
GN_EPS = 64e-5
NV = 528
DBG = {}
TW = 1026
WB = [(0, 342), (342, 342), (684, 342)]
FB = [(j * 512, 512) for j in range(8)]


_RANK = {}


def get_rank(e):
    if "r" not in _RANK:
        _RANK["r"] = (e.partition_id() % 4) * 1024
    return _RANK["r"]


def vec_pc(v):
    v = np.asarray(v, np.float32).reshape(-1)
    return np.ascontiguousarray(v.reshape(-1, 128).T)


def pack_vecs(inp):
    V = np.zeros((128, NV), np.float32)
    V[:, 0:16] = vec_pc(inp["pre_mix_norm"])
    V[:, 16:32] = vec_pc(inp["post_mix_norm"])
    V[:, 32:48] = vec_pc(inp["pre_ffn_norm"])
    V[:, 48:64] = vec_pc(inp["post_ffn_norm"])
    V[:, 64:80] = vec_pc(inp["ple_norm"])
    mu = np.asarray(inp["rw_mu"], np.float32).reshape(-1)
    V[:, 80:104] = vec_pc(mu[0:3072])
    V[:, 104:112] = vec_pc(inp["rw_w0"])
    V[:, 112:120] = vec_pc(inp["rw_a0"])
    V[:, 120:128] = vec_pc(inp["rw_k_k"])
    V[:, 128:136] = vec_pc(inp["rw_k_a"])
    V[:, 136:144] = vec_pc(inp["rw_r_k"])
    V[:, 144:152] = vec_pc(inp["rw_lnx_w"])
    V[:, 152:160] = vec_pc(inp["rw_lnx_b"])
    V[:, 160:164] = vec_pc(inp["mla_q_norm"])
    V[:, 164:168] = vec_pc(inp["mla_kv_norm"])
    V[:, 168:256] = vec_pc(inp["conv_b"])
    cw = np.asarray(inp["conv_w"], np.float32).reshape(3, -1)
    V[:, 256:344] = vec_pc(cw[0])
    V[:, 344:432] = vec_pc(cw[1])
    V[:, 432:520] = vec_pc(cw[2])
    V[0:64, 520] = mu[3072:3136]
    V[0:64, 521] = mu[3136:3200]
    V[0:128, 522] = mu[3200:3328]
    V[0:32, 523] = mu[3328:3360]
    invf = (10000.0 ** (-np.arange(0, 64, 2, dtype=np.float32) / 64)).astype(np.float32)
    V[0:32, 524] = invf
    V[32:64, 524] = invf
    return V


def build_program(debug=()):
    nc = bass.Bass("TRN2", target_bir_lowering=False)
    DBG.clear()
    _RANK.clear()
    for k_ in debug:
        if "=" in k_:
            DBG[k_.split("=")[0]] = int(k_.split("=")[1])
        else:
            DBG[k_] = 1

    def din(name, shape, dt=F32):
        return nc.dram_tensor(name, list(shape), dt, kind="ExternalInput").ap()

    def dscr(name, shape, dt):
        if name in debug:
            return nc.dram_tensor(name, list(shape), dt, kind="ExternalOutput").ap()
        return nc.dram_tensor(name, list(shape), dt).ap()

    x = din("x", [T, D])
    p_in = din("p", [1024, 256])
    pos = din("pos", [1, T], I32)
    pos_w = din("pos_w", [1, TW], I32)
    qk_tab_d = din("qk_tab", [128, 3 * 342])
    d_tab_d = din("d_tab", [128, 32])
    vecs = din("vecs", [128, NV])
    w_in = din("w_in", [D, 8544])
    rw_w2 = din("rw_w2", [64, 1024])
    rw_a2 = din("rw_a2", [64, 1024])
    rw_g2 = din("rw_g2", [160, 1024])
    w_q_up = din("w_q_up", [512, 1536])
    w_kv_up = din("w_kv_up", [512, 2048])
    w_brw = din("w_brw", [1024, D])
    w_bmla = din("w_bmla", [1024, D])
    w_out = din("w_out", [D, D])
    w_up = din("w_up", [D, 2 * DFF])
    w_down = din("w_down", [DFF, D])
    w_ple = din("w_ple", [256, D])
    w_pg = din("w_pg", [D, D])
    out = nc.dram_tensor("out", [1024, D], F32, kind="ExternalOutput").ap()

    hT_full = dscr("hT", [D, T + 2], F32)
    uT_full = dscr("uT", [D, T + 2], BF16)
    yrwT_full = dscr("yrwT", [1024, T + 2], BF16)
    cqnT_full = dscr("cqnT", [512, T + 2], BF16)
    hT = hT_full[:, 2:]
    uT = uT_full[:, 2:]
    yrwT = yrwT_full[:, 2:]
    cqnT = cqnT_full[:, 2:]
    zrwT = dscr("zrwT", [3360, T], F32)
    zmlaT = dscr("zmlaT", [1088, T], F32)
    ckvnT = dscr("ckvnT", [512, T], BF16)
    hwT = dscr("hwT", [D, TW], F32)
    uwT = dscr("uwT", [D, TW], BF16)
    ywT = dscr("ywT", [1024, TW], BF16)
    cqwT = dscr("cqwT", [512, TW], BF16)
    ymlaT = dscr("ymlaT", [1024, TW], BF16)
    pT = dscr("pT", [256, 1024], F32)
    gT = dscr("gT", [4096, TW], BF16)
    mT = dscr("mT", [D, TW], BF16)
    fT = dscr("fT", [D, TW], F32)
    ffT = dscr("ffT", [DFF, TW], BF16)
    qk_tab = qk_tab_d
    d_tab = d_tab_d

    with contextlib.ExitStack() as st0:
        P = Prog(nc, st0)
        V = P.sb("V", [128, NV], F32)
        P.dma("sp", V[:], vecs, writes=["V"], key="V")
        ident = P.sb("ident", [128, 128], F32)
        ident_bf = P.sb("ident_bf", [128, 128], BF16)
        ones_bf = P.sb("ones_bf", [128, 128], BF16)
        bones_bf = P.sb("bones_bf", [128, 128], BF16)
        maskP = P.sb("maskP", [128, 2, 128], F32)
        maskL = P.sb("maskL", [128, 64], F32)
        eye = P.sb("eye", [128, 64], F32)
        rmask = P.sb("rmask", [128, TB], F32)
        omka = P.sb("omka", [128, 8], F32)
        su = P.sb("su", [128, 64], F32)
        ui = P.sb("ui", [128, 64], F32)

        def pool(fn, reads=(), writes=()):
            P.op("pool", fn, reads, writes)

        pool(lambda e: e.memset(ident[:], 1.0), writes=["ident"])
        pool(lambda e: e.affine_select(out=ident[:], in_=ident[:], pattern=[[-1, 128]], compare_op=ALU.is_equal,
                                       fill=0.0, base=0, channel_multiplier=1), reads=["ident"], writes=["ident"])
        pool(lambda e: e.tensor_copy(out=ident_bf[:], in_=ident[:]), reads=["ident"], writes=["ident_bf"])
        pool(lambda e: e.memset(ones_bf[:], 1.0), writes=["ones_bf"])
        pool(lambda e: e.memset(bones_bf[:], 0.0), writes=["bones_bf"])
        pool(lambda e: e.memset(bones_bf[0:64, 0:64], 1.0), reads=["bones_bf"], writes=["bones_bf"])
        pool(lambda e: e.memset(bones_bf[64:128, 64:128], 1.0), reads=["bones_bf"], writes=["bones_bf"])
        for (tl, pat, cm, b0, b1, op_) in ((su, 1, -1, -1, -1, ALU.is_ge), (ui, 1, -1, 0, 0, ALU.is_ge),
                                           (maskL, -1, 1, -1, -1, ALU.is_ge), (eye, -1, 1, 0, 0, ALU.is_equal)):
            pool(lambda e, tl=tl: e.memset(tl[:], 1.0), writes=["msk"])
            for h, bb in ((0, b0), (1, b1)):
                sl = slice(64 * h, 64 * h + 64)
                pool(lambda e, tl=tl, sl=sl, pat=pat, cm=cm, bb=bb, op_=op_: e.affine_select(
                    out=tl[sl, :], in_=tl[sl, :], pattern=[[pat, 64]], compare_op=op_, fill=0.0, base=bb,
                    channel_multiplier=cm), reads=["msk"], writes=["msk"])
        for xx in range(2):
            pool(lambda e, xx=xx: e.tensor_copy(out=maskP[:, xx, 0:64], in_=su[:]), reads=["msk"], writes=["msk"])
            pool(lambda e, xx=xx: e.tensor_copy(out=maskP[:, xx, 64:128], in_=ui[:]), reads=["msk"], writes=["msk"])
        pool(lambda e: e.memset(rmask[:], 1.0), reads=["msk"], writes=["msk"])
        for c in range(8):
            pool(lambda e, c=c: e.memset(rmask[:, c * 64:c * 64 + 1], 0.0), reads=["msk"], writes=["msk"])
        P.op("dve", lambda e: e.tensor_scalar(out=omka[:], in0=V[:, 128:136], scalar1=-1.0, scalar2=1.0, op0=ALU.mult,
                                              op1=ALU.add), reads=["V", "msk"], writes=["omka"])
        CONST = ["V", "msk", "ident", "ident_bf", "ones_bf", "bones_bf", "omka"]

        rr = {"evac": 0}

        class _Stop(Exception):
            pass

        def stop_if(name):
            if name in debug:
                raise _Stop()

        def evac_copy(out_ap, in_ap, reads, writes, scale=None):
            rr["evac"] += 1
            if scale is not None or rr["evac"] % 2 == 0:
                P.op("act", lambda e: e.activation(out=out_ap, in_=in_ap, func=AF.Copy,
                                                   scale=(1.0 if scale is None else scale)), reads, writes)
            else:
                P.op("dve", lambda e: e.tensor_copy(out=out_ap, in_=in_ap), reads, writes)

        def transpose_stage(tag, src, dst, R, C, src_res, dst_res):
            with contextlib.ExitStack() as st:
                prev_stack = P.stack
                P.stack = st
                nb = 2
                tin = [P.sb(f"{tag}_in{i}", [128, C], F32) for i in range(nb)]
                tout = [P.sb(f"{tag}_out{i}", [128, C // 128, 128], F32) for i in range(nb)]
                pst = [P.ps(f"{tag}_ps{i}", [128, 4, 128], F32) for i in range(2)]
                k = 0
                for r in range(R // 128):
                    b = r % nb
                    P.dma("sp", tin[b][:], src[r * 128:(r + 1) * 128, :], reads=[src_res], writes=[f"TR_in{b}"],
                          key=f"TR_in{b}")
                    for c4 in range(0, C // 128, 4):
                        pb = k % 2
                        k += 1
                        n4 = min(4, C // 128 - c4)
                        for i in range(n4):
                            c = c4 + i
                            P.op("pe", lambda e, pb=pb, i=i, c=c, b=b: e.transpose(
                                out=pst[pb][:, i, :], in_=tin[b][:, c * 128:(c + 1) * 128], identity=ident[:]),
                                reads=[f"TR_in{b}", "ident"], writes=[f"TR_ps{pb}"])
                        evac_copy(tout[b][:, c4:c4 + n4, :], pst[pb][:, 0:n4, :], [f"TR_ps{pb}"], [f"TR_out{b}"])
                    P.dma("sp", dst[:, r * 128:(r + 1) * 128].rearrange("(c p) t -> p c t", p=128), tout[b][:],
                          reads=[f"TR_out{b}"], writes=[dst_res], key=f"TR_st{b}")
                P.barrier()
                P.stack = prev_stack

        def rmsnorm_stage(tag, src, F, gcol, src_res, dst=None, dst_res=None, resid=None, resid_res=None, blocks=FB):
            FC = F // 128
            with contextlib.ExitStack() as st:
                prev_stack = P.stack
                P.stack = st
                s = P.sb(f"{tag}_s", [128, FC, TB], F32)
                sq = P.sb(f"{tag}_sq", [128, FC, TB], BF16)
                rstd = P.sb(f"{tag}_rstd", [128, TB], F32)
                ps = P.ps(f"{tag}_ps", [128, TB], F32)
                if resid is None:
                    o = P.sb(f"{tag}_o", [128, FC, TB], BF16)
                else:
                    o = P.sb(f"{tag}_o", [128, FC, TB], F32)
                    hh = P.sb(f"{tag}_h", [128, FC, TB], F32)
                for j, (b0, bs) in enumerate(blocks):
                    cs = slice(b0, b0 + bs)
                    P.dma("sp", s[:, :, 0:bs], src[:, cs].rearrange("(c p) t -> p c t", p=128), reads=[src_res],
                          writes=[f"RN_s"], key=f"RN_s")
                    if resid is not None:
                        P.dma("sp", hh[:, :, 0:bs], resid[:, cs].rearrange("(c p) t -> p c t", p=128), reads=[resid_res],
                              writes=[f"RN_h"], key=f"RN_h")
                    P.op("act", lambda e: e.activation(out=sq[:, :, 0:bs], in_=s[:, :, 0:bs], func=AF.Square), reads=[f"RN_s"],
                         writes=[f"RN_sq"])
                    for c in range(FC):
                        P.op("pe", lambda e, c=c: e.matmul(ps[:, 0:bs], lhsT=ones_bf[:], rhs=sq[:, c, 0:bs], start=(c == 0),
                                                           stop=(c == FC - 1)),
                             reads=[f"RN_sq", "ones_bf"], writes=[f"RN_ps"])
                    P.op("act", lambda e: e.activation(out=rstd[:, 0:bs], in_=ps[:, 0:bs], func=AF.Sqrt, bias=EPS, scale=1.0 / F),
                         reads=[f"RN_ps"], writes=[f"RN_rstd"])
                    P.op("dve", lambda e: e.reciprocal(out=rstd[:, 0:bs], in_=rstd[:, 0:bs]), reads=[f"RN_rstd"],
                         writes=[f"RN_rstd"])
                    for c in range(FC):
                        eng = "pool"
                        P.op("dve", lambda e, c=c: e.scalar_tensor_tensor(
                            out=o[:, c, 0:bs], in0=s[:, c, 0:bs], scalar=V[:, gcol + c:gcol + c + 1], in1=rstd[:, 0:bs],
                            op0=ALU.mult, op1=ALU.mult), reads=[f"RN_s", f"RN_rstd", "V"], writes=[f"RN_o{c}"])
                        if resid is not None:
                            P.op(eng, lambda e, c=c: e.tensor_tensor(out=o[:, c, 0:bs], in0=o[:, c, 0:bs], in1=hh[:, c, 0:bs],
                                                                     op=ALU.add),
                                 reads=[f"RN_o{c}", f"RN_h"], writes=[f"RN_o{c}"])
                    allo = [f"RN_o{c}" for c in range(FC)]
                    if resid is None:
                        P.dma("sp", dst[:, cs].rearrange("(c p) t -> p c t", p=128), o[:, :, 0:bs], reads=allo,
                              writes=[dst_res], key=f"RN_st")
                    else:
                        P.dma("sp", resid[:, cs].rearrange("(c p) t -> p c t", p=128), o[:, :, 0:bs], reads=allo,
                              writes=[resid_res], key=f"RN_st")
                P.barrier()
                P.stack = prev_stack

        def linear_stage(tag, srcs, groups, epi, wbuf_cols, nps=2, blocks=FB):
            with contextlib.ExitStack() as st:
                prev_stack = P.stack
                P.stack = st
                wsb = []
                sbs = []
                for si, (src, K, sdt, sres, W) in enumerate(srcs):
                    KC = (K + 127) // 128
                    wsb.append([P.sb(f"{tag}_w{si}_{i}", [128, KC, wbuf_cols], BF16) for i in range(2)])
                    sbs.append([P.sb(f"{tag}_x{si}_{i}", [128, KC, TB], BF16) for i in range(2)])
                pss = [P.ps(f"{tag}_ps{i}", [128, TB], F32) for i in range(nps)]
                state = {"ps": 0, "xb": 0}
                def grp_layout(grp):
                    offs = []
                    off = 0
                    for (c0, w) in grp:
                        offs.append(off)
                        off += w
                    assert off <= wbuf_cols
                    ranges = []
                    for (c0, w), o_ in zip(grp, offs):
                        if ranges and ranges[-1][0] + ranges[-1][1] == c0 and ranges[-1][2] + ranges[-1][1] == o_:
                            ranges[-1][1] += w
                        else:
                            ranges.append([c0, w, o_])
                    return offs, ranges

                def load_w(gi):
                    wb = gi % 2
                    offs, ranges = grp_layout(groups[gi])
                    for si, (src, K, sdt, sres, W) in enumerate(srcs):
                        KC = (K + 127) // 128
                        for (c0, w, o_) in ranges:
                            for kc in range(KC):
                                kr = min(128, K - kc * 128)
                                P.dma("pool", wsb[si][wb][0:kr, kc, o_:o_ + w], W[kc * 128:kc * 128 + kr, c0:c0 + w],
                                      writes=[f"LN_w{si}_{wb}"], key=f"LN_w{si}_{wb}")

                load_w(0)
                for gi, grp in enumerate(groups):
                    wb = gi % 2
                    offs, ranges = grp_layout(grp)
                    if gi + 1 < len(groups):
                        load_w(gi + 1)
                    for j, (b0, bs) in enumerate(blocks):
                        cs = slice(b0, b0 + bs)
                        xb = state["xb"] % 2
                        state["xb"] += 1
                        for si, (src, K, sdt, sres, W) in enumerate(srcs):
                            KC = (K + 127) // 128
                            q = "sp" if sdt == BF16 else "pool"
                            if K % 128 == 0 and q == "sp":
                                P.dma(q, sbs[si][xb][:, :, 0:bs], src[:, cs].rearrange("(k p) t -> p k t", p=128), reads=[sres],
                                      writes=[f"LN_x{si}_{xb}"], key=f"LN_x{si}_{xb}")
                            else:
                                for kc in range(KC):
                                    kr = min(128, K - kc * 128)
                                    P.dma(q, sbs[si][xb][0:kr, kc, 0:bs], src[kc * 128:kc * 128 + kr, cs], reads=[sres],
                                          writes=[f"LN_x{si}_{xb}"], key=f"LN_x{si}_{xb}")
                        for ci, ((c0, w), o_) in enumerate(zip(grp, offs)):
                            pi = state["ps"] % nps
                            state["ps"] += 1
                            nmm = sum((K + 127) // 128 for (_, K, _, _, _) in srcs)
                            i = 0
                            for si, (src, K, sdt, sres, W) in enumerate(srcs):
                                KC = (K + 127) // 128
                                for kc in range(KC):
                                    kr = min(128, K - kc * 128)
                                    P.op("pe", lambda e, pi=pi, si=si, wb=wb, kc=kc, kr=kr, o_=o_, w=w, xb=xb, i=i, nmm=nmm:
                                         e.matmul(pss[pi][0:w, 0:bs], lhsT=wsb[si][wb][0:kr, kc, o_:o_ + w],
                                                  rhs=sbs[si][xb][0:kr, kc, 0:bs], start=(i == 0), stop=(i == nmm - 1)),
                                         reads=[f"LN_w{si}_{wb}", f"LN_x{si}_{xb}"], writes=[f"LN_ps{pi}"])
                                    i += 1
                            epi(gi, ci, (c0, w), j, pss[pi], f"LN_ps{pi}", (b0, bs))
                P.barrier()
                P.stack = prev_stack

        def store_epi(tag, dst, dst_res, odt, func=AF.Copy, row_of=None, nbuf=3):
            bufs = [P.sb(f"{tag}_eo{i}", [128, TB], odt) for i in range(nbuf)]
            stt = {"i": 0}

            def epi(gi, ci, cw, j, ps, ps_res, blk):
                c0, w = cw
                b0_, bs = blk
                b = stt["i"] % nbuf
                stt["i"] += 1
                r0 = c0 if row_of is None else row_of(c0)
                if func == AF.Copy:
                    evac_copy(bufs[b][0:w, 0:bs], ps[0:w, 0:bs], [ps_res], [f"EO_eo{b}"])
                else:
                    P.op("act", lambda e: e.activation(out=bufs[b][0:w, 0:bs], in_=ps[0:w, 0:bs], func=func), [ps_res],
                         [f"EO_eo{b}"])
                P.dma("act", dst[r0:r0 + w, b0_:b0_ + bs], bufs[b][0:w, 0:bs], reads=[f"EO_eo{b}"],
                      writes=[dst_res], key=f"EO_eo{b}")

            return epi

        def chunks(c0, n, w=128):
            res = []
            c = c0
            while c < c0 + n:
                ww = min(w, c0 + n - c)
                res.append((c, ww))
                c += ww
            return res

        def grouped(ch, n):
            return [ch[i:i + n] for i in range(0, len(ch), n)]

        def main_seq():
            if "skip_s01" in debug:
                rwkv_stage(P, nc, st0, V, zrwT, yrwT, rw_w2, rw_a2, rw_g2, ident_bf, bones_bf, maskP, maskL, eye, rmask, omka,
                           evac_copy)
                stop_if("skip_s01")
            if "skip_s01m" in debug:
                rmsnorm_stage("nq", zmlaT[0:512, :], 512, 160, "dram:zmlaT", dst=cqnT, dst_res="dram:cqnT")
                rmsnorm_stage("nk", zmlaT[512:1024, :], 512, 164, "dram:zmlaT", dst=ckvnT, dst_res="dram:ckvnT")
                mla_stage(P, nc, st0, V, pos, pos_w, qk_tab, d_tab, zmlaT, cqnT_full, ckvnT, w_q_up, w_kv_up, ymlaT, ones_bf,
                          evac_copy)
                stop_if("skip_s01m")
            def win_copy(dst, src_full, rows, res_src, res_dst, key):
                step = 512
                for r0 in range(0, rows, step):
                    def fn(e, r0=r0):
                        off = get_rank(e)
                        return e.dma_start(out=dst[r0:r0 + step, :], in_=src_full[r0:r0 + step, bass.ds(off, TW)])
                    P.custom("pool", fn, reads=[res_src], writes=[res_dst], key=key, raw=True)

            zt_f = P.sb("zpad_f", [128, 16, 2], F32)
            zt_b = P.sb("zpad_b", [128, 16, 2], BF16)
            P.op("pool", lambda e: e.memset(zt_f[:], 0.0), [], ["zpad"])
            P.op("pool", lambda e: e.memset(zt_b[:], 0.0), [], ["zpad"])
            P.dma("sp", hT_full[:, 0:2].rearrange("(c p) t -> p c t", p=128), zt_f[:], reads=["zpad"], writes=["dram:hT"], key="zp")
            P.dma("sp", uT_full[:, 0:2].rearrange("(c p) t -> p c t", p=128), zt_b[:], reads=["zpad"], writes=["dram:uT"], key="zp")
            P.dma("sp", yrwT_full[:, 0:2].rearrange("(c p) t -> p c t", p=128), zt_b[:, 0:8, :], reads=["zpad"],
                  writes=["dram:yrwT"], key="zp")
            P.dma("sp", cqnT_full[:, 0:2].rearrange("(c p) t -> p c t", p=128), zt_b[:, 0:4, :], reads=["zpad"],
                  writes=["dram:cqnT"], key="zp")
            transpose_stage("tx", x, hT, T, D, "dram:x", "dram:hT")
            stop_if("stop0")
            rmsnorm_stage("n1", hT, D, 0, "dram:hT", dst=uT, dst_res="dram:uT")
            stop_if("stop0b")
            with contextlib.ExitStack() as stx:
                P.stack = stx
                e1 = store_epi("l1a", zrwT, "dram:zrwT", F32)
                linear_stage("l1a", [(uT, D, BF16, "dram:uT", w_in)], grouped(chunks(0, 3360), 8), e1, 1024)
                P.barrier()
                P.stack = st0
            with contextlib.ExitStack() as stx:
                P.stack = stx
                e2 = store_epi("l1b", zmlaT, "dram:zmlaT", F32, row_of=lambda c0: c0 - 3360)
                linear_stage("l1b", [(uT, D, BF16, "dram:uT", w_in)], grouped(chunks(3360, 1088), 9), e2, 1152)
                P.barrier()
                P.stack = st0
            stop_if("stop1")
            rwkv_stage(P, nc, st0, V, zrwT, yrwT, rw_w2, rw_a2, rw_g2, ident_bf, bones_bf, maskP, maskL, eye, rmask, omka,
                       evac_copy)
            stop_if("stop2")
            rmsnorm_stage("nq", zmlaT[0:512, :], 512, 160, "dram:zmlaT", dst=cqnT, dst_res="dram:cqnT")
            rmsnorm_stage("nk", zmlaT[512:1024, :], 512, 164, "dram:zmlaT", dst=ckvnT, dst_res="dram:ckvnT")
            win_copy(cqwT, cqnT_full, 512, "dram:cqnT", "dram:cqwT", "wc3")
            P.barrier()
            mla_stage(P, nc, st0, V, pos, pos_w, qk_tab, d_tab, zmlaT, cqwT, ckvnT, w_q_up, w_kv_up, ymlaT, ones_bf,
                      evac_copy)
            stop_if("stop3")
            win_copy(hwT, hT_full, D, "dram:hT", "dram:hwT", "wc0")
            win_copy(uwT, uT_full, D, "dram:uT", "dram:uwT", "wc1")
            win_copy(ywT, yrwT_full, 1024, "dram:yrwT", "dram:ywT", "wc2")
            P.barrier()
            transpose_stage("tp", p_in, pT, 1024, 256, "dram:p", "dram:pT")
            with contextlib.ExitStack() as stx:
                P.stack = stx
                e3 = store_epi("l1c", gT, "dram:gT", BF16, func=AF.Sigmoid, row_of=lambda c0: c0 - 4448)
                linear_stage("l1c", [(uwT, D, BF16, "dram:uwT", w_in)], grouped(chunks(4448, 4096), 8), e3, 1024, blocks=WB)
                P.barrier()
                P.stack = st0
            with contextlib.ExitStack() as stx:
                P.stack = stx
                gA = [P.sb(f"s4_gA{i}", [128, TB], BF16) for i in range(2)]
                gB = [P.sb(f"s4_gB{i}", [128, TB], BF16) for i in range(2)]
                t1 = [P.sb(f"s4_t1{i}", [128, TB], F32) for i in range(2)]
                mo = [P.sb(f"s4_mo{i}", [128, TB], BF16) for i in range(2)]
                t2 = [P.sb(f"s4_t2{i}", [128, TB], F32) for i in range(2)]
                stt = {"i": 0}

                def epiA(gi, ci, cw, j, ps, ps_res, blk):
                    c0, w = cw
                    b0_, bs = blk
                    b = stt["i"] % 2
                    stt["i"] += 1
                    P.dma("sp", gA[b][:, 0:bs], gT[c0:c0 + 128, b0_:b0_ + bs], reads=["dram:gT"], writes=[f"s4_gA{b}"],
                          key=f"s4_gA{b}")
                    P.op("dve", lambda e: e.tensor_tensor(out=t1[b][:, 0:bs], in0=ps[:, 0:bs], in1=gA[b][:, 0:bs], op=ALU.mult),
                         [ps_res, f"s4_gA{b}"], [f"s4_t1{b}"])
                    P.dma("act", fT[c0:c0 + 128, b0_:b0_ + bs], t1[b][:, 0:bs], reads=[f"s4_t1{b}"], writes=["dram:fT"],
                          key=f"s4_t1{b}")

                linear_stage("l4a", [(ywT, 1024, BF16, "dram:ywT", w_brw)], grouped(chunks(0, D), 8), epiA, 1024, blocks=WB)

                def epiB(gi, ci, cw, j, ps, ps_res, blk):
                    c0, w = cw
                    b0_, bs = blk
                    b = stt["i"] % 2
                    stt["i"] += 1
                    P.dma("sp", gB[b][:, 0:bs], gT[2048 + c0:2048 + c0 + 128, b0_:b0_ + bs], reads=["dram:gT"],
                          writes=[f"s4_gB{b}"], key=f"s4_gB{b}")
                    P.dma("sp", t1[b][:, 0:bs], fT[c0:c0 + 128, b0_:b0_ + bs], reads=["dram:fT"], writes=[f"s4_t1{b}"],
                          key=f"s4_t1l{b}")
                    P.op("dve", lambda e: e.tensor_tensor(out=t2[b][:, 0:bs], in0=ps[:, 0:bs], in1=gB[b][:, 0:bs], op=ALU.mult),
                         [ps_res, f"s4_gB{b}"], [f"s4_t2{b}"])
                    P.op("dve", lambda e: e.tensor_tensor(out=mo[b][:, 0:bs], in0=t2[b][:, 0:bs], in1=t1[b][:, 0:bs], op=ALU.add),
                         [f"s4_t2{b}", f"s4_t1{b}"], [f"s4_mo{b}"])
                    P.dma("act", mT[c0:c0 + 128, b0_:b0_ + bs], mo[b][:, 0:bs], reads=[f"s4_mo{b}"], writes=["dram:mT"],
                          key=f"s4_mo{b}")

                linear_stage("l4b", [(ymlaT, 1024, BF16, "dram:ymlaT", w_bmla)], grouped(chunks(0, D), 8), epiB, 1024, blocks=WB)
                P.barrier()
                P.stack = st0
            stop_if("stop4")
            with contextlib.ExitStack() as stx:
                P.stack = stx
                e5 = store_epi("l5", fT, "dram:fT", F32)
                linear_stage("l5", [(mT, D, BF16, "dram:mT", w_out)], grouped(chunks(0, D), 8), e5, 1024, blocks=WB)
                P.barrier()
                P.stack = st0
            rmsnorm_stage("n5", fT, D, 16, "dram:fT", resid=hwT, resid_res="dram:hwT", blocks=WB)
            stop_if("stop5")
            rmsnorm_stage("n6", hwT, D, 32, "dram:hwT", dst=uwT, dst_res="dram:uwT", blocks=WB)
            with contextlib.ExitStack() as stx:
                P.stack = stx
                NCH = 4
                ub = [[P.sb(f"s6_u{h}_{i}", [128, TB + 2], F32) for i in range(NCH)] for h in range(2)]
                cv = [P.sb(f"s6_cv{h}", [128, TB], F32) for h in range(2)]
                tq = P.sb("s6_tq", [128, TB], F32)
                fo = [P.sb(f"s6_fo{i}", [128, TB], BF16) for i in range(2)]
                stt = {"i": 0}

                def epi6(gi, ci, cw, j, ps, ps_res, blk):
                    c0, w = cw
                    b0_, bs = blk
                    half = 0 if ci < NCH else 1
                    cc = ci % NCH
                    u = ub[half][cc]
                    ur = f"s6_u{half}_{cc}"
                    if j == 0:
                        P.op("dve", lambda e: e.memset(u[:, 0:2], 0.0), [ur], [ur])
                    else:
                        P.op("dve", lambda e: e.tensor_copy(out=u[:, 0:2], in_=u[:, bs:bs + 2]), [ur], [ur])
                    evac_copy(u[:, 2:bs + 2], ps[:, 0:bs], [ps_res, ur], [ur])
                    if half == 1:
                        chn = c0 // 128
                        chg = chn - 44
                        for hh, ch in ((0, chg), (1, chn)):
                            uu = ub[hh][cc]
                            uur = f"s6_u{hh}_{cc}"
                            P.op("dve", lambda e, uu=uu, ch=ch, hh=hh: e.tensor_scalar(
                                out=cv[hh][:, 0:bs], in0=uu[:, 0:bs], scalar1=V[:, 256 + ch:257 + ch], scalar2=V[:, 168 + ch:169 + ch],
                                op0=ALU.mult, op1=ALU.add), [uur, "V"], [f"s6_cv{hh}"])
                            P.op("dve", lambda e, uu=uu, ch=ch, hh=hh: e.scalar_tensor_tensor(
                                out=cv[hh][:, 0:bs], in0=uu[:, 1:bs + 1], scalar=V[:, 344 + ch:345 + ch], in1=cv[hh][:, 0:bs],
                                op0=ALU.mult, op1=ALU.add), [uur, "V", f"s6_cv{hh}"], [f"s6_cv{hh}"])
                            P.op("dve", lambda e, uu=uu, ch=ch, hh=hh: e.scalar_tensor_tensor(
                                out=cv[hh][:, 0:bs], in0=uu[:, 2:bs + 2], scalar=V[:, 432 + ch:433 + ch], in1=cv[hh][:, 0:bs],
                                op0=ALU.mult, op1=ALU.add), [uur, "V", f"s6_cv{hh}"], [f"s6_cv{hh}"])
                        P.op("dve", lambda e: e.tensor_tensor(out=tq[:, 0:bs], in0=cv[0][:, 0:bs], in1=cv[0][:, 0:bs], op=ALU.mult),
                             ["s6_cv0"], ["s6_tq"])
                        P.op("dve", lambda e: e.tensor_scalar(out=tq[:, 0:bs], in0=tq[:, 0:bs], scalar1=0.044715, scalar2=1.0,
                                                               op0=ALU.mult, op1=ALU.add), ["s6_tq"], ["s6_tq"])
                        P.op("dve", lambda e: e.tensor_tensor(out=tq[:, 0:bs], in0=tq[:, 0:bs], in1=cv[0][:, 0:bs], op=ALU.mult),
                             ["s6_tq", "s6_cv0"], ["s6_tq"])
                        P.op("act", lambda e: e.activation(out=tq[:, 0:bs], in_=tq[:, 0:bs], func=AF.Sigmoid, scale=1.5957691216),
                             ["s6_tq"], ["s6_tq"])
                        P.op("dve", lambda e: e.tensor_tensor(out=tq[:, 0:bs], in0=tq[:, 0:bs], in1=cv[0][:, 0:bs], op=ALU.mult),
                             ["s6_tq", "s6_cv0"], ["s6_tq"])
                        b = stt["i"] % 2
                        stt["i"] += 1
                        P.op("dve", lambda e: e.tensor_tensor(out=fo[b][:, 0:bs], in0=tq[:, 0:bs], in1=cv[1][:, 0:bs], op=ALU.mult),
                             ["s6_tq", "s6_cv1"], [f"s6_fo{b}"])
                        P.dma("act", ffT[chg * 128:(chg + 1) * 128, b0_:b0_ + bs], fo[b][:, 0:bs], reads=[f"s6_fo{b}"],
                              writes=["dram:ffT"], key=f"s6_fo{b}")

                grps = []
                for g0 in range(0, 44, NCH):
                    grps.append(chunks(g0 * 128, NCH * 128) + chunks(DFF + g0 * 128, NCH * 128))
                linear_stage("l6", [(uwT, D, BF16, "dram:uwT", w_up)], grps, epi6, 2 * NCH * 128, blocks=WB)
                P.barrier()
                P.stack = st0
            with contextlib.ExitStack() as stx:
                P.stack = stx
                e7 = store_epi("l7", fT, "dram:fT", F32)
                linear_stage("l7", [(ffT, DFF, BF16, "dram:ffT", w_down)], grouped(chunks(0, D), 4), e7, 512, blocks=WB)
                P.barrier()
                P.stack = st0
            rmsnorm_stage("n7", fT, D, 48, "dram:fT", resid=hwT, resid_res="dram:hwT", blocks=WB)
            stop_if("stop7")
            PB = [(0, 512), (512, 512)]
            with contextlib.ExitStack() as stx:
                P.stack = stx
                e8 = store_epi("l8a", mT, "dram:mT", BF16, func=AF.Sigmoid)
                linear_stage("l8a", [(hwT[:, 2:TW], D, F32, "dram:hwT", w_pg)], grouped(chunks(0, D), 8), e8, 1024, blocks=PB)
                gl = [P.sb(f"s8_g{i}", [128, TB], BF16) for i in range(2)]
                eo = [P.sb(f"s8_eo{i}", [128, TB], F32) for i in range(2)]
                stt = {"i": 0}

                def epi8(gi, ci, cw, j, ps, ps_res, blk):
                    c0, w = cw
                    b0_, bs = blk
                    b = stt["i"] % 2
                    stt["i"] += 1
                    P.dma("sp", gl[b][:, 0:bs], mT[c0:c0 + 128, b0_:b0_ + bs], reads=["dram:mT"], writes=[f"s8_g{b}"],
                          key=f"s8_g{b}")
                    P.op("dve", lambda e: e.tensor_tensor(out=eo[b][:, 0:bs], in0=ps[:, 0:bs], in1=gl[b][:, 0:bs], op=ALU.mult),
                         [ps_res, f"s8_g{b}"], [f"s8_eo{b}"])
                    P.dma("act", fT[c0:c0 + 128, b0_:b0_ + bs], eo[b][:, 0:bs], reads=[f"s8_eo{b}"], writes=["dram:fT"],
                          key=f"s8_eo{b}")

                linear_stage("l8b", [(pT, 256, F32, "dram:pT", w_ple)], grouped(chunks(0, D), 16), epi8, 2048, blocks=PB)
                P.barrier()
                P.stack = st0
            rmsnorm_stage("n8", fT[:, 0:1024], D, 64, "dram:fT", resid=hwT[:, 2:TW], resid_res="dram:hwT", blocks=PB)
            transpose_stage("to", hwT[:, 2:TW], out, D, 1024, "dram:hwT", "dram:out")

        try:
            main_seq()
        except _Stop:
            P.stack = st0
        P.wait_all("sp", [f"dram:{n}" for n in (["out"] + list(debug)) if not n.startswith("skip") and not n.startswith("stop")])
        P.emit()
    return nc


def rwkv_stage(P, nc, st0, V, zrwT, yrwT, rw_w2, rw_a2, rw_g2, ident_bf, bones_bf, maskP, maskL, eye, rmask, omka,
               evac_copy):
    NEG_E = -math.exp(-0.5)
    with contextlib.ExitStack() as st:
        prev_stack = P.stack
        P.stack = st
        sb = P.sb
        w2b = sb("rk_w2b", [64, 1024], BF16)
        a2b = sb("rk_a2b", [64, 1024], BF16)
        g2b = sb("rk_g2b", [128, 2, 1024], BF16)
        P.dma("pool", w2b[:], rw_w2, writes=["rk_w"], key="rk_w")
        P.dma("pool", a2b[:], rw_a2, writes=["rk_w"], key="rk_w")
        P.dma("pool", g2b[:, 0, :], rw_g2[0:128, :], writes=["rk_w"], key="rk_w")
        P.dma("pool", g2b[0:32, 1, :], rw_g2[128:160, :], writes=["rk_w"], key="rk_w")
        twd = sb("rk_twd", [64, T], BF16)
        ads = sb("rk_ads", [64, T], BF16)
        sg1 = sb("rk_sg1", [128, T], BF16)
        sg2 = sb("rk_sg2", [32, T], BF16)
        zin = sb("rk_zin", [128, TB + 1], F32)
        dd = sb("rk_dd", [128, TB], F32)
        for (row0, nr, mcol, dst, func) in ((3072, 64, 520, twd, AF.Tanh), (3136, 64, 521, ads, AF.Copy),
                                            (3200, 128, 522, sg1, AF.Sigmoid), (3328, 32, 523, sg2, AF.Sigmoid)):
            for j in range(NB):
                if j == 0:
                    P.op("dve", lambda e: e.memset(zin[:, 0:1], 0.0), [], ["rk_zin"])
                    P.dma("sp", zin[0:nr, 1:TB + 1], zrwT[row0:row0 + nr, 0:TB], reads=["dram:zrwT"], writes=["rk_zin"],
                          key="rk_zin")
                else:
                    P.dma("sp", zin[0:nr, :], zrwT[row0:row0 + nr, j * TB - 1:(j + 1) * TB], reads=["dram:zrwT"],
                          writes=["rk_zin"], key="rk_zin")
                P.op("dve", lambda e, nr=nr: e.tensor_tensor(out=dd[0:nr, :], in0=zin[0:nr, 0:TB], in1=zin[0:nr, 1:TB + 1],
                                                             op=ALU.subtract), ["rk_zin"], ["rk_dd"])
                P.op("dve", lambda e, nr=nr, mcol=mcol: e.scalar_tensor_tensor(
                    out=dd[0:nr, :], in0=dd[0:nr, :], scalar=V[0:nr, mcol:mcol + 1], in1=zin[0:nr, 1:TB + 1],
                    op0=ALU.mult, op1=ALU.add), ["rk_dd", "rk_zin", "V"], ["rk_dd"])
                P.op("act", lambda e, nr=nr, dst=dst, func=func, j=j: e.activation(
                    out=dst[0:nr, j * TB:(j + 1) * TB], in_=dd[0:nr, :], func=func), ["rk_dd"], ["rk_lin"])
        zt = {n: sb(f"rk_z{n}", [128, TB + 1], F32) for n in "rkv"}
        xs = {n: sb(f"rk_s{n}", [128, TB], F32) for n in "rkv"}
        tA = sb("rk_tA", [128, TB], F32)
        tB = sb("rk_tB", [128, TB], F32)
        tC = sb("rk_tC", [128, TB], F32)
        av = sb("rk_a", [128, TB], F32)
        kkn = sb("rk_kkn", [128, TB], F32)
        k2 = sb("rk_k2", [128, TB], F32)
        logw = sb("rk_logw", [128, TB], F32)
        cum = sb("rk_cum", [128, TB], F32)
        ginv = sb("rk_ginv", [128, TB], F32)
        gprev = sb("rk_gprev", [128, TB], F32)
        tbf = sb("rk_tbf", [128, TB], BF16)
        vbf = sb("rk_vbf", [128, TB], BF16)
        BK = sb("rk_BK", [128, 8, 2, 64], BF16)
        gam2 = [sb(f"rk_gam{i}", [128, TB], F32) for i in range(2)]
        gv2 = [sb(f"rk_g{i}", [128, TB], F32) for i in range(2)]
        bonus2 = [sb(f"rk_bonus{i}", [128, TB], F32) for i in range(2)]
        AR2 = [sb(f"rk_AR{i}", [128, 8, 2, 64], BF16) for i in range(2)]
        PM2 = [sb(f"rk_PM{i}", [128, 8, 2, 128], BF16) for i in range(2)]
        TM2 = [sb(f"rk_TM{i}", [128, 8, 3, 64], BF16) for i in range(2)]
        MT2 = [sb(f"rk_MT{i}", [128, 8, 64], BF16) for i in range(2)]
        NS8 = sb("rk_NS", [128, 8, 2, 64], BF16)
        L8 = sb("rk_L", [128, 8, 64], BF16)
        maskP8 = sb("rk_maskP8", [128, 8, 2, 128], F32)
        maskL8 = sb("rk_maskL8", [128, 8, 64], F32)
        eye8 = sb("rk_eye8", [128, 8, 64], F32)
        for c in range(8):
            P.op("pool", lambda e, c=c: e.tensor_copy(out=maskP8[:, c, :, :], in_=maskP[:]), ["msk"], ["rk_m8"])
            P.op("pool", lambda e, c=c: e.tensor_copy(out=maskL8[:, c, :], in_=maskL[:]), ["msk"], ["rk_m8"])
            P.op("pool", lambda e, c=c: e.tensor_copy(out=eye8[:, c, :], in_=eye[:]), ["msk"], ["rk_m8"])
        Hf = sb("rk_Hf", [128, 64], F32)
        HG = sb("rk_HG", [128, 64], F32)
        Hb = sb("rk_Hb", [128, 64], BF16)
        Zs = sb("rk_Zs", [128, 64], BF16)
        Us = sb("rk_Us", [128, 64], BF16)
        Yt = sb("rk_Y", [128, TB], F32)
        yo = sb("rk_yo", [128, TB], BF16)
        ps_a = P.ps("rk_psa", [128, TB], F32)
        ps_b = ps_a
        W = P.ps("rk_Wps", [128, 2048], F32)
        W5 = P.ps("rk_W5ps", [128, 512], F32)
        small2 = P.ps("rk_small2", [128, 512], F32)
        yps = P.ps("rk_yps", [128, TB], F32)
        trpv = W[:, 0:1536].rearrange("p (c a t) -> p c a t", c=8, a=3)
        ppv = W[:, 0:2048].rearrange("p (c a t) -> p c a t", c=8, a=2)
        ps1v = W[:, 0:1024].rearrange("p (c t) -> p c t", c=8)
        ps2v = W[:, 1024:1536].rearrange("p (c t) -> p c t", c=8)
        lpv = W5[:, 0:512].rearrange("p (c t) -> p c t", c=8)
        sq = small2[:, 0:192].rearrange("p (a t) -> p a t", a=3)

        def dve(fn, r, w):
            P.op("dve", fn, r, w)

        def act(fn, r, w):
            P.op("act", fn, r, w)

        def pe(fn, r, w):
            P.op("pe", fn, r, w)

        def c3(ap):
            return ap.rearrange("p (c t) -> p c t", t=64)

        def blk_names(pr, j, par):
            pc = slice(pr * 128, (pr + 1) * 128)
            vcol = lambda base: V[:, base + pr:base + pr + 1]
            js = slice(j * TB, (j + 1) * TB)
            return (pc, vcol, js, gam2[par], gv2[par], bonus2[par], AR2[par], PM2[par], TM2[par], MT2[par],
                    f"rk_gam{par}", f"rk_g{par}", f"rk_bonus{par}", f"rk_AR{par}", f"rk_PM{par}", f"rk_TM{par}", f"rk_MT{par}")

        eA = sb("rk_eA", [128, TB], F32)
        eB = sb("rk_eB", [128, TB], F32)
        ebf = sb("rk_ebf", [128, TB], BF16)

        def gen_pre(pr, j, par):
            pc, vcol, js, gam, gv, bonus, AR, PM, TM, MTs, rGAM, rG, rBON, rAR, rPM, rTM, rMT = blk_names(pr, j, par)
            for gi, n in enumerate("rkv"):
                row0 = gi * 1024 + pr * 128
                if j == 0:
                    dve(lambda e, n=n: e.memset(zt[n][:, 0:1], 0.0), [], [f"rk_z{n}"])
                    P.dma("sp", zt[n][:, 1:TB + 1], zrwT[row0:row0 + 128, 0:TB], reads=["dram:zrwT"],
                          writes=[f"rk_z{n}"], key=f"rk_z{n}")
                else:
                    P.dma("sp", zt[n][:], zrwT[row0:row0 + 128, j * TB - 1:(j + 1) * TB], reads=["dram:zrwT"],
                          writes=[f"rk_z{n}"], key=f"rk_z{n}")
                mcol = 80 + gi * 8 + pr
                dve(lambda e, n=n: e.tensor_tensor(out=xs[n][:], in0=zt[n][:, 0:TB], in1=zt[n][:, 1:TB + 1],
                                                   op=ALU.subtract), [f"rk_z{n}"], [f"rk_s{n}"])
                dve(lambda e, n=n, mcol=mcol: e.scalar_tensor_tensor(
                    out=xs[n][:], in0=xs[n][:], scalar=V[:, mcol:mcol + 1], in1=zt[n][:, 1:TB + 1], op0=ALU.mult,
                    op1=ALU.add), [f"rk_s{n}", f"rk_z{n}", "V"], [f"rk_s{n}"])
            js = slice(j * TB, (j + 1) * TB)
            yield
            pe(lambda e: e.matmul(ps_a[:], lhsT=w2b[:, pc], rhs=twd[:, js], start=True, stop=True),
               ["rk_w", "rk_lin"], ["rk_psa"])
            act(lambda e: e.activation(out=logw[:], in_=ps_a[:], func=AF.Sigmoid, bias=vcol(104)), ["rk_psa", "V"],
                ["rk_logw"])
            dve(lambda e: e.tensor_scalar(out=logw[:], in0=logw[:], scalar1=NEG_E, scalar2=None, op0=ALU.mult),
                ["rk_logw"], ["rk_logw"])
            pe(lambda e: e.matmul(ps_b[:], lhsT=a2b[:, pc], rhs=ads[:, js], start=True, stop=True),
               ["rk_w", "rk_lin"], ["rk_psa"])
            act(lambda e: e.activation(out=av[:], in_=ps_b[:], func=AF.Sigmoid, bias=vcol(112)), ["rk_psa", "V"],
                ["rk_a"])
            pe(lambda e: e.matmul(ps_a[:], lhsT=g2b[:, 0, pc], rhs=sg1[:, js], start=True, stop=False),
               ["rk_w", "rk_lin"], ["rk_psa"])
            pe(lambda e: e.matmul(ps_a[:], lhsT=g2b[0:32, 1, pc], rhs=sg2[:, js], start=False, stop=True),
               ["rk_w", "rk_lin"], ["rk_psa"])
            act(lambda e: e.activation(out=gv[:], in_=ps_a[:], func=AF.Copy), ["rk_psa"], [rG])
            yield
            dve(lambda e: e.tensor_scalar(out=tA[:], in0=xs["k"][:], scalar1=vcol(120), scalar2=None, op0=ALU.mult),
                ["rk_sk", "V"], ["rk_tA"])
            dve(lambda e: e.tensor_tensor(out=tbf[:], in0=tA[:], in1=tA[:], op=ALU.mult), ["rk_tA"], ["rk_tbf"])
            pe(lambda e: e.matmul(ps_b[:], lhsT=bones_bf[:], rhs=tbf[:], start=True, stop=True),
               ["bones_bf", "rk_tbf"], ["rk_psa"])
            act(lambda e: e.activation(out=tB[:], in_=ps_b[:], func=AF.Sqrt), ["rk_psa"], ["rk_tB"])
            dve(lambda e: e.tensor_scalar(out=tB[:], in0=tB[:], scalar1=1e-12, scalar2=None, op0=ALU.max),
                ["rk_tB"], ["rk_tB"])
            dve(lambda e: e.reciprocal(out=tB[:], in_=tB[:]), ["rk_tB"], ["rk_tB"])
            dve(lambda e: e.tensor_tensor(out=kkn[:], in0=tA[:], in1=tB[:], op=ALU.mult), ["rk_tA", "rk_tB"],
                ["rk_kkn"])
            yield
            dve(lambda e: e.tensor_scalar(out=tA[:], in0=av[:], scalar1=vcol(128), scalar2=omka[:, pr:pr + 1],
                                          op0=ALU.mult, op1=ALU.add), ["rk_a", "V", "omka"], ["rk_tA"])
            dve(lambda e: e.tensor_tensor(out=k2[:], in0=xs["k"][:], in1=tA[:], op=ALU.mult), ["rk_sk", "rk_tA"],
                ["rk_k2"])
            yield
            dve(lambda e: e.tensor_tensor_scan(out=cum[:], data0=rmask[:], data1=logw[:], initial=0.0, op0=ALU.mult,
                                               op1=ALU.add), ["msk", "rk_logw"], ["rk_cum"])
            act(lambda e: e.activation(out=gam[:], in_=cum[:], func=AF.Exp), ["rk_cum"], [rGAM])
            act(lambda e: e.activation(out=ginv[:], in_=cum[:], func=AF.Exp, scale=-1.0), ["rk_cum"], ["rk_ginv"])
            dve(lambda e: e.tensor_tensor(out=tB[:], in0=cum[:], in1=logw[:], op=ALU.subtract),
                ["rk_cum", "rk_logw"], ["rk_tB"])
            act(lambda e: e.activation(out=gprev[:], in_=tB[:], func=AF.Exp), ["rk_tB"], ["rk_gprev"])
            yield
            dve(lambda e: e.scalar_tensor_tensor(out=AR[:, :, 0, :], in0=c3(kkn[:]), scalar=-1.0, in1=c3(gprev[:]),
                                                 op0=ALU.mult, op1=ALU.mult), ["rk_kkn", "rk_gprev"], [rAR])
            dve(lambda e: e.tensor_tensor(out=AR[:, :, 1, :], in0=c3(xs["r"][:]), in1=c3(gam[:]), op=ALU.mult),
                ["rk_sr", rGAM, rAR], [rAR])
            dve(lambda e: e.tensor_tensor(out=tA[:], in0=kkn[:], in1=av[:], op=ALU.mult), ["rk_kkn", "rk_a"],
                ["rk_tA"])
            dve(lambda e: e.tensor_tensor(out=BK[:, :, 0, :], in0=c3(tA[:]), in1=c3(ginv[:]), op=ALU.mult),
                ["rk_tA", "rk_ginv"], ["rk_BK"])
            dve(lambda e: e.tensor_tensor(out=BK[:, :, 1, :], in0=c3(k2[:]), in1=c3(ginv[:]), op=ALU.mult),
                ["rk_k2", "rk_ginv", "rk_BK"], ["rk_BK"])
            act(lambda e: e.activation(out=vbf[:], in_=xs["v"][:], func=AF.Copy), ["rk_sv"], ["rk_vbf"])
            yield
            dve(lambda e: e.tensor_tensor(out=tC[:], in0=xs["r"][:], in1=k2[:], op=ALU.mult), ["rk_sr", "rk_k2"],
                ["rk_tC"])
            dve(lambda e: e.tensor_scalar(out=tbf[:], in0=tC[:], scalar1=vcol(136), scalar2=None, op0=ALU.mult),
                ["rk_tC", "V", "rk_tbf"], ["rk_tbf"])
            pe(lambda e: e.matmul(ps_b[:], lhsT=bones_bf[:], rhs=tbf[:], start=True, stop=True),
               ["bones_bf", "rk_tbf"], ["rk_psa"])
            dve(lambda e: e.tensor_tensor(out=bonus[:], in0=ps_b[:], in1=xs["v"][:], op=ALU.mult),
                ["rk_psa", "rk_sv"], [rBON])
            yield
            yield
            for c in range(8):
                cc = slice(c * 64, (c + 1) * 64)
                for h in range(2):
                    sl = slice(64 * h, 64 * h + 64)
                    for ti in range(3):
                        in_ap = BK[sl, c, ti, :] if ti < 2 else vbf[sl, cc]
                        pe(lambda e, sl=sl, ti=ti, in_ap=in_ap, c=c: e.matmul(trpv[sl, c, ti, :], lhsT=in_ap,
                                                                             rhs=ident_bf[sl, sl], start=True, stop=True),
                           ["rk_BK", "rk_vbf", "ident_bf"], ["rk_Wps"])
            act(lambda e: e.activation(out=TM[:], in_=trpv, func=AF.Copy), ["rk_Wps"], [rTM])
            yield
            for c in range(8):
                for h in range(2):
                    sl = slice(64 * h, 64 * h + 64)
                    arv = AR[sl, c, :, :].rearrange("p a t -> p (a t)")
                    pe(lambda e, sl=sl, c=c, arv=arv: e.matmul(ppv[sl, c, 0, :], lhsT=BK[sl, c, 0, :], rhs=arv, start=True,
                                                               stop=True), ["rk_BK", rAR], ["rk_Wps"])
                    pe(lambda e, sl=sl, c=c, arv=arv: e.matmul(ppv[sl, c, 1, :], lhsT=BK[sl, c, 1, :], rhs=arv, start=True,
                                                               stop=True), ["rk_BK", rAR], ["rk_Wps"])
                    pe(lambda e, sl=sl, c=c: e.matmul(lpv[sl, c, :], lhsT=AR[sl, c, 0, :], rhs=BK[sl, c, 0, :], start=True,
                                                      stop=True), ["rk_BK", rAR], ["rk_W5ps"])
            dve(lambda e: e.tensor_tensor(out=PM[:], in0=ppv, in1=maskP8[:], op=ALU.mult), ["rk_Wps", "rk_m8"], [rPM])
            dve(lambda e: e.tensor_tensor(out=L8[:], in0=lpv, in1=maskL8[:], op=ALU.mult), ["rk_W5ps", "rk_m8"], ["rk_L"])
            act(lambda e: e.activation(out=NS8[:, :, 0, :], in_=PM[:, :, 0, 0:64], func=AF.Copy), [rPM], ["rk_NS"])
            P.op("pool", lambda e: e.tensor_copy(out=NS8[:, :, 1, :], in_=eye8[:]), ["rk_m8", "rk_NS"], ["rk_NS"])
            yield
            for step in range(6):
                last = step == 5
                for c in range(8):
                    for h in range(2):
                        sl = slice(64 * h, 64 * h + 64)
                        if not last:
                            nsv = NS8[sl, c, :, :].rearrange("p a t -> p (a t)")
                            pe(lambda e, sl=sl, nsv=nsv, c=c: e.matmul(ps1v[sl, c, :], lhsT=L8[sl, c, :], rhs=nsv, start=True,
                                                                       stop=True), ["rk_L", "rk_NS"], ["rk_Wps"])
                            pe(lambda e, sl=sl, c=c: e.matmul(ps2v[sl, c, :], lhsT=NS8[sl, c, 0, :], rhs=L8[sl, c, :],
                                                              start=True, stop=True), ["rk_L", "rk_NS"], ["rk_Wps"])
                        else:
                            pe(lambda e, sl=sl, c=c: e.matmul(ps1v[sl, c, 64:128], lhsT=L8[sl, c, :], rhs=NS8[sl, c, 1, :],
                                                              start=True, stop=True), ["rk_L", "rk_NS"], ["rk_Wps"])
                if not last:
                    dve(lambda e: e.tensor_copy(out=NS8[:, :, 0, :], in_=ps1v[:, :, 0:64]), ["rk_Wps", "rk_NS"], ["rk_NS"])
                    dve(lambda e: e.tensor_tensor(out=NS8[:, :, 1, :], in0=NS8[:, :, 1, :], in1=ps1v[:, :, 64:128],
                                                  op=ALU.add), ["rk_Wps", "rk_NS"], ["rk_NS"])
                    dve(lambda e: e.tensor_copy(out=L8[:], in_=ps2v), ["rk_Wps", "rk_L"], ["rk_L"])
                else:
                    dve(lambda e: e.tensor_tensor(out=MTs[:], in0=NS8[:, :, 1, :], in1=ps1v[:, :, 64:128], op=ALU.add),
                        ["rk_Wps", "rk_NS"], [rMT])
                yield
            yield

        def gen_seq(pr, j, par):
            pc, vcol, js, gam, gv, bonus, AR, PM, TM, MTs, rGAM, rG, rBON, rAR, rPM, rTM, rMT = blk_names(pr, j, par)
            if j == 0:
                dve(lambda e: e.memset(Hf[:], 0.0), [], ["rk_Hf"])
                dve(lambda e: e.memset(Hb[:], 0.0), [], ["rk_Hb"])
            for c in range(8):
                cc = slice(c * 64, (c + 1) * 64)
                gcol = gam[:, c * 64 + 63:c * 64 + 64]
                for h in range(2):
                    sl = slice(64 * h, 64 * h + 64)
                    pe(lambda e, sl=sl, c=c: e.matmul(sq[sl, 0, :], lhsT=AR[sl, c, 0, :], rhs=Hb[sl, :], start=True,
                                                      stop=False), [rAR, "rk_Hb"], ["rk_small2"])
                    pe(lambda e, sl=sl, c=c: e.matmul(sq[sl, 0, :], lhsT=PM[sl, c, 1, 0:64], rhs=TM[sl, c, 2, :],
                                                      start=False, stop=True), [rPM, rTM], ["rk_small2"])
                act(lambda e: e.activation(out=Zs[:], in_=sq[:, 0, :], func=AF.Copy), ["rk_small2"], ["rk_Zs"])
                for h in range(2):
                    sl = slice(64 * h, 64 * h + 64)
                    pe(lambda e, sl=sl, c=c: e.matmul(sq[sl, 1, :], lhsT=MTs[sl, c, :], rhs=Zs[sl, :], start=True,
                                                      stop=True), [rMT, "rk_Zs"], ["rk_small2"])
                act(lambda e: e.activation(out=Us[:], in_=sq[:, 1, :], func=AF.Copy), ["rk_small2"], ["rk_Us"])
                for h in range(2):
                    sl = slice(64 * h, 64 * h + 64)
                    pe(lambda e, sl=sl, c=c: e.matmul(sq[sl, 2, :], lhsT=TM[sl, c, 0, :], rhs=Us[sl, :], start=True,
                                                      stop=False), [rTM, "rk_Us"], ["rk_small2"])
                    pe(lambda e, sl=sl, c=c: e.matmul(sq[sl, 2, :], lhsT=TM[sl, c, 1, :], rhs=TM[sl, c, 2, :],
                                                      start=False, stop=True), [rTM], ["rk_small2"])
                    pe(lambda e, sl=sl, c=c, cc=cc: e.matmul(yps[sl, cc], lhsT=Hb[sl, :], rhs=AR[sl, c, 1, :], start=True,
                                                             stop=False), [rAR, "rk_Hb"], ["rk_yps"])
                    pe(lambda e, sl=sl, c=c, cc=cc: e.matmul(yps[sl, cc], lhsT=Us[sl, :], rhs=PM[sl, c, 0, 64:128],
                                                             start=False, stop=False), ["rk_Us", rPM], ["rk_yps"])
                    pe(lambda e, sl=sl, c=c, cc=cc: e.matmul(yps[sl, cc], lhsT=TM[sl, c, 2, :], rhs=PM[sl, c, 1, 64:128],
                                                             start=False, stop=True), [rTM, rPM], ["rk_yps"])
                P.op("pool", lambda e, gcol=gcol: e.tensor_scalar(out=HG[:], in0=Hf[:], scalar1=gcol, scalar2=None,
                                                                   op0=ALU.mult), ["rk_Hf", rGAM], ["rk_HG"])
                dve(lambda e, gcol=gcol: e.scalar_tensor_tensor(out=Hf[:], in0=sq[:, 2, :], scalar=gcol, in1=HG[:],
                                                                op0=ALU.mult, op1=ALU.add),
                    ["rk_small2", "rk_HG", rGAM, "rk_Hf"], ["rk_Hf"])
                act(lambda e: e.activation(out=Hb[:], in_=Hf[:], func=AF.Copy), ["rk_Hf", "rk_Hb"], ["rk_Hb"])
                yield
            act(lambda e: e.activation(out=Yt[:], in_=yps[:], func=AF.Copy), ["rk_yps"], ["rk_Y"])
            dve(lambda e: e.tensor_copy(out=ebf[:], in_=Yt[:]), ["rk_Y", "rk_ebf"], ["rk_ebf"])
            pe(lambda e: e.matmul(ps_a[:], lhsT=bones_bf[:], rhs=ebf[:], start=True, stop=True), ["bones_bf", "rk_ebf"],
               ["rk_psa"])
            dve(lambda e: e.scalar_tensor_tensor(out=eA[:], in0=ps_a[:], scalar=-1.0 / 64, in1=Yt[:], op0=ALU.mult,
                                                 op1=ALU.add), ["rk_psa", "rk_Y"], ["rk_eA"])
            dve(lambda e: e.tensor_tensor(out=ebf[:], in0=eA[:], in1=eA[:], op=ALU.mult), ["rk_eA", "rk_ebf"],
                ["rk_ebf"])
            pe(lambda e: e.matmul(ps_a[:], lhsT=bones_bf[:], rhs=ebf[:], start=True, stop=True), ["bones_bf", "rk_ebf"],
               ["rk_psa"])
            act(lambda e: e.activation(out=eB[:], in_=ps_a[:], func=AF.Sqrt, bias=GN_EPS, scale=1.0 / 64),
                ["rk_psa"], ["rk_eB"])
            yield
            dve(lambda e: e.reciprocal(out=eB[:], in_=eB[:]), ["rk_eB"], ["rk_eB"])
            dve(lambda e: e.tensor_tensor(out=eA[:], in0=eA[:], in1=eB[:], op=ALU.mult), ["rk_eA", "rk_eB"], ["rk_eA"])
            dve(lambda e: e.tensor_scalar(out=eA[:], in0=eA[:], scalar1=vcol(144), scalar2=vcol(152), op0=ALU.mult,
                                          op1=ALU.add), ["rk_eA", "V"], ["rk_eA"])
            dve(lambda e: e.tensor_tensor(out=eA[:], in0=eA[:], in1=bonus[:], op=ALU.add), ["rk_eA", rBON],
                ["rk_eA"])
            dve(lambda e: e.tensor_tensor(out=yo[:], in0=eA[:], in1=gv[:], op=ALU.mult), ["rk_eA", rG], ["rk_yo"])
            P.dma("act", yrwT[pc, js], yo[:], reads=["rk_yo"], writes=["dram:yrwT"], key="rk_yo")
            yield

        blks = []
        for pr in range(DBG.get("rkp", 8)):
            for j in range(DBG.get("rkb", NB)):
                blks.append((pr, j, len(blks) % 2))
        prevb = None
        for blk in blks + [None]:
            gens = []
            if blk is not None:
                gens.append(gen_pre(*blk))
            if prevb is not None:
                gens.append(gen_seq(*prevb))
            while gens:
                for g in list(gens):
                    try:
                        next(g)
                    except StopIteration:
                        gens.remove(g)
            prevb = blk
        P.barrier()
        P.stack = prev_stack


def mla_stage(P, nc, st0, V, pos, pos_w, qk_tab, d_tab, zmlaT, cqnT_full, ckvnT, w_q_up, w_kv_up, ymlaT, ones_bf, evac_copy):
    SCALE = 192 ** -0.5
    TWO_PI = 2.0 * math.pi
    QB = 342
    with contextlib.ExitStack() as st:
        prev_stack = P.stack
        P.stack = st
        sb = P.sb
        wq = sb("ml_wq", [128, 4, 1536], BF16)
        wkv = sb("ml_wkv", [128, 4, 2048], BF16)
        for kc in range(4):
            P.dma("pool", wq[:, kc, :], w_q_up[kc * 128:(kc + 1) * 128, :], writes=["ml_w"], key="ml_w")
            P.dma("pool", wkv[:, kc, :], w_kv_up[kc * 128:(kc + 1) * 128, :], writes=["ml_w"], key="ml_w")
        qk = sb("ml_qk", [128, 3, QB], F32)
        dt = sb("ml_dt", [128, 32], F32)
        P.dma("sp", qk[:], qk_tab.rearrange("p (a t) -> p a t", a=3), writes=["ml_tab"], key="ml_tab")
        P.dma("sp", dt[:], d_tab, writes=["ml_tab"], key="ml_tab")
        cs = sb("ml_cs", [64, T], F32)
        sn = sb("ml_sn", [64, T], F32)
        csw = sb("ml_csw", [64, TW], F32)
        snw = sb("ml_snw", [64, TW], F32)
        with contextlib.ExitStack() as st2:
            P.stack = st2
            posi = P.sb("ml_posi", [64, T], I32)
            ang = P.sb("ml_ang", [64, T], F32)
            tt = P.sb("ml_tt", [64, T], F32)
            kf = P.sb("ml_kf", [64, T], F32)
            for (psrc, n, cdst, sdst) in ((pos, T, cs, sn), (pos_w, TW, csw, snw)):
                P.dma("sp", posi[:, 0:n], psrc[0:1, :].to_broadcast([64, n]), writes=["ml_posi"], key="ml_posi")
                P.op("dve", lambda e: e.tensor_copy(out=ang[:, 0:n], in_=posi[:, 0:n]), ["ml_posi"], ["ml_ang"])
                P.op("dve", lambda e: e.tensor_scalar(out=ang[:, 0:n], in0=ang[:, 0:n], scalar1=V[0:64, 524:525], scalar2=None,
                                                      op0=ALU.mult), ["ml_ang", "V"], ["ml_ang"])
                for (dst, shift) in ((sdst, 0.5), (cdst, 0.75)):
                    P.op("dve", lambda e, shift=shift: e.tensor_scalar(out=tt[:, 0:n], in0=ang[:, 0:n], scalar1=1.0 / TWO_PI,
                                                                       scalar2=shift, op0=ALU.mult, op1=ALU.add),
                         ["ml_ang", "ml_tt"], ["ml_tt"])
                    P.op("dve", lambda e: e.tensor_copy(out=posi[:, 0:n], in_=tt[:, 0:n]), ["ml_tt", "ml_posi", "ml_ang"],
                         ["ml_posi"])
                    P.op("dve", lambda e: e.tensor_copy(out=kf[:, 0:n], in_=posi[:, 0:n]), ["ml_posi"], ["ml_kf"])
                    P.op("dve", lambda e: e.tensor_tensor(out=tt[:, 0:n], in0=tt[:, 0:n], in1=kf[:, 0:n], op=ALU.subtract),
                         ["ml_tt", "ml_kf"], ["ml_tt"])
                    P.op("dve", lambda e: e.tensor_scalar(out=kf[:, 0:n], in0=tt[:, 0:n], scalar1=0.0, scalar2=None,
                                                          op0=ALU.is_lt), ["ml_tt", "ml_kf"], ["ml_kf"])
                    P.op("dve", lambda e: e.scalar_tensor_tensor(out=tt[:, 0:n], in0=kf[:, 0:n], scalar=-0.5, in1=tt[:, 0:n],
                                                                 op0=ALU.add, op1=ALU.add), ["ml_tt", "ml_kf"], ["ml_tt"])
                    P.op("act", lambda e, dst=dst: e.activation(out=dst[:, 0:n], in_=tt[:, 0:n], func=AF.Sin, scale=TWO_PI),
                         ["ml_tt"], ["ml_rope"])
            P.barrier()
            P.stack = st
        kr = sb("ml_kr", [64, T], BF16)
        kn = sb("ml_kn", [128, T], BF16)
        vtm = sb("ml_vtm", [128, 32, 128], BF16)
        xq = [sb(f"ml_xq{i}", [128, 4, TB], BF16) for i in range(2)]
        raw = sb("ml_raw", [64, TB], F32)
        t1 = sb("ml_t1", [64, TB], F32)
        t2 = sb("ml_t2", [64, TB], F32)
        qn = sb("ml_qn", [128, TB], BF16)
        qr = sb("ml_qr", [64, TB], BF16)
        pT = [sb(f"ml_pT{i}", [128, TB], BF16) for i in range(3)]
        rinv = sb("ml_rinv", [128, TB], F32)
        yo = [sb(f"ml_yo{i}", [128, TB], BF16) for i in range(2)]
        psq = P.ps("ml_psq", [128, TB], F32)
        psr = P.ps("ml_psr", [64, TB], F32)
        psv = P.ps("ml_psv", [128, 4, 128], F32)
        sps = [P.ps(f"ml_sps{i}", [128, TB], F32) for i in range(2)]
        ops_ = P.ps("ml_ops", [128, TB], F32)
        lps = P.ps("ml_lps", [128, TB], F32)

        def rope(src_res, cst, snt, c0, n, out_ap, out_res, scale):
            lo, hi = slice(0, 32), slice(32, 64)
            js = slice(c0, c0 + n)
            d = lambda fn, r, w: P.op("dve", fn, r, w)
            d(lambda e: e.tensor_tensor(out=t1[lo, 0:n], in0=raw[lo, 0:n], in1=cst[lo, js], op=ALU.mult), [src_res, "ml_rope"],
              ["ml_t1"])
            d(lambda e: e.tensor_tensor(out=t2[lo, 0:n], in0=raw[hi, 0:n], in1=snt[hi, js], op=ALU.mult), [src_res, "ml_rope"],
              ["ml_t2"])
            d(lambda e: e.tensor_tensor(out=t1[hi, 0:n], in0=raw[hi, 0:n], in1=cst[hi, js], op=ALU.mult),
              [src_res, "ml_rope", "ml_t1"], ["ml_t1"])
            d(lambda e: e.tensor_tensor(out=t2[hi, 0:n], in0=raw[lo, 0:n], in1=snt[lo, js], op=ALU.mult),
              [src_res, "ml_rope", "ml_t2"], ["ml_t2"])
            d(lambda e: e.tensor_tensor(out=t1[lo, 0:n], in0=t1[lo, 0:n], in1=t2[lo, 0:n], op=ALU.subtract), ["ml_t1", "ml_t2"],
              ["ml_t1"])
            d(lambda e: e.tensor_tensor(out=t1[hi, 0:n], in0=t1[hi, 0:n], in1=t2[hi, 0:n], op=ALU.add), ["ml_t1", "ml_t2"],
              ["ml_t1"])
            P.op("act", lambda e: e.activation(out=out_ap, in_=t1[:, 0:n], func=AF.Copy, scale=scale), ["ml_t1"], [out_res])

        for j in range(NB):
            js = slice(j * TB, (j + 1) * TB)
            P.dma("sp", raw[:], zmlaT[1024:1088, js], reads=["dram:zmlaT"], writes=["ml_raw"], key="ml_raw")
            rope("ml_raw", cs, sn, j * TB, TB, kr[:, js], "ml_kr", 1.0)
        xi = 0
        for hd in range(DBG.get("mlh", 8)):
            for j in range(NB):
                js = slice(j * TB, (j + 1) * TB)
                b = xi % 2
                xi += 1
                P.dma("sp", xq[b][:], ckvnT[:, js].rearrange("(k p) t -> p k t", p=128), reads=["dram:ckvnT"],
                      writes=[f"ml_xq{b}"], key=f"ml_xq{b}")
                for kc in range(4):
                    P.op("pe", lambda e, kc=kc, b=b: e.matmul(psq[:], lhsT=wkv[:, kc, hd * 256:hd * 256 + 128],
                                                              rhs=xq[b][:, kc, :], start=(kc == 0), stop=(kc == 3)),
                         ["ml_w", f"ml_xq{b}"], ["ml_psq"])
                evac_copy(kn[:, js], psq[:], ["ml_psq"], ["ml_kn"])
                for tt_ in range(4):
                    for kc in range(4):
                        P.op("pe", lambda e, kc=kc, b=b, tt_=tt_: e.matmul(
                            psv[:, tt_, :], lhsT=xq[b][:, kc, tt_ * 128:(tt_ + 1) * 128],
                            rhs=wkv[:, kc, hd * 256 + 128:hd * 256 + 256], start=(kc == 0), stop=(kc == 3)),
                            ["ml_w", f"ml_xq{b}"], ["ml_psv"])
                evac_copy(vtm[:, j * 4:(j + 1) * 4, :], psv[:], ["ml_psv"], ["ml_vtm"])
            for jj in range(DBG.get("mlb", 3)):
                n = QB
                b = xi % 2
                xi += 1

                P.dma("sp", xq[b][:, :, 0:QB], cqnT_full[:, jj * QB:(jj + 1) * QB].rearrange("(k p) t -> p k t", p=128),
                      reads=["dram:cqwT"], writes=[f"ml_xq{b}"], key=f"ml_xq{b}")
                for kc in range(4):
                    P.op("pe", lambda e, kc=kc, b=b: e.matmul(psq[:, 0:n], lhsT=wq[:, kc, hd * 192:hd * 192 + 128],
                                                              rhs=xq[b][:, kc, 0:n], start=(kc == 0), stop=(kc == 3)),
                         ["ml_w", f"ml_xq{b}"], ["ml_psq"])
                P.op("act", lambda e: e.activation(out=qn[:, 0:n], in_=psq[:, 0:n], func=AF.Copy, scale=SCALE), ["ml_psq"],
                     ["ml_qn"])
                for kc in range(4):
                    P.op("pe", lambda e, kc=kc, b=b: e.matmul(psr[:, 0:n], lhsT=wq[:, kc, hd * 192 + 128:hd * 192 + 192],
                                                              rhs=xq[b][:, kc, 0:n], start=(kc == 0), stop=(kc == 3)),
                         ["ml_w", f"ml_xq{b}"], ["ml_psr"])
                P.op("act", lambda e: e.activation(out=raw[:, 0:n], in_=psr[:, 0:n], func=AF.Copy), ["ml_psr", "ml_raw"],
                     ["ml_raw"])
                rope("ml_raw", csw, snw, jj * QB, n, qr[:, 0:n], "ml_qr", SCALE)
                nkt = 32

                def score(kt):
                    si = kt % 2
                    ks = slice(kt * 128, (kt + 1) * 128)
                    P.op("pe", lambda e, si=si, ks=ks: e.matmul(sps[si][:, 0:n], lhsT=kn[:, ks], rhs=qn[:, 0:n],
                                                                start=True, stop=False),
                         ["ml_kn", "ml_qn"], [f"ml_sps{si}"])
                    P.op("pe", lambda e, si=si, ks=ks: e.matmul(sps[si][:, 0:n], lhsT=kr[:, ks], rhs=qr[:, 0:n],
                                                                start=False, stop=True),
                         ["ml_kr", "ml_qr"], [f"ml_sps{si}"])

                score(0)
                for kt in range(nkt):
                    si = kt % 2
                    pi = kt % 3
                    if kt + 1 < nkt:
                        score(kt + 1)
                    P.op("act", lambda e, si=si, pi=pi: e.activation(out=pT[pi][:, 0:n], in_=sps[si][:, 0:n], func=AF.Exp),
                         [f"ml_sps{si}"], [f"ml_pT{pi}"])
                    P.op("dve", lambda e, pi=pi, kt=kt, jj=jj: e.scalar_tensor_tensor(
                        out=pT[pi][:, 0:n], in0=qk[:, jj, :], scalar=dt[:, kt:kt + 1], in1=pT[pi][:, 0:n], op0=ALU.is_ge,
                        op1=ALU.mult), [f"ml_pT{pi}", "ml_tab"], [f"ml_pT{pi}"])
                    P.op("pe", lambda e, pi=pi, kt=kt: e.matmul(
                        ops_[:, 0:n], lhsT=vtm[:, kt, :], rhs=pT[pi][:, 0:n], start=(kt == 0), stop=(kt == nkt - 1)),
                        ["ml_vtm", f"ml_pT{pi}"], ["ml_ops"])
                    P.op("pe", lambda e, pi=pi, kt=kt: e.matmul(
                        lps[:, 0:n], lhsT=ones_bf[:], rhs=pT[pi][:, 0:n], start=(kt == 0), stop=(kt == nkt - 1)),
                        ["ones_bf", f"ml_pT{pi}"], ["ml_lps"])
                P.op("dve", lambda e: e.tensor_scalar(out=rinv[:, 0:n], in0=lps[:, 0:n], scalar1=1e-30, scalar2=None,
                                                      op0=ALU.max), ["ml_lps"], ["ml_rinv"])
                P.op("dve", lambda e: e.reciprocal(out=rinv[:, 0:n], in_=rinv[:, 0:n]), ["ml_rinv"], ["ml_rinv"])
                ob = jj % 2
                P.op("dve", lambda e, ob=ob: e.tensor_tensor(out=yo[ob][:, 0:n], in0=ops_[:, 0:n], in1=rinv[:, 0:n], op=ALU.mult),
                     ["ml_ops", "ml_rinv"], [f"ml_yo{ob}"])
                P.dma("act", ymlaT[hd * 128:(hd + 1) * 128, jj * QB:(jj + 1) * QB], yo[ob][:, 0:n], reads=[f"ml_yo{ob}"],
                      writes=["dram:ymlaT"], key=f"ml_yo{ob}")
        P.barrier()
        P.stack = prev_stack


_CACHE = {}


def make_in_maps(inputs):
    sq = lambda a: np.ascontiguousarray(np.asarray(a)[0])
    V = pack_vecs(inputs)
    common = {
        "vecs": V,
        "w_in": sq(inputs["w_in"]), "rw_w2": sq(inputs["rw_w2"]), "rw_a2": sq(inputs["rw_a2"]), "rw_g2": sq(inputs["rw_g2"]),
        "w_q_up": sq(inputs["mla_w_q_up"]), "w_kv_up": sq(inputs["mla_w_kv_up"]),
        "w_brw": sq(inputs["w_branch_rw"]), "w_bmla": sq(inputs["w_branch_mla"]), "w_out": sq(inputs["w_out"]),
        "w_up": sq(inputs["w_up"]), "w_down": sq(inputs["w_down"]), "w_ple": sq(inputs["w_ple"]),
        "w_pg": sq(inputs["w_ple_gate"]),
    }
    i_ = np.arange(342)
    qk = np.zeros((128, 3, 342), np.float32)
    for jj in range(3):
        qc = np.floor_divide(342 * jj - 2 + i_, 64).astype(np.float32)
        qk[:, jj, :] = qc[None, :] - (np.arange(128)[:, None] >= 64).astype(np.float32)
    common["qk_tab"] = np.ascontiguousarray(qk.reshape(128, 3 * 342))
    xs = np.asarray(inputs["x"], np.float32)
    ps = np.asarray(inputs["p"], np.float32)[0]
    posn = np.asarray(inputs["positions"], np.int32)
    maps = []
    for c in range(8):
        b, q = c // 4, c % 4
        m = dict(common)
        m["x"] = np.ascontiguousarray(xs[b])
        m["p"] = np.ascontiguousarray(ps[b, 1024 * q:1024 * (q + 1)])
        m["pos"] = np.ascontiguousarray(posn[b:b + 1])
        pw = np.zeros((1, TW), np.int32)
        lo = 1024 * q - 2
        if lo < 0:
            pw[0, 2:] = posn[b, 0:1024]
        else:
            pw[0, :] = posn[b, lo:lo + TW]
        m["pos_w"] = pw
        m["d_tab"] = np.ascontiguousarray(np.broadcast_to((2.0 * np.arange(32) - 16.0 * q).astype(np.float32)[None, :], (128, 32)))
        maps.append(m)
    return maps


def kernel(**inputs):
    if "nc" not in _CACHE:
        _CACHE["nc"] = build_program()
    nc = _CACHE["nc"]
    maps = make_in_maps(inputs)
    res = run_bass_kernel_spmd(nc, maps, core_ids=list(range(8)))
    outs = [np.asarray(res.results[c]["out"], np.float32) for c in range(8)]
    return np.stack([np.concatenate(outs[0:4], axis=0), np.concatenate(outs[4:8], axis=0)], axis=0)
```

```python
import contextlib
import numpy as np
import concourse.bass as bass
import concourse.mybir as mybir

F32 = mybir.dt.float32
BF16 = mybir.dt.bfloat16
I32 = mybir.dt.int32
ALU = mybir.AluOpType
AF = mybir.ActivationFunctionType
AX = mybir.AxisListType


SAME_ENGINE_SYNC = True


class _Rec:
    def __init__(self):
        self.calls = []

    def __getattr__(self, name):
        def f(*a, **k):
            self.calls.append((name, a, k))
            return self

        return f


def _eager(fn):
    rec = _Rec()
    fn(rec)
    assert len(rec.calls) == 1, rec.calls
    name, a, k = rec.calls[0]
    return lambda e: getattr(e, name)(*a, **k)


class Prog:
    COMPUTE = ("pe", "act", "dve", "pool")

    def __init__(self, nc, stack):
        self.nc = nc
        self.stack = stack
        self.stack0 = stack
        self.streams = {e: [] for e in ("pe", "act", "dve", "pool", "sp")}
        self.sems = {}
        self.cnt = {}
        for e in self.COMPUTE:
            self.sems[e] = stack.enter_context(nc.semaphore("s_" + e))
            self.cnt[e] = 0
        self.seen = {e: {} for e in self.streams}
        self.res = {}
        self.dma_sems = {}
        self.n_ops = 0

    def sb(self, name, shape, dt):
        return self.stack.enter_context(self.nc.sbuf_tensor(name, list(shape), dt))

    def ps(self, name, shape, dt=F32):
        return self.stack.enter_context(self.nc.psum_tensor(name, list(shape), dt))

    def _dma_sem(self, key):
        if key not in self.dma_sems:
            self.dma_sems[key] = self.stack0.enter_context(self.nc.semaphore("d_" + key))
            self.sems["D:" + key] = self.dma_sems[key]
            self.cnt["D:" + key] = 0
        return "D:" + key

    def _deps(self, eng, reads, writes, exclude=None):
        need = {}
        for r in reads:
            ent = self.res.get(r)
            if ent:
                for k, v in ent[0].items():
                    need[k] = max(need.get(k, 0), v)
                if "ps" in r or "small" in r or "trp" in r:
                    for k, v in ent[1].items():
                        if k != eng:
                            need[k] = max(need.get(k, 0), v)
        for w in writes:
            ent = self.res.get(w)
            if ent:
                if not w.startswith("dram:"):
                    for k, v in ent[0].items():
                        need[k] = max(need.get(k, 0), v)
                for k, v in ent[1].items():
                    need[k] = max(need.get(k, 0), v)
        waits = []
        for k, v in need.items():
            if k == eng and (eng == "pe" or not SAME_ENGINE_SYNC):
                continue
            if k == exclude:
                continue
            if self.seen[eng].get(k, 0) >= v:
                continue
            self.seen[eng][k] = v
            waits.append((k, v))
        return waits

    def _mark(self, key, val, reads, writes):
        for r in reads:
            ent = self.res.setdefault(r, [{}, {}])
            ent[1][key] = val
        for w in writes:
            ent = self.res.setdefault(w, [{}, {}])
            if w.startswith("dram:"):
                ent[0][key] = val
            else:
                ent[0] = {key: val}
                ent[1] = {}
            self.res[w] = ent

    def op(self, eng, fn, reads=(), writes=()):
        fn = _eager(fn)
        waits = self._deps(eng, reads, writes)
        self.cnt[eng] += 1
        val = self.cnt[eng]
        sem = self.sems[eng]
        sems = self.sems

        def emit(e, waits=waits, fn=fn, sem=sem):
            for k, v in waits:
                e.wait_ge(sems[k], v)
            fn(e).then_inc(sem, 1)

        self.streams[eng].append(emit)
        self._mark(eng, val, reads, writes)
        self.n_ops += 1

    def dma(self, queue, out, in_, reads=(), writes=(), key=None, **kw):
        assert key is not None
        sk = self._dma_sem(key)
        waits = self._deps(queue, reads, writes, exclude=sk)
        self.cnt[sk] += 16
        val = self.cnt[sk]
        sem = self.sems[sk]
        sems = self.sems

        def emit(e, waits=waits, sem=sem):
            for k, v in waits:
                e.wait_ge(sems[k], v)
            e.dma_start(out=out, in_=in_, **kw).then_inc(sem, 16)

        self.streams[queue].append(emit)
        self._mark(sk, val, reads, writes)
        self.n_ops += 1

    def custom(self, queue, fn, reads=(), writes=(), key=None, raw=False):
        if not raw:
            fn = _eager(fn)
        sk = self._dma_sem(key)
        waits = self._deps(queue, reads, writes, exclude=sk)
        self.cnt[sk] += 16
        val = self.cnt[sk]
        sem = self.sems[sk]
        sems = self.sems

        def emit(e, waits=waits, sem=sem):
            for k, v in waits:
                e.wait_ge(sems[k], v)
            fn(e).then_inc(sem, 16)

        self.streams[queue].append(emit)
        self._mark(sk, val, reads, writes)
        self.n_ops += 1

    def barrier(self):
        snap = dict(self.cnt)
        sems = self.sems
        for eng in self.streams:
            waits = []
            for k, v in snap.items():
                if v <= 0 or self.seen[eng].get(k, 0) >= v:
                    continue
                if k == eng and eng == "pe":
                    continue
                self.seen[eng][k] = v
                waits.append((k, v))

            def emit(e, waits=waits):
                for k, v in waits:
                    e.wait_ge(sems[k], v)

            self.streams[eng].append(emit)

    def wait_all(self, eng, resources):
        waits = self._deps(eng, resources, ())
        sems = self.sems

        def emit(e, waits=waits):
            for k, v in waits:
                e.wait_ge(sems[k], v)

        self.streams[eng].append(emit)

    def emit(self):
        nc = self.nc
        with nc.Block() as block:
            @block.tensor
            def _(e):
                for f in self.streams["pe"]:
                    f(e)

            @block.scalar
            def _(e):
                for f in self.streams["act"]:
                    f(e)

            @block.vector
            def _(e):
                for f in self.streams["dve"]:
                    f(e)

            @block.gpsimd
            def _(e):
                for f in self.streams["pool"]:
                    f(e)

            @block.sync
            def _(e):
                for f in self.streams["sp"]:
                    f(e)

import math
from concourse.bass_utils import run_bass_kernel_spmd

T = 4096
D = 2048
TB = 512
NB = T // TB
DFF = 5632
EPS = 1e-6
GN_EPS = 64e-5
NV = 528
DBG = {}
TW = 1026
WB = [(0, 342), (342, 342), (684, 342)]
FB = [(j * 512, 512) for j in range(8)]


_RANK = {}


def get_rank(e):
    if "r" not in _RANK:
        _RANK["r"] = (e.partition_id() % 4) * 1024
    return _RANK["r"]


def vec_pc(v):
    v = np.asarray(v, np.float32).reshape(-1)
    return np.ascontiguousarray(v.reshape(-1, 128).T)


def pack_vecs(inp):
    V = np.zeros((128, NV), np.float32)
    V[:, 0:16] = vec_pc(inp["pre_mix_norm"])
    V[:, 16:32] = vec_pc(inp["post_mix_norm"])
    V[:, 32:48] = vec_pc(inp["pre_ffn_norm"])
    V[:, 48:64] = vec_pc(inp["post_ffn_norm"])
    V[:, 64:80] = vec_pc(inp["ple_norm"])
    mu = np.asarray(inp["rw_mu"], np.float32).reshape(-1)
    V[:, 80:104] = vec_pc(mu[0:3072])
    V[:, 104:112] = vec_pc(inp["rw_w0"])
    V[:, 112:120] = vec_pc(inp["rw_a0"])
    V[:, 120:128] = vec_pc(inp["rw_k_k"])
    V[:, 128:136] = vec_pc(inp["rw_k_a"])
    V[:, 136:144] = vec_pc(inp["rw_r_k"])
    V[:, 144:152] = vec_pc(inp["rw_lnx_w"])
    V[:, 152:160] = vec_pc(inp["rw_lnx_b"])
    V[:, 160:164] = vec_pc(inp["mla_q_norm"])
    V[:, 164:168] = vec_pc(inp["mla_kv_norm"])
    V[:, 168:256] = vec_pc(inp["conv_b"])
    cw = np.asarray(inp["conv_w"], np.float32).reshape(3, -1)
    V[:, 256:344] = vec_pc(cw[0])
    V[:, 344:432] = vec_pc(cw[1])
    V[:, 432:520] = vec_pc(cw[2])
    V[0:64, 520] = mu[3072:3136]
    V[0:64, 521] = mu[3136:3200]
    V[0:128, 522] = mu[3200:3328]
    V[0:32, 523] = mu[3328:3360]
    invf = (10000.0 ** (-np.arange(0, 64, 2, dtype=np.float32) / 64)).astype(np.float32)
    V[0:32, 524] = invf
    V[32:64, 524] = invf
    return V


def build_program(debug=()):
    nc = bass.Bass("TRN2", target_bir_lowering=False)
    DBG.clear()
    _RANK.clear()
    for k_ in debug:
        if "=" in k_:
            DBG[k_.split("=")[0]] = int(k_.split("=")[1])
        else:
            DBG[k_] = 1

    def din(name, shape, dt=F32):
        return nc.dram_tensor(name, list(shape), dt, kind="ExternalInput").ap()

    def dscr(name, shape, dt):
        if name in debug:
            return nc.dram_tensor(name, list(shape), dt, kind="ExternalOutput").ap()
        return nc.dram_tensor(name, list(shape), dt).ap()

    x = din("x", [T, D])
    xw = din("xw", [1152, D])
    gain_bc = din("gain_bc", [128, D])
    p_in = din("p", [1024, 256])
    pos = din("pos", [1, T], I32)
    pos_w = din("pos_w", [1, TW], I32)
    qk_tab_d = din("qk_tab", [128, 3 * 342])
    d_tab_d = din("d_tab", [128, 32])
    vecs = din("vecs", [128, NV])
    w_in = din("w_in", [D, 8544])
    rw_w2 = din("rw_w2", [64, 1024])
    rw_a2 = din("rw_a2", [64, 1024])
    rw_g2 = din("rw_g2", [160, 1024])
    w_q_up = din("w_q_up", [512, 1536])
    w_kv_up = din("w_kv_up", [512, 2048])
    w_brw = din("w_brw", [1024, D])
    w_bmla = din("w_bmla", [1024, D])
    w_out = din("w_out", [D, D])
    w_up = din("w_up", [D, 2 * DFF])
    w_down = din("w_down", [DFF, D])
    w_ple = din("w_ple", [256, D])
    w_pg = din("w_pg", [D, D])
    out = nc.dram_tensor("out", [1024, D], F32, kind="ExternalOutput").ap()

    uT_full = dscr("uT", [D, T + 2], BF16)
    yrwT_full = dscr("yrwT", [1024, T + 2], BF16)
    cqnT_full = dscr("cqnT", [512, T + 2], BF16)
    uT = uT_full[:, 2:]
    yrwT = yrwT_full[:, 2:]
    cqnT = cqnT_full[:, 2:]
    zrwT = dscr("zrwT", [3360, T], F32)
    zmlaT = dscr("zmlaT", [1088, T], F32)
    ckvnT = dscr("ckvnT", [512, T], BF16)
    hwT = dscr("hwT", [D, 1152], F32)
    uwT = dscr("uwT", [D, TW], BF16)
    ywT = dscr("ywT", [1024, TW], BF16)
    cqwT = dscr("cqwT", [512, TW], BF16)
    ymlaT = dscr("ymlaT", [1024, TW], BF16)
    pT = dscr("pT", [256, 1024], F32)
    gT = dscr("gT", [4096, TW], BF16)
    mT = dscr("mT", [D, TW], BF16)
    fT = dscr("fT", [D, TW], F32)
    ffT = dscr("ffT", [DFF, TW], BF16)
    qk_tab = qk_tab_d
    d_tab = d_tab_d

    with contextlib.ExitStack() as st0:
        P = Prog(nc, st0)
        V = P.sb("V", [128, NV], F32)
        P.dma("sp", V[:], vecs, writes=["V"], key="V")
        ident = P.sb("ident", [128, 128], F32)
        ident_bf = P.sb("ident_bf", [128, 128], BF16)
        ones_bf = P.sb("ones_bf", [128, 128], BF16)
        bones_bf = P.sb("bones_bf", [128, 128], BF16)
        maskP = P.sb("maskP", [128, 2, 128], F32)
        maskL = P.sb("maskL", [128, 64], F32)
        eye = P.sb("eye", [128, 64], F32)
        rmask = P.sb("rmask", [128, TB], F32)
        omka = P.sb("omka", [128, 8], F32)
        su = P.sb("su", [128, 64], F32)
        ui = P.sb("ui", [128, 64], F32)

        def pool(fn, reads=(), writes=()):
            P.op("pool", fn, reads, writes)

        pool(lambda e: e.memset(ident[:], 1.0), writes=["ident"])
        pool(lambda e: e.affine_select(out=ident[:], in_=ident[:], pattern=[[-1, 128]], compare_op=ALU.is_equal,
                                       fill=0.0, base=0, channel_multiplier=1), reads=["ident"], writes=["ident"])
        pool(lambda e: e.tensor_copy(out=ident_bf[:], in_=ident[:]), reads=["ident"], writes=["ident_bf"])
        pool(lambda e: e.memset(ones_bf[:], 1.0), writes=["ones_bf"])
        pool(lambda e: e.memset(bones_bf[:], 0.0), writes=["bones_bf"])
        pool(lambda e: e.memset(bones_bf[0:64, 0:64], 1.0), reads=["bones_bf"], writes=["bones_bf"])
        pool(lambda e: e.memset(bones_bf[64:128, 64:128], 1.0), reads=["bones_bf"], writes=["bones_bf"])
        for (tl, pat, cm, b0, b1, op_) in ((su, 1, -1, -1, -1, ALU.is_ge), (ui, 1, -1, 0, 0, ALU.is_ge),
                                           (maskL, -1, 1, -1, -1, ALU.is_ge), (eye, -1, 1, 0, 0, ALU.is_equal)):
            pool(lambda e, tl=tl: e.memset(tl[:], 1.0), writes=["msk"])
            for h, bb in ((0, b0), (1, b1)):
                sl = slice(64 * h, 64 * h + 64)
                pool(lambda e, tl=tl, sl=sl, pat=pat, cm=cm, bb=bb, op_=op_: e.affine_select(
                    out=tl[sl, :], in_=tl[sl, :], pattern=[[pat, 64]], compare_op=op_, fill=0.0, base=bb,
                    channel_multiplier=cm), reads=["msk"], writes=["msk"])
        for xx in range(2):
            pool(lambda e, xx=xx: e.tensor_copy(out=maskP[:, xx, 0:64], in_=su[:]), reads=["msk"], writes=["msk"])
            pool(lambda e, xx=xx: e.tensor_copy(out=maskP[:, xx, 64:128], in_=ui[:]), reads=["msk"], writes=["msk"])
        pool(lambda e: e.memset(rmask[:], 1.0), reads=["msk"], writes=["msk"])
        for c in range(8):
            pool(lambda e, c=c: e.memset(rmask[:, c * 64:c * 64 + 1], 0.0), reads=["msk"], writes=["msk"])
        P.op("dve", lambda e: e.tensor_scalar(out=omka[:], in0=V[:, 128:136], scalar1=-1.0, scalar2=1.0, op0=ALU.mult,
                                              op1=ALU.add), reads=["V", "msk"], writes=["omka"])
        CONST = ["V", "msk", "ident", "ident_bf", "ones_bf", "bones_bf", "omka"]

        rr = {"evac": 0}

        class _Stop(Exception):
            pass

        def stop_if(name):
            if name in debug:
                raise _Stop()

        def evac_copy(out_ap, in_ap, reads, writes, scale=None):
            rr["evac"] += 1
            if scale is not None or rr["evac"] % 2 == 0:
                P.op("act", lambda e: e.activation(out=out_ap, in_=in_ap, func=AF.Copy,
                                                   scale=(1.0 if scale is None else scale)), reads, writes)
            else:
                P.op("dve", lambda e: e.tensor_copy(out=out_ap, in_=in_ap), reads, writes)

        def transpose_stage(tag, src, dst, R, C, src_res, dst_res):
            with contextlib.ExitStack() as st:
                prev_stack = P.stack
                P.stack = st
                nb = 2
                tin = [P.sb(f"{tag}_in{i}", [128, C], F32) for i in range(nb)]
                tout = [P.sb(f"{tag}_out{i}", [128, C // 128, 128], F32) for i in range(nb)]
                pst = [P.ps(f"{tag}_ps{i}", [128, 4, 128], F32) for i in range(2)]
                k = 0
                for r in range(R // 128):
                    b = r % nb
                    P.dma("sp", tin[b][:], src[r * 128:(r + 1) * 128, :], reads=[src_res], writes=[f"TR_in{b}"],
                          key=f"TR_in{b}")
                    for c4 in range(0, C // 128, 4):
                        pb = k % 2
                        k += 1
                        n4 = min(4, C // 128 - c4)
                        for i in range(n4):
                            c = c4 + i
                            P.op("pe", lambda e, pb=pb, i=i, c=c, b=b: e.transpose(
                                out=pst[pb][:, i, :], in_=tin[b][:, c * 128:(c + 1) * 128], identity=ident[:]),
                                reads=[f"TR_in{b}", "ident"], writes=[f"TR_ps{pb}"])
                        evac_copy(tout[b][:, c4:c4 + n4, :], pst[pb][:, 0:n4, :], [f"TR_ps{pb}"], [f"TR_out{b}"])
                    P.dma("sp", dst[:, r * 128:(r + 1) * 128].rearrange("(c p) t -> p c t", p=128), tout[b][:],
                          reads=[f"TR_out{b}"], writes=[dst_res], key=f"TR_st{b}")
                P.barrier()
                P.stack = prev_stack

        def rmsnorm_stage(tag, src, F, gcol, src_res, dst=None, dst_res=None, resid=None, resid_res=None, blocks=FB):
            FC = F // 128
            with contextlib.ExitStack() as st:
                prev_stack = P.stack
                P.stack = st
                s = P.sb(f"{tag}_s", [128, FC, TB], F32)
                sq = P.sb(f"{tag}_sq", [128, FC, TB], BF16)
                rstd = P.sb(f"{tag}_rstd", [128, TB], F32)
                ps = P.ps(f"{tag}_ps", [128, TB], F32)
                if resid is None:
                    o = P.sb(f"{tag}_o", [128, FC, TB], BF16)
                else:
                    o = P.sb(f"{tag}_o", [128, FC, TB], F32)
                    hh = P.sb(f"{tag}_h", [128, FC, TB], F32)
                for j, (b0, bs) in enumerate(blocks):
                    cs = slice(b0, b0 + bs)
                    P.dma("sp", s[:, :, 0:bs], src[:, cs].rearrange("(c p) t -> p c t", p=128), reads=[src_res],
                          writes=[f"RN_s"], key=f"RN_s")
                    if resid is not None:
                        P.dma("sp", hh[:, :, 0:bs], resid[:, cs].rearrange("(c p) t -> p c t", p=128), reads=[resid_res],
                              writes=[f"RN_h"], key=f"RN_h")
                    P.op("act", lambda e: e.activation(out=sq[:, :, 0:bs], in_=s[:, :, 0:bs], func=AF.Square), reads=[f"RN_s"],
                         writes=[f"RN_sq"])
                    for c in range(FC):
                        P.op("pe", lambda e, c=c: e.matmul(ps[:, 0:bs], lhsT=ones_bf[:], rhs=sq[:, c, 0:bs], start=(c == 0),
                                                           stop=(c == FC - 1)),
                             reads=[f"RN_sq", "ones_bf"], writes=[f"RN_ps"])
                    P.op("act", lambda e: e.activation(out=rstd[:, 0:bs], in_=ps[:, 0:bs], func=AF.Sqrt, bias=EPS, scale=1.0 / F),
                         reads=[f"RN_ps"], writes=[f"RN_rstd"])
                    P.op("dve", lambda e: e.reciprocal(out=rstd[:, 0:bs], in_=rstd[:, 0:bs]), reads=[f"RN_rstd"],
                         writes=[f"RN_rstd"])
                    for c in range(FC):
                        eng = "pool"
                        P.op("dve", lambda e, c=c: e.scalar_tensor_tensor(
                            out=o[:, c, 0:bs], in0=s[:, c, 0:bs], scalar=V[:, gcol + c:gcol + c + 1], in1=rstd[:, 0:bs],
                            op0=ALU.mult, op1=ALU.mult), reads=[f"RN_s", f"RN_rstd", "V"], writes=[f"RN_o{c}"])
                        if resid is not None:
                            P.op(eng, lambda e, c=c: e.tensor_tensor(out=o[:, c, 0:bs], in0=o[:, c, 0:bs], in1=hh[:, c, 0:bs],
                                                                     op=ALU.add),
                                 reads=[f"RN_o{c}", f"RN_h"], writes=[f"RN_o{c}"])
                    allo = [f"RN_o{c}" for c in range(FC)]
                    if resid is None:
                        P.dma("sp", dst[:, cs].rearrange("(c p) t -> p c t", p=128), o[:, :, 0:bs], reads=allo,
                              writes=[dst_res], key=f"RN_st")
                    else:
                        P.dma("sp", resid[:, cs].rearrange("(c p) t -> p c t", p=128), o[:, :, 0:bs], reads=allo,
                              writes=[resid_res], key=f"RN_st")
                P.barrier()
                P.stack = prev_stack

        def linear_stage(tag, srcs, groups, epi, wbuf_cols, nps=2, blocks=FB):
            with contextlib.ExitStack() as st:
                prev_stack = P.stack
                P.stack = st
                wsb = []
                sbs = []
                for si, (src, K, sdt, sres, W) in enumerate(srcs):
                    KC = (K + 127) // 128
                    wsb.append([P.sb(f"{tag}_w{si}_{i}", [128, KC, wbuf_cols], BF16) for i in range(2)])
                    sbs.append([P.sb(f"{tag}_x{si}_{i}", [128, KC, TB], BF16) for i in range(2)])
                pss = [P.ps(f"{tag}_ps{i}", [128, TB], F32) for i in range(nps)]
                state = {"ps": 0, "xb": 0}
                def grp_layout(grp):
                    offs = []
                    off = 0
                    for (c0, w) in grp:
                        offs.append(off)
                        off += w
                    assert off <= wbuf_cols
                    ranges = []
                    for (c0, w), o_ in zip(grp, offs):
                        if ranges and ranges[-1][0] + ranges[-1][1] == c0 and ranges[-1][2] + ranges[-1][1] == o_:
                            ranges[-1][1] += w
                        else:
                            ranges.append([c0, w, o_])
                    return offs, ranges

                def load_w(gi):
                    wb = gi % 2
                    offs, ranges = grp_layout(groups[gi])
                    for si, (src, K, sdt, sres, W) in enumerate(srcs):
                        KC = (K + 127) // 128
                        for (c0, w, o_) in ranges:
                            for kc in range(KC):
                                kr = min(128, K - kc * 128)
                                P.dma("pool", wsb[si][wb][0:kr, kc, o_:o_ + w], W[kc * 128:kc * 128 + kr, c0:c0 + w],
                                      writes=[f"LN_w{si}_{wb}"], key=f"LN_w{si}_{wb}")

                load_w(0)
                for gi, grp in enumerate(groups):
                    wb = gi % 2
                    offs, ranges = grp_layout(grp)
                    if gi + 1 < len(groups):
                        load_w(gi + 1)
                    for j, (b0, bs) in enumerate(blocks):
                        cs = slice(b0, b0 + bs)
                        xb = state["xb"] % 2
                        state["xb"] += 1
                        for si, (src, K, sdt, sres, W) in enumerate(srcs):
                            KC = (K + 127) // 128
                            q = "sp" if sdt == BF16 else "pool"
                            if K % 128 == 0 and q == "sp":
                                P.dma(q, sbs[si][xb][:, :, 0:bs], src[:, cs].rearrange("(k p) t -> p k t", p=128), reads=[sres],
                                      writes=[f"LN_x{si}_{xb}"], key=f"LN_x{si}_{xb}")
                            else:
                                for kc in range(KC):
                                    kr = min(128, K - kc * 128)
                                    P.dma(q, sbs[si][xb][0:kr, kc, 0:bs], src[kc * 128:kc * 128 + kr, cs], reads=[sres],
                                          writes=[f"LN_x{si}_{xb}"], key=f"LN_x{si}_{xb}")
                        for ci, ((c0, w), o_) in enumerate(zip(grp, offs)):
                            pi = state["ps"] % nps
                            state["ps"] += 1
                            nmm = sum((K + 127) // 128 for (_, K, _, _, _) in srcs)
                            i = 0
                            for si, (src, K, sdt, sres, W) in enumerate(srcs):
                                KC = (K + 127) // 128
                                for kc in range(KC):
                                    kr = min(128, K - kc * 128)
                                    P.op("pe", lambda e, pi=pi, si=si, wb=wb, kc=kc, kr=kr, o_=o_, w=w, xb=xb, i=i, nmm=nmm:
                                         e.matmul(pss[pi][0:w, 0:bs], lhsT=wsb[si][wb][0:kr, kc, o_:o_ + w],
                                                  rhs=sbs[si][xb][0:kr, kc, 0:bs], start=(i == 0), stop=(i == nmm - 1)),
                                         reads=[f"LN_w{si}_{wb}", f"LN_x{si}_{xb}"], writes=[f"LN_ps{pi}"])
                                    i += 1
                            epi(gi, ci, (c0, w), j, pss[pi], f"LN_ps{pi}", (b0, bs))
                P.barrier()
                P.stack = prev_stack

        def store_epi(tag, dst, dst_res, odt, func=AF.Copy, row_of=None, nbuf=3):
            bufs = [P.sb(f"{tag}_eo{i}", [128, TB], odt) for i in range(nbuf)]
            stt = {"i": 0}

            def epi(gi, ci, cw, j, ps, ps_res, blk):
                c0, w = cw
                b0_, bs = blk
                b = stt["i"] % nbuf
                stt["i"] += 1
                r0 = c0 if row_of is None else row_of(c0)
                if func == AF.Copy:
                    evac_copy(bufs[b][0:w, 0:bs], ps[0:w, 0:bs], [ps_res], [f"EO_eo{b}"])
                else:
                    P.op("act", lambda e: e.activation(out=bufs[b][0:w, 0:bs], in_=ps[0:w, 0:bs], func=func), [ps_res],
                         [f"EO_eo{b}"])
                P.dma("act", dst[r0:r0 + w, b0_:b0_ + bs], bufs[b][0:w, 0:bs], reads=[f"EO_eo{b}"],
                      writes=[dst_res], key=f"EO_eo{b}")

            return epi

        def chunks(c0, n, w=128):
            res = []
            c = c0
            while c < c0 + n:
                ww = min(w, c0 + n - c)
                res.append((c, ww))
                c += ww
            return res

        def grouped(ch, n):
            return [ch[i:i + n] for i in range(0, len(ch), n)]

        def norm_transpose_stage():
            with contextlib.ExitStack() as st:
                prev_stack = P.stack
                P.stack = st
                gbc = P.sb("nt_gbc", [128, D], F32)
                P.dma("sp", gbc[:], gain_bc, writes=["nt_gbc"], key="nt_gbc")
                xt = [P.sb(f"nt_x{i}", [128, D], F32) for i in range(3)]
                junk = P.sb("nt_junk", [128, D], BF16)
                ub = [P.sb(f"nt_ub{i}", [128, D], BF16) for i in range(2)]
                ss = [P.sb(f"nt_ss{i}", [128, 2], F32) for i in range(2)]
                uo = [P.sb(f"nt_uo{i}", [128, 16, 512], BF16) for i in range(2)]
                pst = [P.ps(f"nt_ps{i}", [128, 8, 128], BF16) for i in range(2)]
                k = 0
                for g in range(T // 512):
                    ob = g % 2
                    for t4 in range(4):
                        r = g * 4 + t4
                        b = r % 3
                        b2 = r % 2
                        P.dma("sp", xt[b][:], x[r * 128:(r + 1) * 128, :], reads=["dram:x"], writes=[f"nt_x{b}"], key=f"nt_x{b}")
                        P.op("dve", lambda e: e.memset(ss[b2][:, 0:1], 0.0), [f"nt_ss{b2}"], [f"nt_ss{b2}"])
                        P.op("act", lambda e: e.activation(out=junk[:], in_=xt[b][:], func=AF.Square, accum_out=ss[b2][:, 0:1]),
                             [f"nt_x{b}", f"nt_ss{b2}"], ["nt_junk", f"nt_ss{b2}"])
                        P.op("act", lambda e: e.activation(out=ss[b2][:, 1:2], in_=ss[b2][:, 0:1], func=AF.Sqrt, bias=EPS,
                                                           scale=1.0 / D), [f"nt_ss{b2}"], [f"nt_ss{b2}"])
                        P.op("dve", lambda e: e.reciprocal(out=ss[b2][:, 1:2], in_=ss[b2][:, 1:2]), [f"nt_ss{b2}"], [f"nt_ss{b2}"])
                        P.op("dve", lambda e: e.scalar_tensor_tensor(out=ub[b2][:], in0=xt[b][:], scalar=ss[b2][:, 1:2], in1=gbc[:],
                                                                     op0=ALU.mult, op1=ALU.mult),
                             [f"nt_x{b}", f"nt_ss{b2}", "nt_gbc"], [f"nt_ub{b2}"])
                        for half in range(2):
                            pb = k % 2
                            k += 1
                            for i in range(8):
                                c = half * 8 + i
                                P.op("pe", lambda e, pb=pb, i=i, c=c: e.transpose(out=pst[pb][:, i, :],
                                                                               in_=ub[b2][:, c * 128:(c + 1) * 128],
                                                                               identity=ident_bf[:]),
                                     [f"nt_ub{b2}", "ident_bf"], [f"nt_ps{pb}"])
                            evac_copy(uo[ob][:, half * 8:(half + 1) * 8, t4 * 128:(t4 + 1) * 128], pst[pb][:],
                                      [f"nt_ps{pb}"], [f"nt_uo{ob}"])
                    P.dma("act", uT[:, g * 512:(g + 1) * 512].rearrange("(c p) t -> p c t", p=128), uo[ob][:],
                          reads=[f"nt_uo{ob}"], writes=["dram:uT"], key=f"nt_uo{ob}")
                P.barrier()
                P.stack = prev_stack

        def main_seq():
            if "skip_s01" in debug:
                rwkv_stage(P, nc, st0, V, zrwT, yrwT, rw_w2, rw_a2, rw_g2, ident_bf, bones_bf, maskP, maskL, eye, rmask, omka,
                           evac_copy)
                stop_if("skip_s01")
            if "skip_s01m" in debug:
                rmsnorm_stage("nq", zmlaT[0:512, :], 512, 160, "dram:zmlaT", dst=cqnT, dst_res="dram:cqnT")
                rmsnorm_stage("nk", zmlaT[512:1024, :], 512, 164, "dram:zmlaT", dst=ckvnT, dst_res="dram:ckvnT")
                mla_stage(P, nc, st0, V, pos, pos_w, qk_tab, d_tab, zmlaT, cqnT_full, ckvnT, w_q_up, w_kv_up, ymlaT, ones_bf,
                          evac_copy)
                stop_if("skip_s01m")
            def win_copy(dst, src_full, rows, res_src, res_dst, key):
                step = 512
                for r0 in range(0, rows, step):
                    def fn(e, r0=r0):
                        off = get_rank(e)
                        return e.dma_start(out=dst[r0:r0 + step, :], in_=src_full[r0:r0 + step, bass.ds(off, TW)])
                    P.custom("pool", fn, reads=[res_src], writes=[res_dst], key=key, raw=True)

            zt_f = P.sb("zpad_f", [128, 16, 2], F32)
            zt_b = P.sb("zpad_b", [128, 16, 2], BF16)
            P.op("pool", lambda e: e.memset(zt_f[:], 0.0), [], ["zpad"])
            P.op("pool", lambda e: e.memset(zt_b[:], 0.0), [], ["zpad"])
            P.dma("sp", uT_full[:, 0:2].rearrange("(c p) t -> p c t", p=128), zt_b[:], reads=["zpad"], writes=["dram:uT"], key="zp")
            P.dma("sp", yrwT_full[:, 0:2].rearrange("(c p) t -> p c t", p=128), zt_b[:, 0:8, :], reads=["zpad"],
                  writes=["dram:yrwT"], key="zp")
            P.dma("sp", cqnT_full[:, 0:2].rearrange("(c p) t -> p c t", p=128), zt_b[:, 0:4, :], reads=["zpad"],
                  writes=["dram:cqnT"], key="zp")
            norm_transpose_stage()
            stop_if("stop0b")
            with contextlib.ExitStack() as stx:
                P.stack = stx
                e1 = store_epi("l1a", zrwT, "dram:zrwT", F32)
                linear_stage("l1a", [(uT, D, BF16, "dram:uT", w_in)], grouped(chunks(0, 3360), 8), e1, 1024)
                P.barrier()
                P.stack = st0
            with contextlib.ExitStack() as stx:
                P.stack = stx
                e2 = store_epi("l1b", zmlaT, "dram:zmlaT", F32, row_of=lambda c0: c0 - 3360)
                linear_stage("l1b", [(uT, D, BF16, "dram:uT", w_in)], grouped(chunks(3360, 1088), 9), e2, 1152)
                P.barrier()
                P.stack = st0
            stop_if("stop1")
            rwkv_stage(P, nc, st0, V, zrwT, yrwT, rw_w2, rw_a2, rw_g2, ident_bf, bones_bf, maskP, maskL, eye, rmask, omka,
                       evac_copy)
            stop_if("stop2")
            rmsnorm_stage("nq", zmlaT[0:512, :], 512, 160, "dram:zmlaT", dst=cqnT, dst_res="dram:cqnT")
            rmsnorm_stage("nk", zmlaT[512:1024, :], 512, 164, "dram:zmlaT", dst=ckvnT, dst_res="dram:ckvnT")
            win_copy(cqwT, cqnT_full, 512, "dram:cqnT", "dram:cqwT", "wc3")
            P.barrier()
            mla_stage(P, nc, st0, V, pos, pos_w, qk_tab, d_tab, zmlaT, cqwT, ckvnT, w_q_up, w_kv_up, ymlaT, ones_bf,
                      evac_copy)
            stop_if("stop3")
            transpose_stage("txw", xw, hwT, 1152, D, "dram:xw", "dram:hwT")
            win_copy(uwT, uT_full, D, "dram:uT", "dram:uwT", "wc1")
            win_copy(ywT, yrwT_full, 1024, "dram:yrwT", "dram:ywT", "wc2")
            P.barrier()
            transpose_stage("tp", p_in, pT, 1024, 256, "dram:p", "dram:pT")
            with contextlib.ExitStack() as stx:
                P.stack = stx
                e3 = store_epi("l1c", gT, "dram:gT", BF16, func=AF.Sigmoid, row_of=lambda c0: c0 - 4448)
                linear_stage("l1c", [(uwT, D, BF16, "dram:uwT", w_in)], grouped(chunks(4448, 4096), 8), e3, 1024, blocks=WB)
                P.barrier()
                P.stack = st0
            with contextlib.ExitStack() as stx:
                P.stack = stx
                gA = [P.sb(f"s4_gA{i}", [128, TB], BF16) for i in range(2)]
                gB = [P.sb(f"s4_gB{i}", [128, TB], BF16) for i in range(2)]
                t1 = [P.sb(f"s4_t1{i}", [128, TB], F32) for i in range(2)]
                mo = [P.sb(f"s4_mo{i}", [128, TB], BF16) for i in range(2)]
                t2 = [P.sb(f"s4_t2{i}", [128, TB], F32) for i in range(2)]
                stt = {"i": 0}

                def epiA(gi, ci, cw, j, ps, ps_res, blk):
                    c0, w = cw
                    b0_, bs = blk
                    b = stt["i"] % 2
                    stt["i"] += 1
                    P.dma("sp", gA[b][:, 0:bs], gT[c0:c0 + 128, b0_:b0_ + bs], reads=["dram:gT"], writes=[f"s4_gA{b}"],
                          key=f"s4_gA{b}")
                    P.op("dve", lambda e: e.tensor_tensor(out=t1[b][:, 0:bs], in0=ps[:, 0:bs], in1=gA[b][:, 0:bs], op=ALU.mult),
                         [ps_res, f"s4_gA{b}"], [f"s4_t1{b}"])
                    P.dma("act", fT[c0:c0 + 128, b0_:b0_ + bs], t1[b][:, 0:bs], reads=[f"s4_t1{b}"], writes=["dram:fT"],
                          key=f"s4_t1{b}")

                linear_stage("l4a", [(ywT, 1024, BF16, "dram:ywT", w_brw)], grouped(chunks(0, D), 8), epiA, 1024, blocks=WB)

                def epiB(gi, ci, cw, j, ps, ps_res, blk):
                    c0, w = cw
                    b0_, bs = blk
                    b = stt["i"] % 2
                    stt["i"] += 1
                    P.dma("sp", gB[b][:, 0:bs], gT[2048 + c0:2048 + c0 + 128, b0_:b0_ + bs], reads=["dram:gT"],
                          writes=[f"s4_gB{b}"], key=f"s4_gB{b}")
                    P.dma("sp", t1[b][:, 0:bs], fT[c0:c0 + 128, b0_:b0_ + bs], reads=["dram:fT"], writes=[f"s4_t1{b}"],
                          key=f"s4_t1l{b}")
                    P.op("dve", lambda e: e.tensor_tensor(out=t2[b][:, 0:bs], in0=ps[:, 0:bs], in1=gB[b][:, 0:bs], op=ALU.mult),
                         [ps_res, f"s4_gB{b}"], [f"s4_t2{b}"])
                    P.op("dve", lambda e: e.tensor_tensor(out=mo[b][:, 0:bs], in0=t2[b][:, 0:bs], in1=t1[b][:, 0:bs], op=ALU.add),
                         [f"s4_t2{b}", f"s4_t1{b}"], [f"s4_mo{b}"])
                    P.dma("act", mT[c0:c0 + 128, b0_:b0_ + bs], mo[b][:, 0:bs], reads=[f"s4_mo{b}"], writes=["dram:mT"],
                          key=f"s4_mo{b}")

                linear_stage("l4b", [(ymlaT, 1024, BF16, "dram:ymlaT", w_bmla)], grouped(chunks(0, D), 8), epiB, 1024, blocks=WB)
                P.barrier()
                P.stack = st0
            stop_if("stop4")
            with contextlib.ExitStack() as stx:
                P.stack = stx
                e5 = store_epi("l5", fT, "dram:fT", F32)
                linear_stage("l5", [(mT, D, BF16, "dram:mT", w_out)], grouped(chunks(0, D), 8), e5, 1024, blocks=WB)
                P.barrier()
                P.stack = st0
            rmsnorm_stage("n5", fT, D, 16, "dram:fT", resid=hwT, resid_res="dram:hwT", blocks=WB)
            stop_if("stop5")
            rmsnorm_stage("n6", hwT, D, 32, "dram:hwT", dst=uwT, dst_res="dram:uwT", blocks=WB)
            with contextlib.ExitStack() as stx:
                P.stack = stx
                NCH = 4
                ub = [[P.sb(f"s6_u{h}_{i}", [128, TB + 2], F32) for i in range(NCH)] for h in range(2)]
                cv = [P.sb(f"s6_cv{h}", [128, TB], F32) for h in range(2)]
                tq = P.sb("s6_tq", [128, TB], F32)
                fo = [P.sb(f"s6_fo{i}", [128, TB], BF16) for i in range(2)]
                stt = {"i": 0}

                def epi6(gi, ci, cw, j, ps, ps_res, blk):
                    c0, w = cw
                    b0_, bs = blk
                    half = 0 if ci < NCH else 1
                    cc = ci % NCH
                    u = ub[half][cc]
                    ur = f"s6_u{half}_{cc}"
                    if j == 0:
                        P.op("dve", lambda e: e.memset(u[:, 0:2], 0.0), [ur], [ur])
                    else:
                        P.op("dve", lambda e: e.tensor_copy(out=u[:, 0:2], in_=u[:, bs:bs + 2]), [ur], [ur])
                    evac_copy(u[:, 2:bs + 2], ps[:, 0:bs], [ps_res, ur], [ur])
                    if half == 1:
                        chn = c0 // 128
                        chg = chn - 44
                        for hh, ch in ((0, chg), (1, chn)):
                            uu = ub[hh][cc]
                            uur = f"s6_u{hh}_{cc}"
                            P.op("dve", lambda e, uu=uu, ch=ch, hh=hh: e.tensor_scalar(
                                out=cv[hh][:, 0:bs], in0=uu[:, 0:bs], scalar1=V[:, 256 + ch:257 + ch], scalar2=V[:, 168 + ch:169 + ch],
                                op0=ALU.mult, op1=ALU.add), [uur, "V"], [f"s6_cv{hh}"])
                            P.op("dve", lambda e, uu=uu, ch=ch, hh=hh: e.scalar_tensor_tensor(
                                out=cv[hh][:, 0:bs], in0=uu[:, 1:bs + 1], scalar=V[:, 344 + ch:345 + ch], in1=cv[hh][:, 0:bs],
                                op0=ALU.mult, op1=ALU.add), [uur, "V", f"s6_cv{hh}"], [f"s6_cv{hh}"])
                            P.op("dve", lambda e, uu=uu, ch=ch, hh=hh: e.scalar_tensor_tensor(
                                out=cv[hh][:, 0:bs], in0=uu[:, 2:bs + 2], scalar=V[:, 432 + ch:433 + ch], in1=cv[hh][:, 0:bs],
                                op0=ALU.mult, op1=ALU.add), [uur, "V", f"s6_cv{hh}"], [f"s6_cv{hh}"])
                        P.op("dve", lambda e: e.tensor_tensor(out=tq[:, 0:bs], in0=cv[0][:, 0:bs], in1=cv[0][:, 0:bs], op=ALU.mult),
                             ["s6_cv0"], ["s6_tq"])
                        P.op("dve", lambda e: e.tensor_scalar(out=tq[:, 0:bs], in0=tq[:, 0:bs], scalar1=0.044715, scalar2=1.0,
                                                               op0=ALU.mult, op1=ALU.add), ["s6_tq"], ["s6_tq"])
                        P.op("dve", lambda e: e.tensor_tensor(out=tq[:, 0:bs], in0=tq[:, 0:bs], in1=cv[0][:, 0:bs], op=ALU.mult),
                             ["s6_tq", "s6_cv0"], ["s6_tq"])
                        P.op("act", lambda e: e.activation(out=tq[:, 0:bs], in_=tq[:, 0:bs], func=AF.Sigmoid, scale=1.5957691216),
                             ["s6_tq"], ["s6_tq"])
                        P.op("dve", lambda e: e.tensor_tensor(out=tq[:, 0:bs], in0=tq[:, 0:bs], in1=cv[0][:, 0:bs], op=ALU.mult),
                             ["s6_tq", "s6_cv0"], ["s6_tq"])
                        b = stt["i"] % 2
                        stt["i"] += 1
                        P.op("dve", lambda e: e.tensor_tensor(out=fo[b][:, 0:bs], in0=tq[:, 0:bs], in1=cv[1][:, 0:bs], op=ALU.mult),
                             ["s6_tq", "s6_cv1"], [f"s6_fo{b}"])
                        P.dma("act", ffT[chg * 128:(chg + 1) * 128, b0_:b0_ + bs], fo[b][:, 0:bs], reads=[f"s6_fo{b}"],
                              writes=["dram:ffT"], key=f"s6_fo{b}")

                grps = []
                for g0 in range(0, 44, NCH):
                    grps.append(chunks(g0 * 128, NCH * 128) + chunks(DFF + g0 * 128, NCH * 128))
                linear_stage("l6", [(uwT, D, BF16, "dram:uwT", w_up)], grps, epi6, 2 * NCH * 128, blocks=WB)
                P.barrier()
                P.stack = st0
            with contextlib.ExitStack() as stx:
                P.stack = stx
                e7 = store_epi("l7", fT, "dram:fT", F32)
                linear_stage("l7", [(ffT, DFF, BF16, "dram:ffT", w_down)], grouped(chunks(0, D), 4), e7, 512, blocks=WB)
                P.barrier()
                P.stack = st0
            rmsnorm_stage("n7", fT, D, 48, "dram:fT", resid=hwT, resid_res="dram:hwT", blocks=WB)
            stop_if("stop7")
            PB = [(0, 512), (512, 512)]
            with contextlib.ExitStack() as stx:
                P.stack = stx
                e8 = store_epi("l8a", mT, "dram:mT", BF16, func=AF.Sigmoid)
                linear_stage("l8a", [(hwT[:, 2:TW], D, F32, "dram:hwT", w_pg)], grouped(chunks(0, D), 8), e8, 1024, blocks=PB)
                gl = [P.sb(f"s8_g{i}", [128, TB], BF16) for i in range(2)]
                eo = [P.sb(f"s8_eo{i}", [128, TB], F32) for i in range(2)]
                stt = {"i": 0}

                def epi8(gi, ci, cw, j, ps, ps_res, blk):
                    c0, w = cw
                    b0_, bs = blk
                    b = stt["i"] % 2
                    stt["i"] += 1
                    P.dma("sp", gl[b][:, 0:bs], mT[c0:c0 + 128, b0_:b0_ + bs], reads=["dram:mT"], writes=[f"s8_g{b}"],
                          key=f"s8_g{b}")
                    P.op("dve", lambda e: e.tensor_tensor(out=eo[b][:, 0:bs], in0=ps[:, 0:bs], in1=gl[b][:, 0:bs], op=ALU.mult),
                         [ps_res, f"s8_g{b}"], [f"s8_eo{b}"])
                    P.dma("act", fT[c0:c0 + 128, b0_:b0_ + bs], eo[b][:, 0:bs], reads=[f"s8_eo{b}"], writes=["dram:fT"],
                          key=f"s8_eo{b}")

                linear_stage("l8b", [(pT, 256, F32, "dram:pT", w_ple)], grouped(chunks(0, D), 16), epi8, 2048, blocks=PB)
                P.barrier()
                P.stack = st0
            rmsnorm_stage("n8", fT[:, 0:1024], D, 64, "dram:fT", resid=hwT[:, 2:TW], resid_res="dram:hwT", blocks=PB)
            transpose_stage("to", hwT[:, 2:TW], out, D, 1024, "dram:hwT", "dram:out")

        try:
            main_seq()
        except _Stop:
            P.stack = st0
        P.wait_all("sp", [f"dram:{n}" for n in (["out"] + list(debug)) if not n.startswith("skip") and not n.startswith("stop")])
        P.emit()
    return nc


def rwkv_stage(P, nc, st0, V, zrwT, yrwT, rw_w2, rw_a2, rw_g2, ident_bf, bones_bf, maskP, maskL, eye, rmask, omka,
               evac_copy):
    NEG_E = -math.exp(-0.5)
    with contextlib.ExitStack() as st:
        prev_stack = P.stack
        P.stack = st
        sb = P.sb
        w2b = sb("rk_w2b", [64, 1024], BF16)
        a2b = sb("rk_a2b", [64, 1024], BF16)
        g2b = sb("rk_g2b", [128, 2, 1024], BF16)
        P.dma("pool", w2b[:], rw_w2, writes=["rk_w"], key="rk_w")
        P.dma("pool", a2b[:], rw_a2, writes=["rk_w"], key="rk_w")
        P.dma("pool", g2b[:, 0, :], rw_g2[0:128, :], writes=["rk_w"], key="rk_w")
        P.dma("pool", g2b[0:32, 1, :], rw_g2[128:160, :], writes=["rk_w"], key="rk_w")
        twd = sb("rk_twd", [64, T], BF16)
        ads = sb("rk_ads", [64, T], BF16)
        sg1 = sb("rk_sg1", [128, T], BF16)
        sg2 = sb("rk_sg2", [32, T], BF16)
        zin = sb("rk_zin", [128, TB + 1], F32)
        dd = sb("rk_dd", [128, TB], F32)
        for (row0, nr, mcol, dst, func) in ((3072, 64, 520, twd, AF.Tanh), (3136, 64, 521, ads, AF.Copy),
                                            (3200, 128, 522, sg1, AF.Sigmoid), (3328, 32, 523, sg2, AF.Sigmoid)):
            for j in range(NB):
                if j == 0:
                    P.op("dve", lambda e: e.memset(zin[:, 0:1], 0.0), [], ["rk_zin"])
                    P.dma("sp", zin[0:nr, 1:TB + 1], zrwT[row0:row0 + nr, 0:TB], reads=["dram:zrwT"], writes=["rk_zin"],
                          key="rk_zin")
                else:
                    P.dma("sp", zin[0:nr, :], zrwT[row0:row0 + nr, j * TB - 1:(j + 1) * TB], reads=["dram:zrwT"],
                          writes=["rk_zin"], key="rk_zin")
                P.op("dve", lambda e, nr=nr: e.tensor_tensor(out=dd[0:nr, :], in0=zin[0:nr, 0:TB], in1=zin[0:nr, 1:TB + 1],
                                                             op=ALU.subtract), ["rk_zin"], ["rk_dd"])
                P.op("dve", lambda e, nr=nr, mcol=mcol: e.scalar_tensor_tensor(
                    out=dd[0:nr, :], in0=dd[0:nr, :], scalar=V[0:nr, mcol:mcol + 1], in1=zin[0:nr, 1:TB + 1],
                    op0=ALU.mult, op1=ALU.add), ["rk_dd", "rk_zin", "V"], ["rk_dd"])
                P.op("act", lambda e, nr=nr, dst=dst, func=func, j=j: e.activation(
                    out=dst[0:nr, j * TB:(j + 1) * TB], in_=dd[0:nr, :], func=func), ["rk_dd"], ["rk_lin"])
        zt = {n: sb(f"rk_z{n}", [128, TB + 1], F32) for n in "rkv"}
        xs = {n: sb(f"rk_s{n}", [128, TB], F32) for n in "rkv"}
        tA = sb("rk_tA", [128, TB], F32)
        tB = sb("rk_tB", [128, TB], F32)
        tC = sb("rk_tC", [128, TB], F32)
        av = sb("rk_a", [128, TB], F32)
        kkn = sb("rk_kkn", [128, TB], F32)
        k2 = sb("rk_k2", [128, TB], F32)
        logw = sb("rk_logw", [128, TB], F32)
        cum = sb("rk_cum", [128, TB], F32)
        ginv = sb("rk_ginv", [128, TB], F32)
        gprev = sb("rk_gprev", [128, TB], F32)
        tbf = sb("rk_tbf", [128, TB], BF16)
        vbf = sb("rk_vbf", [128, TB], BF16)
        BK = sb("rk_BK", [128, 8, 2, 64], BF16)
        gam2 = [sb(f"rk_gam{i}", [128, TB], F32) for i in range(2)]
        gv2 = [sb(f"rk_g{i}", [128, TB], F32) for i in range(2)]
        bonus2 = [sb(f"rk_bonus{i}", [128, TB], F32) for i in range(2)]
        AR2 = [sb(f"rk_AR{i}", [128, 8, 2, 64], BF16) for i in range(2)]
        PM2 = [sb(f"rk_PM{i}", [128, 8, 2, 128], BF16) for i in range(2)]
        TM2 = [sb(f"rk_TM{i}", [128, 8, 3, 64], BF16) for i in range(2)]
        MT2 = [sb(f"rk_MT{i}", [128, 8, 64], BF16) for i in range(2)]
        NS8 = sb("rk_NS", [128, 8, 2, 64], BF16)
        L8 = sb("rk_L", [128, 8, 64], BF16)
        maskP8 = sb("rk_maskP8", [128, 8, 2, 128], F32)
        maskL8 = sb("rk_maskL8", [128, 8, 64], F32)
        eye8 = sb("rk_eye8", [128, 8, 64], F32)
        for c in range(8):
            P.op("pool", lambda e, c=c: e.tensor_copy(out=maskP8[:, c, :, :], in_=maskP[:]), ["msk"], ["rk_m8"])
            P.op("pool", lambda e, c=c: e.tensor_copy(out=maskL8[:, c, :], in_=maskL[:]), ["msk"], ["rk_m8"])
            P.op("pool", lambda e, c=c: e.tensor_copy(out=eye8[:, c, :], in_=eye[:]), ["msk"], ["rk_m8"])
        Hf = sb("rk_Hf", [128, 64], F32)
        HG = sb("rk_HG", [128, 64], F32)
        Hb = sb("rk_Hb", [128, 64], BF16)
        Zs = sb("rk_Zs", [128, 64], BF16)
        Us = sb("rk_Us", [128, 64], BF16)
        Yt = sb("rk_Y", [128, TB], F32)
        yo = sb("rk_yo", [128, TB], BF16)
        ps_a = P.ps("rk_psa", [128, TB], F32)
        ps_b = ps_a
        W = P.ps("rk_Wps", [128, 2048], F32)
        W5 = P.ps("rk_W5ps", [128, 512], F32)
        small2 = P.ps("rk_small2", [128, 512], F32)
        yps = P.ps("rk_yps", [128, TB], F32)
        trpv = W[:, 0:1536].rearrange("p (c a t) -> p c a t", c=8, a=3)
        ppv = W[:, 0:2048].rearrange("p (c a t) -> p c a t", c=8, a=2)
        ps1v = W[:, 0:1024].rearrange("p (c t) -> p c t", c=8)
        ps2v = W[:, 1024:1536].rearrange("p (c t) -> p c t", c=8)
        lpv = W5[:, 0:512].rearrange("p (c t) -> p c t", c=8)
        sq = small2[:, 0:192].rearrange("p (a t) -> p a t", a=3)

        def dve(fn, r, w):
            P.op("dve", fn, r, w)

        def act(fn, r, w):
            P.op("act", fn, r, w)

        def pe(fn, r, w):
            P.op("pe", fn, r, w)

        def c3(ap):
            return ap.rearrange("p (c t) -> p c t", t=64)

        def blk_names(pr, j, par):
            pc = slice(pr * 128, (pr + 1) * 128)
            vcol = lambda base: V[:, base + pr:base + pr + 1]
            js = slice(j * TB, (j + 1) * TB)
            return (pc, vcol, js, gam2[par], gv2[par], bonus2[par], AR2[par], PM2[par], TM2[par], MT2[par],
                    f"rk_gam{par}", f"rk_g{par}", f"rk_bonus{par}", f"rk_AR{par}", f"rk_PM{par}", f"rk_TM{par}", f"rk_MT{par}")

        eA = sb("rk_eA", [128, TB], F32)
        eB = sb("rk_eB", [128, TB], F32)
        ebf = sb("rk_ebf", [128, TB], BF16)

        def gen_pre(pr, j, par):
            pc, vcol, js, gam, gv, bonus, AR, PM, TM, MTs, rGAM, rG, rBON, rAR, rPM, rTM, rMT = blk_names(pr, j, par)
            for gi, n in enumerate("rkv"):
                row0 = gi * 1024 + pr * 128
                if j == 0:
                    dve(lambda e, n=n: e.memset(zt[n][:, 0:1], 0.0), [], [f"rk_z{n}"])
                    P.dma("sp", zt[n][:, 1:TB + 1], zrwT[row0:row0 + 128, 0:TB], reads=["dram:zrwT"],
                          writes=[f"rk_z{n}"], key=f"rk_z{n}")
                else:
                    P.dma("sp", zt[n][:], zrwT[row0:row0 + 128, j * TB - 1:(j + 1) * TB], reads=["dram:zrwT"],
                          writes=[f"rk_z{n}"], key=f"rk_z{n}")
                mcol = 80 + gi * 8 + pr
                dve(lambda e, n=n: e.tensor_tensor(out=xs[n][:], in0=zt[n][:, 0:TB], in1=zt[n][:, 1:TB + 1],
                                                   op=ALU.subtract), [f"rk_z{n}"], [f"rk_s{n}"])
                dve(lambda e, n=n, mcol=mcol: e.scalar_tensor_tensor(
                    out=xs[n][:], in0=xs[n][:], scalar=V[:, mcol:mcol + 1], in1=zt[n][:, 1:TB + 1], op0=ALU.mult,
                    op1=ALU.add), [f"rk_s{n}", f"rk_z{n}", "V"], [f"rk_s{n}"])
            js = slice(j * TB, (j + 1) * TB)
            yield
            pe(lambda e: e.matmul(ps_a[:], lhsT=w2b[:, pc], rhs=twd[:, js], start=True, stop=True),
               ["rk_w", "rk_lin"], ["rk_psa"])
            act(lambda e: e.activation(out=logw[:], in_=ps_a[:], func=AF.Sigmoid, bias=vcol(104)), ["rk_psa", "V"],
                ["rk_logw"])
            dve(lambda e: e.tensor_scalar(out=logw[:], in0=logw[:], scalar1=NEG_E, scalar2=None, op0=ALU.mult),
                ["rk_logw"], ["rk_logw"])
            pe(lambda e: e.matmul(ps_b[:], lhsT=a2b[:, pc], rhs=ads[:, js], start=True, stop=True),
               ["rk_w", "rk_lin"], ["rk_psa"])
            act(lambda e: e.activation(out=av[:], in_=ps_b[:], func=AF.Sigmoid, bias=vcol(112)), ["rk_psa", "V"],
                ["rk_a"])
            pe(lambda e: e.matmul(ps_a[:], lhsT=g2b[:, 0, pc], rhs=sg1[:, js], start=True, stop=False),
               ["rk_w", "rk_lin"], ["rk_psa"])
            pe(lambda e: e.matmul(ps_a[:], lhsT=g2b[0:32, 1, pc], rhs=sg2[:, js], start=False, stop=True),
               ["rk_w", "rk_lin"], ["rk_psa"])
            act(lambda e: e.activation(out=gv[:], in_=ps_a[:], func=AF.Copy), ["rk_psa"], [rG])
            yield
            dve(lambda e: e.tensor_scalar(out=tA[:], in0=xs["k"][:], scalar1=vcol(120), scalar2=None, op0=ALU.mult),
                ["rk_sk", "V"], ["rk_tA"])
            dve(lambda e: e.tensor_tensor(out=tbf[:], in0=tA[:], in1=tA[:], op=ALU.mult), ["rk_tA"], ["rk_tbf"])
            pe(lambda e: e.matmul(ps_b[:], lhsT=bones_bf[:], rhs=tbf[:], start=True, stop=True),
               ["bones_bf", "rk_tbf"], ["rk_psa"])
            act(lambda e: e.activation(out=tB[:], in_=ps_b[:], func=AF.Sqrt), ["rk_psa"], ["rk_tB"])
            dve(lambda e: e.tensor_scalar(out=tB[:], in0=tB[:], scalar1=1e-12, scalar2=None, op0=ALU.max),
                ["rk_tB"], ["rk_tB"])
            dve(lambda e: e.reciprocal(out=tB[:], in_=tB[:]), ["rk_tB"], ["rk_tB"])
            dve(lambda e: e.tensor_tensor(out=kkn[:], in0=tA[:], in1=tB[:], op=ALU.mult), ["rk_tA", "rk_tB"],
                ["rk_kkn"])
            yield
            dve(lambda e: e.tensor_scalar(out=tA[:], in0=av[:], scalar1=vcol(128), scalar2=omka[:, pr:pr + 1],
                                          op0=ALU.mult, op1=ALU.add), ["rk_a", "V", "omka"], ["rk_tA"])
            dve(lambda e: e.tensor_tensor(out=k2[:], in0=xs["k"][:], in1=tA[:], op=ALU.mult), ["rk_sk", "rk_tA"],
                ["rk_k2"])
            yield
            dve(lambda e: e.tensor_tensor_scan(out=cum[:], data0=rmask[:], data1=logw[:], initial=0.0, op0=ALU.mult,
                                               op1=ALU.add), ["msk", "rk_logw"], ["rk_cum"])
            act(lambda e: e.activation(out=gam[:], in_=cum[:], func=AF.Exp), ["rk_cum"], [rGAM])
            act(lambda e: e.activation(out=ginv[:], in_=cum[:], func=AF.Exp, scale=-1.0), ["rk_cum"], ["rk_ginv"])
            dve(lambda e: e.tensor_tensor(out=tB[:], in0=cum[:], in1=logw[:], op=ALU.subtract),
                ["rk_cum", "rk_logw"], ["rk_tB"])
            act(lambda e: e.activation(out=gprev[:], in_=tB[:], func=AF.Exp), ["rk_tB"], ["rk_gprev"])
            yield
            dve(lambda e: e.scalar_tensor_tensor(out=AR[:, :, 0, :], in0=c3(kkn[:]), scalar=-1.0, in1=c3(gprev[:]),
                                                 op0=ALU.mult, op1=ALU.mult), ["rk_kkn", "rk_gprev"], [rAR])
            dve(lambda e: e.tensor_tensor(out=AR[:, :, 1, :], in0=c3(xs["r"][:]), in1=c3(gam[:]), op=ALU.mult),
                ["rk_sr", rGAM, rAR], [rAR])
            dve(lambda e: e.tensor_tensor(out=tA[:], in0=kkn[:], in1=av[:], op=ALU.mult), ["rk_kkn", "rk_a"],
                ["rk_tA"])
            dve(lambda e: e.tensor_tensor(out=BK[:, :, 0, :], in0=c3(tA[:]), in1=c3(ginv[:]), op=ALU.mult),
                ["rk_tA", "rk_ginv"], ["rk_BK"])
            dve(lambda e: e.tensor_tensor(out=BK[:, :, 1, :], in0=c3(k2[:]), in1=c3(ginv[:]), op=ALU.mult),
                ["rk_k2", "rk_ginv", "rk_BK"], ["rk_BK"])
            act(lambda e: e.activation(out=vbf[:], in_=xs["v"][:], func=AF.Copy), ["rk_sv"], ["rk_vbf"])
            yield
            dve(lambda e: e.tensor_tensor(out=tC[:], in0=xs["r"][:], in1=k2[:], op=ALU.mult), ["rk_sr", "rk_k2"],
                ["rk_tC"])
            dve(lambda e: e.tensor_scalar(out=tbf[:], in0=tC[:], scalar1=vcol(136), scalar2=None, op0=ALU.mult),
                ["rk_tC", "V", "rk_tbf"], ["rk_tbf"])
            pe(lambda e: e.matmul(ps_b[:], lhsT=bones_bf[:], rhs=tbf[:], start=True, stop=True),
               ["bones_bf", "rk_tbf"], ["rk_psa"])
            dve(lambda e: e.tensor_tensor(out=bonus[:], in0=ps_b[:], in1=xs["v"][:], op=ALU.mult),
                ["rk_psa", "rk_sv"], [rBON])
            yield
            yield
            for c in range(8):
                cc = slice(c * 64, (c + 1) * 64)
                for h in range(2):
                    sl = slice(64 * h, 64 * h + 64)
                    for ti in range(3):
                        in_ap = BK[sl, c, ti, :] if ti < 2 else vbf[sl, cc]
                        pe(lambda e, sl=sl, ti=ti, in_ap=in_ap, c=c: e.matmul(trpv[sl, c, ti, :], lhsT=in_ap,
                                                                             rhs=ident_bf[sl, sl], start=True, stop=True),
                           ["rk_BK", "rk_vbf", "ident_bf"], ["rk_Wps"])
            act(lambda e: e.activation(out=TM[:], in_=trpv, func=AF.Copy), ["rk_Wps"], [rTM])
            yield
            for c in range(8):
                for h in range(2):
                    sl = slice(64 * h, 64 * h + 64)
                    arv = AR[sl, c, :, :].rearrange("p a t -> p (a t)")
                    pe(lambda e, sl=sl, c=c, arv=arv: e.matmul(ppv[sl, c, 0, :], lhsT=BK[sl, c, 0, :], rhs=arv, start=True,
                                                               stop=True), ["rk_BK", rAR], ["rk_Wps"])
                    pe(lambda e, sl=sl, c=c, arv=arv: e.matmul(ppv[sl, c, 1, :], lhsT=BK[sl, c, 1, :], rhs=arv, start=True,
                                                               stop=True), ["rk_BK", rAR], ["rk_Wps"])
                    pe(lambda e, sl=sl, c=c: e.matmul(lpv[sl, c, :], lhsT=AR[sl, c, 0, :], rhs=BK[sl, c, 0, :], start=True,
                                                      stop=True), ["rk_BK", rAR], ["rk_W5ps"])
            dve(lambda e: e.tensor_tensor(out=PM[:], in0=ppv, in1=maskP8[:], op=ALU.mult), ["rk_Wps", "rk_m8"], [rPM])
            dve(lambda e: e.tensor_tensor(out=L8[:], in0=lpv, in1=maskL8[:], op=ALU.mult), ["rk_W5ps", "rk_m8"], ["rk_L"])
            act(lambda e: e.activation(out=NS8[:, :, 0, :], in_=PM[:, :, 0, 0:64], func=AF.Copy), [rPM], ["rk_NS"])
            P.op("pool", lambda e: e.tensor_copy(out=NS8[:, :, 1, :], in_=eye8[:]), ["rk_m8", "rk_NS"], ["rk_NS"])
            yield
            for step in range(6):
                last = step == 5
                for c in range(8):
                    for h in range(2):
                        sl = slice(64 * h, 64 * h + 64)
                        if not last:
                            nsv = NS8[sl, c, :, :].rearrange("p a t -> p (a t)")
                            pe(lambda e, sl=sl, nsv=nsv, c=c: e.matmul(ps1v[sl, c, :], lhsT=L8[sl, c, :], rhs=nsv, start=True,
                                                                       stop=True), ["rk_L", "rk_NS"], ["rk_Wps"])
                            pe(lambda e, sl=sl, c=c: e.matmul(ps2v[sl, c, :], lhsT=NS8[sl, c, 0, :], rhs=L8[sl, c, :],
                                                              start=True, stop=True), ["rk_L", "rk_NS"], ["rk_Wps"])
                        else:
                            pe(lambda e, sl=sl, c=c: e.matmul(ps1v[sl, c, 64:128], lhsT=L8[sl, c, :], rhs=NS8[sl, c, 1, :],
                                                              start=True, stop=True), ["rk_L", "rk_NS"], ["rk_Wps"])
                if not last:
                    dve(lambda e: e.tensor_copy(out=NS8[:, :, 0, :], in_=ps1v[:, :, 0:64]), ["rk_Wps", "rk_NS"], ["rk_NS"])
                    dve(lambda e: e.tensor_tensor(out=NS8[:, :, 1, :], in0=NS8[:, :, 1, :], in1=ps1v[:, :, 64:128],
                                                  op=ALU.add), ["rk_Wps", "rk_NS"], ["rk_NS"])
                    dve(lambda e: e.tensor_copy(out=L8[:], in_=ps2v), ["rk_Wps", "rk_L"], ["rk_L"])
                else:
                    dve(lambda e: e.tensor_tensor(out=MTs[:], in0=NS8[:, :, 1, :], in1=ps1v[:, :, 64:128], op=ALU.add),
                        ["rk_Wps", "rk_NS"], [rMT])
                yield
            yield

        def gen_seq(pr, j, par):
            pc, vcol, js, gam, gv, bonus, AR, PM, TM, MTs, rGAM, rG, rBON, rAR, rPM, rTM, rMT = blk_names(pr, j, par)
            if j == 0:
                dve(lambda e: e.memset(Hf[:], 0.0), [], ["rk_Hf"])
                dve(lambda e: e.memset(Hb[:], 0.0), [], ["rk_Hb"])
            for c in range(8):
                cc = slice(c * 64, (c + 1) * 64)
                gcol = gam[:, c * 64 + 63:c * 64 + 64]
                for h in range(2):
                    sl = slice(64 * h, 64 * h + 64)
                    pe(lambda e, sl=sl, c=c: e.matmul(sq[sl, 0, :], lhsT=AR[sl, c, 0, :], rhs=Hb[sl, :], start=True,
                                                      stop=False), [rAR, "rk_Hb"], ["rk_small2"])
                    pe(lambda e, sl=sl, c=c: e.matmul(sq[sl, 0, :], lhsT=PM[sl, c, 1, 0:64], rhs=TM[sl, c, 2, :],
                                                      start=False, stop=True), [rPM, rTM], ["rk_small2"])
                act(lambda e: e.activation(out=Zs[:], in_=sq[:, 0, :], func=AF.Copy), ["rk_small2"], ["rk_Zs"])
                for h in range(2):
                    sl = slice(64 * h, 64 * h + 64)
                    pe(lambda e, sl=sl, c=c: e.matmul(sq[sl, 1, :], lhsT=MTs[sl, c, :], rhs=Zs[sl, :], start=True,
                                                      stop=True), [rMT, "rk_Zs"], ["rk_small2"])
                act(lambda e: e.activation(out=Us[:], in_=sq[:, 1, :], func=AF.Copy), ["rk_small2"], ["rk_Us"])
                for h in range(2):
                    sl = slice(64 * h, 64 * h + 64)
                    pe(lambda e, sl=sl, c=c: e.matmul(sq[sl, 2, :], lhsT=TM[sl, c, 0, :], rhs=Us[sl, :], start=True,
                                                      stop=False), [rTM, "rk_Us"], ["rk_small2"])
                    pe(lambda e, sl=sl, c=c: e.matmul(sq[sl, 2, :], lhsT=TM[sl, c, 1, :], rhs=TM[sl, c, 2, :],
                                                      start=False, stop=True), [rTM], ["rk_small2"])
                    pe(lambda e, sl=sl, c=c, cc=cc: e.matmul(yps[sl, cc], lhsT=Hb[sl, :], rhs=AR[sl, c, 1, :], start=True,
                                                             stop=False), [rAR, "rk_Hb"], ["rk_yps"])
                    pe(lambda e, sl=sl, c=c, cc=cc: e.matmul(yps[sl, cc], lhsT=Us[sl, :], rhs=PM[sl, c, 0, 64:128],
                                                             start=False, stop=False), ["rk_Us", rPM], ["rk_yps"])
                    pe(lambda e, sl=sl, c=c, cc=cc: e.matmul(yps[sl, cc], lhsT=TM[sl, c, 2, :], rhs=PM[sl, c, 1, 64:128],
                                                             start=False, stop=True), [rTM, rPM], ["rk_yps"])
                P.op("pool", lambda e, gcol=gcol: e.tensor_scalar(out=HG[:], in0=Hf[:], scalar1=gcol, scalar2=None,
                                                                   op0=ALU.mult), ["rk_Hf", rGAM], ["rk_HG"])
                dve(lambda e, gcol=gcol: e.scalar_tensor_tensor(out=Hf[:], in0=sq[:, 2, :], scalar=gcol, in1=HG[:],
                                                                op0=ALU.mult, op1=ALU.add),
                    ["rk_small2", "rk_HG", rGAM, "rk_Hf"], ["rk_Hf"])
                act(lambda e: e.activation(out=Hb[:], in_=Hf[:], func=AF.Copy), ["rk_Hf", "rk_Hb"], ["rk_Hb"])
                yield
            act(lambda e: e.activation(out=Yt[:], in_=yps[:], func=AF.Copy), ["rk_yps"], ["rk_Y"])
            dve(lambda e: e.tensor_copy(out=ebf[:], in_=Yt[:]), ["rk_Y", "rk_ebf"], ["rk_ebf"])
            pe(lambda e: e.matmul(ps_a[:], lhsT=bones_bf[:], rhs=ebf[:], start=True, stop=True), ["bones_bf", "rk_ebf"],
               ["rk_psa"])
            dve(lambda e: e.scalar_tensor_tensor(out=eA[:], in0=ps_a[:], scalar=-1.0 / 64, in1=Yt[:], op0=ALU.mult,
                                                 op1=ALU.add), ["rk_psa", "rk_Y"], ["rk_eA"])
            dve(lambda e: e.tensor_tensor(out=ebf[:], in0=eA[:], in1=eA[:], op=ALU.mult), ["rk_eA", "rk_ebf"],
                ["rk_ebf"])
            pe(lambda e: e.matmul(ps_a[:], lhsT=bones_bf[:], rhs=ebf[:], start=True, stop=True), ["bones_bf", "rk_ebf"],
               ["rk_psa"])
            act(lambda e: e.activation(out=eB[:], in_=ps_a[:], func=AF.Sqrt, bias=GN_EPS, scale=1.0 / 64),
                ["rk_psa"], ["rk_eB"])
            yield
            dve(lambda e: e.reciprocal(out=eB[:], in_=eB[:]), ["rk_eB"], ["rk_eB"])
            dve(lambda e: e.tensor_tensor(out=eA[:], in0=eA[:], in1=eB[:], op=ALU.mult), ["rk_eA", "rk_eB"], ["rk_eA"])
            dve(lambda e: e.tensor_scalar(out=eA[:], in0=eA[:], scalar1=vcol(144), scalar2=vcol(152), op0=ALU.mult,
                                          op1=ALU.add), ["rk_eA", "V"], ["rk_eA"])
            dve(lambda e: e.tensor_tensor(out=eA[:], in0=eA[:], in1=bonus[:], op=ALU.add), ["rk_eA", rBON],
                ["rk_eA"])
            dve(lambda e: e.tensor_tensor(out=yo[:], in0=eA[:], in1=gv[:], op=ALU.mult), ["rk_eA", rG], ["rk_yo"])
            P.dma("act", yrwT[pc, js], yo[:], reads=["rk_yo"], writes=["dram:yrwT"], key="rk_yo")
            yield

        blks = []
        for pr in range(DBG.get("rkp", 8)):
            for j in range(DBG.get("rkb", NB)):
                blks.append((pr, j, len(blks) % 2))
        prevb = None
        for blk in blks + [None]:
            gens = []
            if blk is not None:
                gens.append(gen_pre(*blk))
            if prevb is not None:
                gens.append(gen_seq(*prevb))
            while gens:
                for g in list(gens):
                    try:
                        next(g)
                    except StopIteration:
                        gens.remove(g)
            prevb = blk
        P.barrier()
        P.stack = prev_stack


def mla_stage(P, nc, st0, V, pos, pos_w, qk_tab, d_tab, zmlaT, cqnT_full, ckvnT, w_q_up, w_kv_up, ymlaT, ones_bf, evac_copy):
    SCALE = 192 ** -0.5
    TWO_PI = 2.0 * math.pi
    QB = 342
    with contextlib.ExitStack() as st:
        prev_stack = P.stack
        P.stack = st
        sb = P.sb
        wq = sb("ml_wq", [128, 4, 1536], BF16)
        wkv = sb("ml_wkv", [128, 4, 2048], BF16)
        for kc in range(4):
            P.dma("pool", wq[:, kc, :], w_q_up[kc * 128:(kc + 1) * 128, :], writes=["ml_w"], key="ml_w")
            P.dma("pool", wkv[:, kc, :], w_kv_up[kc * 128:(kc + 1) * 128, :], writes=["ml_w"], key="ml_w")
        qk = sb("ml_qk", [128, 3, QB], F32)
        dt = sb("ml_dt", [128, 32], F32)
        P.dma("sp", qk[:], qk_tab.rearrange("p (a t) -> p a t", a=3), writes=["ml_tab"], key="ml_tab")
        P.dma("sp", dt[:], d_tab, writes=["ml_tab"], key="ml_tab")
        cs = sb("ml_cs", [64, T], F32)
        sn = sb("ml_sn", [64, T], F32)
        csw = sb("ml_csw", [64, TW], F32)
        snw = sb("ml_snw", [64, TW], F32)
        with contextlib.ExitStack() as st2:
            P.stack = st2
            posi = P.sb("ml_posi", [64, T], I32)
            ang = P.sb("ml_ang", [64, T], F32)
            tt = P.sb("ml_tt", [64, T], F32)
            kf = P.sb("ml_kf", [64, T], F32)
            for (psrc, n, cdst, sdst) in ((pos, T, cs, sn), (pos_w, TW, csw, snw)):
                P.dma("sp", posi[:, 0:n], psrc[0:1, :].to_broadcast([64, n]), writes=["ml_posi"], key="ml_posi")
                P.op("dve", lambda e: e.tensor_copy(out=ang[:, 0:n], in_=posi[:, 0:n]), ["ml_posi"], ["ml_ang"])
                P.op("dve", lambda e: e.tensor_scalar(out=ang[:, 0:n], in0=ang[:, 0:n], scalar1=V[0:64, 524:525], scalar2=None,
                                                      op0=ALU.mult), ["ml_ang", "V"], ["ml_ang"])
                for (dst, shift) in ((sdst, 0.5), (cdst, 0.75)):
                    P.op("dve", lambda e, shift=shift: e.tensor_scalar(out=tt[:, 0:n], in0=ang[:, 0:n], scalar1=1.0 / TWO_PI,
                                                                       scalar2=shift, op0=ALU.mult, op1=ALU.add),
                         ["ml_ang", "ml_tt"], ["ml_tt"])
                    P.op("dve", lambda e: e.tensor_copy(out=posi[:, 0:n], in_=tt[:, 0:n]), ["ml_tt", "ml_posi", "ml_ang"],
                         ["ml_posi"])
                    P.op("dve", lambda e: e.tensor_copy(out=kf[:, 0:n], in_=posi[:, 0:n]), ["ml_posi"], ["ml_kf"])
                    P.op("dve", lambda e: e.tensor_tensor(out=tt[:, 0:n], in0=tt[:, 0:n], in1=kf[:, 0:n], op=ALU.subtract),
                         ["ml_tt", "ml_kf"], ["ml_tt"])
                    P.op("dve", lambda e: e.tensor_scalar(out=kf[:, 0:n], in0=tt[:, 0:n], scalar1=0.0, scalar2=None,
                                                          op0=ALU.is_lt), ["ml_tt", "ml_kf"], ["ml_kf"])
                    P.op("dve", lambda e: e.scalar_tensor_tensor(out=tt[:, 0:n], in0=kf[:, 0:n], scalar=-0.5, in1=tt[:, 0:n],
                                                                 op0=ALU.add, op1=ALU.add), ["ml_tt", "ml_kf"], ["ml_tt"])
                    P.op("act", lambda e, dst=dst: e.activation(out=dst[:, 0:n], in_=tt[:, 0:n], func=AF.Sin, scale=TWO_PI),
                         ["ml_tt"], ["ml_rope"])
            P.barrier()
            P.stack = st
        kr = sb("ml_kr", [64, T], BF16)
        kn = sb("ml_kn", [128, T], BF16)
        vtm = sb("ml_vtm", [128, 32, 128], BF16)
        xq = [sb(f"ml_xq{i}", [128, 4, TB], BF16) for i in range(2)]
        raw = sb("ml_raw", [64, TB], F32)
        t1 = sb("ml_t1", [64, TB], F32)
        t2 = sb("ml_t2", [64, TB], F32)
        qn = sb("ml_qn", [128, TB], BF16)
        qr = sb("ml_qr", [64, TB], BF16)
        pT = [sb(f"ml_pT{i}", [128, TB], BF16) for i in range(3)]
        rinv = sb("ml_rinv", [128, TB], F32)
        yo = [sb(f"ml_yo{i}", [128, TB], BF16) for i in range(2)]
        psq = P.ps("ml_psq", [128, TB], F32)
        psr = P.ps("ml_psr", [64, TB], F32)
        psv = P.ps("ml_psv", [128, 4, 128], F32)
        sps = [P.ps(f"ml_sps{i}", [128, TB], F32) for i in range(2)]
        ops_ = P.ps("ml_ops", [128, TB], F32)
        lps = P.ps("ml_lps", [128, TB], F32)

        def rope(src_res, cst, snt, c0, n, out_ap, out_res, scale):
            lo, hi = slice(0, 32), slice(32, 64)
            js = slice(c0, c0 + n)
            d = lambda fn, r, w: P.op("dve", fn, r, w)
            d(lambda e: e.tensor_tensor(out=t1[lo, 0:n], in0=raw[lo, 0:n], in1=cst[lo, js], op=ALU.mult), [src_res, "ml_rope"],
              ["ml_t1"])
            d(lambda e: e.tensor_tensor(out=t2[lo, 0:n], in0=raw[hi, 0:n], in1=snt[hi, js], op=ALU.mult), [src_res, "ml_rope"],
              ["ml_t2"])
            d(lambda e: e.tensor_tensor(out=t1[hi, 0:n], in0=raw[hi, 0:n], in1=cst[hi, js], op=ALU.mult),
              [src_res, "ml_rope", "ml_t1"], ["ml_t1"])
            d(lambda e: e.tensor_tensor(out=t2[hi, 0:n], in0=raw[lo, 0:n], in1=snt[lo, js], op=ALU.mult),
              [src_res, "ml_rope", "ml_t2"], ["ml_t2"])
            d(lambda e: e.tensor_tensor(out=t1[lo, 0:n], in0=t1[lo, 0:n], in1=t2[lo, 0:n], op=ALU.subtract), ["ml_t1", "ml_t2"],
              ["ml_t1"])
            d(lambda e: e.tensor_tensor(out=t1[hi, 0:n], in0=t1[hi, 0:n], in1=t2[hi, 0:n], op=ALU.add), ["ml_t1", "ml_t2"],
              ["ml_t1"])
            P.op("act", lambda e: e.activation(out=out_ap, in_=t1[:, 0:n], func=AF.Copy, scale=scale), ["ml_t1"], [out_res])

        for j in range(NB):
            js = slice(j * TB, (j + 1) * TB)
            P.dma("sp", raw[:], zmlaT[1024:1088, js], reads=["dram:zmlaT"], writes=["ml_raw"], key="ml_raw")
            rope("ml_raw", cs, sn, j * TB, TB, kr[:, js], "ml_kr", 1.0)
        xi = 0
        for hd in range(DBG.get("mlh", 8)):
            for j in range(NB):
                js = slice(j * TB, (j + 1) * TB)
                b = xi % 2
                xi += 1
                P.dma("sp", xq[b][:], ckvnT[:, js].rearrange("(k p) t -> p k t", p=128), reads=["dram:ckvnT"],
                      writes=[f"ml_xq{b}"], key=f"ml_xq{b}")
                for kc in range(4):
                    P.op("pe", lambda e, kc=kc, b=b: e.matmul(psq[:], lhsT=wkv[:, kc, hd * 256:hd * 256 + 128],
                                                              rhs=xq[b][:, kc, :], start=(kc == 0), stop=(kc == 3)),
                         ["ml_w", f"ml_xq{b}"], ["ml_psq"])
                evac_copy(kn[:, js], psq[:], ["ml_psq"], ["ml_kn"])
                for tt_ in range(4):
                    for kc in range(4):
                        P.op("pe", lambda e, kc=kc, b=b, tt_=tt_: e.matmul(
                            psv[:, tt_, :], lhsT=xq[b][:, kc, tt_ * 128:(tt_ + 1) * 128],
                            rhs=wkv[:, kc, hd * 256 + 128:hd * 256 + 256], start=(kc == 0), stop=(kc == 3)),
                            ["ml_w", f"ml_xq{b}"], ["ml_psv"])
                evac_copy(vtm[:, j * 4:(j + 1) * 4, :], psv[:], ["ml_psv"], ["ml_vtm"])
            for jj in range(DBG.get("mlb", 3)):
                n = QB
                b = xi % 2
                xi += 1

                P.dma("sp", xq[b][:, :, 0:QB], cqnT_full[:, jj * QB:(jj + 1) * QB].rearrange("(k p) t -> p k t", p=128),
                      reads=["dram:cqwT"], writes=[f"ml_xq{b}"], key=f"ml_xq{b}")
                for kc in range(4):
                    P.op("pe", lambda e, kc=kc, b=b: e.matmul(psq[:, 0:n], lhsT=wq[:, kc, hd * 192:hd * 192 + 128],
                                                              rhs=xq[b][:, kc, 0:n], start=(kc == 0), stop=(kc == 3)),
                         ["ml_w", f"ml_xq{b}"], ["ml_psq"])
                P.op("act", lambda e: e.activation(out=qn[:, 0:n], in_=psq[:, 0:n], func=AF.Copy, scale=SCALE), ["ml_psq"],
                     ["ml_qn"])
                for kc in range(4):
                    P.op("pe", lambda e, kc=kc, b=b: e.matmul(psr[:, 0:n], lhsT=wq[:, kc, hd * 192 + 128:hd * 192 + 192],
                                                              rhs=xq[b][:, kc, 0:n], start=(kc == 0), stop=(kc == 3)),
                         ["ml_w", f"ml_xq{b}"], ["ml_psr"])
                P.op("act", lambda e: e.activation(out=raw[:, 0:n], in_=psr[:, 0:n], func=AF.Copy), ["ml_psr", "ml_raw"],
                     ["ml_raw"])
                rope("ml_raw", csw, snw, jj * QB, n, qr[:, 0:n], "ml_qr", SCALE)
                nkt = 32

                def score(kt):
                    si = kt % 2
                    ks = slice(kt * 128, (kt + 1) * 128)
                    P.op("pe", lambda e, si=si, ks=ks: e.matmul(sps[si][:, 0:n], lhsT=kn[:, ks], rhs=qn[:, 0:n],
                                                                start=True, stop=False),
                         ["ml_kn", "ml_qn"], [f"ml_sps{si}"])
                    P.op("pe", lambda e, si=si, ks=ks: e.matmul(sps[si][:, 0:n], lhsT=kr[:, ks], rhs=qr[:, 0:n],
                                                                start=False, stop=True),
                         ["ml_kr", "ml_qr"], [f"ml_sps{si}"])

                score(0)
                for kt in range(nkt):
                    si = kt % 2
                    pi = kt % 3
                    if kt + 1 < nkt:
                        score(kt + 1)
                    P.op("act", lambda e, si=si, pi=pi: e.activation(out=pT[pi][:, 0:n], in_=sps[si][:, 0:n], func=AF.Exp),
                         [f"ml_sps{si}"], [f"ml_pT{pi}"])
                    P.op("dve", lambda e, pi=pi, kt=kt, jj=jj: e.scalar_tensor_tensor(
                        out=pT[pi][:, 0:n], in0=qk[:, jj, :], scalar=dt[:, kt:kt + 1], in1=pT[pi][:, 0:n], op0=ALU.is_ge,
                        op1=ALU.mult), [f"ml_pT{pi}", "ml_tab"], [f"ml_pT{pi}"])
                    P.op("pe", lambda e, pi=pi, kt=kt: e.matmul(
                        ops_[:, 0:n], lhsT=vtm[:, kt, :], rhs=pT[pi][:, 0:n], start=(kt == 0), stop=(kt == nkt - 1)),
                        ["ml_vtm", f"ml_pT{pi}"], ["ml_ops"])
                    P.op("pe", lambda e, pi=pi, kt=kt: e.matmul(
                        lps[:, 0:n], lhsT=ones_bf[:], rhs=pT[pi][:, 0:n], start=(kt == 0), stop=(kt == nkt - 1)),
                        ["ones_bf", f"ml_pT{pi}"], ["ml_lps"])
                P.op("dve", lambda e: e.tensor_scalar(out=rinv[:, 0:n], in0=lps[:, 0:n], scalar1=1e-30, scalar2=None,
                                                      op0=ALU.max), ["ml_lps"], ["ml_rinv"])
                P.op("dve", lambda e: e.reciprocal(out=rinv[:, 0:n], in_=rinv[:, 0:n]), ["ml_rinv"], ["ml_rinv"])
                ob = jj % 2
                P.op("dve", lambda e, ob=ob: e.tensor_tensor(out=yo[ob][:, 0:n], in0=ops_[:, 0:n], in1=rinv[:, 0:n], op=ALU.mult),
                     ["ml_ops", "ml_rinv"], [f"ml_yo{ob}"])
                P.dma("act", ymlaT[hd * 128:(hd + 1) * 128, jj * QB:(jj + 1) * QB], yo[ob][:, 0:n], reads=[f"ml_yo{ob}"],
                      writes=["dram:ymlaT"], key=f"ml_yo{ob}")
        P.barrier()
        P.stack = prev_stack


_CACHE = {}


def make_in_maps(inputs):
    sq = lambda a: np.ascontiguousarray(np.asarray(a)[0])
    V = pack_vecs(inputs)
    common = {
        "vecs": V,
        "w_in": sq(inputs["w_in"]), "rw_w2": sq(inputs["rw_w2"]), "rw_a2": sq(inputs["rw_a2"]), "rw_g2": sq(inputs["rw_g2"]),
        "w_q_up": sq(inputs["mla_w_q_up"]), "w_kv_up": sq(inputs["mla_w_kv_up"]),
        "w_brw": sq(inputs["w_branch_rw"]), "w_bmla": sq(inputs["w_branch_mla"]), "w_out": sq(inputs["w_out"]),
        "w_up": sq(inputs["w_up"]), "w_down": sq(inputs["w_down"]), "w_ple": sq(inputs["w_ple"]),
        "w_pg": sq(inputs["w_ple_gate"]),
    }
    i_ = np.arange(342)
    qk = np.zeros((128, 3, 342), np.float32)
    for jj in range(3):
        qc = np.floor_divide(342 * jj - 2 + i_, 64).astype(np.float32)
        qk[:, jj, :] = qc[None, :] - (np.arange(128)[:, None] >= 64).astype(np.float32)
    common["gain_bc"] = np.ascontiguousarray(np.broadcast_to(np.asarray(inputs["pre_mix_norm"], np.float32).reshape(1, -1), (128, 2048)))
    common["qk_tab"] = np.ascontiguousarray(qk.reshape(128, 3 * 342))
    xs = np.asarray(inputs["x"], np.float32)
    ps = np.asarray(inputs["p"], np.float32)[0]
    posn = np.asarray(inputs["positions"], np.int32)
    maps = []
    for c in range(8):
        b, q = c // 4, c % 4
        m = dict(common)
        m["x"] = np.ascontiguousarray(xs[b])
        xwin = np.zeros((1152, xs.shape[2]), np.float32)
        lo_ = 1024 * q - 2
        if lo_ < 0:
            xwin[2:TW] = xs[b, 0:1024]
        else:
            xwin[0:TW] = xs[b, lo_:lo_ + TW]
        m["xw"] = xwin
        m["p"] = np.ascontiguousarray(ps[b, 1024 * q:1024 * (q + 1)])
        m["pos"] = np.ascontiguousarray(posn[b:b + 1])
        pw = np.zeros((1, TW), np.int32)
        lo = 1024 * q - 2
        if lo < 0:
            pw[0, 2:] = posn[b, 0:1024]
        else:
            pw[0, :] = posn[b, lo:lo + TW]
        m["pos_w"] = pw
        m["d_tab"] = np.ascontiguousarray(np.broadcast_to((2.0 * np.arange(32) - 16.0 * q).astype(np.float32)[None, :], (128, 32)))
        maps.append(m)
    return maps


def kernel(**inputs):
    if "nc" not in _CACHE:
        _CACHE["nc"] = build_program()
    nc = _CACHE["nc"]
    maps = make_in_maps(inputs)
    res = run_bass_kernel_spmd(nc, maps, core_ids=list(range(8)))
    outs = [np.asarray(res.results[c]["out"], np.float32) for c in range(8)]
    return np.stack([np.concatenate(outs[0:4], axis=0), np.concatenate(outs[4:8], axis=0)], axis=0)
```

```python
import contextlib
import numpy as np
import concourse.bass as bass
import concourse.mybir as mybir

F32 = mybir.dt.float32
BF16 = mybir.dt.bfloat16
I32 = mybir.dt.int32
ALU = mybir.AluOpType
AF = mybir.ActivationFunctionType
AX = mybir.AxisListType


SAME_ENGINE_SYNC = True


class _Rec:
    def __init__(self):
        self.calls = []

    def __getattr__(self, name):
        def f(*a, **k):
            self.calls.append((name, a, k))
            return self

        return f


def _eager(fn):
    rec = _Rec()
    fn(rec)
    assert len(rec.calls) == 1, rec.calls
    name, a, k = rec.calls[0]
    return lambda e: getattr(e, name)(*a, **k)


class Prog:
    COMPUTE = ("pe", "act", "dve", "pool")

    def __init__(self, nc, stack):
        self.nc = nc
        self.stack = stack
        self.stack0 = stack
        self.streams = {e: [] for e in ("pe", "act", "dve", "pool", "sp")}
        self.sems = {}
        self.cnt = {}
        for e in self.COMPUTE:
            self.sems[e] = stack.enter_context(nc.semaphore("s_" + e))
            self.cnt[e] = 0
        self.seen = {e: {} for e in self.streams}
        self.res = {}
        self.dma_sems = {}
        self.n_ops = 0

    def sb(self, name, shape, dt):
        return self.stack.enter_context(self.nc.sbuf_tensor(name, list(shape), dt))

    def ps(self, name, shape, dt=F32):
        return self.stack.enter_context(self.nc.psum_tensor(name, list(shape), dt))

    def _dma_sem(self, key):
        if key not in self.dma_sems:
            self.dma_sems[key] = self.stack0.enter_context(self.nc.semaphore("d_" + key))
            self.sems["D:" + key] = self.dma_sems[key]
            self.cnt["D:" + key] = 0
        return "D:" + key

    def _deps(self, eng, reads, writes, exclude=None):
        need = {}
        for r in reads:
            ent = self.res.get(r)
            if ent:
                for k, v in ent[0].items():
                    need[k] = max(need.get(k, 0), v)
                if "ps" in r or "small" in r or "trp" in r:
                    for k, v in ent[1].items():
                        if k != eng:
                            need[k] = max(need.get(k, 0), v)
        for w in writes:
            ent = self.res.get(w)
            if ent:
                if not w.startswith("dram:"):
                    for k, v in ent[0].items():
                        need[k] = max(need.get(k, 0), v)
                for k, v in ent[1].items():
                    need[k] = max(need.get(k, 0), v)
        waits = []
        for k, v in need.items():
            if k == eng and (eng == "pe" or not SAME_ENGINE_SYNC):
                continue
            if k == exclude:
                continue
            if self.seen[eng].get(k, 0) >= v:
                continue
            self.seen[eng][k] = v
            waits.append((k, v))
        return waits

    def _mark(self, key, val, reads, writes):
        for r in reads:
            ent = self.res.setdefault(r, [{}, {}])
            ent[1][key] = val
        for w in writes:
            ent = self.res.setdefault(w, [{}, {}])
            if w.startswith("dram:"):
                ent[0][key] = val
            else:
                ent[0] = {key: val}
                ent[1] = {}
            self.res[w] = ent

    def op(self, eng, fn, reads=(), writes=()):
        fn = _eager(fn)
        waits = self._deps(eng, reads, writes)
        self.cnt[eng] += 1
        val = self.cnt[eng]
        sem = self.sems[eng]
        sems = self.sems

        def emit(e, waits=waits, fn=fn, sem=sem):
            for k, v in waits:
                e.wait_ge(sems[k], v)
            fn(e).then_inc(sem, 1)

        self.streams[eng].append(emit)
        self._mark(eng, val, reads, writes)
        self.n_ops += 1

    def dma(self, queue, out, in_, reads=(), writes=(), key=None, **kw):
        assert key is not None
        sk = self._dma_sem(key)
        waits = self._deps(queue, reads, writes, exclude=sk)
        self.cnt[sk] += 16
        val = self.cnt[sk]
        sem = self.sems[sk]
        sems = self.sems

        def emit(e, waits=waits, sem=sem):
            for k, v in waits:
                e.wait_ge(sems[k], v)
            e.dma_start(out=out, in_=in_, **kw).then_inc(sem, 16)

        self.streams[queue].append(emit)
        self._mark(sk, val, reads, writes)
        self.n_ops += 1

    def custom(self, queue, fn, reads=(), writes=(), key=None, raw=False):
        if not raw:
            fn = _eager(fn)
        sk = self._dma_sem(key)
        waits = self._deps(queue, reads, writes, exclude=sk)
        self.cnt[sk] += 16
        val = self.cnt[sk]
        sem = self.sems[sk]
        sems = self.sems

        def emit(e, waits=waits, sem=sem):
            for k, v in waits:
                e.wait_ge(sems[k], v)
            fn(e).then_inc(sem, 16)

        self.streams[queue].append(emit)
        self._mark(sk, val, reads, writes)
        self.n_ops += 1

    def barrier(self):
        snap = dict(self.cnt)
        sems = self.sems
        for eng in self.streams:
            waits = []
            for k, v in snap.items():
                if v <= 0 or self.seen[eng].get(k, 0) >= v:
                    continue
                if k == eng and eng == "pe":
                    continue
                self.seen[eng][k] = v
                waits.append((k, v))

            def emit(e, waits=waits):
                for k, v in waits:
                    e.wait_ge(sems[k], v)

            self.streams[eng].append(emit)

    def wait_all(self, eng, resources):
        waits = self._deps(eng, resources, ())
        sems = self.sems

        def emit(e, waits=waits):
            for k, v in waits:
                e.wait_ge(sems[k], v)

        self.streams[eng].append(emit)

    def emit(self):
        nc = self.nc
        with nc.Block() as block:
            @block.tensor
            def _(e):
                for f in self.streams["pe"]:
                    f(e)

            @block.scalar
            def _(e):
                for f in self.streams["act"]:
                    f(e)

            @block.vector
            def _(e):
                for f in self.streams["dve"]:
                    f(e)

            @block.gpsimd
            def _(e):
                for f in self.streams["pool"]:
                    f(e)

            @block.sync
            def _(e):
                for f in self.streams["sp"]:
                    f(e)

import math
from concourse.bass_utils import run_bass_kernel_spmd

T = 4096
D = 2048
TB = 512
NB = T // TB
DFF = 5632
EPS = 1e-6
GN_EPS = 64e-5
NV = 528
DBG = {}
TW = 1026
WB = [(0, 342), (342, 342), (684, 342)]
FB = [(j * 512, 512) for j in range(8)]


_RANK = {}


def get_rank(e):
    if "r" not in _RANK:
        _RANK["r"] = (e.partition_id() % 4) * 1024
    return _RANK["r"]


def vec_pc(v):
    v = np.asarray(v, np.float32).reshape(-1)
    return np.ascontiguousarray(v.reshape(-1, 128).T)


def pack_vecs(inp):
    V = np.zeros((128, NV), np.float32)
    V[:, 0:16] = vec_pc(inp["pre_mix_norm"])
    V[:, 16:32] = vec_pc(inp["post_mix_norm"])
    V[:, 32:48] = vec_pc(inp["pre_ffn_norm"])
    V[:, 48:64] = vec_pc(inp["post_ffn_norm"])
    V[:, 64:80] = vec_pc(inp["ple_norm"])
    mu = np.asarray(inp["rw_mu"], np.float32).reshape(-1)
    V[:, 80:104] = vec_pc(mu[0:3072])
    V[:, 104:112] = vec_pc(inp["rw_w0"])
    V[:, 112:120] = vec_pc(inp["rw_a0"])
    V[:, 120:128] = vec_pc(inp["rw_k_k"])
    V[:, 128:136] = vec_pc(inp["rw_k_a"])
    V[:, 136:144] = vec_pc(inp["rw_r_k"])
    V[:, 144:152] = vec_pc(inp["rw_lnx_w"])
    V[:, 152:160] = vec_pc(inp["rw_lnx_b"])
    V[:, 160:164] = vec_pc(inp["mla_q_norm"])
    V[:, 164:168] = vec_pc(inp["mla_kv_norm"])
    V[:, 168:256] = vec_pc(inp["conv_b"])
    cw = np.asarray(inp["conv_w"], np.float32).reshape(3, -1)
    V[:, 256:344] = vec_pc(cw[0])
    V[:, 344:432] = vec_pc(cw[1])
    V[:, 432:520] = vec_pc(cw[2])
    V[0:64, 520] = mu[3072:3136]
    V[0:64, 521] = mu[3136:3200]
    V[0:128, 522] = mu[3200:3328]
    V[0:32, 523] = mu[3328:3360]
    invf = (10000.0 ** (-np.arange(0, 64, 2, dtype=np.float32) / 64)).astype(np.float32)
    V[0:32, 524] = invf
    V[32:64, 524] = invf
    return V


def build_program(debug=()):
    nc = bass.Bass("TRN2", target_bir_lowering=False)
    DBG.clear()
    _RANK.clear()
    for k_ in debug:
        if "=" in k_:
            DBG[k_.split("=")[0]] = int(k_.split("=")[1])
        else:
            DBG[k_] = 1

    def din(name, shape, dt=F32):
        return nc.dram_tensor(name, list(shape), dt, kind="ExternalInput").ap()

    def dscr(name, shape, dt):
        if name in debug:
            return nc.dram_tensor(name, list(shape), dt, kind="ExternalOutput").ap()
        return nc.dram_tensor(name, list(shape), dt).ap()

    x = din("x", [T, D])
    xw = din("xw", [1152, D])
    gain_bc = din("gain_bc", [128, D])
    p_in = din("p", [1024, 256])
    pos = din("pos", [1, T], I32)
    pos_w = din("pos_w", [1, TW], I32)
    qk_tab_d = din("qk_tab", [128, 3 * 342])
    d_tab_d = din("d_tab", [128, 32])
    vecs = din("vecs", [128, NV])
    w_in = din("w_in", [D, 8544])
    rw_w2 = din("rw_w2", [64, 1024])
    rw_a2 = din("rw_a2", [64, 1024])
    rw_g2 = din("rw_g2", [160, 1024])
    w_q_up = din("w_q_up", [512, 1536])
    w_kv_up = din("w_kv_up", [512, 2048])
    w_brw = din("w_brw", [1024, D])
    w_bmla = din("w_bmla", [1024, D])
    w_out = din("w_out", [D, D])
    w_up = din("w_up", [D, 2 * DFF])
    w_down = din("w_down", [DFF, D])
    w_ple = din("w_ple", [256, D])
    w_pg = din("w_pg", [D, D])
    out = nc.dram_tensor("out", [1024, D], F32, kind="ExternalOutput").ap()

    uT_full = dscr("uT", [D, T + 2], BF16)
    yrwT_full = dscr("yrwT", [1024, T + 2], BF16)
    cqnT_full = dscr("cqnT", [512, T + 2], BF16)
    uT = uT_full[:, 2:]
    yrwT = yrwT_full[:, 2:]
    cqnT = cqnT_full[:, 2:]
    zrwT = dscr("zrwT", [3360, T], F32)
    zmlaT = dscr("zmlaT", [1088, T], F32)
    ckvnT = dscr("ckvnT", [512, T], BF16)
    hwT = dscr("hwT", [D, 1152], F32)
    uwT = dscr("uwT", [D, TW], BF16)
    ywT = dscr("ywT", [1024, TW], BF16)
    cqwT = dscr("cqwT", [512, TW], BF16)
    ymlaT = dscr("ymlaT", [1024, TW], BF16)
    pT = dscr("pT", [256, 1024], F32)
    gT = dscr("gT", [4096, TW], BF16)
    mT = dscr("mT", [D, TW], BF16)
    fT = dscr("fT", [D, TW], F32)
    ffT = dscr("ffT", [DFF, TW], BF16)
    qk_tab = qk_tab_d
    d_tab = d_tab_d

    with contextlib.ExitStack() as st0:
        P = Prog(nc, st0)
        V = P.sb("V", [128, NV], F32)
        P.dma("sp", V[:], vecs, writes=["V"], key="V")
        ident = P.sb("ident", [128, 128], F32)
        ident_bf = P.sb("ident_bf", [128, 128], BF16)
        ones_bf = P.sb("ones_bf", [128, 128], BF16)
        bones_bf = P.sb("bones_bf", [128, 128], BF16)
        maskP = P.sb("maskP", [128, 2, 128], F32)
        maskL = P.sb("maskL", [128, 64], F32)
        eye = P.sb("eye", [128, 64], F32)
        rmask = P.sb("rmask", [128, TB], F32)
        omka = P.sb("omka", [128, 8], F32)
        su = P.sb("su", [128, 64], F32)
        ui = P.sb("ui", [128, 64], F32)

        def pool(fn, reads=(), writes=()):
            P.op("pool", fn, reads, writes)

        pool(lambda e: e.memset(ident[:], 1.0), writes=["ident"])
        pool(lambda e: e.affine_select(out=ident[:], in_=ident[:], pattern=[[-1, 128]], compare_op=ALU.is_equal,
                                       fill=0.0, base=0, channel_multiplier=1), reads=["ident"], writes=["ident"])
        pool(lambda e: e.tensor_copy(out=ident_bf[:], in_=ident[:]), reads=["ident"], writes=["ident_bf"])
        pool(lambda e: e.memset(ones_bf[:], 1.0), writes=["ones_bf"])
        pool(lambda e: e.memset(bones_bf[:], 0.0), writes=["bones_bf"])
        pool(lambda e: e.memset(bones_bf[0:64, 0:64], 1.0), reads=["bones_bf"], writes=["bones_bf"])
        pool(lambda e: e.memset(bones_bf[64:128, 64:128], 1.0), reads=["bones_bf"], writes=["bones_bf"])
        for (tl, pat, cm, b0, b1, op_) in ((su, 1, -1, -1, -1, ALU.is_ge), (ui, 1, -1, 0, 0, ALU.is_ge),
                                           (maskL, -1, 1, -1, -1, ALU.is_ge), (eye, -1, 1, 0, 0, ALU.is_equal)):
            pool(lambda e, tl=tl: e.memset(tl[:], 1.0), writes=["msk"])
            for h, bb in ((0, b0), (1, b1)):
                sl = slice(64 * h, 64 * h + 64)
                pool(lambda e, tl=tl, sl=sl, pat=pat, cm=cm, bb=bb, op_=op_: e.affine_select(
                    out=tl[sl, :], in_=tl[sl, :], pattern=[[pat, 64]], compare_op=op_, fill=0.0, base=bb,
                    channel_multiplier=cm), reads=["msk"], writes=["msk"])
        for xx in range(2):
            pool(lambda e, xx=xx: e.tensor_copy(out=maskP[:, xx, 0:64], in_=su[:]), reads=["msk"], writes=["msk"])
            pool(lambda e, xx=xx: e.tensor_copy(out=maskP[:, xx, 64:128], in_=ui[:]), reads=["msk"], writes=["msk"])
        pool(lambda e: e.memset(rmask[:], 1.0), reads=["msk"], writes=["msk"])
        for c in range(8):
            pool(lambda e, c=c: e.memset(rmask[:, c * 64:c * 64 + 1], 0.0), reads=["msk"], writes=["msk"])
        P.op("dve", lambda e: e.tensor_scalar(out=omka[:], in0=V[:, 128:136], scalar1=-1.0, scalar2=1.0, op0=ALU.mult,
                                              op1=ALU.add), reads=["V", "msk"], writes=["omka"])
        CONST = ["V", "msk", "ident", "ident_bf", "ones_bf", "bones_bf", "omka"]

        rr = {"evac": 0}

        class _Stop(Exception):
            pass

        def stop_if(name):
            if name in debug:
                raise _Stop()

        def evac_copy(out_ap, in_ap, reads, writes, scale=None):
            rr["evac"] += 1
            if scale is not None or rr["evac"] % 2 == 0:
                P.op("act", lambda e: e.activation(out=out_ap, in_=in_ap, func=AF.Copy,
                                                   scale=(1.0 if scale is None else scale)), reads, writes)
            else:
                P.op("dve", lambda e: e.tensor_copy(out=out_ap, in_=in_ap), reads, writes)

        def transpose_stage(tag, src, dst, R, C, src_res, dst_res):
            with contextlib.ExitStack() as st:
                prev_stack = P.stack
                P.stack = st
                nb = 2
                tin = [P.sb(f"{tag}_in{i}", [128, C], F32) for i in range(nb)]
                tout = [P.sb(f"{tag}_out{i}", [128, C // 128, 128], F32) for i in range(nb)]
                pst = [P.ps(f"{tag}_ps{i}", [128, 4, 128], F32) for i in range(2)]
                k = 0
                for r in range(R // 128):
                    b = r % nb
                    P.dma("sp", tin[b][:], src[r * 128:(r + 1) * 128, :], reads=[src_res], writes=[f"TR_in{b}"],
                          key=f"TR_in{b}")
                    for c4 in range(0, C // 128, 4):
                        pb = k % 2
                        k += 1
                        n4 = min(4, C // 128 - c4)
                        for i in range(n4):
                            c = c4 + i
                            P.op("pe", lambda e, pb=pb, i=i, c=c, b=b: e.transpose(
                                out=pst[pb][:, i, :], in_=tin[b][:, c * 128:(c + 1) * 128], identity=ident[:]),
                                reads=[f"TR_in{b}", "ident"], writes=[f"TR_ps{pb}"])
                        evac_copy(tout[b][:, c4:c4 + n4, :], pst[pb][:, 0:n4, :], [f"TR_ps{pb}"], [f"TR_out{b}"])
                    P.dma("sp", dst[:, r * 128:(r + 1) * 128].rearrange("(c p) t -> p c t", p=128), tout[b][:],
                          reads=[f"TR_out{b}"], writes=[dst_res], key=f"TR_st{b}")
                P.barrier()
                P.stack = prev_stack

        def rmsnorm_stage(tag, src, F, gcol, src_res, dst=None, dst_res=None, resid=None, resid_res=None, blocks=FB):
            FC = F // 128
            with contextlib.ExitStack() as st:
                prev_stack = P.stack
                P.stack = st
                s = P.sb(f"{tag}_s", [128, FC, TB], F32)
                sq = P.sb(f"{tag}_sq", [128, FC, TB], BF16)
                rstd = P.sb(f"{tag}_rstd", [128, TB], F32)
                ps = P.ps(f"{tag}_ps", [128, TB], F32)
                if resid is None:
                    o = P.sb(f"{tag}_o", [128, FC, TB], BF16)
                else:
                    o = P.sb(f"{tag}_o", [128, FC, TB], F32)
                    hh = P.sb(f"{tag}_h", [128, FC, TB], F32)
                for j, (b0, bs) in enumerate(blocks):
                    cs = slice(b0, b0 + bs)
                    P.dma("sp", s[:, :, 0:bs], src[:, cs].rearrange("(c p) t -> p c t", p=128), reads=[src_res],
                          writes=[f"RN_s"], key=f"RN_s")
                    if resid is not None:
                        P.dma("sp", hh[:, :, 0:bs], resid[:, cs].rearrange("(c p) t -> p c t", p=128), reads=[resid_res],
                              writes=[f"RN_h"], key=f"RN_h")
                    P.op("act", lambda e: e.activation(out=sq[:, :, 0:bs], in_=s[:, :, 0:bs], func=AF.Square), reads=[f"RN_s"],
                         writes=[f"RN_sq"])
                    for c in range(FC):
                        P.op("pe", lambda e, c=c: e.matmul(ps[:, 0:bs], lhsT=ones_bf[:], rhs=sq[:, c, 0:bs], start=(c == 0),
                                                           stop=(c == FC - 1)),
                             reads=[f"RN_sq", "ones_bf"], writes=[f"RN_ps"])
                    P.op("act", lambda e: e.activation(out=rstd[:, 0:bs], in_=ps[:, 0:bs], func=AF.Sqrt, bias=EPS, scale=1.0 / F),
                         reads=[f"RN_ps"], writes=[f"RN_rstd"])
                    P.op("dve", lambda e: e.reciprocal(out=rstd[:, 0:bs], in_=rstd[:, 0:bs]), reads=[f"RN_rstd"],
                         writes=[f"RN_rstd"])
                    for c in range(FC):
                        eng = "pool"
                        P.op("dve", lambda e, c=c: e.scalar_tensor_tensor(
                            out=o[:, c, 0:bs], in0=s[:, c, 0:bs], scalar=V[:, gcol + c:gcol + c + 1], in1=rstd[:, 0:bs],
                            op0=ALU.mult, op1=ALU.mult), reads=[f"RN_s", f"RN_rstd", "V"], writes=[f"RN_o{c}"])
                        if resid is not None:
                            P.op(eng, lambda e, c=c: e.tensor_tensor(out=o[:, c, 0:bs], in0=o[:, c, 0:bs], in1=hh[:, c, 0:bs],
                                                                     op=ALU.add),
                                 reads=[f"RN_o{c}", f"RN_h"], writes=[f"RN_o{c}"])
                    allo = [f"RN_o{c}" for c in range(FC)]
                    if resid is None:
                        P.dma("sp", dst[:, cs].rearrange("(c p) t -> p c t", p=128), o[:, :, 0:bs], reads=allo,
                              writes=[dst_res], key=f"RN_st")
                    else:
                        P.dma("sp", resid[:, cs].rearrange("(c p) t -> p c t", p=128), o[:, :, 0:bs], reads=allo,
                              writes=[resid_res], key=f"RN_st")
                P.barrier()
                P.stack = prev_stack

        def linear_stage(tag, srcs, groups, epi, wbuf_cols, nps=2, blocks=FB):
            with contextlib.ExitStack() as st:
                prev_stack = P.stack
                P.stack = st
                wsb = []
                sbs = []
                for si, (src, K, sdt, sres, W) in enumerate(srcs):
                    KC = (K + 127) // 128
                    wsb.append([P.sb(f"{tag}_w{si}_{i}", [128, KC, wbuf_cols], BF16) for i in range(2)])
                    sbs.append([P.sb(f"{tag}_x{si}_{i}", [128, KC, TB], BF16) for i in range(2)])
                pss = [P.ps(f"{tag}_ps{i}", [128, TB], F32) for i in range(nps)]
                state = {"ps": 0, "xb": 0}
                def grp_layout(grp):
                    offs = []
                    off = 0
                    for (c0, w) in grp:
                        offs.append(off)
                        off += w
                    assert off <= wbuf_cols
                    ranges = []
                    for (c0, w), o_ in zip(grp, offs):
                        if ranges and ranges[-1][0] + ranges[-1][1] == c0 and ranges[-1][2] + ranges[-1][1] == o_:
                            ranges[-1][1] += w
                        else:
                            ranges.append([c0, w, o_])
                    return offs, ranges

                def load_w(gi):
                    wb = gi % 2
                    offs, ranges = grp_layout(groups[gi])
                    for si, (src, K, sdt, sres, W) in enumerate(srcs):
                        KC = (K + 127) // 128
                        for (c0, w, o_) in ranges:
                            for kc in range(KC):
                                kr = min(128, K - kc * 128)
                                P.dma("pool", wsb[si][wb][0:kr, kc, o_:o_ + w], W[kc * 128:kc * 128 + kr, c0:c0 + w],
                                      writes=[f"LN_w{si}_{wb}"], key=f"LN_w{si}_{wb}")

                load_w(0)
                for gi, grp in enumerate(groups):
                    wb = gi % 2
                    offs, ranges = grp_layout(grp)
                    if gi + 1 < len(groups):
                        load_w(gi + 1)
                    for j, (b0, bs) in enumerate(blocks):
                        cs = slice(b0, b0 + bs)
                        xb = state["xb"] % 2
                        state["xb"] += 1
                        for si, (src, K, sdt, sres, W) in enumerate(srcs):
                            KC = (K + 127) // 128
                            q = "sp" if sdt == BF16 else "pool"
                            if K % 128 == 0 and q == "sp":
                                P.dma(q, sbs[si][xb][:, :, 0:bs], src[:, cs].rearrange("(k p) t -> p k t", p=128), reads=[sres],
                                      writes=[f"LN_x{si}_{xb}"], key=f"LN_x{si}_{xb}")
                            else:
                                for kc in range(KC):
                                    kr = min(128, K - kc * 128)
                                    P.dma(q, sbs[si][xb][0:kr, kc, 0:bs], src[kc * 128:kc * 128 + kr, cs], reads=[sres],
                                          writes=[f"LN_x{si}_{xb}"], key=f"LN_x{si}_{xb}")
                        for ci, ((c0, w), o_) in enumerate(zip(grp, offs)):
                            pi = state["ps"] % nps
                            state["ps"] += 1
                            nmm = sum((K + 127) // 128 for (_, K, _, _, _) in srcs)
                            i = 0
                            for si, (src, K, sdt, sres, W) in enumerate(srcs):
                                KC = (K + 127) // 128
                                for kc in range(KC):
                                    kr = min(128, K - kc * 128)
                                    P.op("pe", lambda e, pi=pi, si=si, wb=wb, kc=kc, kr=kr, o_=o_, w=w, xb=xb, i=i, nmm=nmm:
                                         e.matmul(pss[pi][0:w, 0:bs], lhsT=wsb[si][wb][0:kr, kc, o_:o_ + w],
                                                  rhs=sbs[si][xb][0:kr, kc, 0:bs], start=(i == 0), stop=(i == nmm - 1)),
                                         reads=[f"LN_w{si}_{wb}", f"LN_x{si}_{xb}"], writes=[f"LN_ps{pi}"])
                                    i += 1
                            epi(gi, ci, (c0, w), j, pss[pi], f"LN_ps{pi}", (b0, bs))
                P.barrier()
                P.stack = prev_stack

        def store_epi(tag, dst, dst_res, odt, func=AF.Copy, row_of=None, nbuf=3):
            bufs = [P.sb(f"{tag}_eo{i}", [128, TB], odt) for i in range(nbuf)]
            stt = {"i": 0}

            def epi(gi, ci, cw, j, ps, ps_res, blk):
                c0, w = cw
                b0_, bs = blk
                b = stt["i"] % nbuf
                stt["i"] += 1
                r0 = c0 if row_of is None else row_of(c0)
                if func == AF.Copy:
                    evac_copy(bufs[b][0:w, 0:bs], ps[0:w, 0:bs], [ps_res], [f"EO_eo{b}"])
                else:
                    P.op("act", lambda e: e.activation(out=bufs[b][0:w, 0:bs], in_=ps[0:w, 0:bs], func=func), [ps_res],
                         [f"EO_eo{b}"])
                P.dma("act", dst[r0:r0 + w, b0_:b0_ + bs], bufs[b][0:w, 0:bs], reads=[f"EO_eo{b}"],
                      writes=[dst_res], key=f"EO_eo{b}")

            return epi

        def chunks(c0, n, w=128):
            res = []
            c = c0
            while c < c0 + n:
                ww = min(w, c0 + n - c)
                res.append((c, ww))
                c += ww
            return res

        def grouped(ch, n):
            return [ch[i:i + n] for i in range(0, len(ch), n)]

        def norm_transpose_stage():
            with contextlib.ExitStack() as st:
                prev_stack = P.stack
                P.stack = st
                gbc = P.sb("nt_gbc", [128, D], F32)
                P.dma("sp", gbc[:], gain_bc, writes=["nt_gbc"], key="nt_gbc")
                xt = [P.sb(f"nt_x{i}", [128, D], F32) for i in range(3)]
                junk = P.sb("nt_junk", [128, D], BF16)
                ub = [P.sb(f"nt_ub{i}", [128, D], BF16) for i in range(2)]
                ss = [P.sb(f"nt_ss{i}", [128, 2], F32) for i in range(2)]
                uo = [P.sb(f"nt_uo{i}", [128, 16, 512], BF16) for i in range(2)]
                pst = [P.ps(f"nt_ps{i}", [128, 8, 128], BF16) for i in range(2)]
                k = 0
                for g in range(T // 512):
                    ob = g % 2
                    for t4 in range(4):
                        r = g * 4 + t4
                        b = r % 3
                        b2 = r % 2
                        P.dma("sp", xt[b][:], x[r * 128:(r + 1) * 128, :], reads=["dram:x"], writes=[f"nt_x{b}"], key=f"nt_x{b}")
                        P.op("dve", lambda e: e.memset(ss[b2][:, 0:1], 0.0), [f"nt_ss{b2}"], [f"nt_ss{b2}"])
                        P.op("act", lambda e: e.activation(out=junk[:], in_=xt[b][:], func=AF.Square, accum_out=ss[b2][:, 0:1]),
                             [f"nt_x{b}", f"nt_ss{b2}"], ["nt_junk", f"nt_ss{b2}"])
                        P.op("act", lambda e: e.activation(out=ss[b2][:, 1:2], in_=ss[b2][:, 0:1], func=AF.Sqrt, bias=EPS,
                                                           scale=1.0 / D), [f"nt_ss{b2}"], [f"nt_ss{b2}"])
                        P.op("dve", lambda e: e.reciprocal(out=ss[b2][:, 1:2], in_=ss[b2][:, 1:2]), [f"nt_ss{b2}"], [f"nt_ss{b2}"])
                        P.op("dve", lambda e: e.scalar_tensor_tensor(out=ub[b2][:], in0=xt[b][:], scalar=ss[b2][:, 1:2], in1=gbc[:],
                                                                     op0=ALU.mult, op1=ALU.mult),
                             [f"nt_x{b}", f"nt_ss{b2}", "nt_gbc"], [f"nt_ub{b2}"])
                        for half in range(2):
                            pb = k % 2
                            k += 1
                            for i in range(8):
                                c = half * 8 + i
                                P.op("pe", lambda e, pb=pb, i=i, c=c: e.transpose(out=pst[pb][:, i, :],
                                                                               in_=ub[b2][:, c * 128:(c + 1) * 128],
                                                                               identity=ident_bf[:]),
                                     [f"nt_ub{b2}", "ident_bf"], [f"nt_ps{pb}"])
                            evac_copy(uo[ob][:, half * 8:(half + 1) * 8, t4 * 128:(t4 + 1) * 128], pst[pb][:],
                                      [f"nt_ps{pb}"], [f"nt_uo{ob}"])
                    P.dma("act", uT[:, g * 512:(g + 1) * 512].rearrange("(c p) t -> p c t", p=128), uo[ob][:],
                          reads=[f"nt_uo{ob}"], writes=["dram:uT"], key=f"nt_uo{ob}")
                P.barrier()
                P.stack = prev_stack

        def main_seq():
            if "skip_s01" in debug:
                rwkv_stage(P, nc, st0, V, zrwT, yrwT, rw_w2, rw_a2, rw_g2, ident_bf, bones_bf, maskP, maskL, eye, rmask, omka,
                           evac_copy)
                stop_if("skip_s01")
            if "skip_s01m" in debug:
                rmsnorm_stage("nq", zmlaT[0:512, :], 512, 160, "dram:zmlaT", dst=cqnT, dst_res="dram:cqnT")
                rmsnorm_stage("nk", zmlaT[512:1024, :], 512, 164, "dram:zmlaT", dst=ckvnT, dst_res="dram:ckvnT")
                mla_stage(P, nc, st0, V, pos, pos_w, qk_tab, d_tab, zmlaT, cqnT_full, ckvnT, w_q_up, w_kv_up, ymlaT, ones_bf,
                          evac_copy)
                stop_if("skip_s01m")
            def win_copy(dst, src_full, rows, res_src, res_dst, key):
                step = 512
                for r0 in range(0, rows, step):
                    def fn(e, r0=r0):
                        off = get_rank(e)
                        return e.dma_start(out=dst[r0:r0 + step, :], in_=src_full[r0:r0 + step, bass.ds(off, TW)])
                    P.custom("pool", fn, reads=[res_src], writes=[res_dst], key=key, raw=True)

            zt_f = P.sb("zpad_f", [128, 16, 2], F32)
            zt_b = P.sb("zpad_b", [128, 16, 2], BF16)
            P.op("pool", lambda e: e.memset(zt_f[:], 0.0), [], ["zpad"])
            P.op("pool", lambda e: e.memset(zt_b[:], 0.0), [], ["zpad"])
            P.dma("sp", uT_full[:, 0:2].rearrange("(c p) t -> p c t", p=128), zt_b[:], reads=["zpad"], writes=["dram:uT"], key="zp")
            P.dma("sp", yrwT_full[:, 0:2].rearrange("(c p) t -> p c t", p=128), zt_b[:, 0:8, :], reads=["zpad"],
                  writes=["dram:yrwT"], key="zp")
            P.dma("sp", cqnT_full[:, 0:2].rearrange("(c p) t -> p c t", p=128), zt_b[:, 0:4, :], reads=["zpad"],
                  writes=["dram:cqnT"], key="zp")
            norm_transpose_stage()
            stop_if("stop0b")
            with contextlib.ExitStack() as stx:
                P.stack = stx
                e1 = store_epi("l1a", zrwT, "dram:zrwT", F32)
                linear_stage("l1a", [(uT, D, BF16, "dram:uT", w_in)], grouped(chunks(0, 3360), 8), e1, 1024)
                P.barrier()
                P.stack = st0
            with contextlib.ExitStack() as stx:
                P.stack = stx
                e2 = store_epi("l1b", zmlaT, "dram:zmlaT", F32, row_of=lambda c0: c0 - 3360)
                linear_stage("l1b", [(uT, D, BF16, "dram:uT", w_in)], grouped(chunks(3360, 1088), 9), e2, 1152)
                P.barrier()
                P.stack = st0
            stop_if("stop1")
            rwkv_stage(P, nc, st0, V, zrwT, yrwT, rw_w2, rw_a2, rw_g2, ident_bf, bones_bf, maskP, maskL, eye, rmask, omka,
                       evac_copy)
            stop_if("stop2")
            rmsnorm_stage("nq", zmlaT[0:512, :], 512, 160, "dram:zmlaT", dst=cqnT, dst_res="dram:cqnT")
            rmsnorm_stage("nk", zmlaT[512:1024, :], 512, 164, "dram:zmlaT", dst=ckvnT, dst_res="dram:ckvnT")
            win_copy(cqwT, cqnT_full, 512, "dram:cqnT", "dram:cqwT", "wc3")
            P.barrier()
            mla_stage(P, nc, st0, V, pos, pos_w, qk_tab, d_tab, zmlaT, cqwT, ckvnT, w_q_up, w_kv_up, ymlaT, ones_bf,
                      evac_copy)
            stop_if("stop3")
            transpose_stage("txw", xw, hwT, 1152, D, "dram:xw", "dram:hwT")
            win_copy(uwT, uT_full, D, "dram:uT", "dram:uwT", "wc1")
            win_copy(ywT, yrwT_full, 1024, "dram:yrwT", "dram:ywT", "wc2")
            P.barrier()
            transpose_stage("tp", p_in, pT, 1024, 256, "dram:p", "dram:pT")
            with contextlib.ExitStack() as stx:
                P.stack = stx
                e3 = store_epi("l1c", gT, "dram:gT", BF16, func=AF.Sigmoid, row_of=lambda c0: c0 - 4448)
                linear_stage("l1c", [(uwT, D, BF16, "dram:uwT", w_in)], grouped(chunks(4448, 4096), 8), e3, 1024, blocks=WB)
                P.barrier()
                P.stack = st0
            with contextlib.ExitStack() as stx:
                P.stack = stx
                gA = [P.sb(f"s4_gA{i}", [128, TB], BF16) for i in range(2)]
                gB = [P.sb(f"s4_gB{i}", [128, TB], BF16) for i in range(2)]
                t1 = [P.sb(f"s4_t1{i}", [128, TB], F32) for i in range(2)]
                mo = [P.sb(f"s4_mo{i}", [128, TB], BF16) for i in range(2)]
                t2 = [P.sb(f"s4_t2{i}", [128, TB], F32) for i in range(2)]
                stt = {"i": 0}

                def epiA(gi, ci, cw, j, ps, ps_res, blk):
                    c0, w = cw
                    b0_, bs = blk
                    b = stt["i"] % 2
                    stt["i"] += 1
                    P.dma("sp", gA[b][:, 0:bs], gT[c0:c0 + 128, b0_:b0_ + bs], reads=["dram:gT"], writes=[f"s4_gA{b}"],
                          key=f"s4_gA{b}")
                    P.op("dve", lambda e: e.tensor_tensor(out=t1[b][:, 0:bs], in0=ps[:, 0:bs], in1=gA[b][:, 0:bs], op=ALU.mult),
                         [ps_res, f"s4_gA{b}"], [f"s4_t1{b}"])
                    P.dma("act", fT[c0:c0 + 128, b0_:b0_ + bs], t1[b][:, 0:bs], reads=[f"s4_t1{b}"], writes=["dram:fT"],
                          key=f"s4_t1{b}")

                linear_stage("l4a", [(ywT, 1024, BF16, "dram:ywT", w_brw)], grouped(chunks(0, D), 8), epiA, 1024, blocks=WB)

                def epiB(gi, ci, cw, j, ps, ps_res, blk):
                    c0, w = cw
                    b0_, bs = blk
                    b = stt["i"] % 2
                    stt["i"] += 1
                    P.dma("sp", gB[b][:, 0:bs], gT[2048 + c0:2048 + c0 + 128, b0_:b0_ + bs], reads=["dram:gT"],
                          writes=[f"s4_gB{b}"], key=f"s4_gB{b}")
                    P.dma("sp", t1[b][:, 0:bs], fT[c0:c0 + 128, b0_:b0_ + bs], reads=["dram:fT"], writes=[f"s4_t1{b}"],
                          key=f"s4_t1l{b}")
                    P.op("dve", lambda e: e.tensor_tensor(out=t2[b][:, 0:bs], in0=ps[:, 0:bs], in1=gB[b][:, 0:bs], op=ALU.mult),
                         [ps_res, f"s4_gB{b}"], [f"s4_t2{b}"])
                    P.op("dve", lambda e: e.tensor_tensor(out=mo[b][:, 0:bs], in0=t2[b][:, 0:bs], in1=t1[b][:, 0:bs], op=ALU.add),
                         [f"s4_t2{b}", f"s4_t1{b}"], [f"s4_mo{b}"])
                    P.dma("act", mT[c0:c0 + 128, b0_:b0_ + bs], mo[b][:, 0:bs], reads=[f"s4_mo{b}"], writes=["dram:mT"],
                          key=f"s4_mo{b}")

                linear_stage("l4b", [(ymlaT, 1024, BF16, "dram:ymlaT", w_bmla)], grouped(chunks(0, D), 8), epiB, 1024, blocks=WB)
                P.barrier()
                P.stack = st0
            stop_if("stop4")
            with contextlib.ExitStack() as stx:
                P.stack = stx
                e5 = store_epi("l5", fT, "dram:fT", F32)
                linear_stage("l5", [(mT, D, BF16, "dram:mT", w_out)], grouped(chunks(0, D), 8), e5, 1024, blocks=WB)
                P.barrier()
                P.stack = st0
            rmsnorm_stage("n5", fT, D, 16, "dram:fT", resid=hwT, resid_res="dram:hwT", blocks=WB)
            stop_if("stop5")
            rmsnorm_stage("n6", hwT, D, 32, "dram:hwT", dst=uwT, dst_res="dram:uwT", blocks=WB)
            with contextlib.ExitStack() as stx:
                P.stack = stx
                NCH = 4
                ub = [[P.sb(f"s6_u{h}_{i}", [128, TB + 2], F32) for i in range(NCH)] for h in range(2)]
                cv = [P.sb(f"s6_cv{h}", [128, TB], F32) for h in range(2)]
                tq = P.sb("s6_tq", [128, TB], F32)
                fo = [P.sb(f"s6_fo{i}", [128, TB], BF16) for i in range(2)]
                stt = {"i": 0}

                def epi6(gi, ci, cw, j, ps, ps_res, blk):
                    c0, w = cw
                    b0_, bs = blk
                    half = 0 if ci < NCH else 1
                    cc = ci % NCH
                    u = ub[half][cc]
                    ur = f"s6_u{half}_{cc}"
                    if j == 0:
                        P.op("dve", lambda e: e.memset(u[:, 0:2], 0.0), [ur], [ur])
                    else:
                        P.op("dve", lambda e: e.tensor_copy(out=u[:, 0:2], in_=u[:, bs:bs + 2]), [ur], [ur])
                    evac_copy(u[:, 2:bs + 2], ps[:, 0:bs], [ps_res, ur], [ur])
                    if half == 1:
                        chn = c0 // 128
                        chg = chn - 44
                        for hh, ch in ((0, chg), (1, chn)):
                            uu = ub[hh][cc]
                            uur = f"s6_u{hh}_{cc}"
                            P.op("dve", lambda e, uu=uu, ch=ch, hh=hh: e.tensor_scalar(
                                out=cv[hh][:, 0:bs], in0=uu[:, 0:bs], scalar1=V[:, 256 + ch:257 + ch], scalar2=V[:, 168 + ch:169 + ch],
                                op0=ALU.mult, op1=ALU.add), [uur, "V"], [f"s6_cv{hh}"])
                            P.op("dve", lambda e, uu=uu, ch=ch, hh=hh: e.scalar_tensor_tensor(
                                out=cv[hh][:, 0:bs], in0=uu[:, 1:bs + 1], scalar=V[:, 344 + ch:345 + ch], in1=cv[hh][:, 0:bs],
                                op0=ALU.mult, op1=ALU.add), [uur, "V", f"s6_cv{hh}"], [f"s6_cv{hh}"])
                            P.op("dve", lambda e, uu=uu, ch=ch, hh=hh: e.scalar_tensor_tensor(
                                out=cv[hh][:, 0:bs], in0=uu[:, 2:bs + 2], scalar=V[:, 432 + ch:433 + ch], in1=cv[hh][:, 0:bs],
                                op0=ALU.mult, op1=ALU.add), [uur, "V", f"s6_cv{hh}"], [f"s6_cv{hh}"])
                        P.op("dve", lambda e: e.tensor_tensor(out=tq[:, 0:bs], in0=cv[0][:, 0:bs], in1=cv[0][:, 0:bs], op=ALU.mult),
                             ["s6_cv0"], ["s6_tq"])
                        P.op("dve", lambda e: e.tensor_scalar(out=tq[:, 0:bs], in0=tq[:, 0:bs], scalar1=0.044715, scalar2=1.0,
                                                               op0=ALU.mult, op1=ALU.add), ["s6_tq"], ["s6_tq"])
                        P.op("dve", lambda e: e.tensor_tensor(out=tq[:, 0:bs], in0=tq[:, 0:bs], in1=cv[0][:, 0:bs], op=ALU.mult),
                             ["s6_tq", "s6_cv0"], ["s6_tq"])
                        P.op("act", lambda e: e.activation(out=tq[:, 0:bs], in_=tq[:, 0:bs], func=AF.Sigmoid, scale=1.5957691216),
                             ["s6_tq"], ["s6_tq"])
                        P.op("dve", lambda e: e.tensor_tensor(out=tq[:, 0:bs], in0=tq[:, 0:bs], in1=cv[0][:, 0:bs], op=ALU.mult),
                             ["s6_tq", "s6_cv0"], ["s6_tq"])
                        b = stt["i"] % 2
                        stt["i"] += 1
                        P.op("dve", lambda e: e.tensor_tensor(out=fo[b][:, 0:bs], in0=tq[:, 0:bs], in1=cv[1][:, 0:bs], op=ALU.mult),
                             ["s6_tq", "s6_cv1"], [f"s6_fo{b}"])
                        P.dma("act", ffT[chg * 128:(chg + 1) * 128, b0_:b0_ + bs], fo[b][:, 0:bs], reads=[f"s6_fo{b}"],
                              writes=["dram:ffT"], key=f"s6_fo{b}")

                grps = []
                for g0 in range(0, 44, NCH):
                    grps.append(chunks(g0 * 128, NCH * 128) + chunks(DFF + g0 * 128, NCH * 128))
                linear_stage("l6", [(uwT, D, BF16, "dram:uwT", w_up)], grps, epi6, 2 * NCH * 128, blocks=WB)
                P.barrier()
                P.stack = st0
            with contextlib.ExitStack() as stx:
                P.stack = stx
                e7 = store_epi("l7", fT, "dram:fT", F32)
                linear_stage("l7", [(ffT, DFF, BF16, "dram:ffT", w_down)], grouped(chunks(0, D), 4), e7, 512, blocks=WB)
                P.barrier()
                P.stack = st0
            rmsnorm_stage("n7", fT, D, 48, "dram:fT", resid=hwT, resid_res="dram:hwT", blocks=WB)
            stop_if("stop7")
            PB = [(0, 512), (512, 512)]
            with contextlib.ExitStack() as stx:
                P.stack = stx
                e8 = store_epi("l8a", mT, "dram:mT", BF16, func=AF.Sigmoid)
                linear_stage("l8a", [(hwT[:, 2:TW], D, F32, "dram:hwT", w_pg)], grouped(chunks(0, D), 8), e8, 1024, blocks=PB)
                gl = [P.sb(f"s8_g{i}", [128, TB], BF16) for i in range(2)]
                eo = [P.sb(f"s8_eo{i}", [128, TB], F32) for i in range(2)]
                stt = {"i": 0}

                def epi8(gi, ci, cw, j, ps, ps_res, blk):
                    c0, w = cw
                    b0_, bs = blk
                    b = stt["i"] % 2
                    stt["i"] += 1
                    P.dma("sp", gl[b][:, 0:bs], mT[c0:c0 + 128, b0_:b0_ + bs], reads=["dram:mT"], writes=[f"s8_g{b}"],
                          key=f"s8_g{b}")
                    P.op("dve", lambda e: e.tensor_tensor(out=eo[b][:, 0:bs], in0=ps[:, 0:bs], in1=gl[b][:, 0:bs], op=ALU.mult),
                         [ps_res, f"s8_g{b}"], [f"s8_eo{b}"])
                    P.dma("act", fT[c0:c0 + 128, b0_:b0_ + bs], eo[b][:, 0:bs], reads=[f"s8_eo{b}"], writes=["dram:fT"],
                          key=f"s8_eo{b}")

                linear_stage("l8b", [(pT, 256, F32, "dram:pT", w_ple)], grouped(chunks(0, D), 16), epi8, 2048, blocks=PB)
                P.barrier()
                P.stack = st0
            rmsnorm_stage("n8", fT[:, 0:1024], D, 64, "dram:fT", resid=hwT[:, 2:TW], resid_res="dram:hwT", blocks=PB)
            transpose_stage("to", hwT[:, 2:TW], out, D, 1024, "dram:hwT", "dram:out")

        try:
            main_seq()
        except _Stop:
            P.stack = st0
        P.wait_all("sp", [f"dram:{n}" for n in (["out"] + list(debug)) if not n.startswith("skip") and not n.startswith("stop")])
        P.emit()
    return nc


def rwkv_stage(P, nc, st0, V, zrwT, yrwT, rw_w2, rw_a2, rw_g2, ident_bf, bones_bf, maskP, maskL, eye, rmask, omka,
               evac_copy):
    NEG_E = -math.exp(-0.5)
    with contextlib.ExitStack() as st:
        prev_stack = P.stack
        P.stack = st
        sb = P.sb
        w2b = sb("rk_w2b", [64, 1024], BF16)
        a2b = sb("rk_a2b", [64, 1024], BF16)
        g2b = sb("rk_g2b", [128, 2, 1024], BF16)
        P.dma("pool", w2b[:], rw_w2, writes=["rk_w"], key="rk_w")
        P.dma("pool", a2b[:], rw_a2, writes=["rk_w"], key="rk_w")
        P.dma("pool", g2b[:, 0, :], rw_g2[0:128, :], writes=["rk_w"], key="rk_w")
        P.dma("pool", g2b[0:32, 1, :], rw_g2[128:160, :], writes=["rk_w"], key="rk_w")
        twd = sb("rk_twd", [64, T], BF16)
        ads = sb("rk_ads", [64, T], BF16)
        sg1 = sb("rk_sg1", [128, T], BF16)
        sg2 = sb("rk_sg2", [32, T], BF16)
        zin = sb("rk_zin", [128, TB + 1], F32)
        dd = sb("rk_dd", [128, TB], F32)
        for (row0, nr, mcol, dst, func) in ((3072, 64, 520, twd, AF.Tanh), (3136, 64, 521, ads, AF.Copy),
                                            (3200, 128, 522, sg1, AF.Sigmoid), (3328, 32, 523, sg2, AF.Sigmoid)):
            for j in range(NB):
                if j == 0:
                    P.op("dve", lambda e: e.memset(zin[:, 0:1], 0.0), [], ["rk_zin"])
                    P.dma("sp", zin[0:nr, 1:TB + 1], zrwT[row0:row0 + nr, 0:TB], reads=["dram:zrwT"], writes=["rk_zin"],
                          key="rk_zin")
                else:
                    P.dma("sp", zin[0:nr, :], zrwT[row0:row0 + nr, j * TB - 1:(j + 1) * TB], reads=["dram:zrwT"],
                          writes=["rk_zin"], key="rk_zin")
                P.op("dve", lambda e, nr=nr: e.tensor_tensor(out=dd[0:nr, :], in0=zin[0:nr, 0:TB], in1=zin[0:nr, 1:TB + 1],
                                                             op=ALU.subtract), ["rk_zin"], ["rk_dd"])
                P.op("dve", lambda e, nr=nr, mcol=mcol: e.scalar_tensor_tensor(
                    out=dd[0:nr, :], in0=dd[0:nr, :], scalar=V[0:nr, mcol:mcol + 1], in1=zin[0:nr, 1:TB + 1],
                    op0=ALU.mult, op1=ALU.add), ["rk_dd", "rk_zin", "V"], ["rk_dd"])
                P.op("act", lambda e, nr=nr, dst=dst, func=func, j=j: e.activation(
                    out=dst[0:nr, j * TB:(j + 1) * TB], in_=dd[0:nr, :], func=func), ["rk_dd"], ["rk_lin"])
        zt = {n: sb(f"rk_z{n}", [128, TB + 1], F32) for n in "rkv"}
        xs = {n: sb(f"rk_s{n}", [128, TB], F32) for n in "rkv"}
        tA = sb("rk_tA", [128, TB], F32)
        tB = sb("rk_tB", [128, TB], F32)
        tC = sb("rk_tC", [128, TB], F32)
        av = sb("rk_a", [128, TB], F32)
        kkn = sb("rk_kkn", [128, TB], F32)
        k2 = sb("rk_k2", [128, TB], F32)
        logw = sb("rk_logw", [128, TB], F32)
        cum = sb("rk_cum", [128, TB], F32)
        ginv = sb("rk_ginv", [128, TB], F32)
        gprev = sb("rk_gprev", [128, TB], F32)
        tbf = sb("rk_tbf", [128, TB], BF16)
        vbf2 = [sb(f"rk_vbf{i}", [128, TB], BF16) for i in range(2)]
        BK2 = [sb(f"rk_BK{i}", [128, 8, 2, 64], BF16) for i in range(2)]
        gam2 = [sb(f"rk_gam{i}", [128, TB], F32) for i in range(3)]
        gv2 = [sb(f"rk_g{i}", [128, TB], F32) for i in range(3)]
        bonus2 = [sb(f"rk_bonus{i}", [128, TB], F32) for i in range(3)]
        AR2 = [sb(f"rk_AR{i}", [128, 8, 2, 64], BF16) for i in range(3)]
        PM2 = [sb(f"rk_PM{i}", [128, 8, 2, 128], BF16) for i in range(2)]
        TM2 = [sb(f"rk_TM{i}", [128, 8, 3, 64], BF16) for i in range(2)]
        MT2 = [sb(f"rk_MT{i}", [128, 8, 64], BF16) for i in range(2)]
        NS8 = sb("rk_NS", [128, 8, 2, 64], BF16)
        L8 = sb("rk_L", [128, 8, 64], BF16)
        maskP8 = sb("rk_maskP8", [128, 8, 2, 128], F32)
        maskL8 = sb("rk_maskL8", [128, 8, 64], F32)
        eye8 = sb("rk_eye8", [128, 8, 64], F32)
        for c in range(8):
            P.op("pool", lambda e, c=c: e.tensor_copy(out=maskP8[:, c, :, :], in_=maskP[:]), ["msk"], ["rk_m8"])
            P.op("pool", lambda e, c=c: e.tensor_copy(out=maskL8[:, c, :], in_=maskL[:]), ["msk"], ["rk_m8"])
            P.op("pool", lambda e, c=c: e.tensor_copy(out=eye8[:, c, :], in_=eye[:]), ["msk"], ["rk_m8"])
        Hf = sb("rk_Hf", [128, 64], F32)
        HG = sb("rk_HG", [128, 64], F32)
        Hb = sb("rk_Hb", [128, 64], BF16)
        Zs = sb("rk_Zs", [128, 64], BF16)
        Us = sb("rk_Us", [128, 64], BF16)
        Yt = sb("rk_Y", [128, TB], F32)
        yo = sb("rk_yo", [128, TB], BF16)
        ps_a = P.ps("rk_psa", [128, TB], F32)
        ps_b = ps_a
        W = P.ps("rk_Wps", [128, 2048], F32)
        W5 = P.ps("rk_W5ps", [128, 512], F32)
        small2 = P.ps("rk_small2", [128, 512], F32)
        yps = P.ps("rk_yps", [128, TB], F32)
        trpv = W[:, 0:1536].rearrange("p (c a t) -> p c a t", c=8, a=3)
        ppv = W[:, 0:2048].rearrange("p (c a t) -> p c a t", c=8, a=2)
        ps1v = W[:, 0:1024].rearrange("p (c t) -> p c t", c=8)
        ps2v = W[:, 1024:1536].rearrange("p (c t) -> p c t", c=8)
        lpv = W5[:, 0:512].rearrange("p (c t) -> p c t", c=8)
        sq = small2[:, 0:192].rearrange("p (a t) -> p a t", a=3)

        def dve(fn, r, w):
            P.op("dve", fn, r, w)

        def act(fn, r, w):
            P.op("act", fn, r, w)

        def pe(fn, r, w):
            P.op("pe", fn, r, w)

        def c3(ap):
            return ap.rearrange("p (c t) -> p c t", t=64)

        def blk_names(pr, j, idx):
            p3 = idx % 3
            par = idx % 2
            pc = slice(pr * 128, (pr + 1) * 128)
            vcol = lambda base: V[:, base + pr:base + pr + 1]
            js = slice(j * TB, (j + 1) * TB)
            return (pc, vcol, js, gam2[p3], gv2[p3], bonus2[p3], AR2[p3], PM2[par], TM2[par], MT2[par],
                    f"rk_gam{p3}", f"rk_g{p3}", f"rk_bonus{p3}", f"rk_AR{p3}", f"rk_PM{par}", f"rk_TM{par}", f"rk_MT{par}",
                    BK2[par], vbf2[par], f"rk_BK{par}", f"rk_vbf{par}")

        eA = sb("rk_eA", [128, TB], F32)
        eB = sb("rk_eB", [128, TB], F32)
        ebf = sb("rk_ebf", [128, TB], BF16)

        def gen_preA(pr, j, idx):
            pc, vcol, js, gam, gv, bonus, AR, PM, TM, MTs, rGAM, rG, rBON, rAR, rPM, rTM, rMT, BK, vbf, rBK, rVBF = blk_names(pr, j, idx)
            for gi, n in enumerate("rkv"):
                row0 = gi * 1024 + pr * 128
                if j == 0:
                    dve(lambda e, n=n: e.memset(zt[n][:, 0:1], 0.0), [], [f"rk_z{n}"])
                    P.dma("sp", zt[n][:, 1:TB + 1], zrwT[row0:row0 + 128, 0:TB], reads=["dram:zrwT"],
                          writes=[f"rk_z{n}"], key=f"rk_z{n}")
                else:
                    P.dma("sp", zt[n][:], zrwT[row0:row0 + 128, j * TB - 1:(j + 1) * TB], reads=["dram:zrwT"],
                          writes=[f"rk_z{n}"], key=f"rk_z{n}")
                mcol = 80 + gi * 8 + pr
                dve(lambda e, n=n: e.tensor_tensor(out=xs[n][:], in0=zt[n][:, 0:TB], in1=zt[n][:, 1:TB + 1],
                                                   op=ALU.subtract), [f"rk_z{n}"], [f"rk_s{n}"])
                dve(lambda e, n=n, mcol=mcol: e.scalar_tensor_tensor(
                    out=xs[n][:], in0=xs[n][:], scalar=V[:, mcol:mcol + 1], in1=zt[n][:, 1:TB + 1], op0=ALU.mult,
                    op1=ALU.add), [f"rk_s{n}", f"rk_z{n}", "V"], [f"rk_s{n}"])
            js = slice(j * TB, (j + 1) * TB)
            yield
            pe(lambda e: e.matmul(ps_a[:], lhsT=w2b[:, pc], rhs=twd[:, js], start=True, stop=True),
               ["rk_w", "rk_lin"], ["rk_psa"])
            act(lambda e: e.activation(out=logw[:], in_=ps_a[:], func=AF.Sigmoid, bias=vcol(104)), ["rk_psa", "V"],
                ["rk_logw"])
            dve(lambda e: e.tensor_scalar(out=logw[:], in0=logw[:], scalar1=NEG_E, scalar2=None, op0=ALU.mult),
                ["rk_logw"], ["rk_logw"])
            pe(lambda e: e.matmul(ps_b[:], lhsT=a2b[:, pc], rhs=ads[:, js], start=True, stop=True),
               ["rk_w", "rk_lin"], ["rk_psa"])
            act(lambda e: e.activation(out=av[:], in_=ps_b[:], func=AF.Sigmoid, bias=vcol(112)), ["rk_psa", "V"],
                ["rk_a"])
            pe(lambda e: e.matmul(ps_a[:], lhsT=g2b[:, 0, pc], rhs=sg1[:, js], start=True, stop=False),
               ["rk_w", "rk_lin"], ["rk_psa"])
            pe(lambda e: e.matmul(ps_a[:], lhsT=g2b[0:32, 1, pc], rhs=sg2[:, js], start=False, stop=True),
               ["rk_w", "rk_lin"], ["rk_psa"])
            act(lambda e: e.activation(out=gv[:], in_=ps_a[:], func=AF.Copy), ["rk_psa"], [rG])
            yield
            dve(lambda e: e.tensor_scalar(out=tA[:], in0=xs["k"][:], scalar1=vcol(120), scalar2=None, op0=ALU.mult),
                ["rk_sk", "V"], ["rk_tA"])
            dve(lambda e: e.tensor_tensor(out=tbf[:], in0=tA[:], in1=tA[:], op=ALU.mult), ["rk_tA"], ["rk_tbf"])
            pe(lambda e: e.matmul(ps_b[:], lhsT=bones_bf[:], rhs=tbf[:], start=True, stop=True),
               ["bones_bf", "rk_tbf"], ["rk_psa"])
            act(lambda e: e.activation(out=tB[:], in_=ps_b[:], func=AF.Sqrt), ["rk_psa"], ["rk_tB"])
            dve(lambda e: e.tensor_scalar(out=tB[:], in0=tB[:], scalar1=1e-12, scalar2=None, op0=ALU.max),
                ["rk_tB"], ["rk_tB"])
            dve(lambda e: e.reciprocal(out=tB[:], in_=tB[:]), ["rk_tB"], ["rk_tB"])
            dve(lambda e: e.tensor_tensor(out=kkn[:], in0=tA[:], in1=tB[:], op=ALU.mult), ["rk_tA", "rk_tB"],
                ["rk_kkn"])
            yield
            dve(lambda e: e.tensor_scalar(out=tA[:], in0=av[:], scalar1=vcol(128), scalar2=omka[:, pr:pr + 1],
                                          op0=ALU.mult, op1=ALU.add), ["rk_a", "V", "omka"], ["rk_tA"])
            dve(lambda e: e.tensor_tensor(out=k2[:], in0=xs["k"][:], in1=tA[:], op=ALU.mult), ["rk_sk", "rk_tA"],
                ["rk_k2"])
            yield
            dve(lambda e: e.tensor_tensor_scan(out=cum[:], data0=rmask[:], data1=logw[:], initial=0.0, op0=ALU.mult,
                                               op1=ALU.add), ["msk", "rk_logw"], ["rk_cum"])
            act(lambda e: e.activation(out=gam[:], in_=cum[:], func=AF.Exp), ["rk_cum"], [rGAM])
            act(lambda e: e.activation(out=ginv[:], in_=cum[:], func=AF.Exp, scale=-1.0), ["rk_cum"], ["rk_ginv"])
            dve(lambda e: e.tensor_tensor(out=tB[:], in0=cum[:], in1=logw[:], op=ALU.subtract),
                ["rk_cum", "rk_logw"], ["rk_tB"])
            act(lambda e: e.activation(out=gprev[:], in_=tB[:], func=AF.Exp), ["rk_tB"], ["rk_gprev"])
            yield
            dve(lambda e: e.scalar_tensor_tensor(out=AR[:, :, 0, :], in0=c3(kkn[:]), scalar=-1.0, in1=c3(gprev[:]),
                                                 op0=ALU.mult, op1=ALU.mult), ["rk_kkn", "rk_gprev"], [rAR])
            dve(lambda e: e.tensor_tensor(out=AR[:, :, 1, :], in0=c3(xs["r"][:]), in1=c3(gam[:]), op=ALU.mult),
                ["rk_sr", rGAM, rAR], [rAR])
            dve(lambda e: e.tensor_tensor(out=tA[:], in0=kkn[:], in1=av[:], op=ALU.mult), ["rk_kkn", "rk_a"],
                ["rk_tA"])
            dve(lambda e: e.tensor_tensor(out=BK[:, :, 0, :], in0=c3(tA[:]), in1=c3(ginv[:]), op=ALU.mult),
                ["rk_tA", "rk_ginv"], [rBK])
            dve(lambda e: e.tensor_tensor(out=BK[:, :, 1, :], in0=c3(k2[:]), in1=c3(ginv[:]), op=ALU.mult),
                ["rk_k2", "rk_ginv", rBK], [rBK])
            act(lambda e: e.activation(out=vbf[:], in_=xs["v"][:], func=AF.Copy), ["rk_sv"], [rVBF])
            yield
            dve(lambda e: e.tensor_tensor(out=tC[:], in0=xs["r"][:], in1=k2[:], op=ALU.mult), ["rk_sr", "rk_k2"],
                ["rk_tC"])
            dve(lambda e: e.tensor_scalar(out=tbf[:], in0=tC[:], scalar1=vcol(136), scalar2=None, op0=ALU.mult),
                ["rk_tC", "V", "rk_tbf"], ["rk_tbf"])
            pe(lambda e: e.matmul(ps_b[:], lhsT=bones_bf[:], rhs=tbf[:], start=True, stop=True),
               ["bones_bf", "rk_tbf"], ["rk_psa"])
            dve(lambda e: e.tensor_tensor(out=bonus[:], in0=ps_b[:], in1=xs["v"][:], op=ALU.mult),
                ["rk_psa", "rk_sv"], [rBON])
            yield

        def gen_preB(pr, j, idx):
            pc, vcol, js, gam, gv, bonus, AR, PM, TM, MTs, rGAM, rG, rBON, rAR, rPM, rTM, rMT, BK, vbf, rBK, rVBF = blk_names(pr, j, idx)
            yield
            for c in range(8):
                cc = slice(c * 64, (c + 1) * 64)
                for h in range(2):
                    sl = slice(64 * h, 64 * h + 64)
                    for ti in range(3):
                        in_ap = BK[sl, c, ti, :] if ti < 2 else vbf[sl, cc]
                        pe(lambda e, sl=sl, ti=ti, in_ap=in_ap, c=c: e.matmul(trpv[sl, c, ti, :], lhsT=in_ap,
                                                                             rhs=ident_bf[sl, sl], start=True, stop=True),
                           [rBK, rVBF, "ident_bf"], ["rk_Wps"])
            act(lambda e: e.activation(out=TM[:], in_=trpv, func=AF.Copy), ["rk_Wps"], [rTM])
            yield
            for c in range(8):
                for h in range(2):
                    sl = slice(64 * h, 64 * h + 64)
                    arv = AR[sl, c, :, :].rearrange("p a t -> p (a t)")
                    pe(lambda e, sl=sl, c=c, arv=arv: e.matmul(ppv[sl, c, 0, :], lhsT=BK[sl, c, 0, :], rhs=arv, start=True,
                                                               stop=True), [rBK, rAR], ["rk_Wps"])
                    pe(lambda e, sl=sl, c=c, arv=arv: e.matmul(ppv[sl, c, 1, :], lhsT=BK[sl, c, 1, :], rhs=arv, start=True,
                                                               stop=True), [rBK, rAR], ["rk_Wps"])
                    pe(lambda e, sl=sl, c=c: e.matmul(lpv[sl, c, :], lhsT=AR[sl, c, 0, :], rhs=BK[sl, c, 0, :], start=True,
                                                      stop=True), [rBK, rAR], ["rk_W5ps"])
            dve(lambda e: e.tensor_tensor(out=PM[:], in0=ppv, in1=maskP8[:], op=ALU.mult), ["rk_Wps", "rk_m8"], [rPM])
            dve(lambda e: e.tensor_tensor(out=L8[:], in0=lpv, in1=maskL8[:], op=ALU.mult), ["rk_W5ps", "rk_m8"], ["rk_L"])
            act(lambda e: e.activation(out=NS8[:, :, 0, :], in_=PM[:, :, 0, 0:64], func=AF.Copy), [rPM], ["rk_NS"])
            P.op("pool", lambda e: e.tensor_copy(out=NS8[:, :, 1, :], in_=eye8[:]), ["rk_m8", "rk_NS"], ["rk_NS"])
            yield
            for step in range(6):
                last = step == 5
                for c in range(8):
                    for h in range(2):
                        sl = slice(64 * h, 64 * h + 64)
                        if not last:
                            nsv = NS8[sl, c, :, :].rearrange("p a t -> p (a t)")
                            pe(lambda e, sl=sl, nsv=nsv, c=c: e.matmul(ps1v[sl, c, :], lhsT=L8[sl, c, :], rhs=nsv, start=True,
                                                                       stop=True), ["rk_L", "rk_NS"], ["rk_Wps"])
                            pe(lambda e, sl=sl, c=c: e.matmul(ps2v[sl, c, :], lhsT=NS8[sl, c, 0, :], rhs=L8[sl, c, :],
                                                              start=True, stop=True), ["rk_L", "rk_NS"], ["rk_Wps"])
                        else:
                            pe(lambda e, sl=sl, c=c: e.matmul(ps1v[sl, c, 64:128], lhsT=L8[sl, c, :], rhs=NS8[sl, c, 1, :],
                                                              start=True, stop=True), ["rk_L", "rk_NS"], ["rk_Wps"])
                if not last:
                    dve(lambda e: e.tensor_copy(out=NS8[:, :, 0, :], in_=ps1v[:, :, 0:64]), ["rk_Wps", "rk_NS"], ["rk_NS"])
                    dve(lambda e: e.tensor_tensor(out=NS8[:, :, 1, :], in0=NS8[:, :, 1, :], in1=ps1v[:, :, 64:128],
                                                  op=ALU.add), ["rk_Wps", "rk_NS"], ["rk_NS"])
                    dve(lambda e: e.tensor_copy(out=L8[:], in_=ps2v), ["rk_Wps", "rk_L"], ["rk_L"])
                else:
                    dve(lambda e: e.tensor_tensor(out=MTs[:], in0=NS8[:, :, 1, :], in1=ps1v[:, :, 64:128], op=ALU.add),
                        ["rk_Wps", "rk_NS"], [rMT])
                yield
            yield

        def gen_seq(pr, j, idx):
            pc, vcol, js, gam, gv, bonus, AR, PM, TM, MTs, rGAM, rG, rBON, rAR, rPM, rTM, rMT, BK, vbf, rBK, rVBF = blk_names(pr, j, idx)
            if j == 0:
                dve(lambda e: e.memset(Hf[:], 0.0), [], ["rk_Hf"])
                dve(lambda e: e.memset(Hb[:], 0.0), [], ["rk_Hb"])
            for c in range(8):
                cc = slice(c * 64, (c + 1) * 64)
                gcol = gam[:, c * 64 + 63:c * 64 + 64]
                for h in range(2):
                    sl = slice(64 * h, 64 * h + 64)
                    pe(lambda e, sl=sl, c=c: e.matmul(sq[sl, 0, :], lhsT=AR[sl, c, 0, :], rhs=Hb[sl, :], start=True,
                                                      stop=False), [rAR, "rk_Hb"], ["rk_small2"])
                    pe(lambda e, sl=sl, c=c: e.matmul(sq[sl, 0, :], lhsT=PM[sl, c, 1, 0:64], rhs=TM[sl, c, 2, :],
                                                      start=False, stop=True), [rPM, rTM], ["rk_small2"])
                act(lambda e: e.activation(out=Zs[:], in_=sq[:, 0, :], func=AF.Copy), ["rk_small2"], ["rk_Zs"])
                for h in range(2):
                    sl = slice(64 * h, 64 * h + 64)
                    pe(lambda e, sl=sl, c=c: e.matmul(sq[sl, 1, :], lhsT=MTs[sl, c, :], rhs=Zs[sl, :], start=True,
                                                      stop=True), [rMT, "rk_Zs"], ["rk_small2"])
                act(lambda e: e.activation(out=Us[:], in_=sq[:, 1, :], func=AF.Copy), ["rk_small2"], ["rk_Us"])
                for h in range(2):
                    sl = slice(64 * h, 64 * h + 64)
                    pe(lambda e, sl=sl, c=c: e.matmul(sq[sl, 2, :], lhsT=TM[sl, c, 0, :], rhs=Us[sl, :], start=True,
                                                      stop=False), [rTM, "rk_Us"], ["rk_small2"])
                    pe(lambda e, sl=sl, c=c: e.matmul(sq[sl, 2, :], lhsT=TM[sl, c, 1, :], rhs=TM[sl, c, 2, :],
                                                      start=False, stop=True), [rTM], ["rk_small2"])
                    pe(lambda e, sl=sl, c=c, cc=cc: e.matmul(yps[sl, cc], lhsT=Hb[sl, :], rhs=AR[sl, c, 1, :], start=True,
                                                             stop=False), [rAR, "rk_Hb"], ["rk_yps"])
                    pe(lambda e, sl=sl, c=c, cc=cc: e.matmul(yps[sl, cc], lhsT=Us[sl, :], rhs=PM[sl, c, 0, 64:128],
                                                             start=False, stop=False), ["rk_Us", rPM], ["rk_yps"])
                    pe(lambda e, sl=sl, c=c, cc=cc: e.matmul(yps[sl, cc], lhsT=TM[sl, c, 2, :], rhs=PM[sl, c, 1, 64:128],
                                                             start=False, stop=True), [rTM, rPM], ["rk_yps"])
                P.op("pool", lambda e, gcol=gcol: e.tensor_scalar(out=HG[:], in0=Hf[:], scalar1=gcol, scalar2=None,
                                                                   op0=ALU.mult), ["rk_Hf", rGAM], ["rk_HG"])
                dve(lambda e, gcol=gcol: e.scalar_tensor_tensor(out=Hf[:], in0=sq[:, 2, :], scalar=gcol, in1=HG[:],
                                                                op0=ALU.mult, op1=ALU.add),
                    ["rk_small2", "rk_HG", rGAM, "rk_Hf"], ["rk_Hf"])
                act(lambda e: e.activation(out=Hb[:], in_=Hf[:], func=AF.Copy), ["rk_Hf", "rk_Hb"], ["rk_Hb"])
                yield
            act(lambda e: e.activation(out=Yt[:], in_=yps[:], func=AF.Copy), ["rk_yps"], ["rk_Y"])
            dve(lambda e: e.tensor_copy(out=ebf[:], in_=Yt[:]), ["rk_Y", "rk_ebf"], ["rk_ebf"])
            pe(lambda e: e.matmul(ps_a[:], lhsT=bones_bf[:], rhs=ebf[:], start=True, stop=True), ["bones_bf", "rk_ebf"],
               ["rk_psa"])
            dve(lambda e: e.scalar_tensor_tensor(out=eA[:], in0=ps_a[:], scalar=-1.0 / 64, in1=Yt[:], op0=ALU.mult,
                                                 op1=ALU.add), ["rk_psa", "rk_Y"], ["rk_eA"])
            dve(lambda e: e.tensor_tensor(out=ebf[:], in0=eA[:], in1=eA[:], op=ALU.mult), ["rk_eA", "rk_ebf"],
                ["rk_ebf"])
            pe(lambda e: e.matmul(ps_a[:], lhsT=bones_bf[:], rhs=ebf[:], start=True, stop=True), ["bones_bf", "rk_ebf"],
               ["rk_psa"])
            act(lambda e: e.activation(out=eB[:], in_=ps_a[:], func=AF.Sqrt, bias=GN_EPS, scale=1.0 / 64),
                ["rk_psa"], ["rk_eB"])
            yield
            dve(lambda e: e.reciprocal(out=eB[:], in_=eB[:]), ["rk_eB"], ["rk_eB"])
            dve(lambda e: e.tensor_tensor(out=eA[:], in0=eA[:], in1=eB[:], op=ALU.mult), ["rk_eA", "rk_eB"], ["rk_eA"])
            dve(lambda e: e.tensor_scalar(out=eA[:], in0=eA[:], scalar1=vcol(144), scalar2=vcol(152), op0=ALU.mult,
                                          op1=ALU.add), ["rk_eA", "V"], ["rk_eA"])
            dve(lambda e: e.tensor_tensor(out=eA[:], in0=eA[:], in1=bonus[:], op=ALU.add), ["rk_eA", rBON],
                ["rk_eA"])
            dve(lambda e: e.tensor_tensor(out=yo[:], in0=eA[:], in1=gv[:], op=ALU.mult), ["rk_eA", rG], ["rk_yo"])
            P.dma("act", yrwT[pc, js], yo[:], reads=["rk_yo"], writes=["dram:yrwT"], key="rk_yo")
            yield

        blks = []
        for pr in range(DBG.get("rkp", 8)):
            for j in range(DBG.get("rkb", NB)):
                blks.append((pr, j, len(blks)))
        nb_ = len(blks)
        for i in range(nb_ + 2):
            gens = []
            if i < nb_:
                gens.append(gen_preA(*blks[i]))
            if 0 <= i - 1 < nb_:
                gens.append(gen_preB(*blks[i - 1]))
            if 0 <= i - 2 < nb_:
                gens.append(gen_seq(*blks[i - 2]))
            while gens:
                for g in list(gens):
                    try:
                        next(g)
                    except StopIteration:
                        gens.remove(g)
        P.barrier()
        P.stack = prev_stack


def mla_stage(P, nc, st0, V, pos, pos_w, qk_tab, d_tab, zmlaT, cqnT_full, ckvnT, w_q_up, w_kv_up, ymlaT, ones_bf, evac_copy):
    SCALE = 192 ** -0.5
    TWO_PI = 2.0 * math.pi
    QB = 342
    with contextlib.ExitStack() as st:
        prev_stack = P.stack
        P.stack = st
        sb = P.sb
        wq = sb("ml_wq", [128, 4, 1536], BF16)
        wkv = sb("ml_wkv", [128, 4, 2048], BF16)
        for kc in range(4):
            P.dma("pool", wq[:, kc, :], w_q_up[kc * 128:(kc + 1) * 128, :], writes=["ml_w"], key="ml_w")
            P.dma("pool", wkv[:, kc, :], w_kv_up[kc * 128:(kc + 1) * 128, :], writes=["ml_w"], key="ml_w")
        qk = sb("ml_qk", [128, 3, QB], F32)
        dt = sb("ml_dt", [128, 32], F32)
        P.dma("sp", qk[:], qk_tab.rearrange("p (a t) -> p a t", a=3), writes=["ml_tab"], key="ml_tab")
        P.dma("sp", dt[:], d_tab, writes=["ml_tab"], key="ml_tab")
        cs = sb("ml_cs", [64, T], F32)
        sn = sb("ml_sn", [64, T], F32)
        csw = sb("ml_csw", [64, TW], F32)
        snw = sb("ml_snw", [64, TW], F32)
        with contextlib.ExitStack() as st2:
            P.stack = st2
            posi = P.sb("ml_posi", [64, T], I32)
            ang = P.sb("ml_ang", [64, T], F32)
            tt = P.sb("ml_tt", [64, T], F32)
            kf = P.sb("ml_kf", [64, T], F32)
            for (psrc, n, cdst, sdst) in ((pos, T, cs, sn), (pos_w, TW, csw, snw)):
                P.dma("sp", posi[:, 0:n], psrc[0:1, :].to_broadcast([64, n]), writes=["ml_posi"], key="ml_posi")
                P.op("dve", lambda e: e.tensor_copy(out=ang[:, 0:n], in_=posi[:, 0:n]), ["ml_posi"], ["ml_ang"])
                P.op("dve", lambda e: e.tensor_scalar(out=ang[:, 0:n], in0=ang[:, 0:n], scalar1=V[0:64, 524:525], scalar2=None,
                                                      op0=ALU.mult), ["ml_ang", "V"], ["ml_ang"])
                for (dst, shift) in ((sdst, 0.5), (cdst, 0.75)):
                    P.op("dve", lambda e, shift=shift: e.tensor_scalar(out=tt[:, 0:n], in0=ang[:, 0:n], scalar1=1.0 / TWO_PI,
                                                                       scalar2=shift, op0=ALU.mult, op1=ALU.add),
                         ["ml_ang", "ml_tt"], ["ml_tt"])
                    P.op("dve", lambda e: e.tensor_copy(out=posi[:, 0:n], in_=tt[:, 0:n]), ["ml_tt", "ml_posi", "ml_ang"],
                         ["ml_posi"])
                    P.op("dve", lambda e: e.tensor_copy(out=kf[:, 0:n], in_=posi[:, 0:n]), ["ml_posi"], ["ml_kf"])
                    P.op("dve", lambda e: e.tensor_tensor(out=tt[:, 0:n], in0=tt[:, 0:n], in1=kf[:, 0:n], op=ALU.subtract),
                         ["ml_tt", "ml_kf"], ["ml_tt"])
                    P.op("dve", lambda e: e.tensor_scalar(out=kf[:, 0:n], in0=tt[:, 0:n], scalar1=0.0, scalar2=None,
                                                          op0=ALU.is_lt), ["ml_tt", "ml_kf"], ["ml_kf"])
                    P.op("dve", lambda e: e.scalar_tensor_tensor(out=tt[:, 0:n], in0=kf[:, 0:n], scalar=-0.5, in1=tt[:, 0:n],
                                                                 op0=ALU.add, op1=ALU.add), ["ml_tt", "ml_kf"], ["ml_tt"])
                    P.op("act", lambda e, dst=dst: e.activation(out=dst[:, 0:n], in_=tt[:, 0:n], func=AF.Sin, scale=TWO_PI),
                         ["ml_tt"], ["ml_rope"])
            P.barrier()
            P.stack = st
        kr = sb("ml_kr", [64, T], BF16)
        kn = sb("ml_kn", [128, T], BF16)
        vtm = sb("ml_vtm", [128, 32, 128], BF16)
        xq = [sb(f"ml_xq{i}", [128, 4, TB], BF16) for i in range(2)]
        raw = sb("ml_raw", [64, TB], F32)
        t1 = sb("ml_t1", [64, TB], F32)
        t2 = sb("ml_t2", [64, TB], F32)
        qn = sb("ml_qn", [128, TB], BF16)
        qr = sb("ml_qr", [64, TB], BF16)
        pT = [sb(f"ml_pT{i}", [128, TB], BF16) for i in range(3)]
        rinv = sb("ml_rinv", [128, TB], F32)
        yo = [sb(f"ml_yo{i}", [128, TB], BF16) for i in range(2)]
        psq = P.ps("ml_psq", [128, TB], F32)
        psr = P.ps("ml_psr", [64, TB], F32)
        psv = P.ps("ml_psv", [128, 4, 128], F32)
        sps = [P.ps(f"ml_sps{i}", [128, TB], F32) for i in range(2)]
        ops_ = P.ps("ml_ops", [128, TB], F32)
        lps = P.ps("ml_lps", [128, TB], F32)

        def rope(src_res, cst, snt, c0, n, out_ap, out_res, scale):
            lo, hi = slice(0, 32), slice(32, 64)
            js = slice(c0, c0 + n)
            d = lambda fn, r, w: P.op("dve", fn, r, w)
            d(lambda e: e.tensor_tensor(out=t1[lo, 0:n], in0=raw[lo, 0:n], in1=cst[lo, js], op=ALU.mult), [src_res, "ml_rope"],
              ["ml_t1"])
            d(lambda e: e.tensor_tensor(out=t2[lo, 0:n], in0=raw[hi, 0:n], in1=snt[hi, js], op=ALU.mult), [src_res, "ml_rope"],
              ["ml_t2"])
            d(lambda e: e.tensor_tensor(out=t1[hi, 0:n], in0=raw[hi, 0:n], in1=cst[hi, js], op=ALU.mult),
              [src_res, "ml_rope", "ml_t1"], ["ml_t1"])
            d(lambda e: e.tensor_tensor(out=t2[hi, 0:n], in0=raw[lo, 0:n], in1=snt[lo, js], op=ALU.mult),
              [src_res, "ml_rope", "ml_t2"], ["ml_t2"])
            d(lambda e: e.tensor_tensor(out=t1[lo, 0:n], in0=t1[lo, 0:n], in1=t2[lo, 0:n], op=ALU.subtract), ["ml_t1", "ml_t2"],
              ["ml_t1"])
            d(lambda e: e.tensor_tensor(out=t1[hi, 0:n], in0=t1[hi, 0:n], in1=t2[hi, 0:n], op=ALU.add), ["ml_t1", "ml_t2"],
              ["ml_t1"])
            P.op("act", lambda e: e.activation(out=out_ap, in_=t1[:, 0:n], func=AF.Copy, scale=scale), ["ml_t1"], [out_res])

        for j in range(NB):
            js = slice(j * TB, (j + 1) * TB)
            P.dma("sp", raw[:], zmlaT[1024:1088, js], reads=["dram:zmlaT"], writes=["ml_raw"], key="ml_raw")
            rope("ml_raw", cs, sn, j * TB, TB, kr[:, js], "ml_kr", 1.0)
        xi = 0
        for hd in range(DBG.get("mlh", 8)):
            for j in range(NB):
                js = slice(j * TB, (j + 1) * TB)
                b = xi % 2
                xi += 1
                P.dma("sp", xq[b][:], ckvnT[:, js].rearrange("(k p) t -> p k t", p=128), reads=["dram:ckvnT"],
                      writes=[f"ml_xq{b}"], key=f"ml_xq{b}")
                for kc in range(4):
                    P.op("pe", lambda e, kc=kc, b=b: e.matmul(psq[:], lhsT=wkv[:, kc, hd * 256:hd * 256 + 128],
                                                              rhs=xq[b][:, kc, :], start=(kc == 0), stop=(kc == 3)),
                         ["ml_w", f"ml_xq{b}"], ["ml_psq"])
                evac_copy(kn[:, js], psq[:], ["ml_psq"], ["ml_kn"])
                for tt_ in range(4):
                    for kc in range(4):
                        P.op("pe", lambda e, kc=kc, b=b, tt_=tt_: e.matmul(
                            psv[:, tt_, :], lhsT=xq[b][:, kc, tt_ * 128:(tt_ + 1) * 128],
                            rhs=wkv[:, kc, hd * 256 + 128:hd * 256 + 256], start=(kc == 0), stop=(kc == 3)),
                            ["ml_w", f"ml_xq{b}"], ["ml_psv"])
                evac_copy(vtm[:, j * 4:(j + 1) * 4, :], psv[:], ["ml_psv"], ["ml_vtm"])
            for jj in range(DBG.get("mlb", 3)):
                n = QB
                b = xi % 2
                xi += 1

                P.dma("sp", xq[b][:, :, 0:QB], cqnT_full[:, jj * QB:(jj + 1) * QB].rearrange("(k p) t -> p k t", p=128),
                      reads=["dram:cqwT"], writes=[f"ml_xq{b}"], key=f"ml_xq{b}")
                for kc in range(4):
                    P.op("pe", lambda e, kc=kc, b=b: e.matmul(psq[:, 0:n], lhsT=wq[:, kc, hd * 192:hd * 192 + 128],
                                                              rhs=xq[b][:, kc, 0:n], start=(kc == 0), stop=(kc == 3)),
                         ["ml_w", f"ml_xq{b}"], ["ml_psq"])
                P.op("act", lambda e: e.activation(out=qn[:, 0:n], in_=psq[:, 0:n], func=AF.Copy, scale=SCALE), ["ml_psq"],
                     ["ml_qn"])
                for kc in range(4):
                    P.op("pe", lambda e, kc=kc, b=b: e.matmul(psr[:, 0:n], lhsT=wq[:, kc, hd * 192 + 128:hd * 192 + 192],
                                                              rhs=xq[b][:, kc, 0:n], start=(kc == 0), stop=(kc == 3)),
                         ["ml_w", f"ml_xq{b}"], ["ml_psr"])
                P.op("act", lambda e: e.activation(out=raw[:, 0:n], in_=psr[:, 0:n], func=AF.Copy), ["ml_psr", "ml_raw"],
                     ["ml_raw"])
                rope("ml_raw", csw, snw, jj * QB, n, qr[:, 0:n], "ml_qr", SCALE)
                nkt = 32

                def score(kt):
                    si = kt % 2
                    ks = slice(kt * 128, (kt + 1) * 128)
                    P.op("pe", lambda e, si=si, ks=ks: e.matmul(sps[si][:, 0:n], lhsT=kn[:, ks], rhs=qn[:, 0:n],
                                                                start=True, stop=False),
                         ["ml_kn", "ml_qn"], [f"ml_sps{si}"])
                    P.op("pe", lambda e, si=si, ks=ks: e.matmul(sps[si][:, 0:n], lhsT=kr[:, ks], rhs=qr[:, 0:n],
                                                                start=False, stop=True),
                         ["ml_kr", "ml_qr"], [f"ml_sps{si}"])

                score(0)
                for kt in range(nkt):
                    si = kt % 2
                    pi = kt % 3
                    if kt + 1 < nkt:
                        score(kt + 1)
                    P.op("act", lambda e, si=si, pi=pi: e.activation(out=pT[pi][:, 0:n], in_=sps[si][:, 0:n], func=AF.Exp),
                         [f"ml_sps{si}"], [f"ml_pT{pi}"])
                    P.op("dve", lambda e, pi=pi, kt=kt, jj=jj: e.scalar_tensor_tensor(
                        out=pT[pi][:, 0:n], in0=qk[:, jj, :], scalar=dt[:, kt:kt + 1], in1=pT[pi][:, 0:n], op0=ALU.is_ge,
                        op1=ALU.mult), [f"ml_pT{pi}", "ml_tab"], [f"ml_pT{pi}"])
                    P.op("pe", lambda e, pi=pi, kt=kt: e.matmul(
                        ops_[:, 0:n], lhsT=vtm[:, kt, :], rhs=pT[pi][:, 0:n], start=(kt == 0), stop=(kt == nkt - 1)),
                        ["ml_vtm", f"ml_pT{pi}"], ["ml_ops"])
                    P.op("pe", lambda e, pi=pi, kt=kt: e.matmul(
                        lps[:, 0:n], lhsT=ones_bf[:], rhs=pT[pi][:, 0:n], start=(kt == 0), stop=(kt == nkt - 1)),
                        ["ones_bf", f"ml_pT{pi}"], ["ml_lps"])
                P.op("dve", lambda e: e.tensor_scalar(out=rinv[:, 0:n], in0=lps[:, 0:n], scalar1=1e-30, scalar2=None,
                                                      op0=ALU.max), ["ml_lps"], ["ml_rinv"])
                P.op("dve", lambda e: e.reciprocal(out=rinv[:, 0:n], in_=rinv[:, 0:n]), ["ml_rinv"], ["ml_rinv"])
                ob = jj % 2
                P.op("dve", lambda e, ob=ob: e.tensor_tensor(out=yo[ob][:, 0:n], in0=ops_[:, 0:n], in1=rinv[:, 0:n], op=ALU.mult),
                     ["ml_ops", "ml_rinv"], [f"ml_yo{ob}"])
                P.dma("act", ymlaT[hd * 128:(hd + 1) * 128, jj * QB:(jj + 1) * QB], yo[ob][:, 0:n], reads=[f"ml_yo{ob}"],
                      writes=["dram:ymlaT"], key=f"ml_yo{ob}")
        P.barrier()
        P.stack = prev_stack


_CACHE = {}


def make_in_maps(inputs):
    sq = lambda a: np.ascontiguousarray(np.asarray(a)[0])
    V = pack_vecs(inputs)
    common = {
        "vecs": V,
        "w_in": sq(inputs["w_in"]), "rw_w2": sq(inputs["rw_w2"]), "rw_a2": sq(inputs["rw_a2"]), "rw_g2": sq(inputs["rw_g2"]),
        "w_q_up": sq(inputs["mla_w_q_up"]), "w_kv_up": sq(inputs["mla_w_kv_up"]),
        "w_brw": sq(inputs["w_branch_rw"]), "w_bmla": sq(inputs["w_branch_mla"]), "w_out": sq(inputs["w_out"]),
        "w_up": sq(inputs["w_up"]), "w_down": sq(inputs["w_down"]), "w_ple": sq(inputs["w_ple"]),
        "w_pg": sq(inputs["w_ple_gate"]),
    }
    i_ = np.arange(342)
    qk = np.zeros((128, 3, 342), np.float32)
    for jj in range(3):
        qc = np.floor_divide(342 * jj - 2 + i_, 64).astype(np.float32)
        qk[:, jj, :] = qc[None, :] - (np.arange(128)[:, None] >= 64).astype(np.float32)
    common["gain_bc"] = np.ascontiguousarray(np.broadcast_to(np.asarray(inputs["pre_mix_norm"], np.float32).reshape(1, -1), (128, 2048)))
    common["qk_tab"] = np.ascontiguousarray(qk.reshape(128, 3 * 342))
    xs = np.asarray(inputs["x"], np.float32)
    ps = np.asarray(inputs["p"], np.float32)[0]
    posn = np.asarray(inputs["positions"], np.int32)
    maps = []
    for c in range(8):
        b, q = c // 4, c % 4
        m = dict(common)
        m["x"] = np.ascontiguousarray(xs[b])
        xwin = np.zeros((1152, xs.shape[2]), np.float32)
        lo_ = 1024 * q - 2
        if lo_ < 0:
            xwin[2:TW] = xs[b, 0:1024]
        else:
            xwin[0:TW] = xs[b, lo_:lo_ + TW]
        m["xw"] = xwin
        m["p"] = np.ascontiguousarray(ps[b, 1024 * q:1024 * (q + 1)])
        m["pos"] = np.ascontiguousarray(posn[b:b + 1])
        pw = np.zeros((1, TW), np.int32)
        lo = 1024 * q - 2
        if lo < 0:
            pw[0, 2:] = posn[b, 0:1024]
        else:
            pw[0, :] = posn[b, lo:lo + TW]
        m["pos_w"] = pw
        m["d_tab"] = np.ascontiguousarray(np.broadcast_to((2.0 * np.arange(32) - 16.0 * q).astype(np.float32)[None, :], (128, 32)))
        maps.append(m)
    return maps


def kernel(**inputs):
    if "nc" not in _CACHE:
        _CACHE["nc"] = build_program()
    nc = _CACHE["nc"]
    maps = make_in_maps(inputs)
    res = run_bass_kernel_spmd(nc, maps, core_ids=list(range(8)))
    outs = [np.asarray(res.results[c]["out"], np.float32) for c in range(8)]
    return np.stack([np.concatenate(outs[0:4], axis=0), np.concatenate(outs[4:8], axis=0)], axis=0)
```

```python
import contextlib
import numpy as np
import concourse.bass as bass
import concourse.mybir as mybir

F32 = mybir.dt.float32
BF16 = mybir.dt.bfloat16
I32 = mybir.dt.int32
ALU = mybir.AluOpType
AF = mybir.ActivationFunctionType
AX = mybir.AxisListType


SAME_ENGINE_SYNC = True


class _Rec:
    def __init__(self):
        self.calls = []

    def __getattr__(self, name):
        def f(*a, **k):
            self.calls.append((name, a, k))
            return self

        return f


def _eager(fn):
    rec = _Rec()
    fn(rec)
    assert len(rec.calls) == 1, rec.calls
    name, a, k = rec.calls[0]
    return lambda e: getattr(e, name)(*a, **k)


class Prog:
    COMPUTE = ("pe", "act", "dve", "pool")

    def __init__(self, nc, stack):
        self.nc = nc
        self.stack = stack
        self.stack0 = stack
        self.streams = {e: [] for e in ("pe", "act", "dve", "pool", "sp")}
        self.sems = {}
        self.cnt = {}
        for e in self.COMPUTE:
            self.sems[e] = stack.enter_context(nc.semaphore("s_" + e))
            self.cnt[e] = 0
        self.seen = {e: {} for e in self.streams}
        self.res = {}
        self.dma_sems = {}
        self.n_ops = 0

    def sb(self, name, shape, dt):
        return self.stack.enter_context(self.nc.sbuf_tensor(name, list(shape), dt))

    def ps(self, name, shape, dt=F32):
        return self.stack.enter_context(self.nc.psum_tensor(name, list(shape), dt))

    def _dma_sem(self, key):
        if key not in self.dma_sems:
            self.dma_sems[key] = self.stack0.enter_context(self.nc.semaphore("d_" + key))
            self.sems["D:" + key] = self.dma_sems[key]
            self.cnt["D:" + key] = 0
        return "D:" + key

    def _deps(self, eng, reads, writes, exclude=None):
        need = {}
        for r in reads:
            ent = self.res.get(r)
            if ent:
                for k, v in ent[0].items():
                    need[k] = max(need.get(k, 0), v)
                if "ps" in r or "small" in r or "trp" in r:
                    for k, v in ent[1].items():
                        if k != eng:
                            need[k] = max(need.get(k, 0), v)
        for w in writes:
            ent = self.res.get(w)
            if ent:
                if not w.startswith("dram:"):
                    for k, v in ent[0].items():
                        need[k] = max(need.get(k, 0), v)
                for k, v in ent[1].items():
                    need[k] = max(need.get(k, 0), v)
        waits = []
        for k, v in need.items():
            if k == eng and (eng == "pe" or not SAME_ENGINE_SYNC):
                continue
            if k == exclude:
                continue
            if self.seen[eng].get(k, 0) >= v:
                continue
            self.seen[eng][k] = v
            waits.append((k, v))
        return waits

    def _mark(self, key, val, reads, writes):
        for r in reads:
            ent = self.res.setdefault(r, [{}, {}])
            ent[1][key] = val
        for w in writes:
            ent = self.res.setdefault(w, [{}, {}])
            if w.startswith("dram:"):
                ent[0][key] = val
            else:
                ent[0] = {key: val}
                ent[1] = {}
            self.res[w] = ent

    def op(self, eng, fn, reads=(), writes=()):
        fn = _eager(fn)
        waits = self._deps(eng, reads, writes)
        self.cnt[eng] += 1
        val = self.cnt[eng]
        sem = self.sems[eng]
        sems = self.sems

        def emit(e, waits=waits, fn=fn, sem=sem):
            for k, v in waits:
                e.wait_ge(sems[k], v)
            fn(e).then_inc(sem, 1)

        self.streams[eng].append(emit)
        self._mark(eng, val, reads, writes)
        self.n_ops += 1

    def dma(self, queue, out, in_, reads=(), writes=(), key=None, **kw):
        assert key is not None
        sk = self._dma_sem(key)
        waits = self._deps(queue, reads, writes, exclude=sk)
        self.cnt[sk] += 16
        val = self.cnt[sk]
        sem = self.sems[sk]
        sems = self.sems

        def emit(e, waits=waits, sem=sem):
            for k, v in waits:
                e.wait_ge(sems[k], v)
            e.dma_start(out=out, in_=in_, **kw).then_inc(sem, 16)

        self.streams[queue].append(emit)
        self._mark(sk, val, reads, writes)
        self.n_ops += 1

    def custom(self, queue, fn, reads=(), writes=(), key=None, raw=False):
        if not raw:
            fn = _eager(fn)
        sk = self._dma_sem(key)
        waits = self._deps(queue, reads, writes, exclude=sk)
        self.cnt[sk] += 16
        val = self.cnt[sk]
        sem = self.sems[sk]
        sems = self.sems

        def emit(e, waits=waits, sem=sem):
            for k, v in waits:
                e.wait_ge(sems[k], v)
            fn(e).then_inc(sem, 16)

        self.streams[queue].append(emit)
        self._mark(sk, val, reads, writes)
        self.n_ops += 1

    def barrier(self):
        snap = dict(self.cnt)
        sems = self.sems
        for eng in self.streams:
            waits = []
            for k, v in snap.items():
                if v <= 0 or self.seen[eng].get(k, 0) >= v:
                    continue
                if k == eng and eng == "pe":
                    continue
                self.seen[eng][k] = v
                waits.append((k, v))

            def emit(e, waits=waits):
                for k, v in waits:
                    e.wait_ge(sems[k], v)

            self.streams[eng].append(emit)

    def wait_all(self, eng, resources):
        waits = self._deps(eng, resources, ())
        sems = self.sems

        def emit(e, waits=waits):
            for k, v in waits:
                e.wait_ge(sems[k], v)

        self.streams[eng].append(emit)

    def emit(self):
        nc = self.nc
        with nc.Block() as block:
            @block.tensor
            def _(e):
                for f in self.streams["pe"]:
                    f(e)

            @block.scalar
            def _(e):
                for f in self.streams["act"]:
                    f(e)

            @block.vector
            def _(e):
                for f in self.streams["dve"]:
                    f(e)

            @block.gpsimd
            def _(e):
                for f in self.streams["pool"]:
                    f(e)

            @block.sync
            def _(e):
                for f in self.streams["sp"]:
                    f(e)

import math
from concourse.bass_utils import run_bass_kernel_spmd

T = 4096
D = 2048
TB = 512
NB = T // TB
DFF = 5632
EPS = 1e-6
GN_EPS = 64e-5
NV = 528
DBG = {}
TW = 1026
WB = [(0, 342), (342, 342), (684, 342)]
FB = [(j * 512, 512) for j in range(8)]


_RANK = {}


def get_rank(e):
    if "r" not in _RANK:
        _RANK["r"] = (e.partition_id() % 4) * 1024
    return _RANK["r"]


def vec_pc(v):
    v = np.asarray(v, np.float32).reshape(-1)
    return np.ascontiguousarray(v.reshape(-1, 128).T)


def pack_vecs(inp):
    V = np.zeros((128, NV), np.float32)
    V[:, 0:16] = vec_pc(inp["pre_mix_norm"])
    V[:, 16:32] = vec_pc(inp["post_mix_norm"])
    V[:, 32:48] = vec_pc(inp["pre_ffn_norm"])
    V[:, 48:64] = vec_pc(inp["post_ffn_norm"])
    V[:, 64:80] = vec_pc(inp["ple_norm"])
    mu = np.asarray(inp["rw_mu"], np.float32).reshape(-1)
    V[:, 80:104] = vec_pc(mu[0:3072])
    V[:, 104:112] = vec_pc(inp["rw_w0"])
    V[:, 112:120] = vec_pc(inp["rw_a0"])
    V[:, 120:128] = vec_pc(inp["rw_k_k"])
    V[:, 128:136] = vec_pc(inp["rw_k_a"])
    V[:, 136:144] = vec_pc(inp["rw_r_k"])
    V[:, 144:152] = vec_pc(inp["rw_lnx_w"])
    V[:, 152:160] = vec_pc(inp["rw_lnx_b"])
    V[:, 160:164] = vec_pc(inp["mla_q_norm"])
    V[:, 164:168] = vec_pc(inp["mla_kv_norm"])
    V[:, 168:256] = vec_pc(inp["conv_b"])
    cw = np.asarray(inp["conv_w"], np.float32).reshape(3, -1)
    V[:, 256:344] = vec_pc(cw[0])
    V[:, 344:432] = vec_pc(cw[1])
    V[:, 432:520] = vec_pc(cw[2])
    V[0:64, 520] = mu[3072:3136]
    V[0:64, 521] = mu[3136:3200]
    V[0:128, 522] = mu[3200:3328]
    V[0:32, 523] = mu[3328:3360]
    invf = (10000.0 ** (-np.arange(0, 64, 2, dtype=np.float32) / 64)).astype(np.float32)
    V[0:32, 524] = invf
    V[32:64, 524] = invf
    return V


def build_program(debug=()):
    nc = bass.Bass("TRN2", target_bir_lowering=False)
    DBG.clear()
    _RANK.clear()
    for k_ in debug:
        if "=" in k_:
            DBG[k_.split("=")[0]] = int(k_.split("=")[1])
        else:
            DBG[k_] = 1

    def din(name, shape, dt=F32):
        return nc.dram_tensor(name, list(shape), dt, kind="ExternalInput").ap()

    def dscr(name, shape, dt):
        if name in debug:
            return nc.dram_tensor(name, list(shape), dt, kind="ExternalOutput").ap()
        return nc.dram_tensor(name, list(shape), dt).ap()

    x = din("x", [T, D])
    xw = din("xw", [1152, D])
    gain_bc = din("gain_bc", [128, D])
    p_in = din("p", [1024, 256])
    pos = din("pos", [1, T], I32)
    pos_w = din("pos_w", [1, TW], I32)
    qk_tab_d = din("qk_tab", [128, 3 * 342])
    d_tab_d = din("d_tab", [128, 32])
    vecs = din("vecs", [128, NV])
    w_in = din("w_in", [D, 8544])
    rw_w2 = din("rw_w2", [64, 1024])
    rw_a2 = din("rw_a2", [64, 1024])
    rw_g2 = din("rw_g2", [160, 1024])
    w_q_up = din("w_q_up", [512, 1536])
    w_kv_up = din("w_kv_up", [512, 2048])
    w_brw = din("w_brw", [1024, D])
    w_bmla = din("w_bmla", [1024, D])
    w_out = din("w_out", [D, D])
    w_up = din("w_up", [D, 2 * DFF])
    w_down = din("w_down", [DFF, D])
    w_ple = din("w_ple", [256, D])
    w_pg = din("w_pg", [D, D])
    out = nc.dram_tensor("out", [1024, D], F32, kind="ExternalOutput").ap()

    uT_full = dscr("uT", [D, T + 2], BF16)
    yrwT_full = dscr("yrwT", [1024, T + 2], BF16)
    cqnT_full = dscr("cqnT", [512, T + 2], BF16)
    uT = uT_full[:, 2:]
    yrwT = yrwT_full[:, 2:]
    cqnT = cqnT_full[:, 2:]
    zrwT = dscr("zrwT", [3360, T], F32)
    zmlaT = dscr("zmlaT", [1088, T], F32)
    ckvnT = dscr("ckvnT", [512, T], BF16)
    hwT = dscr("hwT", [D, 1152], F32)
    uwT = dscr("uwT", [D, TW], BF16)
    ywT = dscr("ywT", [1024, TW], BF16)
    cqwT = dscr("cqwT", [512, TW], BF16)
    ymlaT = dscr("ymlaT", [1024, TW], BF16)
    pT = dscr("pT", [256, 1024], F32)
    gT = dscr("gT", [4096, TW], BF16)
    mT = dscr("mT", [D, TW], BF16)
    fT = dscr("fT", [D, TW], F32)
    ffT = dscr("ffT", [DFF, TW], BF16)
    qk_tab = qk_tab_d
    d_tab = d_tab_d

    with contextlib.ExitStack() as st0:
        P = Prog(nc, st0)
        V = P.sb("V", [128, NV], F32)
        P.dma("sp", V[:], vecs, writes=["V"], key="V")
        ident = P.sb("ident", [128, 128], F32)
        ident_bf = P.sb("ident_bf", [128, 128], BF16)
        ones_bf = P.sb("ones_bf", [128, 128], BF16)
        bones_bf = P.sb("bones_bf", [128, 128], BF16)
        maskP = P.sb("maskP", [128, 2, 128], F32)
        maskL = P.sb("maskL", [128, 64], F32)
        eye = P.sb("eye", [128, 64], F32)
        rmask = P.sb("rmask", [128, TB], F32)
        omka = P.sb("omka", [128, 8], F32)
        su = P.sb("su", [128, 64], F32)
        ui = P.sb("ui", [128, 64], F32)

        def pool(fn, reads=(), writes=()):
            P.op("pool", fn, reads, writes)

        pool(lambda e: e.memset(ident[:], 1.0), writes=["ident"])
        pool(lambda e: e.affine_select(out=ident[:], in_=ident[:], pattern=[[-1, 128]], compare_op=ALU.is_equal,
                                       fill=0.0, base=0, channel_multiplier=1), reads=["ident"], writes=["ident"])
        pool(lambda e: e.tensor_copy(out=ident_bf[:], in_=ident[:]), reads=["ident"], writes=["ident_bf"])
        pool(lambda e: e.memset(ones_bf[:], 1.0), writes=["ones_bf"])
        pool(lambda e: e.memset(bones_bf[:], 0.0), writes=["bones_bf"])
        pool(lambda e: e.memset(bones_bf[0:64, 0:64], 1.0), reads=["bones_bf"], writes=["bones_bf"])
        pool(lambda e: e.memset(bones_bf[64:128, 64:128], 1.0), reads=["bones_bf"], writes=["bones_bf"])
        for (tl, pat, cm, b0, b1, op_) in ((su, 1, -1, -1, -1, ALU.is_ge), (ui, 1, -1, 0, 0, ALU.is_ge),
                                           (maskL, -1, 1, -1, -1, ALU.is_ge), (eye, -1, 1, 0, 0, ALU.is_equal)):
            pool(lambda e, tl=tl: e.memset(tl[:], 1.0), writes=["msk"])
            for h, bb in ((0, b0), (1, b1)):
                sl = slice(64 * h, 64 * h + 64)
                pool(lambda e, tl=tl, sl=sl, pat=pat, cm=cm, bb=bb, op_=op_: e.affine_select(
                    out=tl[sl, :], in_=tl[sl, :], pattern=[[pat, 64]], compare_op=op_, fill=0.0, base=bb,
                    channel_multiplier=cm), reads=["msk"], writes=["msk"])
        for xx in range(2):
            pool(lambda e, xx=xx: e.tensor_copy(out=maskP[:, xx, 0:64], in_=su[:]), reads=["msk"], writes=["msk"])
            pool(lambda e, xx=xx: e.tensor_copy(out=maskP[:, xx, 64:128], in_=ui[:]), reads=["msk"], writes=["msk"])
        pool(lambda e: e.memset(rmask[:], 1.0), reads=["msk"], writes=["msk"])
        for c in range(8):
            pool(lambda e, c=c: e.memset(rmask[:, c * 64:c * 64 + 1], 0.0), reads=["msk"], writes=["msk"])
        P.op("dve", lambda e: e.tensor_scalar(out=omka[:], in0=V[:, 128:136], scalar1=-1.0, scalar2=1.0, op0=ALU.mult,
                                              op1=ALU.add), reads=["V", "msk"], writes=["omka"])
        CONST = ["V", "msk", "ident", "ident_bf", "ones_bf", "bones_bf", "omka"]

        rr = {"evac": 0}

        class _Stop(Exception):
            pass

        def stop_if(name):
            if name in debug:
                raise _Stop()

        def evac_copy(out_ap, in_ap, reads, writes, scale=None):
            rr["evac"] += 1
            if scale is not None or rr["evac"] % 2 == 0:
                P.op("act", lambda e: e.activation(out=out_ap, in_=in_ap, func=AF.Copy,
                                                   scale=(1.0 if scale is None else scale)), reads, writes)
            else:
                P.op("dve", lambda e: e.tensor_copy(out=out_ap, in_=in_ap), reads, writes)

        def transpose_stage(tag, src, dst, R, C, src_res, dst_res):
            with contextlib.ExitStack() as st:
                prev_stack = P.stack
                P.stack = st
                nb = 2
                tin = [P.sb(f"{tag}_in{i}", [128, C], F32) for i in range(nb)]
                tout = [P.sb(f"{tag}_out{i}", [128, C // 128, 128], F32) for i in range(nb)]
                pst = [P.ps(f"{tag}_ps{i}", [128, 4, 128], F32) for i in range(2)]
                k = 0
                for r in range(R // 128):
                    b = r % nb
                    P.dma("sp", tin[b][:], src[r * 128:(r + 1) * 128, :], reads=[src_res], writes=[f"TR_in{b}"],
                          key=f"TR_in{b}")
                    for c4 in range(0, C // 128, 4):
                        pb = k % 2
                        k += 1
                        n4 = min(4, C // 128 - c4)
                        for i in range(n4):
                            c = c4 + i
                            P.op("pe", lambda e, pb=pb, i=i, c=c, b=b: e.transpose(
                                out=pst[pb][:, i, :], in_=tin[b][:, c * 128:(c + 1) * 128], identity=ident[:]),
                                reads=[f"TR_in{b}", "ident"], writes=[f"TR_ps{pb}"])
                        evac_copy(tout[b][:, c4:c4 + n4, :], pst[pb][:, 0:n4, :], [f"TR_ps{pb}"], [f"TR_out{b}"])
                    P.dma("sp", dst[:, r * 128:(r + 1) * 128].rearrange("(c p) t -> p c t", p=128), tout[b][:],
                          reads=[f"TR_out{b}"], writes=[dst_res], key=f"TR_st{b}")
                P.barrier()
                P.stack = prev_stack

        def rmsnorm_stage(tag, src, F, gcol, src_res, dst=None, dst_res=None, resid=None, resid_res=None, blocks=FB):
            FC = F // 128
            with contextlib.ExitStack() as st:
                prev_stack = P.stack
                P.stack = st
                s = P.sb(f"{tag}_s", [128, FC, TB], F32)
                sq = P.sb(f"{tag}_sq", [128, FC, TB], BF16)
                rstd = P.sb(f"{tag}_rstd", [128, TB], F32)
                ps = P.ps(f"{tag}_ps", [128, TB], F32)
                if resid is None:
                    o = P.sb(f"{tag}_o", [128, FC, TB], BF16)
                else:
                    o = P.sb(f"{tag}_o", [128, FC, TB], F32)
                    hh = P.sb(f"{tag}_h", [128, FC, TB], F32)
                for j, (b0, bs) in enumerate(blocks):
                    cs = slice(b0, b0 + bs)
                    P.dma("sp", s[:, :, 0:bs], src[:, cs].rearrange("(c p) t -> p c t", p=128), reads=[src_res],
                          writes=[f"RN_s"], key=f"RN_s")
                    if resid is not None:
                        P.dma("sp", hh[:, :, 0:bs], resid[:, cs].rearrange("(c p) t -> p c t", p=128), reads=[resid_res],
                              writes=[f"RN_h"], key=f"RN_h")
                    P.op("act", lambda e: e.activation(out=sq[:, :, 0:bs], in_=s[:, :, 0:bs], func=AF.Square), reads=[f"RN_s"],
                         writes=[f"RN_sq"])
                    for c in range(FC):
                        P.op("pe", lambda e, c=c: e.matmul(ps[:, 0:bs], lhsT=ones_bf[:], rhs=sq[:, c, 0:bs], start=(c == 0),
                                                           stop=(c == FC - 1)),
                             reads=[f"RN_sq", "ones_bf"], writes=[f"RN_ps"])
                    P.op("act", lambda e: e.activation(out=rstd[:, 0:bs], in_=ps[:, 0:bs], func=AF.Sqrt, bias=EPS, scale=1.0 / F),
                         reads=[f"RN_ps"], writes=[f"RN_rstd"])
                    P.op("dve", lambda e: e.reciprocal(out=rstd[:, 0:bs], in_=rstd[:, 0:bs]), reads=[f"RN_rstd"],
                         writes=[f"RN_rstd"])
                    for c in range(FC):
                        eng = "pool"
                        P.op("dve", lambda e, c=c: e.scalar_tensor_tensor(
                            out=o[:, c, 0:bs], in0=s[:, c, 0:bs], scalar=V[:, gcol + c:gcol + c + 1], in1=rstd[:, 0:bs],
                            op0=ALU.mult, op1=ALU.mult), reads=[f"RN_s", f"RN_rstd", "V"], writes=[f"RN_o{c}"])
                        if resid is not None:
                            P.op(eng, lambda e, c=c: e.tensor_tensor(out=o[:, c, 0:bs], in0=o[:, c, 0:bs], in1=hh[:, c, 0:bs],
                                                                     op=ALU.add),
                                 reads=[f"RN_o{c}", f"RN_h"], writes=[f"RN_o{c}"])
                    allo = [f"RN_o{c}" for c in range(FC)]
                    if resid is None:
                        P.dma("sp", dst[:, cs].rearrange("(c p) t -> p c t", p=128), o[:, :, 0:bs], reads=allo,
                              writes=[dst_res], key=f"RN_st")
                    else:
                        P.dma("sp", resid[:, cs].rearrange("(c p) t -> p c t", p=128), o[:, :, 0:bs], reads=allo,
                              writes=[resid_res], key=f"RN_st")
                P.barrier()
                P.stack = prev_stack

        def linear_stage(tag, srcs, groups, epi, wbuf_cols, nps=2, blocks=FB):
            with contextlib.ExitStack() as st:
                prev_stack = P.stack
                P.stack = st
                wsb = []
                sbs = []
                for si, (src, K, sdt, sres, W) in enumerate(srcs):
                    KC = (K + 127) // 128
                    wsb.append([P.sb(f"{tag}_w{si}_{i}", [128, KC, wbuf_cols], BF16) for i in range(2)])
                    sbs.append([P.sb(f"{tag}_x{si}_{i}", [128, KC, TB], BF16) for i in range(2)])
                pss = [P.ps(f"{tag}_ps{i}", [128, TB], F32) for i in range(nps)]
                state = {"ps": 0, "xb": 0}
                def grp_layout(grp):
                    offs = []
                    off = 0
                    for (c0, w) in grp:
                        offs.append(off)
                        off += w
                    assert off <= wbuf_cols
                    ranges = []
                    for (c0, w), o_ in zip(grp, offs):
                        if ranges and ranges[-1][0] + ranges[-1][1] == c0 and ranges[-1][2] + ranges[-1][1] == o_:
                            ranges[-1][1] += w
                        else:
                            ranges.append([c0, w, o_])
                    return offs, ranges

                def load_w(gi):
                    wb = gi % 2
                    offs, ranges = grp_layout(groups[gi])
                    for si, (src, K, sdt, sres, W) in enumerate(srcs):
                        KC = (K + 127) // 128
                        for (c0, w, o_) in ranges:
                            for kc in range(KC):
                                kr = min(128, K - kc * 128)
                                P.dma("pool", wsb[si][wb][0:kr, kc, o_:o_ + w], W[kc * 128:kc * 128 + kr, c0:c0 + w],
                                      writes=[f"LN_w{si}_{wb}"], key=f"LN_w{si}_{wb}")

                load_w(0)
                for gi, grp in enumerate(groups):
                    wb = gi % 2
                    offs, ranges = grp_layout(grp)
                    if gi + 1 < len(groups):
                        load_w(gi + 1)
                    for j, (b0, bs) in enumerate(blocks):
                        cs = slice(b0, b0 + bs)
                        xb = state["xb"] % 2
                        state["xb"] += 1
                        for si, (src, K, sdt, sres, W) in enumerate(srcs):
                            KC = (K + 127) // 128
                            q = "sp" if sdt == BF16 else "pool"
                            if K % 128 == 0 and q == "sp":
                                P.dma(q, sbs[si][xb][:, :, 0:bs], src[:, cs].rearrange("(k p) t -> p k t", p=128), reads=[sres],
                                      writes=[f"LN_x{si}_{xb}"], key=f"LN_x{si}_{xb}")
                            else:
                                for kc in range(KC):
                                    kr = min(128, K - kc * 128)
                                    P.dma(q, sbs[si][xb][0:kr, kc, 0:bs], src[kc * 128:kc * 128 + kr, cs], reads=[sres],
                                          writes=[f"LN_x{si}_{xb}"], key=f"LN_x{si}_{xb}")
                        for ci, ((c0, w), o_) in enumerate(zip(grp, offs)):
                            pi = state["ps"] % nps
                            state["ps"] += 1
                            nmm = sum((K + 127) // 128 for (_, K, _, _, _) in srcs)
                            i = 0
                            for si, (src, K, sdt, sres, W) in enumerate(srcs):
                                KC = (K + 127) // 128
                                for kc in range(KC):
                                    kr = min(128, K - kc * 128)
                                    P.op("pe", lambda e, pi=pi, si=si, wb=wb, kc=kc, kr=kr, o_=o_, w=w, xb=xb, i=i, nmm=nmm:
                                         e.matmul(pss[pi][0:w, 0:bs], lhsT=wsb[si][wb][0:kr, kc, o_:o_ + w],
                                                  rhs=sbs[si][xb][0:kr, kc, 0:bs], start=(i == 0), stop=(i == nmm - 1)),
                                         reads=[f"LN_w{si}_{wb}", f"LN_x{si}_{xb}"], writes=[f"LN_ps{pi}"])
                                    i += 1
                            epi(gi, ci, (c0, w), j, pss[pi], f"LN_ps{pi}", (b0, bs))
                P.barrier()
                P.stack = prev_stack

        def store_epi(tag, dst, dst_res, odt, func=AF.Copy, row_of=None, nbuf=3):
            bufs = [P.sb(f"{tag}_eo{i}", [128, TB], odt) for i in range(nbuf)]
            stt = {"i": 0}

            def epi(gi, ci, cw, j, ps, ps_res, blk):
                c0, w = cw
                b0_, bs = blk
                b = stt["i"] % nbuf
                stt["i"] += 1
                r0 = c0 if row_of is None else row_of(c0)
                if func == AF.Copy:
                    evac_copy(bufs[b][0:w, 0:bs], ps[0:w, 0:bs], [ps_res], [f"EO_eo{b}"])
                else:
                    P.op("act", lambda e: e.activation(out=bufs[b][0:w, 0:bs], in_=ps[0:w, 0:bs], func=func), [ps_res],
                         [f"EO_eo{b}"])
                P.dma("act", dst[r0:r0 + w, b0_:b0_ + bs], bufs[b][0:w, 0:bs], reads=[f"EO_eo{b}"],
                      writes=[dst_res], key=f"EO_eo{b}")

            return epi

        def chunks(c0, n, w=128):
            res = []
            c = c0
            while c < c0 + n:
                ww = min(w, c0 + n - c)
                res.append((c, ww))
                c += ww
            return res

        def grouped(ch, n):
            return [ch[i:i + n] for i in range(0, len(ch), n)]

        def norm_transpose_stage():
            with contextlib.ExitStack() as st:
                prev_stack = P.stack
                P.stack = st
                gbc = P.sb("nt_gbc", [128, D], F32)
                P.dma("sp", gbc[:], gain_bc, writes=["nt_gbc"], key="nt_gbc")
                xt = [P.sb(f"nt_x{i}", [128, D], F32) for i in range(3)]
                junk = P.sb("nt_junk", [128, D], BF16)
                ub = [P.sb(f"nt_ub{i}", [128, D], BF16) for i in range(2)]
                ss = [P.sb(f"nt_ss{i}", [128, 2], F32) for i in range(2)]
                uo = [P.sb(f"nt_uo{i}", [128, 16, 512], BF16) for i in range(2)]
                pst = [P.ps(f"nt_ps{i}", [128, 8, 128], BF16) for i in range(2)]
                k = 0
                for g in range(T // 512):
                    ob = g % 2
                    for t4 in range(4):
                        r = g * 4 + t4
                        b = r % 3
                        b2 = r % 2
                        P.dma("sp", xt[b][:], x[r * 128:(r + 1) * 128, :], reads=["dram:x"], writes=[f"nt_x{b}"], key=f"nt_x{b}")
                        P.op("dve", lambda e: e.memset(ss[b2][:, 0:1], 0.0), [f"nt_ss{b2}"], [f"nt_ss{b2}"])
                        P.op("act", lambda e: e.activation(out=junk[:], in_=xt[b][:], func=AF.Square, accum_out=ss[b2][:, 0:1]),
                             [f"nt_x{b}", f"nt_ss{b2}"], ["nt_junk", f"nt_ss{b2}"])
                        P.op("act", lambda e: e.activation(out=ss[b2][:, 1:2], in_=ss[b2][:, 0:1], func=AF.Sqrt, bias=EPS,
                                                           scale=1.0 / D), [f"nt_ss{b2}"], [f"nt_ss{b2}"])
                        P.op("dve", lambda e: e.reciprocal(out=ss[b2][:, 1:2], in_=ss[b2][:, 1:2]), [f"nt_ss{b2}"], [f"nt_ss{b2}"])
                        P.op("dve", lambda e: e.scalar_tensor_tensor(out=ub[b2][:], in0=xt[b][:], scalar=ss[b2][:, 1:2], in1=gbc[:],
                                                                     op0=ALU.mult, op1=ALU.mult),
                             [f"nt_x{b}", f"nt_ss{b2}", "nt_gbc"], [f"nt_ub{b2}"])
                        for half in range(2):
                            pb = k % 2
                            k += 1
                            for i in range(8):
                                c = half * 8 + i
                                P.op("pe", lambda e, pb=pb, i=i, c=c: e.transpose(out=pst[pb][:, i, :],
                                                                               in_=ub[b2][:, c * 128:(c + 1) * 128],
                                                                               identity=ident_bf[:]),
                                     [f"nt_ub{b2}", "ident_bf"], [f"nt_ps{pb}"])
                            evac_copy(uo[ob][:, half * 8:(half + 1) * 8, t4 * 128:(t4 + 1) * 128], pst[pb][:],
                                      [f"nt_ps{pb}"], [f"nt_uo{ob}"])
                    P.dma("act", uT[:, g * 512:(g + 1) * 512].rearrange("(c p) t -> p c t", p=128), uo[ob][:],
                          reads=[f"nt_uo{ob}"], writes=["dram:uT"], key=f"nt_uo{ob}")
                P.barrier()
                P.stack = prev_stack

        def main_seq():
            if "skip_s01" in debug:
                rwkv_stage(P, nc, st0, V, zrwT, yrwT, rw_w2, rw_a2, rw_g2, ident_bf, bones_bf, maskP, maskL, eye, rmask, omka,
                           evac_copy)
                stop_if("skip_s01")
            if "skip_s01m" in debug:
                rmsnorm_stage("nq", zmlaT[0:512, :], 512, 160, "dram:zmlaT", dst=cqnT, dst_res="dram:cqnT")
                rmsnorm_stage("nk", zmlaT[512:1024, :], 512, 164, "dram:zmlaT", dst=ckvnT, dst_res="dram:ckvnT")
                mla_stage(P, nc, st0, V, pos, pos_w, qk_tab, d_tab, zmlaT, cqnT_full, ckvnT, w_q_up, w_kv_up, ymlaT, ones_bf,
                          evac_copy)
                stop_if("skip_s01m")
            def win_copy(dst, src_full, rows, res_src, res_dst, key):
                step = 512
                for r0 in range(0, rows, step):
                    def fn(e, r0=r0):
                        off = get_rank(e)
                        return e.dma_start(out=dst[r0:r0 + step, :], in_=src_full[r0:r0 + step, bass.ds(off, TW)])
                    P.custom("pool", fn, reads=[res_src], writes=[res_dst], key=key, raw=True)

            zt_f = P.sb("zpad_f", [128, 16, 2], F32)
            zt_b = P.sb("zpad_b", [128, 16, 2], BF16)
            P.op("pool", lambda e: e.memset(zt_f[:], 0.0), [], ["zpad"])
            P.op("pool", lambda e: e.memset(zt_b[:], 0.0), [], ["zpad"])
            P.dma("sp", uT_full[:, 0:2].rearrange("(c p) t -> p c t", p=128), zt_b[:], reads=["zpad"], writes=["dram:uT"], key="zp")
            P.dma("sp", yrwT_full[:, 0:2].rearrange("(c p) t -> p c t", p=128), zt_b[:, 0:8, :], reads=["zpad"],
                  writes=["dram:yrwT"], key="zp")
            P.dma("sp", cqnT_full[:, 0:2].rearrange("(c p) t -> p c t", p=128), zt_b[:, 0:4, :], reads=["zpad"],
                  writes=["dram:cqnT"], key="zp")
            norm_transpose_stage()
            stop_if("stop0b")
            with contextlib.ExitStack() as stx:
                P.stack = stx
                e1 = store_epi("l1a", zrwT, "dram:zrwT", F32)
                linear_stage("l1a", [(uT, D, BF16, "dram:uT", w_in)], grouped(chunks(0, 3360), 8), e1, 1024)
                P.barrier()
                P.stack = st0
            with contextlib.ExitStack() as stx:
                P.stack = stx
                e2 = store_epi("l1b", zmlaT, "dram:zmlaT", F32, row_of=lambda c0: c0 - 3360)
                linear_stage("l1b", [(uT, D, BF16, "dram:uT", w_in)], grouped(chunks(3360, 1088), 9), e2, 1152)
                P.barrier()
                P.stack = st0
            stop_if("stop1")
            rwkv_stage(P, nc, st0, V, zrwT, yrwT, rw_w2, rw_a2, rw_g2, ident_bf, bones_bf, maskP, maskL, eye, rmask, omka,
                       evac_copy)
            stop_if("stop2")
            rmsnorm_stage("nq", zmlaT[0:512, :], 512, 160, "dram:zmlaT", dst=cqnT, dst_res="dram:cqnT")
            rmsnorm_stage("nk", zmlaT[512:1024, :], 512, 164, "dram:zmlaT", dst=ckvnT, dst_res="dram:ckvnT")
            win_copy(cqwT, cqnT_full, 512, "dram:cqnT", "dram:cqwT", "wc3")
            P.barrier()
            mla_stage(P, nc, st0, V, pos, pos_w, qk_tab, d_tab, zmlaT, cqwT, ckvnT, w_q_up, w_kv_up, ymlaT, ones_bf,
                      evac_copy)
            stop_if("stop3")
            transpose_stage("txw", xw, hwT, 1152, D, "dram:xw", "dram:hwT")
            win_copy(uwT, uT_full, D, "dram:uT", "dram:uwT", "wc1")
            win_copy(ywT, yrwT_full, 1024, "dram:yrwT", "dram:ywT", "wc2")
            P.barrier()
            transpose_stage("tp", p_in, pT, 1024, 256, "dram:p", "dram:pT")
            with contextlib.ExitStack() as stx:
                P.stack = stx
                e3 = store_epi("l1c", gT, "dram:gT", BF16, func=AF.Sigmoid, row_of=lambda c0: c0 - 4448)
                linear_stage("l1c", [(uwT, D, BF16, "dram:uwT", w_in)], grouped(chunks(4448, 4096), 8), e3, 1024, blocks=WB)
                P.barrier()
                P.stack = st0
            with contextlib.ExitStack() as stx:
                P.stack = stx
                gA = [P.sb(f"s4_gA{i}", [128, TB], BF16) for i in range(2)]
                gB = [P.sb(f"s4_gB{i}", [128, TB], BF16) for i in range(2)]
                t1 = [P.sb(f"s4_t1{i}", [128, TB], F32) for i in range(2)]
                mo = [P.sb(f"s4_mo{i}", [128, TB], BF16) for i in range(2)]
                t2 = [P.sb(f"s4_t2{i}", [128, TB], F32) for i in range(2)]
                stt = {"i": 0}

                def epiA(gi, ci, cw, j, ps, ps_res, blk):
                    c0, w = cw
                    b0_, bs = blk
                    b = stt["i"] % 2
                    stt["i"] += 1
                    P.dma("sp", gA[b][:, 0:bs], gT[c0:c0 + 128, b0_:b0_ + bs], reads=["dram:gT"], writes=[f"s4_gA{b}"],
                          key=f"s4_gA{b}")
                    P.op("dve", lambda e: e.tensor_tensor(out=t1[b][:, 0:bs], in0=ps[:, 0:bs], in1=gA[b][:, 0:bs], op=ALU.mult),
                         [ps_res, f"s4_gA{b}"], [f"s4_t1{b}"])
                    P.dma("act", fT[c0:c0 + 128, b0_:b0_ + bs], t1[b][:, 0:bs], reads=[f"s4_t1{b}"], writes=["dram:fT"],
                          key=f"s4_t1{b}")

                linear_stage("l4a", [(ywT, 1024, BF16, "dram:ywT", w_brw)], grouped(chunks(0, D), 8), epiA, 1024, blocks=WB)

                def epiB(gi, ci, cw, j, ps, ps_res, blk):
                    c0, w = cw
                    b0_, bs = blk
                    b = stt["i"] % 2
                    stt["i"] += 1
                    P.dma("sp", gB[b][:, 0:bs], gT[2048 + c0:2048 + c0 + 128, b0_:b0_ + bs], reads=["dram:gT"],
                          writes=[f"s4_gB{b}"], key=f"s4_gB{b}")
                    P.dma("sp", t1[b][:, 0:bs], fT[c0:c0 + 128, b0_:b0_ + bs], reads=["dram:fT"], writes=[f"s4_t1{b}"],
                          key=f"s4_t1l{b}")
                    P.op("dve", lambda e: e.tensor_tensor(out=t2[b][:, 0:bs], in0=ps[:, 0:bs], in1=gB[b][:, 0:bs], op=ALU.mult),
                         [ps_res, f"s4_gB{b}"], [f"s4_t2{b}"])
                    P.op("dve", lambda e: e.tensor_tensor(out=mo[b][:, 0:bs], in0=t2[b][:, 0:bs], in1=t1[b][:, 0:bs], op=ALU.add),
                         [f"s4_t2{b}", f"s4_t1{b}"], [f"s4_mo{b}"])
                    P.dma("act", mT[c0:c0 + 128, b0_:b0_ + bs], mo[b][:, 0:bs], reads=[f"s4_mo{b}"], writes=["dram:mT"],
                          key=f"s4_mo{b}")

                linear_stage("l4b", [(ymlaT, 1024, BF16, "dram:ymlaT", w_bmla)], grouped(chunks(0, D), 8), epiB, 1024, blocks=WB)
                P.barrier()
                P.stack = st0
            stop_if("stop4")
            with contextlib.ExitStack() as stx:
                P.stack = stx
                e5 = store_epi("l5", fT, "dram:fT", F32)
                linear_stage("l5", [(mT, D, BF16, "dram:mT", w_out)], grouped(chunks(0, D), 8), e5, 1024, blocks=WB)
                P.barrier()
                P.stack = st0
            rmsnorm_stage("n5", fT, D, 16, "dram:fT", resid=hwT, resid_res="dram:hwT", blocks=WB)
            stop_if("stop5")
            rmsnorm_stage("n6", hwT, D, 32, "dram:hwT", dst=uwT, dst_res="dram:uwT", blocks=WB)
            with contextlib.ExitStack() as stx:
                P.stack = stx
                NCH = 4
                ub = [[P.sb(f"s6_u{h}_{i}", [128, TB + 2], F32) for i in range(NCH)] for h in range(2)]
                cv = [P.sb(f"s6_cv{h}", [128, TB], F32) for h in range(2)]
                tq = P.sb("s6_tq", [128, TB], F32)
                fo = [P.sb(f"s6_fo{i}", [128, TB], BF16) for i in range(2)]
                stt = {"i": 0}

                def epi6(gi, ci, cw, j, ps, ps_res, blk):
                    c0, w = cw
                    b0_, bs = blk
                    half = 0 if ci < NCH else 1
                    cc = ci % NCH
                    u = ub[half][cc]
                    ur = f"s6_u{half}_{cc}"
                    if j == 0:
                        P.op("dve", lambda e: e.memset(u[:, 0:2], 0.0), [ur], [ur])
                    else:
                        P.op("dve", lambda e: e.tensor_copy(out=u[:, 0:2], in_=u[:, bs:bs + 2]), [ur], [ur])
                    evac_copy(u[:, 2:bs + 2], ps[:, 0:bs], [ps_res, ur], [ur])
                    if half == 1:
                        chn = c0 // 128
                        chg = chn - 44
                        for hh, ch in ((0, chg), (1, chn)):
                            uu = ub[hh][cc]
                            uur = f"s6_u{hh}_{cc}"
                            P.op("dve", lambda e, uu=uu, ch=ch, hh=hh: e.tensor_scalar(
                                out=cv[hh][:, 0:bs], in0=uu[:, 0:bs], scalar1=V[:, 256 + ch:257 + ch], scalar2=V[:, 168 + ch:169 + ch],
                                op0=ALU.mult, op1=ALU.add), [uur, "V"], [f"s6_cv{hh}"])
                            P.op("dve", lambda e, uu=uu, ch=ch, hh=hh: e.scalar_tensor_tensor(
                                out=cv[hh][:, 0:bs], in0=uu[:, 1:bs + 1], scalar=V[:, 344 + ch:345 + ch], in1=cv[hh][:, 0:bs],
                                op0=ALU.mult, op1=ALU.add), [uur, "V", f"s6_cv{hh}"], [f"s6_cv{hh}"])
                            P.op("dve", lambda e, uu=uu, ch=ch, hh=hh: e.scalar_tensor_tensor(
                                out=cv[hh][:, 0:bs], in0=uu[:, 2:bs + 2], scalar=V[:, 432 + ch:433 + ch], in1=cv[hh][:, 0:bs],
                                op0=ALU.mult, op1=ALU.add), [uur, "V", f"s6_cv{hh}"], [f"s6_cv{hh}"])
                        P.op("dve", lambda e: e.tensor_tensor(out=tq[:, 0:bs], in0=cv[0][:, 0:bs], in1=cv[0][:, 0:bs], op=ALU.mult),
                             ["s6_cv0"], ["s6_tq"])
                        P.op("dve", lambda e: e.tensor_scalar(out=tq[:, 0:bs], in0=tq[:, 0:bs], scalar1=0.044715, scalar2=1.0,
                                                               op0=ALU.mult, op1=ALU.add), ["s6_tq"], ["s6_tq"])
                        P.op("dve", lambda e: e.tensor_tensor(out=tq[:, 0:bs], in0=tq[:, 0:bs], in1=cv[0][:, 0:bs], op=ALU.mult),
                             ["s6_tq", "s6_cv0"], ["s6_tq"])
                        P.op("act", lambda e: e.activation(out=tq[:, 0:bs], in_=tq[:, 0:bs], func=AF.Sigmoid, scale=1.5957691216),
                             ["s6_tq"], ["s6_tq"])
                        P.op("dve", lambda e: e.tensor_tensor(out=tq[:, 0:bs], in0=tq[:, 0:bs], in1=cv[0][:, 0:bs], op=ALU.mult),
                             ["s6_tq", "s6_cv0"], ["s6_tq"])
                        b = stt["i"] % 2
                        stt["i"] += 1
                        P.op("dve", lambda e: e.tensor_tensor(out=fo[b][:, 0:bs], in0=tq[:, 0:bs], in1=cv[1][:, 0:bs], op=ALU.mult),
                             ["s6_tq", "s6_cv1"], [f"s6_fo{b}"])
                        P.dma("act", ffT[chg * 128:(chg + 1) * 128, b0_:b0_ + bs], fo[b][:, 0:bs], reads=[f"s6_fo{b}"],
                              writes=["dram:ffT"], key=f"s6_fo{b}")

                grps = []
                for g0 in range(0, 44, NCH):
                    grps.append(chunks(g0 * 128, NCH * 128) + chunks(DFF + g0 * 128, NCH * 128))
                linear_stage("l6", [(uwT, D, BF16, "dram:uwT", w_up)], grps, epi6, 2 * NCH * 128, blocks=WB)
                P.barrier()
                P.stack = st0
            with contextlib.ExitStack() as stx:
                P.stack = stx
                e7 = store_epi("l7", fT, "dram:fT", F32)
                linear_stage("l7", [(ffT, DFF, BF16, "dram:ffT", w_down)], grouped(chunks(0, D), 4), e7, 512, blocks=WB)
                P.barrier()
                P.stack = st0
            rmsnorm_stage("n7", fT, D, 48, "dram:fT", resid=hwT, resid_res="dram:hwT", blocks=WB)
            stop_if("stop7")
            PB = [(0, 512), (512, 512)]
            with contextlib.ExitStack() as stx:
                P.stack = stx
                e8 = store_epi("l8a", mT, "dram:mT", BF16, func=AF.Sigmoid)
                linear_stage("l8a", [(hwT[:, 2:TW], D, F32, "dram:hwT", w_pg)], grouped(chunks(0, D), 8), e8, 1024, blocks=PB)
                gl = [P.sb(f"s8_g{i}", [128, TB], BF16) for i in range(2)]
                eo = [P.sb(f"s8_eo{i}", [128, TB], F32) for i in range(2)]
                stt = {"i": 0}

                def epi8(gi, ci, cw, j, ps, ps_res, blk):
                    c0, w = cw
                    b0_, bs = blk
                    b = stt["i"] % 2
                    stt["i"] += 1
                    P.dma("sp", gl[b][:, 0:bs], mT[c0:c0 + 128, b0_:b0_ + bs], reads=["dram:mT"], writes=[f"s8_g{b}"],
                          key=f"s8_g{b}")
                    P.op("dve", lambda e: e.tensor_tensor(out=eo[b][:, 0:bs], in0=ps[:, 0:bs], in1=gl[b][:, 0:bs], op=ALU.mult),
                         [ps_res, f"s8_g{b}"], [f"s8_eo{b}"])
                    P.dma("act", fT[c0:c0 + 128, b0_:b0_ + bs], eo[b][:, 0:bs], reads=[f"s8_eo{b}"], writes=["dram:fT"],
                          key=f"s8_eo{b}")

                linear_stage("l8b", [(pT, 256, F32, "dram:pT", w_ple)], grouped(chunks(0, D), 16), epi8, 2048, blocks=PB)
                P.barrier()
                P.stack = st0
            rmsnorm_stage("n8", fT[:, 0:1024], D, 64, "dram:fT", resid=hwT[:, 2:TW], resid_res="dram:hwT", blocks=PB)
            transpose_stage("to", hwT[:, 2:TW], out, D, 1024, "dram:hwT", "dram:out")

        try:
            main_seq()
        except _Stop:
            P.stack = st0
        P.wait_all("sp", [f"dram:{n}" for n in (["out"] + list(debug)) if not n.startswith("skip") and not n.startswith("stop")])
        P.emit()
    return nc


def rwkv_stage(P, nc, st0, V, zrwT, yrwT, rw_w2, rw_a2, rw_g2, ident_bf, bones_bf, maskP, maskL, eye, rmask, omka,
               evac_copy):
    NEG_E = -math.exp(-0.5)
    with contextlib.ExitStack() as st:
        prev_stack = P.stack
        P.stack = st
        sb = P.sb
        w2b = sb("rk_w2b", [64, 1024], BF16)
        a2b = sb("rk_a2b", [64, 1024], BF16)
        g2b = sb("rk_g2b", [128, 2, 1024], BF16)
        P.dma("pool", w2b[:], rw_w2, writes=["rk_w"], key="rk_w")
        P.dma("pool", a2b[:], rw_a2, writes=["rk_w"], key="rk_w")
        P.dma("pool", g2b[:, 0, :], rw_g2[0:128, :], writes=["rk_w"], key="rk_w")
        P.dma("pool", g2b[0:32, 1, :], rw_g2[128:160, :], writes=["rk_w"], key="rk_w")
        twd = sb("rk_twd", [64, T], BF16)
        ads = sb("rk_ads", [64, T], BF16)
        sg1 = sb("rk_sg1", [128, T], BF16)
        sg2 = sb("rk_sg2", [32, T], BF16)
        zin = sb("rk_zin", [128, TB + 1], F32)
        dd = sb("rk_dd", [128, TB], F32)
        for (row0, nr, mcol, dst, func) in ((3072, 64, 520, twd, AF.Tanh), (3136, 64, 521, ads, AF.Copy),
                                            (3200, 128, 522, sg1, AF.Sigmoid), (3328, 32, 523, sg2, AF.Sigmoid)):
            for j in range(NB):
                if j == 0:
                    P.op("dve", lambda e: e.memset(zin[:, 0:1], 0.0), [], ["rk_zin"])
                    P.dma("sp", zin[0:nr, 1:TB + 1], zrwT[row0:row0 + nr, 0:TB], reads=["dram:zrwT"], writes=["rk_zin"],
                          key="rk_zin")
                else:
                    P.dma("sp", zin[0:nr, :], zrwT[row0:row0 + nr, j * TB - 1:(j + 1) * TB], reads=["dram:zrwT"],
                          writes=["rk_zin"], key="rk_zin")
                P.op("dve", lambda e, nr=nr: e.tensor_tensor(out=dd[0:nr, :], in0=zin[0:nr, 0:TB], in1=zin[0:nr, 1:TB + 1],
                                                             op=ALU.subtract), ["rk_zin"], ["rk_dd"])
                P.op("dve", lambda e, nr=nr, mcol=mcol: e.scalar_tensor_tensor(
                    out=dd[0:nr, :], in0=dd[0:nr, :], scalar=V[0:nr, mcol:mcol + 1], in1=zin[0:nr, 1:TB + 1],
                    op0=ALU.mult, op1=ALU.add), ["rk_dd", "rk_zin", "V"], ["rk_dd"])
                P.op("act", lambda e, nr=nr, dst=dst, func=func, j=j: e.activation(
                    out=dst[0:nr, j * TB:(j + 1) * TB], in_=dd[0:nr, :], func=func), ["rk_dd"], ["rk_lin"])
        zt = {n: sb(f"rk_z{n}", [128, TB + 1], F32) for n in "rkv"}
        xs = {n: sb(f"rk_s{n}", [128, TB], F32) for n in "rkv"}
        tA = sb("rk_tA", [128, TB], F32)
        tB = sb("rk_tB", [128, TB], F32)
        tC = sb("rk_tC", [128, TB], F32)
        av = sb("rk_a", [128, TB], F32)
        kkn = sb("rk_kkn", [128, TB], F32)
        k2 = sb("rk_k2", [128, TB], F32)
        logw = sb("rk_logw", [128, TB], F32)
        cum = sb("rk_cum", [128, TB], F32)
        ginv = sb("rk_ginv", [128, TB], F32)
        gprev = sb("rk_gprev", [128, TB], F32)
        tbf = sb("rk_tbf", [128, TB], BF16)
        vbf2 = [sb(f"rk_vbf{i}", [128, TB], BF16) for i in range(2)]
        BK2 = [sb(f"rk_BK{i}", [128, 8, 2, 64], BF16) for i in range(2)]
        gam2 = [sb(f"rk_gam{i}", [128, TB], F32) for i in range(3)]
        gv2 = [sb(f"rk_g{i}", [128, TB], F32) for i in range(3)]
        bonus2 = [sb(f"rk_bonus{i}", [128, TB], F32) for i in range(3)]
        AR2 = [sb(f"rk_AR{i}", [128, 8, 2, 64], BF16) for i in range(3)]
        PM2 = [sb(f"rk_PM{i}", [128, 8, 2, 128], BF16) for i in range(2)]
        TM2 = [sb(f"rk_TM{i}", [128, 8, 3, 64], BF16) for i in range(2)]
        MT2 = [sb(f"rk_MT{i}", [128, 8, 64], BF16) for i in range(2)]
        NS8 = sb("rk_NS", [128, 8, 2, 64], BF16)
        L8 = sb("rk_L", [128, 8, 64], BF16)
        maskP8 = sb("rk_maskP8", [128, 8, 2, 128], F32)
        maskL8 = sb("rk_maskL8", [128, 8, 64], F32)
        eye8 = sb("rk_eye8", [128, 8, 64], F32)
        for c in range(8):
            P.op("pool", lambda e, c=c: e.tensor_copy(out=maskP8[:, c, :, :], in_=maskP[:]), ["msk"], ["rk_m8"])
            P.op("pool", lambda e, c=c: e.tensor_copy(out=maskL8[:, c, :], in_=maskL[:]), ["msk"], ["rk_m8"])
            P.op("pool", lambda e, c=c: e.tensor_copy(out=eye8[:, c, :], in_=eye[:]), ["msk"], ["rk_m8"])
        Hf = sb("rk_Hf", [128, 64], F32)
        HG = sb("rk_HG", [128, 64], F32)
        Hb = sb("rk_Hb", [128, 64], BF16)
        Zs = sb("rk_Zs", [128, 64], BF16)
        Us = sb("rk_Us", [128, 64], BF16)
        Yt = sb("rk_Y", [128, TB], F32)
        yo = sb("rk_yo", [128, TB], BF16)
        ps_a = P.ps("rk_psa", [128, TB], F32)
        ps_b = ps_a
        W = P.ps("rk_Wps", [128, 2048], F32)
        W5 = P.ps("rk_W5ps", [128, 512], F32)
        small2 = P.ps("rk_small2", [128, 512], F32)
        yps = P.ps("rk_yps", [128, TB], F32)
        trpv = W[:, 0:1536].rearrange("p (c a t) -> p c a t", c=8, a=3)
        ppv = W[:, 0:2048].rearrange("p (c a t) -> p c a t", c=8, a=2)
        ps1v = W[:, 0:1024].rearrange("p (c t) -> p c t", c=8)
        ps2v = W[:, 1024:1536].rearrange("p (c t) -> p c t", c=8)
        lpv = W5[:, 0:512].rearrange("p (c t) -> p c t", c=8)
        sq = small2[:, 0:192].rearrange("p (a t) -> p a t", a=3)

        def dve(fn, r, w):
            P.op("dve", fn, r, w)

        def act(fn, r, w):
            P.op("act", fn, r, w)

        def pe(fn, r, w):
            P.op("pe", fn, r, w)

        def c3(ap):
            return ap.rearrange("p (c t) -> p c t", t=64)

        def blk_names(pr, j, idx):
            p3 = idx % 3
            par = idx % 2
            pc = slice(pr * 128, (pr + 1) * 128)
            vcol = lambda base: V[:, base + pr:base + pr + 1]
            js = slice(j * TB, (j + 1) * TB)
            return (pc, vcol, js, gam2[p3], gv2[p3], bonus2[p3], AR2[p3], PM2[par], TM2[par], MT2[par],
                    f"rk_gam{p3}", f"rk_g{p3}", f"rk_bonus{p3}", f"rk_AR{p3}", f"rk_PM{par}", f"rk_TM{par}", f"rk_MT{par}",
                    BK2[par], vbf2[par], f"rk_BK{par}", f"rk_vbf{par}")

        eA = sb("rk_eA", [128, TB], F32)
        eB = sb("rk_eB", [128, TB], F32)
        ebf = sb("rk_ebf", [128, TB], BF16)

        def gen_preA(pr, j, idx):
            pc, vcol, js, gam, gv, bonus, AR, PM, TM, MTs, rGAM, rG, rBON, rAR, rPM, rTM, rMT, BK, vbf, rBK, rVBF = blk_names(pr, j, idx)
            for gi, n in enumerate("rkv"):
                row0 = gi * 1024 + pr * 128
                if j == 0:
                    dve(lambda e, n=n: e.memset(zt[n][:, 0:1], 0.0), [], [f"rk_z{n}"])
                    P.dma("sp", zt[n][:, 1:TB + 1], zrwT[row0:row0 + 128, 0:TB], reads=["dram:zrwT"],
                          writes=[f"rk_z{n}"], key=f"rk_z{n}")
                else:
                    P.dma("sp", zt[n][:], zrwT[row0:row0 + 128, j * TB - 1:(j + 1) * TB], reads=["dram:zrwT"],
                          writes=[f"rk_z{n}"], key=f"rk_z{n}")
                mcol = 80 + gi * 8 + pr
                P.op("pool", lambda e, n=n: e.tensor_tensor(out=xs[n][:], in0=zt[n][:, 0:TB], in1=zt[n][:, 1:TB + 1],
                                                            op=ALU.subtract), [f"rk_z{n}"], [f"rk_s{n}"])
                dve(lambda e, n=n, mcol=mcol: e.scalar_tensor_tensor(
                    out=xs[n][:], in0=xs[n][:], scalar=V[:, mcol:mcol + 1], in1=zt[n][:, 1:TB + 1], op0=ALU.mult,
                    op1=ALU.add), [f"rk_s{n}", f"rk_z{n}", "V"], [f"rk_s{n}"])
            js = slice(j * TB, (j + 1) * TB)
            yield
            pe(lambda e: e.matmul(ps_a[:], lhsT=w2b[:, pc], rhs=twd[:, js], start=True, stop=True),
               ["rk_w", "rk_lin"], ["rk_psa"])
            act(lambda e: e.activation(out=logw[:], in_=ps_a[:], func=AF.Sigmoid, bias=vcol(104)), ["rk_psa", "V"],
                ["rk_logw"])
            pe(lambda e: e.matmul(ps_b[:], lhsT=a2b[:, pc], rhs=ads[:, js], start=True, stop=True),
               ["rk_w", "rk_lin"], ["rk_psa"])
            act(lambda e: e.activation(out=av[:], in_=ps_b[:], func=AF.Sigmoid, bias=vcol(112)), ["rk_psa", "V"],
                ["rk_a"])
            pe(lambda e: e.matmul(ps_a[:], lhsT=g2b[:, 0, pc], rhs=sg1[:, js], start=True, stop=False),
               ["rk_w", "rk_lin"], ["rk_psa"])
            pe(lambda e: e.matmul(ps_a[:], lhsT=g2b[0:32, 1, pc], rhs=sg2[:, js], start=False, stop=True),
               ["rk_w", "rk_lin"], ["rk_psa"])
            act(lambda e: e.activation(out=gv[:], in_=ps_a[:], func=AF.Copy), ["rk_psa"], [rG])
            yield
            dve(lambda e: e.tensor_scalar(out=tA[:], in0=xs["k"][:], scalar1=vcol(120), scalar2=None, op0=ALU.mult),
                ["rk_sk", "V"], ["rk_tA"])
            act(lambda e: e.activation(out=tbf[:], in_=xs["k"][:], func=AF.Square, scale=vcol(120)), ["rk_sk", "V"], ["rk_tbf"])
            pe(lambda e: e.matmul(ps_b[:], lhsT=bones_bf[:], rhs=tbf[:], start=True, stop=True),
               ["bones_bf", "rk_tbf"], ["rk_psa"])
            dve(lambda e: e.tensor_scalar(out=tB[:], in0=ps_b[:], scalar1=1e-24, scalar2=None, op0=ALU.max),
                ["rk_psa"], ["rk_tB"])
            act(lambda e: e.activation(out=tB[:], in_=tB[:], func=AF.Ln), ["rk_tB"], ["rk_tB"])
            act(lambda e: e.activation(out=tB[:], in_=tB[:], func=AF.Exp, scale=-0.5), ["rk_tB"], ["rk_tB"])
            dve(lambda e: e.tensor_tensor(out=kkn[:], in0=tA[:], in1=tB[:], op=ALU.mult), ["rk_tA", "rk_tB"],
                ["rk_kkn"])
            yield
            dve(lambda e: e.tensor_scalar(out=tA[:], in0=av[:], scalar1=vcol(128), scalar2=omka[:, pr:pr + 1],
                                          op0=ALU.mult, op1=ALU.add), ["rk_a", "V", "omka"], ["rk_tA"])
            dve(lambda e: e.tensor_tensor(out=k2[:], in0=xs["k"][:], in1=tA[:], op=ALU.mult), ["rk_sk", "rk_tA"],
                ["rk_k2"])
            yield
            dve(lambda e: e.tensor_tensor_scan(out=cum[:], data0=rmask[:], data1=logw[:], initial=0.0, op0=ALU.mult,
                                               op1=ALU.add), ["msk", "rk_logw"], ["rk_cum"])
            act(lambda e: e.activation(out=gam[:], in_=cum[:], func=AF.Exp, scale=NEG_E), ["rk_cum"], [rGAM])
            act(lambda e: e.activation(out=ginv[:], in_=cum[:], func=AF.Exp, scale=-NEG_E), ["rk_cum"], ["rk_ginv"])
            dve(lambda e: e.tensor_tensor(out=tB[:], in0=cum[:], in1=logw[:], op=ALU.subtract),
                ["rk_cum", "rk_logw"], ["rk_tB"])
            act(lambda e: e.activation(out=gprev[:], in_=tB[:], func=AF.Exp, scale=NEG_E), ["rk_tB"], ["rk_gprev"])
            yield
            dve(lambda e: e.scalar_tensor_tensor(out=AR[:, :, 0, :], in0=c3(kkn[:]), scalar=-1.0, in1=c3(gprev[:]),
                                                 op0=ALU.mult, op1=ALU.mult), ["rk_kkn", "rk_gprev"], [rAR])
            dve(lambda e: e.tensor_tensor(out=AR[:, :, 1, :], in0=c3(xs["r"][:]), in1=c3(gam[:]), op=ALU.mult),
                ["rk_sr", rGAM, rAR], [rAR])
            P.op("pool", lambda e: e.tensor_tensor(out=tA[:], in0=kkn[:], in1=av[:], op=ALU.mult), ["rk_kkn", "rk_a"],
                 ["rk_tA"])
            dve(lambda e: e.tensor_tensor(out=BK[:, :, 0, :], in0=c3(tA[:]), in1=c3(ginv[:]), op=ALU.mult),
                ["rk_tA", "rk_ginv"], [rBK])
            dve(lambda e: e.tensor_tensor(out=BK[:, :, 1, :], in0=c3(k2[:]), in1=c3(ginv[:]), op=ALU.mult),
                ["rk_k2", "rk_ginv", rBK], [rBK])
            act(lambda e: e.activation(out=vbf[:], in_=xs["v"][:], func=AF.Copy), ["rk_sv"], [rVBF])
            yield
            P.op("pool", lambda e: e.tensor_tensor(out=tC[:], in0=xs["r"][:], in1=k2[:], op=ALU.mult), ["rk_sr", "rk_k2"],
                 ["rk_tC"])
            act(lambda e: e.activation(out=tbf[:], in_=tC[:], func=AF.Copy, scale=vcol(136)),
                ["rk_tC", "V", "rk_tbf"], ["rk_tbf"])
            pe(lambda e: e.matmul(ps_b[:], lhsT=bones_bf[:], rhs=tbf[:], start=True, stop=True),
               ["bones_bf", "rk_tbf"], ["rk_psa"])
            dve(lambda e: e.tensor_tensor(out=bonus[:], in0=ps_b[:], in1=xs["v"][:], op=ALU.mult),
                ["rk_psa", "rk_sv"], [rBON])
            yield

        def gen_preB(pr, j, idx):
            pc, vcol, js, gam, gv, bonus, AR, PM, TM, MTs, rGAM, rG, rBON, rAR, rPM, rTM, rMT, BK, vbf, rBK, rVBF = blk_names(pr, j, idx)
            yield
            for c in range(8):
                cc = slice(c * 64, (c + 1) * 64)
                for h in range(2):
                    sl = slice(64 * h, 64 * h + 64)
                    for ti in range(3):
                        in_ap = BK[sl, c, ti, :] if ti < 2 else vbf[sl, cc]
                        pe(lambda e, sl=sl, ti=ti, in_ap=in_ap, c=c: e.matmul(trpv[sl, c, ti, :], lhsT=in_ap,
                                                                             rhs=ident_bf[sl, sl], start=True, stop=True),
                           [rBK, rVBF, "ident_bf"], ["rk_Wps", "rk_W2ps"])
            act(lambda e: e.activation(out=TM[:], in_=trpv, func=AF.Copy), ["rk_Wps", "rk_W2ps"], [rTM])
            yield
            for c in range(8):
                for h in range(2):
                    sl = slice(64 * h, 64 * h + 64)
                    arv = AR[sl, c, :, :].rearrange("p a t -> p (a t)")
                    pe(lambda e, sl=sl, c=c, arv=arv: e.matmul(ppv[sl, c, 0, :], lhsT=BK[sl, c, 0, :], rhs=arv, start=True,
                                                               stop=True), [rBK, rAR], ["rk_Wps", "rk_W2ps"])
                    pe(lambda e, sl=sl, c=c, arv=arv: e.matmul(ppv[sl, c, 1, :], lhsT=BK[sl, c, 1, :], rhs=arv, start=True,
                                                               stop=True), [rBK, rAR], ["rk_Wps", "rk_W2ps"])
                    pe(lambda e, sl=sl, c=c: e.matmul(lpv[sl, c, :], lhsT=AR[sl, c, 0, :], rhs=BK[sl, c, 0, :], start=True,
                                                      stop=True), [rBK, rAR], ["rk_W5ps"])
            dve(lambda e: e.tensor_tensor(out=PM[:], in0=ppv, in1=maskP8[:], op=ALU.mult), ["rk_Wps", "rk_W2ps", "rk_m8"], [rPM])
            dve(lambda e: e.tensor_tensor(out=L8[:], in0=lpv, in1=maskL8[:], op=ALU.mult), ["rk_W5ps", "rk_m8"], ["rk_L"])
            act(lambda e: e.activation(out=NS8[:, :, 0, :], in_=PM[:, :, 0, 0:64], func=AF.Copy), [rPM], ["rk_NS"])
            P.op("pool", lambda e: e.tensor_copy(out=NS8[:, :, 1, :], in_=eye8[:]), ["rk_m8", "rk_NS"], ["rk_NS"])
            yield
            for step in range(6):
                last = step == 5
                for c in range(8):
                    for h in range(2):
                        sl = slice(64 * h, 64 * h + 64)
                        if not last:
                            nsv = NS8[sl, c, :, :].rearrange("p a t -> p (a t)")
                            pe(lambda e, sl=sl, nsv=nsv, c=c: e.matmul(ps1v[sl, c, :], lhsT=L8[sl, c, :], rhs=nsv, start=True,
                                                                       stop=True), ["rk_L", "rk_NS"], ["rk_Wps"])
                            pe(lambda e, sl=sl, c=c: e.matmul(ps2v[sl, c, :], lhsT=NS8[sl, c, 0, :], rhs=L8[sl, c, :],
                                                              start=True, stop=True), ["rk_L", "rk_NS"], ["rk_W2ps"])
                        else:
                            pe(lambda e, sl=sl, c=c: e.matmul(ps1v[sl, c, 64:128], lhsT=L8[sl, c, :], rhs=NS8[sl, c, 1, :],
                                                              start=True, stop=True), ["rk_L", "rk_NS"], ["rk_Wps"])
                if not last:
                    dve(lambda e: e.tensor_copy(out=NS8[:, :, 0, :], in_=ps1v[:, :, 0:64]), ["rk_Wps", "rk_NS"], ["rk_NS"])
                    dve(lambda e: e.tensor_tensor(out=NS8[:, :, 1, :], in0=NS8[:, :, 1, :], in1=ps1v[:, :, 64:128],
                                                  op=ALU.add), ["rk_Wps", "rk_NS"], ["rk_NS"])
                    act(lambda e: e.activation(out=L8[:], in_=ps2v, func=AF.Copy), ["rk_W2ps", "rk_L"], ["rk_L"])
                else:
                    dve(lambda e: e.tensor_tensor(out=MTs[:], in0=NS8[:, :, 1, :], in1=ps1v[:, :, 64:128], op=ALU.add),
                        ["rk_Wps", "rk_NS"], [rMT])
                yield
            yield

        def gen_seq(pr, j, idx):
            pc, vcol, js, gam, gv, bonus, AR, PM, TM, MTs, rGAM, rG, rBON, rAR, rPM, rTM, rMT, BK, vbf, rBK, rVBF = blk_names(pr, j, idx)
            if j == 0:
                dve(lambda e: e.memset(Hf[:], 0.0), [], ["rk_Hf"])
                dve(lambda e: e.memset(Hb[:], 0.0), [], ["rk_Hb"])
            for c in range(8):
                cc = slice(c * 64, (c + 1) * 64)
                gcol = gam[:, c * 64 + 63:c * 64 + 64]
                for h in range(2):
                    sl = slice(64 * h, 64 * h + 64)
                    pe(lambda e, sl=sl, c=c: e.matmul(sq[sl, 0, :], lhsT=AR[sl, c, 0, :], rhs=Hb[sl, :], start=True,
                                                      stop=False), [rAR, "rk_Hb"], ["rk_small2"])
                    pe(lambda e, sl=sl, c=c: e.matmul(sq[sl, 0, :], lhsT=PM[sl, c, 1, 0:64], rhs=TM[sl, c, 2, :],
                                                      start=False, stop=True), [rPM, rTM], ["rk_small2"])
                act(lambda e: e.activation(out=Zs[:], in_=sq[:, 0, :], func=AF.Copy), ["rk_small2"], ["rk_Zs"])
                for h in range(2):
                    sl = slice(64 * h, 64 * h + 64)
                    pe(lambda e, sl=sl, c=c: e.matmul(sq[sl, 1, :], lhsT=MTs[sl, c, :], rhs=Zs[sl, :], start=True,
                                                      stop=True), [rMT, "rk_Zs"], ["rk_small2"])
                act(lambda e: e.activation(out=Us[:], in_=sq[:, 1, :], func=AF.Copy), ["rk_small2"], ["rk_Us"])
                for h in range(2):
                    sl = slice(64 * h, 64 * h + 64)
                    pe(lambda e, sl=sl, c=c: e.matmul(sq[sl, 2, :], lhsT=TM[sl, c, 0, :], rhs=Us[sl, :], start=True,
                                                      stop=False), [rTM, "rk_Us"], ["rk_small2"])
                    pe(lambda e, sl=sl, c=c: e.matmul(sq[sl, 2, :], lhsT=TM[sl, c, 1, :], rhs=TM[sl, c, 2, :],
                                                      start=False, stop=True), [rTM], ["rk_small2"])
                    pe(lambda e, sl=sl, c=c, cc=cc: e.matmul(yps[sl, cc], lhsT=Hb[sl, :], rhs=AR[sl, c, 1, :], start=True,
                                                             stop=False), [rAR, "rk_Hb"], ["rk_yps"])
                    pe(lambda e, sl=sl, c=c, cc=cc: e.matmul(yps[sl, cc], lhsT=Us[sl, :], rhs=PM[sl, c, 0, 64:128],
                                                             start=False, stop=False), ["rk_Us", rPM], ["rk_yps"])
                    pe(lambda e, sl=sl, c=c, cc=cc: e.matmul(yps[sl, cc], lhsT=TM[sl, c, 2, :], rhs=PM[sl, c, 1, 64:128],
                                                             start=False, stop=True), [rTM, rPM], ["rk_yps"])
                P.op("pool", lambda e, gcol=gcol: e.tensor_scalar(out=HG[:], in0=Hf[:], scalar1=gcol, scalar2=None,
                                                                   op0=ALU.mult), ["rk_Hf", rGAM], ["rk_HG"])
                dve(lambda e, gcol=gcol: e.scalar_tensor_tensor(out=Hf[:], in0=sq[:, 2, :], scalar=gcol, in1=HG[:],
                                                                op0=ALU.mult, op1=ALU.add),
                    ["rk_small2", "rk_HG", rGAM, "rk_Hf"], ["rk_Hf"])
                act(lambda e: e.activation(out=Hb[:], in_=Hf[:], func=AF.Copy), ["rk_Hf", "rk_Hb"], ["rk_Hb"])
                yield
            act(lambda e: e.activation(out=Yt[:], in_=yps[:], func=AF.Copy), ["rk_yps"], ["rk_Y"])
            dve(lambda e: e.tensor_copy(out=ebf[:], in_=Yt[:]), ["rk_Y", "rk_ebf"], ["rk_ebf"])
            pe(lambda e: e.matmul(ps_a[:], lhsT=bones_bf[:], rhs=ebf[:], start=True, stop=True), ["bones_bf", "rk_ebf"],
               ["rk_psa"])
            dve(lambda e: e.scalar_tensor_tensor(out=eA[:], in0=ps_a[:], scalar=-1.0 / 64, in1=Yt[:], op0=ALU.mult,
                                                 op1=ALU.add), ["rk_psa", "rk_Y"], ["rk_eA"])
            dve(lambda e: e.tensor_tensor(out=ebf[:], in0=eA[:], in1=eA[:], op=ALU.mult), ["rk_eA", "rk_ebf"],
                ["rk_ebf"])
            pe(lambda e: e.matmul(ps_a[:], lhsT=bones_bf[:], rhs=ebf[:], start=True, stop=True), ["bones_bf", "rk_ebf"],
               ["rk_psa"])
            act(lambda e: e.activation(out=eB[:], in_=ps_a[:], func=AF.Ln, bias=GN_EPS, scale=1.0 / 64),
                ["rk_psa"], ["rk_eB"])
            yield
            act(lambda e: e.activation(out=eB[:], in_=eB[:], func=AF.Exp, scale=-0.5), ["rk_eB"], ["rk_eB"])
            dve(lambda e: e.tensor_tensor(out=eA[:], in0=eA[:], in1=eB[:], op=ALU.mult), ["rk_eA", "rk_eB"], ["rk_eA"])
            dve(lambda e: e.tensor_scalar(out=eA[:], in0=eA[:], scalar1=vcol(144), scalar2=vcol(152), op0=ALU.mult,
                                          op1=ALU.add), ["rk_eA", "V"], ["rk_eA"])
            dve(lambda e: e.tensor_tensor(out=eA[:], in0=eA[:], in1=bonus[:], op=ALU.add), ["rk_eA", rBON],
                ["rk_eA"])
            dve(lambda e: e.tensor_tensor(out=yo[:], in0=eA[:], in1=gv[:], op=ALU.mult), ["rk_eA", rG], ["rk_yo"])
            P.dma("act", yrwT[pc, js], yo[:], reads=["rk_yo"], writes=["dram:yrwT"], key="rk_yo")
            yield

        blks = []
        for pr in range(DBG.get("rkp", 8)):
            for j in range(DBG.get("rkb", NB)):
                blks.append((pr, j, len(blks)))
        nb_ = len(blks)
        ORDER = DBG.get("rko", 0)
        for i in range(nb_ + 2):
            gA = gen_preA(*blks[i]) if i < nb_ else None
            gB = gen_preB(*blks[i - 1]) if 0 <= i - 1 < nb_ else None
            gS = gen_seq(*blks[i - 2]) if 0 <= i - 2 < nb_ else None
            if ORDER == 0:
                gens = [g for g in (gA, gB, gS) if g is not None]
            elif ORDER == 1:
                gens = [g for g in (gS, gB, gA) if g is not None]
            else:
                gens = [g for g in (gS, gA, gS, gB) if g is not None]
            live = set(id(g) for g in gens)
            while live:
                for g in gens:
                    if id(g) not in live:
                        continue
                    try:
                        next(g)
                    except StopIteration:
                        live.discard(id(g))
        P.barrier()
        P.stack = prev_stack


def mla_stage(P, nc, st0, V, pos, pos_w, qk_tab, d_tab, zmlaT, cqnT_full, ckvnT, w_q_up, w_kv_up, ymlaT, ones_bf, evac_copy):
    SCALE = 192 ** -0.5
    TWO_PI = 2.0 * math.pi
    QB = 342
    with contextlib.ExitStack() as st:
        prev_stack = P.stack
        P.stack = st
        sb = P.sb
        wq = sb("ml_wq", [128, 4, 1536], BF16)
        wkv = sb("ml_wkv", [128, 4, 2048], BF16)
        for kc in range(4):
            P.dma("pool", wq[:, kc, :], w_q_up[kc * 128:(kc + 1) * 128, :], writes=["ml_w"], key="ml_w")
            P.dma("pool", wkv[:, kc, :], w_kv_up[kc * 128:(kc + 1) * 128, :], writes=["ml_w"], key="ml_w")
        qk = sb("ml_qk", [128, 3, QB], F32)
        dt = sb("ml_dt", [128, 32], F32)
        P.dma("sp", qk[:], qk_tab.rearrange("p (a t) -> p a t", a=3), writes=["ml_tab"], key="ml_tab")
        P.dma("sp", dt[:], d_tab, writes=["ml_tab"], key="ml_tab")
        cs = sb("ml_cs", [64, T], F32)
        sn = sb("ml_sn", [64, T], F32)
        csw = sb("ml_csw", [64, TW], F32)
        snw = sb("ml_snw", [64, TW], F32)
        with contextlib.ExitStack() as st2:
            P.stack = st2
            posi = P.sb("ml_posi", [64, T], I32)
            ang = P.sb("ml_ang", [64, T], F32)
            tt = P.sb("ml_tt", [64, T], F32)
            kf = P.sb("ml_kf", [64, T], F32)
            for (psrc, n, cdst, sdst) in ((pos, T, cs, sn), (pos_w, TW, csw, snw)):
                P.dma("sp", posi[:, 0:n], psrc[0:1, :].to_broadcast([64, n]), writes=["ml_posi"], key="ml_posi")
                P.op("dve", lambda e: e.tensor_copy(out=ang[:, 0:n], in_=posi[:, 0:n]), ["ml_posi"], ["ml_ang"])
                P.op("dve", lambda e: e.tensor_scalar(out=ang[:, 0:n], in0=ang[:, 0:n], scalar1=V[0:64, 524:525], scalar2=None,
                                                      op0=ALU.mult), ["ml_ang", "V"], ["ml_ang"])
                for (dst, shift) in ((sdst, 0.5), (cdst, 0.75)):
                    P.op("dve", lambda e, shift=shift: e.tensor_scalar(out=tt[:, 0:n], in0=ang[:, 0:n], scalar1=1.0 / TWO_PI,
                                                                       scalar2=shift, op0=ALU.mult, op1=ALU.add),
                         ["ml_ang", "ml_tt"], ["ml_tt"])
                    P.op("dve", lambda e: e.tensor_copy(out=posi[:, 0:n], in_=tt[:, 0:n]), ["ml_tt", "ml_posi", "ml_ang"],
                         ["ml_posi"])
                    P.op("dve", lambda e: e.tensor_copy(out=kf[:, 0:n], in_=posi[:, 0:n]), ["ml_posi"], ["ml_kf"])
                    P.op("dve", lambda e: e.tensor_tensor(out=tt[:, 0:n], in0=tt[:, 0:n], in1=kf[:, 0:n], op=ALU.subtract),
                         ["ml_tt", "ml_kf"], ["ml_tt"])
                    P.op("dve", lambda e: e.tensor_scalar(out=kf[:, 0:n], in0=tt[:, 0:n], scalar1=0.0, scalar2=None,
                                                          op0=ALU.is_lt), ["ml_tt", "ml_kf"], ["ml_kf"])
                    P.op("dve", lambda e: e.scalar_tensor_tensor(out=tt[:, 0:n], in0=kf[:, 0:n], scalar=-0.5, in1=tt[:, 0:n],
                                                                 op0=ALU.add, op1=ALU.add), ["ml_tt", "ml_kf"], ["ml_tt"])
                    P.op("act", lambda e, dst=dst: e.activation(out=dst[:, 0:n], in_=tt[:, 0:n], func=AF.Sin, scale=TWO_PI),
                         ["ml_tt"], ["ml_rope"])
            P.barrier()
            P.stack = st
        kr = sb("ml_kr", [64, T], BF16)
        kn = sb("ml_kn", [128, T], BF16)
        vtm = sb("ml_vtm", [128, 32, 128], BF16)
        xq = [sb(f"ml_xq{i}", [128, 4, TB], BF16) for i in range(2)]
        raw = sb("ml_raw", [64, TB], F32)
        t1 = sb("ml_t1", [64, TB], F32)
        t2 = sb("ml_t2", [64, TB], F32)
        qn = sb("ml_qn", [128, TB], BF16)
        qr = sb("ml_qr", [64, TB], BF16)
        pT = [sb(f"ml_pT{i}", [128, TB], BF16) for i in range(3)]
        rinv = sb("ml_rinv", [128, TB], F32)
        yo = [sb(f"ml_yo{i}", [128, TB], BF16) for i in range(2)]
        psq = P.ps("ml_psq", [128, TB], F32)
        psr = P.ps("ml_psr", [64, TB], F32)
        psv = P.ps("ml_psv", [128, 4, 128], F32)
        sps = [P.ps(f"ml_sps{i}", [128, TB], F32) for i in range(2)]
        ops_ = P.ps("ml_ops", [128, TB], F32)
        lps = P.ps("ml_lps", [128, TB], F32)

        def rope(src_res, cst, snt, c0, n, out_ap, out_res, scale):
            lo, hi = slice(0, 32), slice(32, 64)
            js = slice(c0, c0 + n)
            d = lambda fn, r, w: P.op("dve", fn, r, w)
            d(lambda e: e.tensor_tensor(out=t1[lo, 0:n], in0=raw[lo, 0:n], in1=cst[lo, js], op=ALU.mult), [src_res, "ml_rope"],
              ["ml_t1"])
            d(lambda e: e.tensor_tensor(out=t2[lo, 0:n], in0=raw[hi, 0:n], in1=snt[hi, js], op=ALU.mult), [src_res, "ml_rope"],
              ["ml_t2"])
            d(lambda e: e.tensor_tensor(out=t1[hi, 0:n], in0=raw[hi, 0:n], in1=cst[hi, js], op=ALU.mult),
              [src_res, "ml_rope", "ml_t1"], ["ml_t1"])
            d(lambda e: e.tensor_tensor(out=t2[hi, 0:n], in0=raw[lo, 0:n], in1=snt[lo, js], op=ALU.mult),
              [src_res, "ml_rope", "ml_t2"], ["ml_t2"])
            d(lambda e: e.tensor_tensor(out=t1[lo, 0:n], in0=t1[lo, 0:n], in1=t2[lo, 0:n], op=ALU.subtract), ["ml_t1", "ml_t2"],
              ["ml_t1"])
            d(lambda e: e.tensor_tensor(out=t1[hi, 0:n], in0=t1[hi, 0:n], in1=t2[hi, 0:n], op=ALU.add), ["ml_t1", "ml_t2"],
              ["ml_t1"])
            P.op("act", lambda e: e.activation(out=out_ap, in_=t1[:, 0:n], func=AF.Copy, scale=scale), ["ml_t1"], [out_res])

        for j in range(NB):
            js = slice(j * TB, (j + 1) * TB)
            P.dma("sp", raw[:], zmlaT[1024:1088, js], reads=["dram:zmlaT"], writes=["ml_raw"], key="ml_raw")
            rope("ml_raw", cs, sn, j * TB, TB, kr[:, js], "ml_kr", 1.0)
        xi = 0
        for hd in range(DBG.get("mlh", 8)):
            for j in range(NB):
                js = slice(j * TB, (j + 1) * TB)
                b = xi % 2
                xi += 1
                P.dma("sp", xq[b][:], ckvnT[:, js].rearrange("(k p) t -> p k t", p=128), reads=["dram:ckvnT"],
                      writes=[f"ml_xq{b}"], key=f"ml_xq{b}")
                for kc in range(4):
                    P.op("pe", lambda e, kc=kc, b=b: e.matmul(psq[:], lhsT=wkv[:, kc, hd * 256:hd * 256 + 128],
                                                              rhs=xq[b][:, kc, :], start=(kc == 0), stop=(kc == 3)),
                         ["ml_w", f"ml_xq{b}"], ["ml_psq"])
                evac_copy(kn[:, js], psq[:], ["ml_psq"], ["ml_kn"])
                for tt_ in range(4):
                    for kc in range(4):
                        P.op("pe", lambda e, kc=kc, b=b, tt_=tt_: e.matmul(
                            psv[:, tt_, :], lhsT=xq[b][:, kc, tt_ * 128:(tt_ + 1) * 128],
                            rhs=wkv[:, kc, hd * 256 + 128:hd * 256 + 256], start=(kc == 0), stop=(kc == 3)),
                            ["ml_w", f"ml_xq{b}"], ["ml_psv"])
                evac_copy(vtm[:, j * 4:(j + 1) * 4, :], psv[:], ["ml_psv"], ["ml_vtm"])
            for jj in range(DBG.get("mlb", 3)):
                n = QB
                b = xi % 2
                xi += 1

                P.dma("sp", xq[b][:, :, 0:QB], cqnT_full[:, jj * QB:(jj + 1) * QB].rearrange("(k p) t -> p k t", p=128),
                      reads=["dram:cqwT"], writes=[f"ml_xq{b}"], key=f"ml_xq{b}")
                for kc in range(4):
                    P.op("pe", lambda e, kc=kc, b=b: e.matmul(psq[:, 0:n], lhsT=wq[:, kc, hd * 192:hd * 192 + 128],
                                                              rhs=xq[b][:, kc, 0:n], start=(kc == 0), stop=(kc == 3)),
                         ["ml_w", f"ml_xq{b}"], ["ml_psq"])
                P.op("act", lambda e: e.activation(out=qn[:, 0:n], in_=psq[:, 0:n], func=AF.Copy, scale=SCALE), ["ml_psq"],
                     ["ml_qn"])
                for kc in range(4):
                    P.op("pe", lambda e, kc=kc, b=b: e.matmul(psr[:, 0:n], lhsT=wq[:, kc, hd * 192 + 128:hd * 192 + 192],
                                                              rhs=xq[b][:, kc, 0:n], start=(kc == 0), stop=(kc == 3)),
                         ["ml_w", f"ml_xq{b}"], ["ml_psr"])
                P.op("act", lambda e: e.activation(out=raw[:, 0:n], in_=psr[:, 0:n], func=AF.Copy), ["ml_psr", "ml_raw"],
                     ["ml_raw"])
                rope("ml_raw", csw, snw, jj * QB, n, qr[:, 0:n], "ml_qr", SCALE)
                nkt = 32

                def score(kt):
                    si = kt % 2
                    ks = slice(kt * 128, (kt + 1) * 128)
                    P.op("pe", lambda e, si=si, ks=ks: e.matmul(sps[si][:, 0:n], lhsT=kn[:, ks], rhs=qn[:, 0:n],
                                                                start=True, stop=False),
                         ["ml_kn", "ml_qn"], [f"ml_sps{si}"])
                    P.op("pe", lambda e, si=si, ks=ks: e.matmul(sps[si][:, 0:n], lhsT=kr[:, ks], rhs=qr[:, 0:n],
                                                                start=False, stop=True),
                         ["ml_kr", "ml_qr"], [f"ml_sps{si}"])

                score(0)
                for kt in range(nkt):
                    si = kt % 2
                    pi = kt % 3
                    if kt + 1 < nkt:
                        score(kt + 1)
                    P.op("act", lambda e, si=si, pi=pi: e.activation(out=pT[pi][:, 0:n], in_=sps[si][:, 0:n], func=AF.Exp),
                         [f"ml_sps{si}"], [f"ml_pT{pi}"])
                    P.op("dve", lambda e, pi=pi, kt=kt, jj=jj: e.scalar_tensor_tensor(
                        out=pT[pi][:, 0:n], in0=qk[:, jj, :], scalar=dt[:, kt:kt + 1], in1=pT[pi][:, 0:n], op0=ALU.is_ge,
                        op1=ALU.mult), [f"ml_pT{pi}", "ml_tab"], [f"ml_pT{pi}"])
                    P.op("pe", lambda e, pi=pi, kt=kt: e.matmul(
                        ops_[:, 0:n], lhsT=vtm[:, kt, :], rhs=pT[pi][:, 0:n], start=(kt == 0), stop=(kt == nkt - 1)),
                        ["ml_vtm", f"ml_pT{pi}"], ["ml_ops"])
                    P.op("pe", lambda e, pi=pi, kt=kt: e.matmul(
                        lps[:, 0:n], lhsT=ones_bf[:], rhs=pT[pi][:, 0:n], start=(kt == 0), stop=(kt == nkt - 1)),
                        ["ones_bf", f"ml_pT{pi}"], ["ml_lps"])
                P.op("dve", lambda e: e.tensor_scalar(out=rinv[:, 0:n], in0=lps[:, 0:n], scalar1=1e-30, scalar2=None,
                                                      op0=ALU.max), ["ml_lps"], ["ml_rinv"])
                P.op("dve", lambda e: e.reciprocal(out=rinv[:, 0:n], in_=rinv[:, 0:n]), ["ml_rinv"], ["ml_rinv"])
                ob = jj % 2
                P.op("dve", lambda e, ob=ob: e.tensor_tensor(out=yo[ob][:, 0:n], in0=ops_[:, 0:n], in1=rinv[:, 0:n], op=ALU.mult),
                     ["ml_ops", "ml_rinv"], [f"ml_yo{ob}"])
                P.dma("act", ymlaT[hd * 128:(hd + 1) * 128, jj * QB:(jj + 1) * QB], yo[ob][:, 0:n], reads=[f"ml_yo{ob}"],
                      writes=["dram:ymlaT"], key=f"ml_yo{ob}")
        P.barrier()
        P.stack = prev_stack


_CACHE = {}


def make_in_maps(inputs):
    sq = lambda a: np.ascontiguousarray(np.asarray(a)[0])
    V = pack_vecs(inputs)
    common = {
        "vecs": V,
        "w_in": sq(inputs["w_in"]), "rw_w2": sq(inputs["rw_w2"]), "rw_a2": sq(inputs["rw_a2"]), "rw_g2": sq(inputs["rw_g2"]),
        "w_q_up": sq(inputs["mla_w_q_up"]), "w_kv_up": sq(inputs["mla_w_kv_up"]),
        "w_brw": sq(inputs["w_branch_rw"]), "w_bmla": sq(inputs["w_branch_mla"]), "w_out": sq(inputs["w_out"]),
        "w_up": sq(inputs["w_up"]), "w_down": sq(inputs["w_down"]), "w_ple": sq(inputs["w_ple"]),
        "w_pg": sq(inputs["w_ple_gate"]),
    }
    i_ = np.arange(342)
    qk = np.zeros((128, 3, 342), np.float32)
    for jj in range(3):
        qc = np.floor_divide(342 * jj - 2 + i_, 64).astype(np.float32)
        qk[:, jj, :] = qc[None, :] - (np.arange(128)[:, None] >= 64).astype(np.float32)
    common["gain_bc"] = np.ascontiguousarray(np.broadcast_to(np.asarray(inputs["pre_mix_norm"], np.float32).reshape(1, -1), (128, 2048)))
    common["qk_tab"] = np.ascontiguousarray(qk.reshape(128, 3 * 342))
    xs = np.asarray(inputs["x"], np.float32)
    ps = np.asarray(inputs["p"], np.float32)[0]
    posn = np.asarray(inputs["positions"], np.int32)
    maps = []
    for c in range(8):
        b, q = c // 4, c % 4
        m = dict(common)
        m["x"] = np.ascontiguousarray(xs[b])
        xwin = np.zeros((1152, xs.shape[2]), np.float32)
        lo_ = 1024 * q - 2
        if lo_ < 0:
            xwin[2:TW] = xs[b, 0:1024]
        else:
            xwin[0:TW] = xs[b, lo_:lo_ + TW]
        m["xw"] = xwin
        m["p"] = np.ascontiguousarray(ps[b, 1024 * q:1024 * (q + 1)])
        m["pos"] = np.ascontiguousarray(posn[b:b + 1])
        pw = np.zeros((1, TW), np.int32)
        lo = 1024 * q - 2
        if lo < 0:
            pw[0, 2:] = posn[b, 0:1024]
        else:
            pw[0, :] = posn[b, lo:lo + TW]
        m["pos_w"] = pw
        m["d_tab"] = np.ascontiguousarray(np.broadcast_to((2.0 * np.arange(32) - 16.0 * q).astype(np.float32)[None, :], (128, 32)))
        maps.append(m)
    return maps


def kernel(**inputs):
    if "nc" not in _CACHE:
        _CACHE["nc"] = build_program()
    nc = _CACHE["nc"]
    maps = make_in_maps(inputs)
    res = run_bass_kernel_spmd(nc, maps, core_ids=list(range(8)))
    outs = [np.asarray(res.results[c]["out"], np.float32) for c in range(8)]
    return np.stack([np.concatenate(outs[0:4], axis=0), np.concatenate(outs[4:8], axis=0)], axis=0)
```
